# Optimizing a Trainium2 kernel written in Bass

```python
import jax, jax.numpy as jnp
from jax import lax
import numpy as np

D_MODEL = 1024
BATCH = 8
SEQ = 4096
DEPTH = 4
DEC_BATCH = 4
DEC_SEQ = 4096
PAST_LEN = 128

CHUNK = 128
A_GROUPS = 8
A_WIDTH = D_MODEL
A_GROUP_DIM = A_WIDTH // A_GROUPS
B_PATTERNS = ((128, 1), (512, 4), (2048, 16))
B_HEADS = 4
B_HEAD_DIM = 128
B_GROUP_WIDTH = B_HEADS * B_HEAD_DIM
N_ATTN_HEADS = len(B_PATTERNS) * B_HEADS
ALIBI_SLOPES = np.array([2.0 ** (-8.0 * (h + 1) / N_ATTN_HEADS) for h in range(N_ATTN_HEADS)], dtype=np.float32).reshape(len(B_PATTERNS), B_HEADS)
IN_SIZES = (A_WIDTH, A_WIDTH) + (B_GROUP_WIDTH,) * (3 * len(B_PATTERNS)) + (D_MODEL, D_MODEL)
D_IN = sum(IN_SIZES)
N_EXPERTS = 32
TOP_K = 4
D_EXPERT = D_MODEL
SWIGLU_LIMIT = 7.0
SWIGLU_ALPHA = 1.702
ROUTE_BLOCK = 128
DN_ALPHA = (2.0 * DEPTH) ** 0.25
DN_BETA = (8.0 * DEPTH) ** -0.25
LN_EPS = 1e-5
NEG_INF = -1e30

kernel_name = 'hybrid_gmlp_dilated_moe_encoder'


def layer_norm(x, g, b):
    xf = x.astype(jnp.float32)
    xc = xf - xf.mean(-1, keepdims=True)
    var = jnp.mean(xc * xc, -1, keepdims=True)
    return (xc * lax.rsqrt(var + LN_EPS) * g.astype(jnp.float32) + b.astype(jnp.float32)).astype(x.dtype)


def spatial_gating(u, v, ln_g, ln_b, w_s, b_s):
    bsz, s, _ = v.shape
    vn = layer_norm(v, ln_g, ln_b).reshape(bsz, s // CHUNK, CHUNK, A_GROUPS, A_GROUP_DIM)
    mixed = jnp.einsum('gts,bnsgc->bntgc', w_s, vn) + b_s.T[None, None, :, :, None]
    return u * mixed.reshape(bsz, s, A_WIDTH)


def dilated_window_attention(q, k, v, dilation, n_side, slopes):
    bsz, s, h, hd = q.shape
    L = s // dilation
    nb = -(-L // n_side)
    Lp = nb * n_side

    def to_sub(t):
        return t.reshape(bsz, L, dilation, h, hd).transpose(0, 2, 1, 3, 4)

    qb = jnp.pad(to_sub(q), ((0, 0), (0, 0), (0, Lp - L), (0, 0), (0, 0))).reshape(bsz, dilation, nb, n_side, h, hd)
    pad_kv = ((0, 0), (0, 0), (n_side, Lp - L + n_side), (0, 0), (0, 0))
    kb = jnp.pad(to_sub(k), pad_kv).reshape(bsz, dilation, nb + 2, n_side, h, hd)
    vb = jnp.pad(to_sub(v), pad_kv).reshape(bsz, dilation, nb + 2, n_side, h, hd)

    def band(t):
        return jnp.concatenate([t[:, :, 0:nb], t[:, :, 1:nb + 1], t[:, :, 2:nb + 2]], axis=3)

    kw, vw = band(kb), band(vb)
    a_idx = np.arange(n_side)[:, None]
    c_idx = np.arange(3 * n_side)[None, :]
    rel = c_idx - n_side - a_idx
    q_sub = np.arange(nb)[:, None, None] * n_side + a_idx[None]
    k_sub = q_sub + rel[None]
    valid = (np.abs(rel)[None] <= n_side) & (k_sub >= 0) & (k_sub < L)
    alibi = (-(slopes[:, None, None] * (dilation * np.abs(rel))[None])).astype(np.float32)

    scores = jnp.einsum('bdiahk,bdichk->bdihac', qb, kw, preferred_element_type=jnp.float32) * (hd ** -0.5)
    scores = jnp.where(valid[None, None, :, None], scores + alibi[None, None, None], NEG_INF)
    m = scores.max(-1, keepdims=True)
    p = jnp.exp(scores - m)
    den = p.sum(-1, keepdims=True)
    out = jnp.einsum('bdihac,bdichk->bdiahk', (p / den).astype(v.dtype), vw)
    lse = (m + jnp.log(den))[..., 0]
    out = out.reshape(bsz, dilation, Lp, h, hd)[:, :, :L].transpose(0, 2, 1, 3, 4).reshape(bsz, s, h, hd)
    lse = lse.transpose(0, 1, 2, 4, 3).reshape(bsz, dilation, Lp, h)[:, :, :L].transpose(0, 2, 1, 3).reshape(bsz, s, h)
    return out, lse


def token_mixer(x, w_in, b_in, ln_v_g, ln_v_b, w_s, b_s, w_pa, w_pb, w_o):
    bsz, s, _ = x.shape
    z = jnp.dot(x, w_in) + b_in
    parts = jnp.split(z, [int(i) for i in np.cumsum(IN_SIZES)[:-1]], axis=-1)
    a_out = spatial_gating(jax.nn.gelu(parts[0], approximate=False), jax.nn.gelu(parts[1], approximate=False),
                           ln_v_g, ln_v_b, w_s, b_s)
    outs, lses = [], []
    for g, (window, dilation) in enumerate(B_PATTERNS):
        q, k, v = [t.reshape(bsz, s, B_HEADS, B_HEAD_DIM) for t in parts[2 + 3 * g: 5 + 3 * g]]
        o, lse = dilated_window_attention(q, k, v, dilation, (window // 2) // dilation, ALIBI_SLOPES[g])
        outs.append(o)
        lses.append(lse)
    wts = jax.nn.softmax(jnp.stack(lses), axis=0)
    b_out = jnp.einsum('gbsh,gbshk->bshk', wts.astype(x.dtype), jnp.stack(outs)).reshape(bsz, s, B_GROUP_WIDTH)
    gate_a = jax.nn.sigmoid(parts[-2])
    gate_b = jax.nn.sigmoid(parts[-1])
    merged = gate_a * jnp.dot(a_out, w_pa) + gate_b * jnp.dot(b_out, w_pb)
    return jnp.dot(merged, w_o)


def expert_ffn(xb, e, w_gu, b_gu, w_down, b_down):
    gu = jnp.dot(xb, w_gu[e]) + b_gu[e]
    gate, up = jnp.split(gu, 2, axis=-1)
    gate = jnp.minimum(gate, SWIGLU_LIMIT)
    up = jnp.clip(up, -SWIGLU_LIMIT, SWIGLU_LIMIT)
    h = (up + 1.0) * gate * jax.nn.sigmoid(SWIGLU_ALPHA * gate)
    return jnp.dot(h, w_down[e]) + b_down[e]


def moe(x, w_r, b_r, w_gu, b_gu, w_down, b_down):
    bsz, s, dm = x.shape
    t = bsz * s
    xf = x.reshape(t, dm)
    logits = jnp.dot(xf, w_r, preferred_element_type=jnp.float32) + b_r.astype(jnp.float32)
    top_v, top_i = lax.top_k(logits, TOP_K)
    gates = jax.nn.softmax(top_v, axis=-1).astype(x.dtype)
    n = t * TOP_K
    flat_e = top_i.reshape(n).astype(jnp.int32)
    flat_tok = jnp.repeat(jnp.arange(t, dtype=jnp.int32), TOP_K)
    order = jnp.argsort(flat_e)
    se, stok, sg = flat_e[order], flat_tok[order], gates.reshape(n)[order]
    counts = jnp.bincount(flat_e, length=N_EXPERTS)
    padded = (counts + ROUTE_BLOCK - 1) // ROUTE_BLOCK * ROUTE_BLOCK
    start = jnp.cumsum(counts) - counts
    pend = jnp.cumsum(padded)
    pstart = pend - padded
    dest = pstart[se] + jnp.arange(n, dtype=jnp.int32) - start[se]
    cap = -(-n // ROUTE_BLOCK) * ROUTE_BLOCK + N_EXPERTS * ROUTE_BLOCK
    n_blk = cap // ROUTE_BLOCK
    buf_tok = jnp.full((cap,), t, jnp.int32).at[dest].set(stok)
    blk_e = jnp.minimum(jnp.searchsorted(pend, jnp.arange(n_blk, dtype=jnp.int32) * ROUTE_BLOCK, side='right'), N_EXPERTS - 1)
    xpad = jnp.concatenate([xf, jnp.zeros((1, dm), xf.dtype)], axis=0)
    xb = xpad[buf_tok].reshape(n_blk, ROUTE_BLOCK, dm)
    yb = lax.map(lambda args: expert_ffn(args[0], args[1], w_gu, b_gu, w_down, b_down), (xb, blk_e))
    ya = yb.reshape(cap, dm)[dest] * sg[:, None]
    y = jax.ops.segment_sum(ya, stok, num_segments=t)
    return y.reshape(bsz, s, dm)


def encoder_trunk(x, params):
    (w_in, b_in, ln_v_g, ln_v_b, w_s, b_s, w_pa, w_pb, w_o, ln1_g, ln1_b,
     w_r, b_r, w_gu, b_gu, w_down, b_down, ln2_g, ln2_b) = params
    for l in range(DEPTH):
        mix = token_mixer(x, w_in[l], b_in[l], ln_v_g[l], ln_v_b[l], w_s[l], b_s[l], w_pa[l], w_pb[l], w_o[l])
        x = layer_norm(DN_ALPHA * x + mix, ln1_g[l], ln1_b[l])
        ffn = moe(x, w_r[l], b_r[l], w_gu[l], b_gu[l], w_down[l], b_down[l])
        x = layer_norm(DN_ALPHA * x + ffn, ln2_g[l], ln2_b[l])
    return x


def setup_inputs(seed: int = 0) -> dict:
    key = jax.random.key(seed)
    ks = jax.random.split(key, 24)
    nrm = jax.random.normal
    f32 = jnp.float32
    L, D, E, F = DEPTH, D_MODEL, N_EXPERTS, D_EXPERT
    return {
        'x_prompt': nrm(ks[0], (BATCH, SEQ, D), f32),
        'x_sample': nrm(ks[1], (DEC_BATCH, DEC_SEQ, D), f32),
        'w_in': nrm(ks[2], (L, D, D_IN), f32) * D ** -0.5,
        'b_in': nrm(ks[3], (L, D_IN), f32) * 0.02,
        'ln_v_g': 1.0 + 0.1 * nrm(ks[4], (L, A_WIDTH), f32),
        'ln_v_b': 0.02 * nrm(ks[5], (L, A_WIDTH), f32),
        'w_s': nrm(ks[6], (L, A_GROUPS, CHUNK, CHUNK), f32) * CHUNK ** -0.5,
        'b_s': 1.0 + 0.1 * nrm(ks[7], (L, A_GROUPS, CHUNK), f32),
        'w_pa': nrm(ks[8], (L, A_WIDTH, D), f32) * (A_WIDTH ** -0.5 * DN_BETA),
        'w_pb': nrm(ks[9], (L, B_GROUP_WIDTH, D), f32) * (B_GROUP_WIDTH ** -0.5 * DN_BETA),
        'w_o': nrm(ks[10], (L, D, D), f32) * (D ** -0.5 * DN_BETA),
        'ln1_g': 1.0 + 0.1 * nrm(ks[11], (L, D), f32),
        'ln1_b': 0.02 * nrm(ks[12], (L, D), f32),
        'w_r': nrm(ks[13], (L, D, E), f32) * D ** -0.5,
        'b_r': 0.01 * nrm(ks[14], (L, E), f32),
        'w_gu': nrm(ks[15], (L, E, D, 2 * F), f32) * (D ** -0.5 * DN_BETA),
        'b_gu': 0.02 * nrm(ks[16], (L, E, 2 * F), f32),
        'w_down': nrm(ks[17], (L, E, F, D), f32) * (F ** -0.5 * DN_BETA),
        'b_down': 0.02 * nrm(ks[18], (L, E, D), f32),
        'ln2_g': 1.0 + 0.1 * nrm(ks[19], (L, D), f32),
        'ln2_b': 0.02 * nrm(ks[20], (L, D), f32),
    }


def reference(x_prompt, x_sample, w_in, b_in, ln_v_g, ln_v_b, w_s, b_s, w_pa, w_pb, w_o, ln1_g, ln1_b,
              w_r, b_r, w_gu, b_gu, w_down, b_down, ln2_g, ln2_b):
    params = (w_in, b_in, ln_v_g, ln_v_b, w_s, b_s, w_pa, w_pb, w_o, ln1_g, ln1_b,
              w_r, b_r, w_gu, b_gu, w_down, b_down, ln2_g, ln2_b)
    y_prompt = encoder_trunk(x_prompt, params)
    y_sample = encoder_trunk(x_sample, params)
    return (y_prompt, y_sample)
```

```python
import numpy as np
import ml_dtypes
from contextlib import ExitStack
import concourse.bass as bass
import concourse.mybir as mybir
from concourse.bass_utils import run_bass_kernel_spmd

F32 = mybir.dt.float32
BF16 = mybir.dt.bfloat16
I32 = mybir.dt.int32
AF = mybir.ActivationFunctionType
ALU = mybir.AluOpType
AX = mybir.AxisListType

S = 4096
D = 1024
DIN = 8704
NE = 32
TOPK = 4
PATTERNS = ((128, 1), (512, 4), (2048, 16))
NHEAD = 12
SLOPES = np.array([2.0 ** (-8.0 * (h + 1) / NHEAD) for h in range(NHEAD)], dtype=np.float32).reshape(3, 4)
DEPTH_FULL = 4
ALPHA = (2.0 * DEPTH_FULL) ** 0.25
EPS = 1e-5
NEG = -1e30
LIMIT = 7.0
SW_ALPHA = 1.702
SCALE = 128.0 ** -0.5
BIG = 1 << 28


class Buf:
    def __init__(self, t, name):
        self.t = t
        self.name = name
        self.lw = None
        self.rd = {}
        self.ds = None

    def __getitem__(self, idx):
        return self.t[idx]


class DSem:
    def __init__(self, h, key):
        self.h = h
        self.key = key
        self.cnt = 0


class KB:
    SAME_ENG = True

    def __init__(self, nc):
        self.nc = nc
        self.E = {"pe": nc.tensor, "act": nc.scalar, "dve": nc.vector, "pool": nc.gpsimd, "sp": nc.sync}
        self.sem = {k: nc.alloc_semaphore("es_" + k) for k in self.E}
        self.cnt = {k: 0 for k in self.E}
        self.seen = {k: {} for k in self.E}
        self.semobj = dict(self.sem)
        self.dsems = {}
        self.dfree = []
        self.uid = 0
        self.psrr = 0

    def sb(self, stack, name, shape, dt):
        self.uid += 1
        t = stack.enter_context(self.nc.sbuf_tensor(f"{name}_{self.uid}", list(shape), dt))
        b = Buf(t, name)
        stack.callback(self._release, b)
        return b

    def ps(self, stack, name, shape, dt):
        self.uid += 1
        t = stack.enter_context(self.nc.psum_tensor(f"{name}_{self.uid}", list(shape), dt))
        return Buf(t, name)

    def _release(self, b):
        if b.ds is not None:
            self.dfree.append(b.ds)
            b.ds = None

    def _getds(self, b):
        if b.ds is None:
            if self.dfree:
                b.ds = self.dfree.pop()
            else:
                key = f"d{len(self.dsems)}"
                d = DSem(self.nc.alloc_semaphore("ds_" + key), key)
                self.dsems[key] = d
                self.semobj[key] = d.h
                b.ds = d
        return b.ds

    def _deps(self, reads, writes):
        d = []
        for b in reads:
            if b.lw is not None:
                d.append(b.lw)
        for b in writes:
            if b.lw is not None:
                d.append(b.lw)
            d.extend(b.rd.items())
        return d

    def _wait(self, eng, deps):
        need = {}
        for k, v in deps:
            if k == eng and (eng == "pe" or not self.SAME_ENG):
                continue
            if need.get(k, 0) < v:
                need[k] = v
        for k, v in need.items():
            if self.seen[eng].get(k, 0) < v:
                self.E[eng].wait_ge(self.semobj[k], v)
                self.seen[eng][k] = v

    def _mark(self, tok, reads, writes):
        k, v = tok
        for b in writes:
            b.lw = tok
            b.rd = {}
        for b in reads:
            if b in writes:
                continue
            if b.rd.get(k, 0) < v:
                b.rd[k] = v

    def op(self, eng, fn, reads=(), writes=()):
        self._wait(eng, self._deps(reads, writes))
        ins = fn(self.E[eng])
        self.cnt[eng] += 1
        ins.then_inc(self.sem[eng], 1)
        tok = (eng, self.cnt[eng])
        self._mark(tok, reads, writes)
        return tok

    def pe(self, fns, reads=(), writes=()):
        self._wait("pe", self._deps(reads, writes))
        ins = None
        for f in fns:
            ins = f(self.nc.tensor)
        self.cnt["pe"] += 1
        ins.then_inc(self.sem["pe"], 1)
        tok = ("pe", self.cnt["pe"])
        self._mark(tok, reads, writes)
        return tok

    def mm(self, out, pairs, reads=(), writes=()):
        n = len(pairs)
        fns = []
        for i, (l, r) in enumerate(pairs):
            fns.append(lambda t, l=l, r=r, i=i: t.matmul(out, lhsT=l, rhs=r, start=(i == 0), stop=(i == n - 1)))
        return self.pe(fns, reads, writes)

    def dma(self, q, sbuf, fn, reads=(), writes=()):
        self._wait(q, self._deps(reads, writes))
        d = self._getds(sbuf)
        ins = fn(self.E[q])
        d.cnt += 16
        ins.then_inc(d.h, 16)
        tok = (d.key, d.cnt)
        self._mark(tok, reads, writes)
        return tok

    def load(self, q, dst, dst_ap, src_ap, **kw):
        return self.dma(q, dst, lambda e: e.dma_start(out=dst_ap, in_=src_ap, **kw), reads=(), writes=(dst,))

    def store(self, q, src, dst_ap, src_ap, **kw):
        return self.dma(q, src, lambda e: e.dma_start(out=dst_ap, in_=src_ap, **kw), reads=(src,), writes=())

    def barrier(self):
        tot = dict(self.cnt)
        for k, d in self.dsems.items():
            tot[k] = d.cnt
        for e in self.E:
            for k, v in tot.items():
                if v > 0 and self.seen[e].get(k, 0) < v:
                    self.E[e].wait_ge(self.semobj[k], v)
                    self.seen[e][k] = v

    def nextps(self, banks):
        b = banks[self.psrr % len(banks)]
        self.psrr += 1
        return b


def _consts(NT, NBLK, CAP):
    c = {}
    c["ident_f"] = np.eye(128, dtype=np.float32)
    c["ident_b"] = np.eye(128, dtype=np.float32).astype(ml_dtypes.bfloat16)
    tri = (np.arange(128)[:, None] < np.arange(128)[None, :]).astype(np.float32)
    c["tri_b"] = tri.astype(ml_dtypes.bfloat16)
    c["ones_b"] = np.ones((128, 128), dtype=ml_dtypes.bfloat16)
    c["ones_f"] = np.ones((128, 128), dtype=np.float32)
    tab = np.zeros((3, 128, 12, 256), dtype=np.float32)
    a = np.arange(128)[:, None]
    cc = np.arange(256)[None, :]
    rel = cc - 64 - a
    for g, (win, d) in enumerate(PATTERNS):
        for h in range(4):
            base = np.where(np.abs(rel) <= 64, -SLOPES[g, h] * d * np.abs(rel), NEG).astype(np.float32)
            for var in range(3):
                t = base.copy()
                if var == 1:
                    t[:, :64] = NEG
                if var == 2:
                    t[:, 192:] = NEG
                tab[g, :, h * 3 + var, :] = t
    c["att_bias"] = tab
    c["iota32"] = np.tile(np.arange(32, dtype=np.float32)[None, :], (128, 1))
    c["ecap"] = np.tile((np.arange(32, dtype=np.float32) * CAP)[None, :], (128, 1))
    c["pcol"] = np.arange(128, dtype=np.float32)[:, None].copy()
    c["jthr"] = np.tile((np.arange(64, dtype=np.float32) * 128.0)[None, :], (32, 1))
    c["iota_b"] = np.tile(np.arange(NBLK, dtype=np.float32)[None, :], (128, 1))
    c["triu32"] = (np.arange(32)[:, None] <= np.arange(32)[None, :]).astype(np.float32)
    tok = np.zeros((128, NT // 128, 4), dtype=np.int32)
    tok[:, :, 0] = np.arange(NT // 128)[None, :] * 128 + np.arange(128)[:, None]
    c["tokid4"] = tok
    ri = np.zeros((128, 128, 4), dtype=np.int32)
    ri[:, :, 0] = NT
    c["rinit"] = ri
    return c


CONST_DT = {"ident_f": F32, "ident_b": BF16, "tri_b": BF16, "ones_b": BF16, "ones_f": F32, "att_bias": F32,
            "iota32": F32, "ecap": F32, "pcol": F32, "jthr": F32, "iota_b": F32, "triu32": F32,
            "tokid4": I32, "rinit": I32}


def build_program(NL=4, NSEQ=2, debug=False, phases=None):
    NT = NSEQ * S
    NTILE = NT // 128
    CAP = NT
    NBLK = NT * TOPK // 128 + NE
    RROWS = NE * CAP + 128
    nc = bass.Bass("TRN2", target_bir_lowering=False)
    K = KB(nc)
    sk = "ExternalOutput" if debug else "Internal"

    def din(name, shape, dt=F32):
        return nc.dram_tensor(name, list(shape), dt, kind="ExternalInput").ap()

    def dscr(name, shape, dt=F32):
        return nc.dram_tensor(name, list(shape), dt, kind=sk).ap()

    x_in = din("x", [NT, D])
    w_in = din("w_in", [NL, D, DIN])
    w_st = din("w_st", [NL, 128, 8, 128])
    w_pa = din("w_pa", [NL, D, D])
    w_pb = din("w_pb", [NL, 512, D])
    w_o = din("w_o", [NL, D, D])
    w_r = din("w_r", [NL, D, NE])
    w_gu = din("w_gu", [NL, NE, D, 2 * D])
    w_dn = din("w_down", [NL, NE, D, D])
    b_gu = din("b_gu", [NL, NE, 2 * D])
    b_dn = din("b_down", [NL, NE, D])
    bin_col = din("bin_col", [NL, 128, 68])
    bin_row = din("bin_row", [NL, 1, DIN])
    lnv_col = din("lnv_col", [NL, 128, 16])
    bs_row = din("bs_row", [NL, 1, D])
    ln1_row = din("ln1_row", [NL, 2, D])
    ln2_row = din("ln2_row", [NL, 2, D])
    br_row = din("br_row", [NL, 1, NE])
    cshape = {k: v.shape for k, v in _consts(128 * 2, 8, 1).items()}
    cshape["iota_b"] = (128, NBLK)
    cshape["tokid4"] = (128, NTILE, 4)
    cin = {k: din("c_" + k, cshape[k], CONST_DT[k]) for k in cshape}
    y_out = nc.dram_tensor("y", [NT, D], F32, kind="ExternalOutput").ap()

    XS = dscr("XS", [NT, D])
    X1 = dscr("X1", [NT + 128, D])
    NTP = [NSEQ * d * (S // d + 128) for (_, d) in PATTERNS]
    QT = [dscr(f"QT{g}", [4, 128, NTP[g]], BF16) for g in range(3)]
    KT = [dscr(f"KT{g}", [4, 128, NTP[g]], BF16) for g in range(3)]
    VV = [dscr(f"VV{g}", [NTP[g], 512], BF16) for g in range(3)]
    OGall = dscr("OG", [3, NT, 516])
    OG = [OGall[g] for g in range(3)]
    MA = dscr("MA", [8, 128, NT], BF16)
    ROUTE = dscr("ROUTE", [RROWS, 4], I32)
    YB = dscr("YB", [NBLK * 128, D])

    top = ExitStack()
    ident_f = K.sb(top, "ident_f", [128, 128], F32)
    ident_b = K.sb(top, "ident_b", [128, 128], BF16)
    ones_b = K.sb(top, "ones_b", [128, 128], BF16)
    ones_f = K.sb(top, "ones_f", [128, 128], F32)
    tri_b = K.sb(top, "tri_b", [128, 128], BF16)
    iota32 = K.sb(top, "iota32", [128, 32], F32)
    ecap = K.sb(top, "ecap", [128, 32], F32)
    pcol = K.sb(top, "pcol", [128, 1], F32)
    tokid4 = K.sb(top, "tokid4", [128, NTILE, 4], I32)
    GATES = K.sb(top, "GATES", [128, NTILE, 4], F32)
    EKF = K.sb(top, "EKF", [128, NTILE, 4], F32)
    POSK = K.sb(top, "POSK", [128, NTILE, 4], F32)
    RUN = K.sb(top, "RUN", [128, 32], F32)
    BS128 = K.sb(top, "BS128", [128, 32], F32)
    RIDX = K.sb(top, "RIDX", [128, NBLK], I32)
    WIDX = K.sb(top, "WIDX", [128, NBLK], I32)
    BIDX = K.sb(top, "BIDX", [128, NBLK], I32)
    PSB = [K.ps(top, f"psb{i}", [128, 512], F32) for i in range(8)]

    for nm, b in (("ident_f", ident_f), ("ident_b", ident_b), ("ones_b", ones_b), ("ones_f", ones_f), ("tri_b", tri_b),
                  ("iota32", iota32), ("ecap", ecap), ("pcol", pcol), ("tokid4", tokid4)):
        K.load("sp", b, b.t[:], cin[nm])

    def evac(i, out_ap, in_ap, reads, writes):
        if i % 2 == 0:
            return K.op("act", lambda e: e.activation(out=out_ap, in_=in_ap, func=AF.Copy), reads, writes)
        return K.op("dve", lambda e: e.tensor_copy(out=out_ap, in_=in_ap), reads, writes)

    def load_xT(st, xt, xT, src_rows_ap, TM):
        nj = TM // 128
        K.load("sp", xt, xt.t[:, 0:nj, :], src_rows_ap)

    def transposes_x(xt, xT, TM, evi=0):
        nj = TM // 128
        for kc in range(8):
            pb = K.nextps(PSB)
            fns = [lambda t, j=j, kc=kc, pb=pb: t.transpose(out=pb.t[:, j * 128:(j + 1) * 128], in_=xt.t[:, j, kc * 128:(kc + 1) * 128], identity=ident_f.t[:])
                   for j in range(nj)]
            K.pe(fns, reads=(xt, ident_f), writes=(pb,))
            evac(kc + evi, xT.t[:, kc, 0:TM], pb.t[:, 0:TM], (pb,), (xT,))

    def ln_stats(st, v_ap_fn, vbuf, mv, stt, rstd):
        K.op("dve", lambda e: e.bn_stats(out=stt.t[:, 0:6], in_=v_ap_fn(0)), (vbuf,), (stt,))
        K.op("dve", lambda e: e.bn_stats(out=stt.t[:, 6:12], in_=v_ap_fn(1)), (vbuf,), (stt,))
        K.op("dve", lambda e: e.bn_aggr(out=mv.t[:, 0:2], in_=stt.t[:, 0:12].rearrange("p (c s) -> p c s", s=6)), (stt,), (mv,))
        K.op("act", lambda e: e.activation(out=rstd.t[:, 0:1], in_=mv.t[:, 1:2], func=AF.Sqrt, bias=EPS), (mv,), (rstd,))
        K.op("dve", lambda e: e.reciprocal(out=rstd.t[:, 0:1], in_=rstd.t[:, 0:1]), (rstd,), (rstd,))

    with ExitStack() as st:
        zt = K.sb(st, "zt", [128, 4096], BF16)
        K.op("pool", lambda e: e.memset(zt.t[:], 0.0), (), (zt,))
        for g in range(3):
            n = NTP[g]
            for h in range(4):
                for c0 in range(0, n, 4096):
                    w = min(4096, n - c0)
                    K.store("sp", zt, KT[g][h, :, c0:c0 + w], zt.t[:, 0:w])
            vv = VV[g].rearrange("(a p) f -> p a f", p=128)
            na = n // 128
            for a0 in range(0, na, 8):
                w = min(8, na - a0)
                K.store("sp", zt, vv[:, a0:a0 + w, :], zt.t[:, 0:w * 512].rearrange("p (a f) -> p a f", f=512))
        K.store("sp", zt, X1[NT:NT + 128, :], zt.t[:].bitcast(F32)[:, 0:1024])
        K.barrier()

    for l in range(NL):
        XA = x_in if l == 0 else XS
        XO = y_out if l == NL - 1 else XS
        if phases is None or "qkv" in phases:
          with ExitStack() as st:
            wg = [K.sb(st, f"wg{i}", [128, 8, 1536], BF16) for i in range(2)]
            bcol = K.sb(st, "bcol", [128, 68], F32)
            brow = K.sb(st, "brow", [1, DIN], BF16)
            xts = [K.sb(st, f"xt{i}", [128, 4, D], F32) for i in range(2)]
            xTs = [K.sb(st, f"xT{i}", [128, 8, 512], BF16) for i in range(2)]
            qks = [K.sb(st, f"qk{i}", [128, 8, 512], BF16) for i in range(2)]
            vss = [K.sb(st, f"vs{i}", [128, 4, 512], BF16) for i in range(2)]
            K.load("sp", bcol, bcol.t[:], bin_col[l])
            K.load("pool", brow, brow.t[:], bin_row[l])
            it = 0
            for g, (win, d) in enumerate(PATTERNS):
                L = S // d
                LP = L + 128
                TM = min(512, L)
                nj = TM // 128
                c0 = 2048 + g * 1536
                W = wg[g % 2]
                K.load("pool", W, W.t[:], w_in[l][:, c0:c0 + 1536].rearrange("(kc p) c -> p kc c", p=128))
                XAv = XA.rearrange("(s i dd) c -> s dd i c", s=NSEQ, dd=d)
                tiles = [(s, r, m) for s in range(NSEQ) for r in range(d) for m in range(L // TM)]

                def src(tl):
                    s, r, m = tl
                    return XAv[s, r, m * TM:(m + 1) * TM].rearrange("(j a) c -> a j c", a=128)
                K.load("sp", xts[it % 2], xts[it % 2].t[:, 0:nj, :], src(tiles[0]))
                for ti, tl in enumerate(tiles):
                    s, r, m = tl
                    xt, xT, qk, vs = xts[it % 2], xTs[it % 2], qks[it % 2], vss[it % 2]
                    if ti + 1 < len(tiles):
                        nx = xts[(it + 1) % 2]
                        K.load("sp", nx, nx.t[:, 0:nj, :], src(tiles[ti + 1]))
                    transposes_x(xt, xT, TM)
                    col0 = (s * d + r) * LP + 64 + m * TM
                    for hs in range(8):
                        pb = K.nextps(PSB)
                        K.mm(pb.t[:, 0:TM], [(W.t[:, kc, hs * 128:(hs + 1) * 128], xT.t[:, kc, 0:TM]) for kc in range(8)],
                             reads=(W, xT), writes=(pb,))
                        bc = (c0 // 128) + hs
                        K.op("act", lambda e, pb=pb, hs=hs, bc=bc: e.activation(out=qk.t[:, hs, 0:TM], in_=pb.t[:, 0:TM], func=AF.Identity,
                                                                              bias=bcol.t[:, bc:bc + 1]),
                             (pb, bcol), (qk,))
                    K.store("sp", qk, QT[g][:, :, col0:col0 + TM].rearrange("h p c -> p h c"), qk.t[:, 0:4, 0:TM])
                    K.store("sp", qk, KT[g][:, :, col0:col0 + TM].rearrange("h p c -> p h c"), qk.t[:, 4:8, 0:TM])
                    for j in range(nj):
                        pb = K.nextps(PSB)
                        pairs = [(xT.t[:, kc, j * 128:(j + 1) * 128], W.t[:, kc, 1024:1536]) for kc in range(8)]
                        pairs.append((ones_b.t[0:1, :], brow.t[0:1, c0 + 1024:c0 + 1536]))
                        K.mm(pb.t[:, :], pairs, reads=(W, xT, ones_b, brow), writes=(pb,))
                        evac(j, vs.t[:, j, :], pb.t[:, :], (pb,), (vs,))
                    K.store("sp", vs, VV[g][col0:col0 + TM, :].rearrange("(j p) f -> p j f", p=128), vs.t[:, 0:nj, :])
                    it += 1
            K.barrier()

        if phases is None or "att" in phases:
          with ExitStack() as st:
            bts = [K.sb(st, f"bt{i}", [128, 12, 256], F32) for i in range(3)]
            qcs = [K.sb(st, f"qc{i}", [128, 4, 1024], BF16) for i in range(2)]
            kcs = [K.sb(st, f"kc{i}", [128, 4, 1152], BF16) for i in range(2)]
            vcs = [K.sb(st, f"vc{i}", [128, 9, 512], BF16) for i in range(2)]
            NB_ = 3
            ogs = [K.sb(st, f"og{i}", [128, 516], F32) for i in range(NB_)]
            nmx = [K.sb(st, f"nmx{i}", [128, 4], F32) for i in range(NB_)]
            den = [K.sb(st, f"den{i}", [128, 4], F32) for i in range(NB_)]
            rdn = [K.sb(st, f"rdn{i}", [128, 4], F32) for i in range(NB_)]
            lnd = [K.sb(st, f"lnd{i}", [128, 4], F32) for i in range(NB_)]
            NU_ = 4
            sbs = [K.sb(st, f"sS{i}", [128, 256], F32) for i in range(NU_)]
            pbs = [K.sb(st, f"sP{i}", [128, 256], BF16) for i in range(NU_)]
            pts = [K.sb(st, f"sPT{i}", [128, 256], BF16) for i in range(NU_)]
            for g in range(3):
                K.load("sp", bts[g], bts[g].t[:], cin["att_bias"][g])
            chunks = []
            for g, (win, d) in enumerate(PATTERNS):
                L = S // d
                CH = min(1024, L)
                for s_ in range(NSEQ):
                    for r in range(d):
                        for ch in range(L // CH):
                            chunks.append((g, d, L, CH, s_, r, ch))

            def ldchunk(ci):
                g, d, L, CH, s_, r, ch = chunks[ci]
                LP = L + 128
                nb = CH // 128
                col0 = (s_ * d + r) * LP + 64 + ch * CH
                qc, kc_, vc = qcs[ci % 2], kcs[ci % 2], vcs[ci % 2]
                K.load("sp", qc, qc.t[:, :, 0:CH], QT[g][:, :, col0:col0 + CH].rearrange("h p c -> p h c"))
                K.load("sp", kc_, kc_.t[:, :, 0:CH + 128], KT[g][:, :, col0 - 64:col0 + CH + 64].rearrange("h p c -> p h c"))
                K.load("sp", vc, vc.t[:, 0:nb + 1, :], VV[g][col0 - 64:col0 + CH + 64, :].rearrange("(c p) f -> p c f", p=128))

            units = []
            bidx = 0
            for ci, (g, d, L, CH, s_, r, ch) in enumerate(chunks):
                nb = CH // 128
                nblk = L // 128
                for i in range(nb):
                    blk = ch * nb + i
                    var = 1 if blk == 0 else (2 if blk == nblk - 1 else 0)
                    for h in range(4):
                        units.append(dict(ci=ci, g=g, d=d, s=s_, r=r, i=i, h=h, var=var, b=bidx, p0=ch * CH + i * 128,
                                          lastc=(i == nb - 1 and h == 3)))
                    bidx += 1
            NU = len(units)
            state = {}

            def stA(u):
                U = units[u]
                qc, kc_ = qcs[U["ci"] % 2], kcs[U["ci"] % 2]
                bt = bts[U["g"]]
                i, h, var = U["i"], U["h"], U["var"]
                sS, sP = sbs[u % NU_], pbs[u % NU_]
                nm_, dn_ = nmx[U["b"] % NB_], den[U["b"] % NB_]
                pS = K.nextps(PSB)
                K.mm(pS.t[:, 0:256], [(qc.t[:, h, i * 128:(i + 1) * 128], kc_.t[:, h, i * 128:i * 128 + 256])], reads=(qc, kc_), writes=(pS,))
                K.op("dve", lambda e: e.scalar_tensor_tensor(out=sS.t[:], in0=pS.t[:, 0:256], scalar=SCALE, in1=bt.t[:, h * 3 + var, :], op0=ALU.mult, op1=ALU.add),
                     (pS, bt), (sS,))
                K.op("dve", lambda e: e.tensor_reduce(out=nm_.t[:, h:h + 1], in_=sS.t[:], axis=AX.X, op=ALU.max, negate=True), (sS,), (nm_,))
                K.op("act", lambda e: e.activation(out=sP.t[:], in_=sS.t[:], func=AF.Exp, bias=nm_.t[:, h:h + 1], accum_out=dn_.t[:, h:h + 1]),
                     (sS, nm_), (sP, dn_))

            def stB(u):
                sP, sPT = pbs[u % NU_], pts[u % NU_]
                pT = K.nextps(PSB)
                pTb = pT.t[:].bitcast(BF16)
                K.pe([lambda t, c=c: t.transpose(out=pTb[:, c * 128:(c + 1) * 128], in_=sP.t[:, c * 128:(c + 1) * 128], identity=ident_b.t[:])
                      for c in range(2)], reads=(sP, ident_b), writes=(pT,))
                K.op("act", lambda e: e.activation(out=sPT.t[:], in_=pTb[:, 0:256], func=AF.Copy), (pT,), (sPT,))

            def stC(u):
                U = units[u]
                vc = vcs[U["ci"] % 2]
                i, h = U["i"], U["h"]
                sPT = pts[u % NU_]
                q_ = U["b"] % NB_
                og, nm_, dn_, rd_, ld_ = ogs[q_], nmx[q_], den[q_], rdn[q_], lnd[q_]
                pO = K.nextps(PSB)
                K.mm(pO.t[:, 0:128], [(sPT.t[:, c * 128:(c + 1) * 128], vc.t[:, i + c, h * 128:(h + 1) * 128]) for c in range(2)], reads=(sPT, vc), writes=(pO,))
                K.op("dve", lambda e: e.reciprocal(out=rd_.t[:, h:h + 1], in_=dn_.t[:, h:h + 1]), (dn_,), (rd_,))
                K.op("act", lambda e: e.activation(out=og.t[:, h * 128:(h + 1) * 128], in_=pO.t[:, 0:128], func=AF.Identity, scale=rd_.t[:, h:h + 1]),
                     (pO, rd_), (og,))
                if h == 3:
                    K.op("act", lambda e: e.activation(out=ld_.t[:], in_=dn_.t[:], func=AF.Ln), (dn_,), (ld_,))
                    K.op("dve", lambda e: e.tensor_tensor(out=og.t[:, 512:516], in0=ld_.t[:], in1=nm_.t[:], op=ALU.subtract), (ld_, nm_), (og,))
                    XOv = OG[U["g"]].rearrange("(s i dd) c -> s dd i c", s=NSEQ, dd=U["d"])
                    K.store("sp", og, XOv[U["s"], U["r"], U["p0"]:U["p0"] + 128, :], og.t[:])
                if U["lastc"] and U["ci"] + 2 < len(chunks):
                    ldchunk(U["ci"] + 2)

            ldchunk(0)
            if len(chunks) > 1:
                ldchunk(1)
            for idx in range(NU + 2):
                if idx < NU:
                    stA(idx)
                if 1 <= idx <= NU:
                    stB(idx - 1)
                if idx >= 2:
                    stC(idx - 2)
            K.barrier()

        if phases is None or "pc1" in phases:
          with ExitStack() as st:
            Wu = K.sb(st, "Wu", [128, 8, 1024], BF16)
            Wv = K.sb(st, "Wv", [128, 8, 1024], BF16)
            Wga = K.sb(st, "Wga", [128, 8, 1024], BF16)
            Wpa = K.sb(st, "Wpa", [128, 8, 1024], BF16)
            Wst = K.sb(st, "Wst", [128, 8, 128], BF16)
            bcol = K.sb(st, "bcol", [128, 68], F32)
            brow = K.sb(st, "brow", [1, 1024], BF16)
            lnv = K.sb(st, "lnv", [128, 16], F32)
            bsb = K.sb(st, "bsb", [128, 8, 128], F32)
            Cgt = K.sb(st, "Cgt", [128, 8, 128], F32)
            xts = [K.sb(st, f"xt{i}", [128, 4, D], F32) for i in range(2)]
            xT = K.sb(st, "xT", [128, 8, 512], BF16)
            uT = K.sb(st, "uT", [128, 8, 512], BF16)
            gaT = K.sb(st, "gaT", [128, 8, 512], BF16)
            aoT = K.sb(st, "aoT", [128, 8, 512], BF16)
            maTs = [K.sb(st, f"maT{i}", [128, 8, 512], BF16) for i in range(2)]
            vsb = [K.sb(st, f"v{i}", [128, 1024], F32) for i in range(2)]
            nsb = [K.sb(st, f"n{i}", [128, 1024], BF16) for i in range(2)]
            tmpa = [K.sb(st, f"tmpa{i}", [128, 8, 128], F32) for i in range(2)]
            stt = [K.sb(st, f"stt{i}", [128, 12], F32) for i in range(2)]
            mv = [K.sb(st, f"mv{i}", [128, 2], F32) for i in range(2)]
            rstd = [K.sb(st, f"rstd{i}", [128, 1], F32) for i in range(2)]
            wsrc = w_in[l]
            K.load("pool", Wu, Wu.t[:], wsrc[:, 0:1024].rearrange("(kc p) c -> p kc c", p=128))
            K.load("pool", Wv, Wv.t[:], wsrc[:, 1024:2048].rearrange("(kc p) c -> p kc c", p=128))
            K.load("pool", Wga, Wga.t[:], wsrc[:, 6656:7680].rearrange("(kc p) c -> p kc c", p=128))
            K.load("pool", Wpa, Wpa.t[:], w_pa[l].rearrange("(kc p) c -> p kc c", p=128))
            K.load("pool", Wst, Wst.t[:], w_st[l])
            K.load("pool", brow, brow.t[:], bin_row[l][:, 1024:2048])
            K.load("sp", bcol, bcol.t[:], bin_col[l])
            K.load("sp", lnv, lnv.t[:], lnv_col[l])
            K.load("sp", bsb, bsb.t[:], bs_row[l].rearrange("o (g t) -> o g t", g=8).to_broadcast([128, 8, 128]))
            for half in range(2):
                pb = K.nextps(PSB)
                K.mm(pb.t[:, :], [(ones_b.t[:, :], Wst.t[:, half * 4:(half + 1) * 4, :])], reads=(ones_b, Wst), writes=(pb,))
                for gg in range(4):
                    g_ = half * 4 + gg
                    K.op("dve", lambda e, pb=pb, gg=gg, g_=g_: e.scalar_tensor_tensor(
                        out=Cgt.t[:, g_, :], in0=pb.t[:, gg * 128:(gg + 1) * 128], scalar=lnv.t[:, 8 + g_:9 + g_], in1=bsb.t[:, g_, :],
                        op0=ALU.mult, op1=ALU.add), (pb, lnv, bsb), (Cgt,))
            XAt = XA.rearrange("(m j a) c -> m a j c", j=4, a=128)
            NM = NT // 512
            K.load("sp", xts[0], xts[0].t[:], XAt[0])
            sj = 0
            for m in range(NM):
                xt = xts[m % 2]
                maT = maTs[m % 2]
                if m + 1 < NM:
                    K.load("sp", xts[(m + 1) % 2], xts[(m + 1) % 2].t[:], XAt[m + 1])
                transposes_x(xt, xT, 512)
                for c8 in range(8):
                    pb = K.nextps(PSB)
                    K.mm(pb.t[:, :], [(Wu.t[:, kc, c8 * 128:(c8 + 1) * 128], xT.t[:, kc, :]) for kc in range(8)], reads=(Wu, xT), writes=(pb,))
                    K.op("act", lambda e, pb=pb, c8=c8: e.activation(out=uT.t[:, c8, :], in_=pb.t[:, :], func=AF.Gelu, bias=bcol.t[:, c8:c8 + 1]),
                         (pb, bcol), (uT,))
                for j in range(4):
                    v_, n_, ta, st_, mv_, rs_ = vsb[sj % 2], nsb[sj % 2], tmpa[sj % 2], stt[sj % 2], mv[sj % 2], rstd[sj % 2]
                    for nh in range(2):
                        pb = K.nextps(PSB)
                        pairs = [(xT.t[:, kc, j * 128:(j + 1) * 128], Wv.t[:, kc, nh * 512:(nh + 1) * 512]) for kc in range(8)]
                        pairs.append((ones_b.t[0:1, :], brow.t[0:1, nh * 512:(nh + 1) * 512]))
                        K.mm(pb.t[:, :], pairs, reads=(xT, Wv, ones_b, brow), writes=(pb,))
                        K.op("act", lambda e, pb=pb, nh=nh, v_=v_: e.activation(out=v_.t[:, nh * 512:(nh + 1) * 512], in_=pb.t[:, :], func=AF.Gelu),
                             (pb,), (v_,))
                    ln_stats(st, lambda c, v_=v_: v_.t[:, c * 512:(c + 1) * 512], v_, mv_, st_, rs_)
                    K.op("dve", lambda e, v_=v_, n_=n_, mv_=mv_, rs_=rs_: e.tensor_scalar(
                        out=n_.t[:], in0=v_.t[:], scalar1=mv_.t[:, 0:1], scalar2=rs_.t[:, 0:1], op0=ALU.subtract, op1=ALU.mult),
                        (v_, mv_, rs_), (n_,))
                    for half in range(2):
                        pb = K.nextps(PSB)
                        fns = [lambda t, gg=gg, half=half, pb=pb, n_=n_: t.matmul(
                            pb.t[:, gg * 128:(gg + 1) * 128], lhsT=n_.t[:, (half * 4 + gg) * 128:(half * 4 + gg + 1) * 128],
                            rhs=Wst.t[:, half * 4 + gg, :], start=True, stop=True) for gg in range(4)]
                        K.pe(fns, reads=(n_, Wst), writes=(pb,))
                        for gg in range(4):
                            g_ = half * 4 + gg
                            K.op("dve", lambda e, pb=pb, gg=gg, g_=g_, ta=ta: e.scalar_tensor_tensor(
                                out=ta.t[:, g_, :], in0=pb.t[:, gg * 128:(gg + 1) * 128], scalar=lnv.t[:, g_:g_ + 1], in1=Cgt.t[:, g_, :],
                                op0=ALU.mult, op1=ALU.add), (pb, lnv, Cgt), (ta,))
                    K.op("pool", lambda e, ta=ta, j=j: e.tensor_tensor(out=aoT.t[:, :, j * 128:(j + 1) * 128], in0=ta.t[:], in1=uT.t[:, :, j * 128:(j + 1) * 128], op=ALU.mult),
                         (ta, uT), (aoT,))
                    sj += 1
                for c8 in range(8):
                    pb = K.nextps(PSB)
                    K.mm(pb.t[:, :], [(Wga.t[:, kc, c8 * 128:(c8 + 1) * 128], xT.t[:, kc, :]) for kc in range(8)], reads=(Wga, xT), writes=(pb,))
                    K.op("act", lambda e, pb=pb, c8=c8: e.activation(out=gaT.t[:, c8, :], in_=pb.t[:, :], func=AF.Sigmoid, bias=bcol.t[:, 52 + c8:53 + c8]),
                         (pb, bcol), (gaT,))
                for c8 in range(8):
                    pb = K.nextps(PSB)
                    K.mm(pb.t[:, :], [(Wpa.t[:, kc, c8 * 128:(c8 + 1) * 128], aoT.t[:, kc, :]) for kc in range(8)], reads=(Wpa, aoT), writes=(pb,))
                    K.op("dve", lambda e, pb=pb, c8=c8, maT=maT: e.tensor_tensor(out=maT.t[:, c8, :], in0=pb.t[:, :], in1=gaT.t[:, c8, :], op=ALU.mult),
                         (pb, gaT), (maT,))
                K.store("sp", maT, MA[:, :, m * 512:(m + 1) * 512].rearrange("n p t -> p n t"), maT.t[:])
            K.barrier()

        if phases is None or "pc2" in phases:
          with ExitStack() as st:
            Wgb = K.sb(st, "Wgb", [128, 8, 1024], BF16)
            Wpb = K.sb(st, "Wpb", [128, 4, 1024], BF16)
            Wo = K.sb(st, "Wo", [128, 8, 1024], BF16)
            Wr = K.sb(st, "Wr", [128, 8, 32], F32)
            bcol = K.sb(st, "bcol", [128, 68], F32)
            g1b = K.sb(st, "g1b", [128, 1024], F32)
            b1b = K.sb(st, "b1b", [128, 1024], F32)
            brb = K.sb(st, "brb", [128, 32], F32)
            rin = K.sb(st, "rin", [128, 128, 4], I32)
            xts = [K.sb(st, f"xt{i}", [128, 4, D], F32) for i in range(2)]
            xT = K.sb(st, "xT", [128, 8, 512], BF16)
            gbT = K.sb(st, "gbT", [128, 8, 512], BF16)
            maTs = [K.sb(st, f"maT{i}", [128, 8, 512], BF16) for i in range(2)]
            boT = K.sb(st, "boT", [128, 4, 512], BF16)
            mgT = K.sb(st, "mgT", [128, 8, 512], BF16)
            tmpm = [K.sb(st, f"tmpm{i}", [128, 512], F32) for i in range(2)]
            bof = [K.sb(st, f"bof{i}", [128, 512], F32) for i in range(2)]
            sm = [K.sb(st, f"sm{i}", [128, 64], F32) for i in range(2)]
            x1p = [K.sb(st, f"x1p{i}", [128, 1024], F32) for i in range(2)]
            x1 = [K.sb(st, f"x1{i}", [128, 1024], F32) for i in range(2)]
            x1T = [K.sb(st, f"x1T{i}", [128, 8, 128], F32) for i in range(2)]
            stt = [K.sb(st, f"stt{i}", [128, 12], F32) for i in range(2)]
            mv = [K.sb(st, f"mv{i}", [128, 2], F32) for i in range(2)]
            rstd = [K.sb(st, f"rstd{i}", [128, 1], F32) for i in range(2)]
            lg = [K.sb(st, f"lg{i}", [128, 32], F32) for i in range(2)]
            t8 = [K.sb(st, f"t8{i}", [128, 8], F32) for i in range(2)]
            rs = [K.sb(st, f"rs{i}", [128, 16], F32) for i in range(2)]
            mk = [K.sb(st, f"mk{i}", [128, 32], BF16) for i in range(2)]
            posg = [K.sb(st, f"posg{i}", [128, 32], F32) for i in range(2)]
            slotv = [K.sb(st, f"slotv{i}", [128, 32], F32) for i in range(2)]
            oh = [K.sb(st, f"oh{i}", [128, 32], F32) for i in range(2)]
            ohx = [K.sb(st, f"ohx{i}", [128, 32], F32) for i in range(2)]
            slf = [K.sb(st, f"slf{i}", [128, 4], F32) for i in range(2)]
            sli = [K.sb(st, f"sli{i}", [128, 4], I32) for i in range(2)]
            wsrc = w_in[l]
            K.load("pool", Wgb, Wgb.t[:], wsrc[:, 7680:8704].rearrange("(kc p) c -> p kc c", p=128))
            K.load("pool", Wpb, Wpb.t[:], w_pb[l].rearrange("(kc p) c -> p kc c", p=128))
            K.load("pool", Wo, Wo.t[:], w_o[l].rearrange("(kc p) c -> p kc c", p=128))
            K.load("sp", Wr, Wr.t[:], w_r[l].rearrange("(kc p) c -> p kc c", p=128))
            K.load("sp", bcol, bcol.t[:], bin_col[l])
            K.load("sp", g1b, g1b.t[:], ln1_row[l][0:1, :].to_broadcast([128, 1024]))
            K.load("sp", b1b, b1b.t[:], ln1_row[l][1:2, :].to_broadcast([128, 1024]))
            K.load("sp", brb, brb.t[:], br_row[l].to_broadcast([128, 32]))
            K.load("sp", rin, rin.t[:], cin["rinit"])
            for c0 in range(0, NE * CAP, 16384):
                K.store("sp", rin, ROUTE[c0:c0 + 16384, :].rearrange("(p a) f -> p a f", p=128), rin.t[:])
            K.store("sp", rin, ROUTE[NE * CAP:NE * CAP + 128, :], rin.t[:, 0, :])
            K.op("dve", lambda e: e.memset(RUN.t[:], 0.0), (), (RUN,))
            K.barrier()
            XAt = XA.rearrange("(m j a) c -> m a j c", j=4, a=128)
            NM = NT // 512
            K.load("sp", xts[0], xts[0].t[:], XAt[0])
            K.load("sp", maTs[0], maTs[0].t[:], MA[:, :, 0:512].rearrange("n p t -> p n t"))
            og3 = [K.sb(st, f"og3x{i}", [128, 3, 516], F32) for i in range(4)]
            bo = [K.sb(st, f"box{i}", [128, 512], BF16) for i in range(4)]
            nmr = [K.sb(st, f"nmr{i}", [128, 1], F32) for i in range(2)]

            def M1(m, j):
                tile = m * 4 + j
                o3, sm_, bo_, bof_ = og3[j], sm[j % 2], bo[j], bof[j % 2]
                K.load("sp", o3, o3.t[:], OGall[:, tile * 128:(tile + 1) * 128, :].rearrange("g p c -> p g c"))
                K.op("dve", lambda e: e.tensor_tensor(out=sm_.t[:, 0:4], in0=o3.t[:, 0, 512:516], in1=o3.t[:, 1, 512:516], op=ALU.max), (o3,), (sm_,))
                K.op("dve", lambda e: e.tensor_tensor(out=sm_.t[:, 0:4], in0=sm_.t[:, 0:4], in1=o3.t[:, 2, 512:516], op=ALU.max), (o3, sm_), (sm_,))
                for g in range(3):
                    K.op("dve", lambda e, g=g: e.tensor_tensor(out=sm_.t[:, 4 + 4 * g:8 + 4 * g], in0=o3.t[:, g, 512:516], in1=sm_.t[:, 0:4], op=ALU.subtract), (o3, sm_), (sm_,))
                K.op("act", lambda e: e.activation(out=sm_.t[:, 4:16], in_=sm_.t[:, 4:16], func=AF.Exp), (sm_,), (sm_,))
                K.op("dve", lambda e: e.tensor_tensor(out=sm_.t[:, 16:20], in0=sm_.t[:, 4:8], in1=sm_.t[:, 8:12], op=ALU.add), (sm_,), (sm_,))
                K.op("dve", lambda e: e.tensor_tensor(out=sm_.t[:, 16:20], in0=sm_.t[:, 16:20], in1=sm_.t[:, 12:16], op=ALU.add), (sm_,), (sm_,))
                K.op("dve", lambda e: e.reciprocal(out=sm_.t[:, 20:24], in_=sm_.t[:, 16:20]), (sm_,), (sm_,))
                for g in range(3):
                    K.op("dve", lambda e, g=g: e.tensor_tensor(out=sm_.t[:, 24 + 4 * g:28 + 4 * g], in0=sm_.t[:, 4 + 4 * g:8 + 4 * g], in1=sm_.t[:, 20:24], op=ALU.mult), (sm_,), (sm_,))
                for h in range(4):
                    hs = slice(h * 128, (h + 1) * 128)
                    K.op("pool", lambda e, h=h, hs=hs: e.tensor_scalar(out=bof_.t[:, hs], in0=o3.t[:, 0, hs], scalar1=sm_.t[:, 24 + h:25 + h], scalar2=None, op0=ALU.mult), (o3, sm_), (bof_,))
                    K.op("dve", lambda e, h=h, hs=hs: e.scalar_tensor_tensor(out=bof_.t[:, hs], in0=o3.t[:, 1, hs], scalar=sm_.t[:, 28 + h:29 + h], in1=bof_.t[:, hs], op0=ALU.mult, op1=ALU.add), (o3, sm_, bof_), (bof_,))
                    K.op("dve", lambda e, h=h, hs=hs: e.scalar_tensor_tensor(out=bo_.t[:, hs], in0=o3.t[:, 2, hs], scalar=sm_.t[:, 32 + h:33 + h], in1=bof_.t[:, hs], op0=ALU.mult, op1=ALU.add), (o3, sm_, bof_), (bo_,))

            def M2(m, j):
                bo_ = bo[j]
                pT = K.nextps(PSB)
                pTb = pT.t[:].bitcast(BF16)
                K.pe([lambda t, c=c: t.transpose(out=pTb[:, c * 128:(c + 1) * 128], in_=bo_.t[:, c * 128:(c + 1) * 128], identity=ident_b.t[:])
                      for c in range(4)], reads=(bo_, ident_b), writes=(pT,))
                K.op("act", lambda e: e.activation(out=boT.t[:, :, j * 128:(j + 1) * 128], in_=pTb[:, 0:512].rearrange("p (c t) -> p c t", c=4), func=AF.Copy), (pT,), (boT,))

            def Pst(m, j, xt):
                tile = m * 4 + j
                q = tile % 2
                xp, x1_, st_, mv_, rs_, nm_ = x1p[q], x1[q], stt[q], mv[q], rstd[q], nmr[q]
                for nh in range(2):
                    pb = K.nextps(PSB)
                    K.mm(pb.t[:, :], [(mgT.t[:, kc, j * 128:(j + 1) * 128], Wo.t[:, kc, nh * 512:(nh + 1) * 512]) for kc in range(8)], reads=(mgT, Wo), writes=(pb,))
                    K.op("dve", lambda e, pb=pb, nh=nh: e.scalar_tensor_tensor(
                        out=xp.t[:, nh * 512:(nh + 1) * 512], in0=xt.t[:, j, nh * 512:(nh + 1) * 512], scalar=ALPHA, in1=pb.t[:, :], op0=ALU.mult, op1=ALU.add),
                        (pb, xt), (xp,))
                ln_stats(st, lambda c: xp.t[:, c * 512:(c + 1) * 512], xp, mv_, st_, rs_)
                K.op("dve", lambda e: e.scalar_tensor_tensor(out=nm_.t[:], in0=mv_.t[:, 0:1], scalar=-1.0, in1=rs_.t[:, 0:1], op0=ALU.mult, op1=ALU.mult), (mv_, rs_), (nm_,))
                K.op("act", lambda e: e.activation(out=xp.t[:], in_=xp.t[:], func=AF.Identity, scale=rs_.t[:, 0:1], bias=nm_.t[:, 0:1]), (xp, rs_, nm_), (xp,))
                K.op("dve", lambda e: e.tensor_tensor(out=xp.t[:], in0=xp.t[:], in1=g1b.t[:], op=ALU.mult), (xp, g1b), (xp,))
                K.op("pool", lambda e: e.tensor_tensor(out=x1_.t[:], in0=xp.t[:], in1=b1b.t[:], op=ALU.add), (xp, b1b), (x1_,))
                K.store("sp", x1_, X1[tile * 128:(tile + 1) * 128, :], x1_.t[:])

            def Rst(m, j):
                tile = m * 4 + j
                q = tile % 2
                x1_, x1T_ = x1[q], x1T[q]
                lg_, t8_, r_, mk_, pg_, sv_, oh_, ox_, sf_, si_ = lg[q], t8[q], rs[q], mk[q], posg[q], slotv[q], oh[q], ohx[q], slf[q], sli[q]
                for half in range(2):
                    pb = K.nextps(PSB)
                    K.pe([lambda t, c=c, half=half, pb=pb: t.transpose(out=pb.t[:, c * 128:(c + 1) * 128], in_=x1_.t[:, (half * 4 + c) * 128:(half * 4 + c + 1) * 128], identity=ident_f.t[:])
                          for c in range(4)], reads=(x1_, ident_f), writes=(pb,))
                    evac(half, x1T_.t[:, half * 4:(half + 1) * 4, :], pb.t[:, :].rearrange("p (c t) -> p c t", c=4), (pb,), (x1T_,))
                pb = K.nextps(PSB)
                K.mm(pb.t[:, 0:32], [(x1T_.t[:, kc, :], Wr.t[:, kc, :]) for kc in range(8)], reads=(x1T_, Wr), writes=(pb,))
                K.op("dve", lambda e, pb=pb: e.tensor_tensor(out=lg_.t[:], in0=pb.t[:, 0:32], in1=brb.t[:], op=ALU.add), (pb, brb), (lg_,))
                K.op("dve", lambda e: e.max(out=t8_.t[:], in_=lg_.t[:]), (lg_,), (t8_,))
                K.op("dve", lambda e: e.tensor_scalar(out=r_.t[:, 0:1], in0=t8_.t[:, 0:1], scalar1=-1.0, scalar2=None, op0=ALU.mult), (t8_,), (r_,))
                K.op("act", lambda e: e.activation(out=r_.t[:, 1:5], in_=t8_.t[:, 0:4], func=AF.Exp, bias=r_.t[:, 0:1], accum_out=r_.t[:, 5:6]), (t8_, r_), (r_,))
                K.op("dve", lambda e: e.reciprocal(out=r_.t[:, 6:7], in_=r_.t[:, 5:6]), (r_,), (r_,))
                K.op("dve", lambda e: e.tensor_scalar(out=GATES.t[:, tile, :], in0=r_.t[:, 1:5], scalar1=r_.t[:, 6:7], scalar2=None, op0=ALU.mult), (r_,), (GATES,))
                K.op("dve", lambda e: e.tensor_scalar(out=mk_.t[:], in0=lg_.t[:], scalar1=t8_.t[:, 3:4], scalar2=None, op0=ALU.is_ge), (lg_, t8_), (mk_,))
                pb = K.nextps(PSB)
                K.mm(pb.t[:, 0:32], [(tri_b.t[:, :], mk_.t[:, :])], reads=(tri_b, mk_), writes=(pb,))
                K.op("dve", lambda e, pb=pb: e.tensor_tensor(out=pg_.t[:], in0=pb.t[:, 0:32], in1=RUN.t[:], op=ALU.add), (pb, RUN), (pg_,))
                pb2 = K.nextps(PSB)
                K.mm(pb2.t[:, 0:32], [(ones_b.t[:, :], mk_.t[:, :])], reads=(ones_b, mk_), writes=(pb2,))
                K.op("dve", lambda e, pb2=pb2: e.tensor_tensor(out=RUN.t[:], in0=pb2.t[:, 0:32], in1=RUN.t[:], op=ALU.add), (pb2, RUN), (RUN,))
                K.op("dve", lambda e: e.tensor_tensor(out=sv_.t[:], in0=pg_.t[:], in1=ecap.t[:], op=ALU.add), (pg_, ecap), (sv_,))
                for k in range(4):
                    K.op("dve", lambda e, k=k: e.tensor_scalar(out=oh_.t[:], in0=lg_.t[:], scalar1=t8_.t[:, k:k + 1], scalar2=None, op0=ALU.is_equal), (lg_, t8_), (oh_,))
                    K.op("dve", lambda e: e.tensor_tensor(out=ox_.t[:], in0=oh_.t[:], in1=sv_.t[:], op=ALU.mult), (oh_, sv_), (ox_,))
                    K.op("dve", lambda e, k=k: e.tensor_reduce(out=sf_.t[:, k:k + 1], in_=ox_.t[:], axis=AX.X, op=ALU.add), (ox_,), (sf_,))
                    K.op("dve", lambda e: e.tensor_tensor(out=ox_.t[:], in0=oh_.t[:], in1=pg_.t[:], op=ALU.mult), (oh_, pg_), (ox_,))
                    K.op("dve", lambda e, k=k: e.tensor_reduce(out=POSK.t[:, tile, k:k + 1], in_=ox_.t[:], axis=AX.X, op=ALU.add), (ox_,), (POSK,))
                    K.op("dve", lambda e: e.tensor_tensor(out=ox_.t[:], in0=oh_.t[:], in1=iota32.t[:], op=ALU.mult), (oh_, iota32), (ox_,))
                    K.op("dve", lambda e, k=k: e.tensor_reduce(out=EKF.t[:, tile, k:k + 1], in_=ox_.t[:], axis=AX.X, op=ALU.add), (ox_,), (EKF,))
                K.op("dve", lambda e: e.tensor_copy(out=si_.t[:], in_=sf_.t[:]), (sf_,), (si_,))
                for k in range(4):
                    K.dma("pool", si_, lambda e, k=k: e.indirect_dma_start(
                        out=ROUTE[:, :], out_offset=bass.IndirectOffsetOnAxis(ap=si_.t[:, k:k + 1], axis=0),
                        in_=tokid4.t[:, tile, :], in_offset=None), reads=(si_, tokid4), writes=())

            for m in range(NM):
                xt = xts[m % 2]
                maT = maTs[m % 2]
                if m + 1 < NM:
                    K.load("sp", xts[(m + 1) % 2], xts[(m + 1) % 2].t[:], XAt[m + 1])
                    K.load("sp", maTs[(m + 1) % 2], maTs[(m + 1) % 2].t[:], MA[:, :, (m + 1) * 512:(m + 2) * 512].rearrange("n p t -> p n t"))
                for j in range(4):
                    M1(m, j)
                transposes_x(xt, xT, 512)
                for c8 in range(8):
                    pb = K.nextps(PSB)
                    K.mm(pb.t[:, :], [(Wgb.t[:, kc, c8 * 128:(c8 + 1) * 128], xT.t[:, kc, :]) for kc in range(8)], reads=(Wgb, xT), writes=(pb,))
                    K.op("act", lambda e, pb=pb, c8=c8: e.activation(out=gbT.t[:, c8, :], in_=pb.t[:, :], func=AF.Sigmoid, bias=bcol.t[:, 60 + c8:61 + c8]),
                         (pb, bcol), (gbT,))
                for j in range(4):
                    M2(m, j)
                for c8 in range(8):
                    pb = K.nextps(PSB)
                    tm = tmpm[c8 % 2]
                    K.mm(pb.t[:, :], [(Wpb.t[:, kc, c8 * 128:(c8 + 1) * 128], boT.t[:, kc, :]) for kc in range(4)], reads=(Wpb, boT), writes=(pb,))
                    K.op("dve", lambda e, pb=pb, c8=c8, tm=tm: e.tensor_tensor(out=tm.t[:], in0=pb.t[:, :], in1=gbT.t[:, c8, :], op=ALU.mult), (pb, gbT), (tm,))
                    K.op("pool", lambda e, c8=c8, tm=tm: e.tensor_tensor(out=mgT.t[:, c8, :], in0=tm.t[:], in1=maT.t[:, c8, :], op=ALU.add), (tm, maT), (mgT,))
                Pst(m, 0, xt); Pst(m, 1, xt); Rst(m, 0); Pst(m, 2, xt); Rst(m, 1); Pst(m, 3, xt); Rst(m, 2); Rst(m, 3)
            K.barrier()

        if phases is None or "bl" in phases:
          with ExitStack() as st:
            cntc = K.sb(st, "cntc", [32, 1], F32)
            tmp32 = K.sb(st, "tmp32", [32, 32], F32)
            jthr = K.sb(st, "jthr", [32, 64], F32)
            tmpj = K.sb(st, "tmpj", [32, 64], F32)
            nbc = K.sb(st, "nbc", [32, 1], F32)
            endc = K.sb(st, "endc", [32, 1], F32)
            triu = K.sb(st, "triu", [32, 32], F32)
            iob = K.sb(st, "iob", [128, NBLK], F32)
            Gm = K.sb(st, "Gm", [32, NBLK], F32)
            nbm = K.sb(st, "nbm", [32, 128], F32)
            ebr = K.sb(st, "ebr", [128, NBLK], F32)
            eb = K.sb(st, "eb", [128, NBLK], F32)
            stb = K.sb(st, "stb", [128, NBLK], F32)
            val = K.sb(st, "val", [128, NBLK], F32)
            sbase = K.sb(st, "sbase", [128, NBLK], F32)
            need = K.sb(st, "need", [128, NBLK], F32)
            tA = K.sb(st, "tA", [128, NBLK], F32)
            tB = K.sb(st, "tB", [128, NBLK], F32)
            endr = K.sb(st, "endr", [128, 32], F32)
            nbr = K.sb(st, "nbr", [128, 32], F32)
            K.load("sp", jthr, jthr.t[:], cin["jthr"])
            K.load("sp", triu, triu.t[:], cin["triu32"])
            K.load("sp", iob, iob.t[:], cin["iota_b"])
            K.op("dve", lambda e: e.tensor_tensor(out=tmp32.t[:], in0=RUN.t[0:32, :], in1=ident_f.t[0:32, 0:32], op=ALU.mult), (RUN, ident_f), (tmp32,))
            K.op("dve", lambda e: e.tensor_reduce(out=cntc.t[:], in_=tmp32.t[:], axis=AX.X, op=ALU.add), (tmp32,), (cntc,))
            K.op("dve", lambda e: e.tensor_scalar(out=tmpj.t[:], in0=jthr.t[:], scalar1=cntc.t[:, 0:1], scalar2=None, op0=ALU.is_lt), (jthr, cntc), (tmpj,))
            K.op("dve", lambda e: e.tensor_reduce(out=nbc.t[:], in_=tmpj.t[:], axis=AX.X, op=ALU.add), (tmpj,), (nbc,))
            pb = K.nextps(PSB)
            K.mm(pb.t[0:32, 0:1], [(triu.t[:, :], nbc.t[:, 0:1])], reads=(triu, nbc), writes=(pb,))
            K.op("dve", lambda e, pb=pb: e.tensor_copy(out=endc.t[:], in_=pb.t[0:32, 0:1]), (pb,), (endc,))
            K.op("dve", lambda e: e.tensor_scalar(out=Gm.t[:], in0=iob.t[0:32, :], scalar1=endc.t[:, 0:1], scalar2=None, op0=ALU.is_ge), (iob, endc), (Gm,))
            K.op("dve", lambda e: e.tensor_scalar(out=nbm.t[:], in0=ones_f.t[0:32, :], scalar1=nbc.t[:, 0:1], scalar2=None, op0=ALU.mult), (ones_f, nbc), (nbm,))
            pb = K.nextps(PSB)
            K.mm(pb.t[:, 0:NBLK], [(ones_f.t[0:32, :], Gm.t[:, :])], reads=(ones_f, Gm), writes=(pb,))
            K.op("dve", lambda e, pb=pb: e.tensor_copy(out=ebr.t[:], in_=pb.t[:, 0:NBLK]), (pb,), (ebr,))
            pb = K.nextps(PSB)
            K.mm(pb.t[:, 0:NBLK], [(nbm.t[:, :], Gm.t[:, :])], reads=(nbm, Gm), writes=(pb,))
            K.op("dve", lambda e, pb=pb: e.tensor_copy(out=stb.t[:], in_=pb.t[:, 0:NBLK]), (pb,), (stb,))
            pb = K.nextps(PSB)
            K.mm(pb.t[:, 0:32], [(nbm.t[:, :], triu.t[:, :])], reads=(nbm, triu), writes=(pb,))
            K.op("dve", lambda e, pb=pb: e.tensor_copy(out=endr.t[:], in_=pb.t[:, 0:32]), (pb,), (endr,))
            pb = K.nextps(PSB)
            K.mm(pb.t[:, 0:32], [(nbm.t[:, :], ident_f.t[0:32, 0:32])], reads=(nbm, ident_f), writes=(pb,))
            K.op("dve", lambda e, pb=pb: e.tensor_copy(out=nbr.t[:], in_=pb.t[:, 0:32]), (pb,), (nbr,))
            K.op("dve", lambda e: e.tensor_tensor(out=BS128.t[:], in0=endr.t[:], in1=nbr.t[:], op=ALU.subtract), (endr, nbr), (BS128,))
            K.op("dve", lambda e: e.tensor_scalar(out=BS128.t[:], in0=BS128.t[:], scalar1=128.0, scalar2=None, op0=ALU.mult), (BS128,), (BS128,))
            K.op("dve", lambda e: e.tensor_scalar(out=val.t[:], in0=ebr.t[:], scalar1=31.5, scalar2=None, op0=ALU.is_lt), (ebr,), (val,))
            K.op("dve", lambda e: e.tensor_scalar(out=eb.t[:], in0=ebr.t[:], scalar1=31.0, scalar2=None, op0=ALU.min), (ebr,), (eb,))
            K.op("dve", lambda e: e.tensor_tensor(out=tA.t[:], in0=iob.t[:], in1=stb.t[:], op=ALU.subtract), (iob, stb), (tA,))
            K.op("dve", lambda e: e.tensor_scalar(out=tA.t[:], in0=tA.t[:], scalar1=128.0, scalar2=float(-NE * CAP), op0=ALU.mult, op1=ALU.add), (tA,), (tA,))
            K.op("dve", lambda e: e.scalar_tensor_tensor(out=tA.t[:], in0=eb.t[:], scalar=float(CAP), in1=tA.t[:], op0=ALU.mult, op1=ALU.add), (eb, tA), (tA,))
            K.op("dve", lambda e: e.tensor_tensor(out=tA.t[:], in0=tA.t[:], in1=val.t[:], op=ALU.mult), (tA, val), (tA,))
            K.op("dve", lambda e: e.tensor_scalar(out=sbase.t[:], in0=tA.t[:], scalar1=pcol.t[:, 0:1], scalar2=float(NE * CAP), op0=ALU.add, op1=ALU.add), (tA, pcol), (sbase,))
            K.op("dve", lambda e: e.tensor_copy(out=RIDX.t[:], in_=sbase.t[:]), (sbase,), (RIDX,))
            K.op("dve", lambda e: e.memset(need.t[:], 1.0), (), (need,))
            K.op("dve", lambda e: e.tensor_tensor(out=need.t[:, 1:NBLK], in0=eb.t[:, 1:NBLK], in1=eb.t[:, 0:NBLK - 1], op=ALU.not_equal), (eb,), (need,))
            K.op("dve", lambda e: e.memset(need.t[:, NBLK // 2:NBLK // 2 + 1], 1.0), (), (need,))
            nn = K.sb(st, "nn", [128, NBLK], F32)
            K.op("dve", lambda e: e.tensor_scalar(out=nn.t[:], in0=need.t[:], scalar1=float(-BIG), scalar2=float(BIG), op0=ALU.mult, op1=ALU.add), (need,), (nn,))
            K.op("dve", lambda e: e.tensor_scalar(out=tB.t[:], in0=eb.t[:], scalar1=128.0, scalar2=pcol.t[:, 0:1], op0=ALU.mult, op1=ALU.add), (eb, pcol), (tB,))
            K.op("dve", lambda e: e.tensor_scalar(out=tB.t[:], in0=tB.t[:], scalar1=float(l * NE * 128), scalar2=None, op0=ALU.add), (tB,), (tB,))
            K.op("dve", lambda e: e.tensor_tensor(out=tB.t[:], in0=tB.t[:], in1=need.t[:], op=ALU.mult), (tB, need), (tB,))
            K.op("dve", lambda e: e.tensor_tensor(out=tB.t[:], in0=tB.t[:], in1=nn.t[:], op=ALU.add), (tB, nn), (tB,))
            K.op("dve", lambda e: e.tensor_copy(out=WIDX.t[:], in_=tB.t[:]), (tB,), (WIDX,))
            K.op("dve", lambda e: e.scalar_tensor_tensor(out=tB.t[:], in0=eb.t[:], scalar=float(l * NE), in1=need.t[:], op0=ALU.add, op1=ALU.mult), (eb, need), (tB,))
            K.op("dve", lambda e: e.tensor_tensor(out=tB.t[:], in0=tB.t[:], in1=nn.t[:], op=ALU.add), (tB, nn), (tB,))
            K.op("dve", lambda e: e.tensor_copy(out=BIDX.t[:], in_=tB.t[:]), (tB,), (BIDX,))
            K.barrier()

        if phases is None or "moe" in phases:
          with ExitStack() as st:
            Wg = [K.sb(st, f"Wg{i}", [128, 8 * 2048], BF16) for i in range(2)]
            Wd = [K.sb(st, f"Wd{i}", [128, 8 * 1024], BF16) for i in range(2)]
            Bg = [K.sb(st, f"Bg{i}", [128, 2048], BF16) for i in range(2)]
            Bd = [K.sb(st, f"Bd{i}", [128, 1024], BF16) for i in range(2)]
            rt = [K.sb(st, f"rt{i}", [128, 4], I32) for i in range(4)]
            xg = [K.sb(st, f"xg{i}", [128, 1024], BF16) for i in range(3)]
            xgT = [K.sb(st, f"xgT{i}", [128, 8, 128], BF16) for i in range(2)]
            gs = [K.sb(st, f"gs{i}", [128, 1024], F32) for i in range(2)]
            sg = [K.sb(st, f"sg{i}", [128, 1024], F32) for i in range(2)]
            u1 = [K.sb(st, f"u1{i}", [128, 1024], F32) for i in range(2)]
            hb = [K.sb(st, f"hb{i}", [128, 1024], BF16) for i in range(2)]
            hT = [K.sb(st, f"hT{i}", [128, 8, 128], BF16) for i in range(2)]
            ysb = [K.sb(st, f"ysb{i}", [128, 1024], F32) for i in range(2)]
            if "bcW" not in K.__dict__:
                rW = nc.gpsimd.alloc_register("bcW")
                nc.gpsimd.reg_mov(rW, NL * NE * 128 - 1)
                K.bcW = nc.gpsimd.snap(rW, donate=True)
                rB = nc.gpsimd.alloc_register("bcB")
                nc.gpsimd.reg_mov(rB, NL * NE - 1)
                K.bcB = nc.gpsimd.snap(rB, donate=True)
            wguv = w_gu.rearrange("l e (p k) f -> (l e p) (k f)", k=8)
            wdnv = w_dn.rearrange("l e (p k) f -> (l e p) (k f)", k=8)
            bguv = b_gu.rearrange("l e f -> (l e) f")
            bdnv = b_dn.rearrange("l e f -> (l e) f")
            HB = NBLK // 2

            def blk(s_):
                return s_ // 2 if s_ % 2 == 0 else HB + s_ // 2

            def wloadG(s_):
                if s_ >= NBLK:
                    return
                par, b = s_ % 2, blk(s_)
                K.dma("pool", Wg[par], lambda e: e.indirect_dma_start(out=Wg[par].t[:, :], out_offset=None, in_=wguv,
                      in_offset=bass.IndirectOffsetOnAxis(ap=WIDX.t[:, b:b + 1], axis=0), bounds_check=K.bcW, oob_is_err=False),
                      reads=(WIDX,), writes=(Wg[par],))

            def bloadG(s_):
                if s_ >= NBLK:
                    return
                par, b = s_ % 2, blk(s_)
                K.dma("pool", Bg[par], lambda e: e.indirect_dma_start(out=Bg[par].t[:, :], out_offset=None, in_=bguv,
                      in_offset=bass.IndirectOffsetOnAxis(ap=BIDX.t[:, b:b + 1], axis=0), bounds_check=K.bcB, oob_is_err=False),
                      reads=(BIDX,), writes=(Bg[par],))

            def wloadD(s_):
                if s_ >= NBLK:
                    return
                par, b = s_ % 2, blk(s_)
                K.dma("pool", Wd[par], lambda e: e.indirect_dma_start(out=Wd[par].t[:, :], out_offset=None, in_=wdnv,
                      in_offset=bass.IndirectOffsetOnAxis(ap=WIDX.t[:, b:b + 1], axis=0), bounds_check=K.bcW, oob_is_err=False),
                      reads=(WIDX,), writes=(Wd[par],))
                K.dma("pool", Bd[par], lambda e: e.indirect_dma_start(out=Bd[par].t[:, :], out_offset=None, in_=bdnv,
                      in_offset=bass.IndirectOffsetOnAxis(ap=BIDX.t[:, b:b + 1], axis=0), bounds_check=K.bcB, oob_is_err=False),
                      reads=(BIDX,), writes=(Bd[par],))

            def rgath(s_):
                if s_ >= NBLK:
                    return
                r_, b = rt[s_ % 4], blk(s_)
                K.dma("pool", r_, lambda e: e.indirect_dma_start(out=r_.t[:, :], out_offset=None, in_=ROUTE[:, :],
                      in_offset=bass.IndirectOffsetOnAxis(ap=RIDX.t[:, b:b + 1], axis=0)), reads=(RIDX,), writes=(r_,))

            def xgath(s_):
                if s_ >= NBLK:
                    return
                r_, x_ = rt[s_ % 4], xg[s_ % 3]
                K.dma("pool", x_, lambda e: e.indirect_dma_start(out=x_.t[:, :], out_offset=None, in_=X1[:, :],
                      in_offset=bass.IndirectOffsetOnAxis(ap=r_.t[:, 0:1], axis=0)), reads=(r_,), writes=(x_,))

            pT_, pG, pH, pY = PSB[0], PSB[1:5], PSB[5], PSB[6:8]

            def S1a(s_):
                x_, xT_ = xg[s_ % 3], xgT[s_ % 2]
                xv = x_.t[:, :].rearrange("t (p k) -> t k p", k=8)
                pTb = pT_.t[:].bitcast(BF16)
                K.pe([lambda t, c=c: t.transpose(out=pTb[:, c * 128:(c + 1) * 128], in_=xv[:, c, :], identity=ident_b.t[:])
                      for c in range(8)], reads=(x_, ident_b), writes=(pT_,))
                K.op("act", lambda e: e.activation(out=xT_.t[:], in_=pTb[:, 0:1024].rearrange("p (c t) -> p c t", c=8), func=AF.Copy), (pT_,), (xT_,))

            def S1b(s_):
                par = s_ % 2
                xT_ = xgT[par]
                for n in range(4):
                    pb = pG[n]
                    pairs = [(xT_.t[:, kc, :], Wg[par].t[:, kc * 2048 + n * 512:kc * 2048 + (n + 1) * 512]) for kc in range(8)]
                    K.mm(pb.t[:, :], pairs, reads=(xT_, Wg[par]), writes=(pb,))

            def S2a(s_):
                par = s_ % 2
                g_, u_ = gs[par], u1[par]
                for n in range(2):
                    K.op("dve", lambda e, n=n: e.tensor_tensor(out=g_.t[:, n * 512:(n + 1) * 512], in0=pG[n].t[:, :], in1=Bg[par].t[:, n * 512:(n + 1) * 512], op=ALU.add), (pG[n], Bg[par]), (g_,))
                    K.op("dve", lambda e, n=n: e.tensor_tensor(out=u_.t[:, n * 512:(n + 1) * 512], in0=pG[2 + n].t[:, :], in1=Bg[par].t[:, 1024 + n * 512:1024 + (n + 1) * 512], op=ALU.add), (pG[2 + n], Bg[par]), (u_,))

            def S2b(s_):
                par, b = s_ % 2, blk(s_)
                g_, s2_, u_, h_, hT_, y_ = gs[par], sg[par], u1[par], hb[par], hT[par], ysb[par]
                K.op("dve", lambda e: e.tensor_scalar(out=g_.t[:], in0=g_.t[:], scalar1=LIMIT, scalar2=None, op0=ALU.min), (g_,), (g_,))
                K.op("act", lambda e: e.activation(out=s2_.t[:], in_=g_.t[:], func=AF.Sigmoid, scale=SW_ALPHA), (g_,), (s2_,))
                K.op("pool", lambda e: e.tensor_scalar(out=u_.t[:], in0=u_.t[:], scalar1=LIMIT, scalar2=-LIMIT, op0=ALU.min, op1=ALU.max), (u_,), (u_,))
                K.op("dve", lambda e: e.scalar_tensor_tensor(out=u_.t[:], in0=u_.t[:], scalar=1.0, in1=g_.t[:], op0=ALU.add, op1=ALU.mult), (u_, g_), (u_,))
                K.op("dve", lambda e: e.tensor_tensor(out=h_.t[:], in0=u_.t[:], in1=s2_.t[:], op=ALU.mult), (u_, s2_), (h_,))
                hv = h_.t[:, :].rearrange("t (p k) -> t k p", k=8)
                pTb = pH.t[:].bitcast(BF16)
                K.pe([lambda t, c=c: t.transpose(out=pTb[:, c * 128:(c + 1) * 128], in_=hv[:, c, :], identity=ident_b.t[:])
                      for c in range(8)], reads=(h_, ident_b), writes=(pH,))
                K.op("act", lambda e: e.activation(out=hT_.t[:], in_=pTb[:, 0:1024].rearrange("p (c t) -> p c t", c=8), func=AF.Copy), (pH,), (hT_,))
                for n in range(2):
                    pb = pY[n]
                    pairs = [(hT_.t[:, kc, :], Wd[par].t[:, kc * 1024 + n * 512:kc * 1024 + (n + 1) * 512]) for kc in range(8)]
                    K.mm(pb.t[:, :], pairs, reads=(hT_, Wd[par]), writes=(pb,))
                    K.op("dve", lambda e, n=n, pb=pb: e.tensor_tensor(out=y_.t[:, n * 512:(n + 1) * 512], in0=pb.t[:, :], in1=Bd[par].t[:, n * 512:(n + 1) * 512], op=ALU.add), (pb, Bd[par]), (y_,))
                K.store("sp", y_, YB[b * 128:(b + 1) * 128, :], y_.t[:])

            wloadG(0); wloadG(1); bloadG(0); bloadG(1); wloadD(0); wloadD(1)
            rgath(0); rgath(1); rgath(2)
            xgath(0); xgath(1)
            S1a(0); S1b(0)
            wloadG(2)
            for s_ in range(NBLK):
                rgath(s_ + 3)
                xgath(s_ + 2)
                if s_ + 1 < NBLK:
                    S1a(s_ + 1)
                S2a(s_)
                bloadG(s_ + 2)
                if s_ + 1 < NBLK:
                    S1b(s_ + 1)
                    wloadG(s_ + 3)
                S2b(s_)
                wloadD(s_ + 2)
            K.barrier()

        if phases is None or "comb" in phases:
          with ExitStack() as st:
            g2b = K.sb(st, "g2b", [128, 1024], F32)
            b2b = K.sb(st, "b2b", [128, 1024], F32)
            yk = [[K.sb(st, f"yk{i}_{k}", [128, 1024], F32) for k in range(4)] for i in range(2)]
            x1t = [K.sb(st, f"x1t{i}", [128, 1024], F32) for i in range(2)]
            acc = [K.sb(st, f"acc{i}", [128, 1024], F32) for i in range(2)]
            xo = [K.sb(st, f"xo{i}", [128, 1024], F32) for i in range(2)]
            oh = [K.sb(st, f"oh{i}", [128, 32], F32) for i in range(2)]
            yrf = [K.sb(st, f"yrf{i}", [128, 4], F32) for i in range(2)]
            yri = [K.sb(st, f"yri{i}", [128, 4], I32) for i in range(2)]
            nmr = [K.sb(st, f"nmrc{i}", [128, 1], F32) for i in range(2)]
            stt = [K.sb(st, f"stt{i}", [128, 12], F32) for i in range(2)]
            mv = [K.sb(st, f"mv{i}", [128, 2], F32) for i in range(2)]
            rstd = [K.sb(st, f"rstd{i}", [128, 1], F32) for i in range(2)]
            K.load("sp", g2b, g2b.t[:], ln2_row[l][0:1, :].to_broadcast([128, 1024]))
            K.load("sp", b2b, b2b.t[:], ln2_row[l][1:2, :].to_broadcast([128, 1024]))

            def cload(tile):
                q = tile % 2
                for k in range(4):
                    K.op("dve", lambda e, k=k, q=q, tile=tile: e.tensor_scalar(out=oh[q].t[:], in0=iota32.t[:], scalar1=EKF.t[:, tile, k:k + 1], scalar2=None, op0=ALU.is_equal), (iota32, EKF), (oh[q],))
                    K.op("dve", lambda e, q=q: e.tensor_tensor(out=oh[q].t[:], in0=oh[q].t[:], in1=BS128.t[:], op=ALU.mult), (oh[q], BS128), (oh[q],))
                    K.op("dve", lambda e, k=k, q=q: e.tensor_reduce(out=yrf[q].t[:, k:k + 1], in_=oh[q].t[:], axis=AX.X, op=ALU.add), (oh[q],), (yrf[q],))
                K.op("dve", lambda e, q=q, tile=tile: e.tensor_tensor(out=yrf[q].t[:], in0=yrf[q].t[:], in1=POSK.t[:, tile, :], op=ALU.add), (yrf[q], POSK), (yrf[q],))
                K.op("dve", lambda e, q=q: e.tensor_copy(out=yri[q].t[:], in_=yrf[q].t[:]), (yrf[q],), (yri[q],))
                for k in range(4):
                    K.dma("pool", yk[q][k], lambda e, k=k, q=q: e.indirect_dma_start(out=yk[q][k].t[:, :], out_offset=None, in_=YB[:, :],
                          in_offset=bass.IndirectOffsetOnAxis(ap=yri[q].t[:, k:k + 1], axis=0)), reads=(yri[q],), writes=(yk[q][k],))
                K.load("sp", x1t[q], x1t[q].t[:], X1[tile * 128:(tile + 1) * 128, :])
            cload(0)
            for tile in range(NTILE):
                q = tile % 2
                if tile + 1 < NTILE:
                    cload(tile + 1)
                a_, o_ = acc[q], xo[q]
                K.op("dve", lambda e: e.tensor_scalar(out=a_.t[:], in0=yk[q][0].t[:], scalar1=GATES.t[:, tile, 0:1], scalar2=None, op0=ALU.mult), (yk[q][0], GATES), (a_,))
                for k in range(1, 4):
                    K.op("dve", lambda e, k=k: e.scalar_tensor_tensor(out=a_.t[:], in0=yk[q][k].t[:], scalar=GATES.t[:, tile, k:k + 1], in1=a_.t[:], op0=ALU.mult, op1=ALU.add), (yk[q][k], GATES, a_), (a_,))
                K.op("dve", lambda e: e.scalar_tensor_tensor(out=a_.t[:], in0=x1t[q].t[:], scalar=ALPHA, in1=a_.t[:], op0=ALU.mult, op1=ALU.add), (x1t[q], a_), (a_,))
                ln_stats(st, lambda c: a_.t[:, c * 512:(c + 1) * 512], a_, mv[q], stt[q], rstd[q])
                K.op("dve", lambda e: e.scalar_tensor_tensor(out=nmr[q].t[:], in0=mv[q].t[:, 0:1], scalar=-1.0, in1=rstd[q].t[:, 0:1], op0=ALU.mult, op1=ALU.mult), (mv[q], rstd[q]), (nmr[q],))
                K.op("act", lambda e: e.activation(out=a_.t[:], in_=a_.t[:], func=AF.Identity, scale=rstd[q].t[:, 0:1], bias=nmr[q].t[:, 0:1]), (a_, rstd[q], nmr[q]), (a_,))
                K.op("dve", lambda e: e.tensor_tensor(out=a_.t[:], in0=a_.t[:], in1=g2b.t[:], op=ALU.mult), (a_, g2b), (a_,))
                K.op("pool", lambda e: e.tensor_tensor(out=o_.t[:], in0=a_.t[:], in1=b2b.t[:], op=ALU.add), (a_, b2b), (o_,))
                K.store("sp", o_, XO[tile * 128:(tile + 1) * 128, :], o_.t[:])
            K.barrier()

    K.barrier()
    top.close()
    return nc, dict(NT=NT, NBLK=NBLK, CAP=CAP, NL=NL, NSEQ=NSEQ)


def host_inputs(params, NL, NT, NBLK, CAP):
    f = lambda a: np.ascontiguousarray(np.asarray(a, dtype=np.float32))
    d = {}
    d["w_in"] = f(params["w_in"][:NL])
    d["w_st"] = f(np.transpose(np.asarray(params["w_s"][:NL]), (0, 3, 1, 2)))
    d["w_pa"] = f(params["w_pa"][:NL])
    d["w_pb"] = f(params["w_pb"][:NL])
    d["w_o"] = f(params["w_o"][:NL])
    d["w_r"] = f(params["w_r"][:NL])
    d["w_gu"] = f(params["w_gu"][:NL])
    d["w_down"] = f(params["w_down"][:NL])
    d["b_gu"] = f(params["b_gu"][:NL])
    d["b_down"] = f(params["b_down"][:NL])
    b_in = np.asarray(params["b_in"][:NL], dtype=np.float32)
    d["bin_col"] = f(np.transpose(b_in.reshape(NL, 68, 128), (0, 2, 1)))
    d["bin_row"] = f(b_in.reshape(NL, 1, DIN))
    lg = np.asarray(params["ln_v_g"][:NL], dtype=np.float32).reshape(NL, 8, 128)
    lb = np.asarray(params["ln_v_b"][:NL], dtype=np.float32).reshape(NL, 8, 128)
    d["lnv_col"] = f(np.concatenate([np.transpose(lg, (0, 2, 1)), np.transpose(lb, (0, 2, 1))], axis=2))
    d["bs_row"] = f(np.asarray(params["b_s"][:NL]).reshape(NL, 1, 1024))
    d["ln1_row"] = f(np.stack([np.asarray(params["ln1_g"][:NL]), np.asarray(params["ln1_b"][:NL])], axis=1))
    d["ln2_row"] = f(np.stack([np.asarray(params["ln2_g"][:NL]), np.asarray(params["ln2_b"][:NL])], axis=1))
    d["br_row"] = f(np.asarray(params["b_r"][:NL]).reshape(NL, 1, NE))
    for k, v in _consts(NT, NBLK, CAP).items():
        d["c_" + k] = v
    return d


def kernel(x_prompt, x_sample, w_in, b_in, ln_v_g, ln_v_b, w_s, b_s, w_pa, w_pb, w_o, ln1_g, ln1_b,
           w_r, b_r, w_gu, b_gu, w_down, b_down, ln2_g, ln2_b):
    params = dict(w_in=w_in, b_in=b_in, ln_v_g=ln_v_g, ln_v_b=ln_v_b, w_s=w_s, b_s=b_s, w_pa=w_pa, w_pb=w_pb, w_o=w_o,
                  ln1_g=ln1_g, ln1_b=ln1_b, w_r=w_r, b_r=b_r, w_gu=w_gu, b_gu=b_gu, w_down=w_down, b_down=b_down,
                  ln2_g=ln2_g, ln2_b=ln2_b)
    NL, NSEQ = 4, 2
    nc, meta = build_program(NL=NL, NSEQ=NSEQ)
    shared = host_inputs(params, NL, meta["NT"], meta["NBLK"], meta["CAP"])
    xp = np.asarray(x_prompt, dtype=np.float32)
    xs = np.asarray(x_sample, dtype=np.float32)
    seqs = []
    for c in range(8):
        if c < 4:
            seqs.append(np.concatenate([xp[2 * c], xp[2 * c + 1]], axis=0))
        else:
            seqs.append(np.concatenate([xs[c - 4], xs[c - 4]], axis=0))
    in_maps = []
    for c in range(8):
        m = dict(shared)
        m["x"] = np.ascontiguousarray(seqs[c])
        in_maps.append(m)
    res = run_bass_kernel_spmd(nc, in_maps, core_ids=list(range(8)))
    yp = np.zeros_like(xp)
    ys = np.zeros_like(xs)
    for c in range(8):
        y = np.asarray(res.results[c]["y"], dtype=np.float32).reshape(2, S, D)
        if c < 4:
            yp[2 * c] = y[0]
            yp[2 * c + 1] = y[1]
        else:
            ys[c - 4] = y[0]
    return (yp, ys)
```

```python
import numpy as np
import ml_dtypes
from contextlib import ExitStack
import concourse.bass as bass
import concourse.mybir as mybir
from concourse.bass_utils import run_bass_kernel_spmd

F32 = mybir.dt.float32
BF16 = mybir.dt.bfloat16
I32 = mybir.dt.int32
AF = mybir.ActivationFunctionType
ALU = mybir.AluOpType
AX = mybir.AxisListType

S = 4096
D = 1024
DIN = 8704
NE = 32
TOPK = 4
PATTERNS = ((128, 1), (512, 4), (2048, 16))
NHEAD = 12
SLOPES = np.array([2.0 ** (-8.0 * (h + 1) / NHEAD) for h in range(NHEAD)], dtype=np.float32).reshape(3, 4)
DEPTH_FULL = 4
ALPHA = (2.0 * DEPTH_FULL) ** 0.25
EPS = 1e-5
NEG = -1e30
LIMIT = 7.0
SW_ALPHA = 1.702
SCALE = 128.0 ** -0.5
BIG = 1 << 28


class Buf:
    def __init__(self, t, name):
        self.t = t
        self.name = name
        self.lw = None
        self.rd = {}
        self.ds = None

    def __getitem__(self, idx):
        return self.t[idx]


class DSem:
    def __init__(self, h, key):
        self.h = h
        self.key = key
        self.cnt = 0


class KB:
    SAME_ENG = True

    def __init__(self, nc):
        self.nc = nc
        self.E = {"pe": nc.tensor, "act": nc.scalar, "dve": nc.vector, "pool": nc.gpsimd, "sp": nc.sync}
        self.sem = {k: nc.alloc_semaphore("es_" + k) for k in self.E}
        self.cnt = {k: 0 for k in self.E}
        self.seen = {k: {} for k in self.E}
        self.semobj = dict(self.sem)
        self.dsems = {}
        self.dfree = []
        self.uid = 0
        self.psrr = 0

    def sb(self, stack, name, shape, dt):
        self.uid += 1
        t = stack.enter_context(self.nc.sbuf_tensor(f"{name}_{self.uid}", list(shape), dt))
        b = Buf(t, name)
        stack.callback(self._release, b)
        return b

    def ps(self, stack, name, shape, dt):
        self.uid += 1
        t = stack.enter_context(self.nc.psum_tensor(f"{name}_{self.uid}", list(shape), dt))
        return Buf(t, name)

    def _release(self, b):
        if b.ds is not None:
            self.dfree.append(b.ds)
            b.ds = None

    def _getds(self, b):
        if b.ds is None:
            if self.dfree:
                b.ds = self.dfree.pop()
            else:
                key = f"d{len(self.dsems)}"
                d = DSem(self.nc.alloc_semaphore("ds_" + key), key)
                self.dsems[key] = d
                self.semobj[key] = d.h
                b.ds = d
        return b.ds

    def _deps(self, reads, writes):
        d = []
        for b in reads:
            if b.lw is not None:
                d.append(b.lw)
        for b in writes:
            if b.lw is not None:
                d.append(b.lw)
            d.extend(b.rd.items())
        return d

    def _wait(self, eng, deps):
        need = {}
        for k, v in deps:
            if k == eng and (eng == "pe" or not self.SAME_ENG):
                continue
            if need.get(k, 0) < v:
                need[k] = v
        for k, v in need.items():
            if self.seen[eng].get(k, 0) < v:
                self.E[eng].wait_ge(self.semobj[k], v)
                self.seen[eng][k] = v

    def _mark(self, tok, reads, writes):
        k, v = tok
        for b in writes:
            b.lw = tok
            b.rd = {}
        for b in reads:
            if b in writes:
                continue
            if b.rd.get(k, 0) < v:
                b.rd[k] = v

    def op(self, eng, fn, reads=(), writes=()):
        self._wait(eng, self._deps(reads, writes))
        ins = fn(self.E[eng])
        self.cnt[eng] += 1
        ins.then_inc(self.sem[eng], 1)
        tok = (eng, self.cnt[eng])
        self._mark(tok, reads, writes)
        return tok

    def pe(self, fns, reads=(), writes=()):
        self._wait("pe", self._deps(reads, writes))
        ins = None
        for f in fns:
            ins = f(self.nc.tensor)
        self.cnt["pe"] += 1
        ins.then_inc(self.sem["pe"], 1)
        tok = ("pe", self.cnt["pe"])
        self._mark(tok, reads, writes)
        return tok

    def mm(self, out, pairs, reads=(), writes=()):
        n = len(pairs)
        fns = []
        for i, (l, r) in enumerate(pairs):
            fns.append(lambda t, l=l, r=r, i=i: t.matmul(out, lhsT=l, rhs=r, start=(i == 0), stop=(i == n - 1)))
        return self.pe(fns, reads, writes)

    def dma(self, q, sbuf, fn, reads=(), writes=()):
        self._wait(q, self._deps(reads, writes))
        d = self._getds(sbuf)
        ins = fn(self.E[q])
        d.cnt += 16
        ins.then_inc(d.h, 16)
        tok = (d.key, d.cnt)
        self._mark(tok, reads, writes)
        return tok

    def load(self, q, dst, dst_ap, src_ap, **kw):
        return self.dma(q, dst, lambda e: e.dma_start(out=dst_ap, in_=src_ap, **kw), reads=(), writes=(dst,))

    def store(self, q, src, dst_ap, src_ap, **kw):
        return self.dma(q, src, lambda e: e.dma_start(out=dst_ap, in_=src_ap, **kw), reads=(src,), writes=())

    def barrier(self):
        tot = dict(self.cnt)
        for k, d in self.dsems.items():
            tot[k] = d.cnt
        for e in self.E:
            for k, v in tot.items():
                if v > 0 and self.seen[e].get(k, 0) < v:
                    self.E[e].wait_ge(self.semobj[k], v)
                    self.seen[e][k] = v

    def nextps(self, banks):
        b = banks[self.psrr % len(banks)]
        self.psrr += 1
        return b


def _consts(NT, NBLK, CAP):
    c = {}
    c["ident_f"] = np.eye(128, dtype=np.float32)
    c["ident_b"] = np.eye(128, dtype=np.float32).astype(ml_dtypes.bfloat16)
    tri = (np.arange(128)[:, None] < np.arange(128)[None, :]).astype(np.float32)
    c["tri_b"] = tri.astype(ml_dtypes.bfloat16)
    c["ones_b"] = np.ones((128, 128), dtype=ml_dtypes.bfloat16)
    c["ones_f"] = np.ones((128, 128), dtype=np.float32)
    tab = np.zeros((3, 128, 12, 256), dtype=np.float32)
    a = np.arange(128)[:, None]
    cc = np.arange(256)[None, :]
    rel = cc - 64 - a
    for g, (win, d) in enumerate(PATTERNS):
        for h in range(4):
            base = np.where(np.abs(rel) <= 64, -SLOPES[g, h] * d * np.abs(rel), NEG).astype(np.float32)
            for var in range(3):
                t = base.copy()
                if var == 1:
                    t[:, :64] = NEG
                if var == 2:
                    t[:, 192:] = NEG
                tab[g, :, h * 3 + var, :] = t
    c["att_bias"] = tab
    c["iota32"] = np.tile(np.arange(32, dtype=np.float32)[None, :], (128, 1))
    c["ecap"] = np.tile((np.arange(32, dtype=np.float32) * CAP)[None, :], (128, 1))
    c["pcol"] = np.arange(128, dtype=np.float32)[:, None].copy()
    c["jthr"] = np.tile((np.arange(64, dtype=np.float32) * 128.0)[None, :], (32, 1))
    c["iota_b"] = np.tile(np.arange(NBLK, dtype=np.float32)[None, :], (128, 1))
    c["triu32"] = (np.arange(32)[:, None] <= np.arange(32)[None, :]).astype(np.float32)
    tok = np.zeros((128, NT // 128, 4), dtype=np.int32)
    tok[:, :, 0] = np.arange(NT // 128)[None, :] * 128 + np.arange(128)[:, None]
    c["tokid4"] = tok
    ri = np.zeros((128, 128, 4), dtype=np.int32)
    ri[:, :, 0] = NT
    c["rinit"] = ri
    return c


CONST_DT = {"ident_f": F32, "ident_b": BF16, "tri_b": BF16, "ones_b": BF16, "ones_f": F32, "att_bias": F32,
            "iota32": F32, "ecap": F32, "pcol": F32, "jthr": F32, "iota_b": F32, "triu32": F32,
            "tokid4": I32, "rinit": I32}


def build_program(NL=4, NSEQ=2, debug=False, phases=None):
    NT = NSEQ * S
    NTILE = NT // 128
    CAP = NT
    NBLK = NT * TOPK // 128 + NE
    RROWS = NE * CAP + 128
    nc = bass.Bass("TRN2", target_bir_lowering=False)
    K = KB(nc)
    sk = "ExternalOutput" if debug else "Internal"

    def din(name, shape, dt=F32):
        return nc.dram_tensor(name, list(shape), dt, kind="ExternalInput").ap()

    def dscr(name, shape, dt=F32):
        return nc.dram_tensor(name, list(shape), dt, kind=sk).ap()

    x_in = din("x", [NT, D])
    w_in = din("w_in", [NL, D, DIN])
    w_st = din("w_st", [NL, 128, 8, 128])
    w_pa = din("w_pa", [NL, D, D])
    w_pb = din("w_pb", [NL, 512, D])
    w_o = din("w_o", [NL, D, D])
    w_r = din("w_r", [NL, D, NE])
    w_gu = din("w_gu", [NL, NE, D, 2 * D])
    w_dn = din("w_down", [NL, NE, D, D])
    b_gu = din("b_gu", [NL, NE, 2 * D])
    b_dn = din("b_down", [NL, NE, D])
    bin_col = din("bin_col", [NL, 128, 68])
    bin_row = din("bin_row", [NL, 1, DIN])
    lnv_col = din("lnv_col", [NL, 128, 16])
    bs_row = din("bs_row", [NL, 1, D])
    ln1_row = din("ln1_row", [NL, 2, D])
    ln2_row = din("ln2_row", [NL, 2, D])
    br_row = din("br_row", [NL, 1, NE])
    cshape = {k: v.shape for k, v in _consts(128 * 2, 8, 1).items()}
    cshape["iota_b"] = (128, NBLK)
    cshape["tokid4"] = (128, NTILE, 4)
    cin = {k: din("c_" + k, cshape[k], CONST_DT[k]) for k in cshape}
    y_out = nc.dram_tensor("y", [NT, D], F32, kind="ExternalOutput").ap()

    XS = dscr("XS", [NT, D])
    X1 = dscr("X1", [NT + 128, D])
    NTP = [NSEQ * d * (S // d + 128) for (_, d) in PATTERNS]
    QT = [dscr(f"QT{g}", [4, 128, NTP[g]], BF16) for g in range(3)]
    KT = [dscr(f"KT{g}", [4, 128, NTP[g]], BF16) for g in range(3)]
    VV = [dscr(f"VV{g}", [NTP[g], 512], BF16) for g in range(3)]
    OGall = dscr("OG", [3, NT, 516])
    OG = [OGall[g] for g in range(3)]
    MA = dscr("MA", [8, 128, NT], BF16)
    ROUTE = dscr("ROUTE", [RROWS, 4], I32)
    YB = dscr("YB", [NBLK * 128, D])

    top = ExitStack()
    ident_f = K.sb(top, "ident_f", [128, 128], F32)
    ident_b = K.sb(top, "ident_b", [128, 128], BF16)
    ones_b = K.sb(top, "ones_b", [128, 128], BF16)
    ones_f = K.sb(top, "ones_f", [128, 128], F32)
    tri_b = K.sb(top, "tri_b", [128, 128], BF16)
    iota32 = K.sb(top, "iota32", [128, 32], F32)
    ecap = K.sb(top, "ecap", [128, 32], F32)
    pcol = K.sb(top, "pcol", [128, 1], F32)
    tokid4 = K.sb(top, "tokid4", [128, NTILE, 4], I32)
    GATES = K.sb(top, "GATES", [128, NTILE, 4], F32)
    EKF = K.sb(top, "EKF", [128, NTILE, 4], F32)
    POSK = K.sb(top, "POSK", [128, NTILE, 4], F32)
    RUN = K.sb(top, "RUN", [128, 32], F32)
    BS128 = K.sb(top, "BS128", [128, 32], F32)
    RIDX = K.sb(top, "RIDX", [128, NBLK], I32)
    WIDX = K.sb(top, "WIDX", [128, NBLK], I32)
    BIDX = K.sb(top, "BIDX", [128, NBLK], I32)
    PSB = [K.ps(top, f"psb{i}", [128, 512], F32) for i in range(8)]

    for nm, b in (("ident_f", ident_f), ("ident_b", ident_b), ("ones_b", ones_b), ("ones_f", ones_f), ("tri_b", tri_b),
                  ("iota32", iota32), ("ecap", ecap), ("pcol", pcol), ("tokid4", tokid4)):
        K.load("sp", b, b.t[:], cin[nm])

    def evac(i, out_ap, in_ap, reads, writes):
        if i % 2 == 0:
            return K.op("act", lambda e: e.activation(out=out_ap, in_=in_ap, func=AF.Copy), reads, writes)
        return K.op("dve", lambda e: e.tensor_copy(out=out_ap, in_=in_ap), reads, writes)

    def load_xT(st, xt, xT, src_rows_ap, TM):
        nj = TM // 128
        K.load("sp", xt, xt.t[:, 0:nj, :], src_rows_ap)

    def transposes_x(xt, xT, TM, evi=0, act_only=False):
        nj = TM // 128
        for kc in range(8):
            pb = K.nextps(PSB)
            fns = [lambda t, j=j, kc=kc, pb=pb: t.transpose(out=pb.t[:, j * 128:(j + 1) * 128], in_=xt.t[:, j, kc * 128:(kc + 1) * 128], identity=ident_f.t[:])
                   for j in range(nj)]
            K.pe(fns, reads=(xt, ident_f), writes=(pb,))
            evac(0 if act_only else kc + evi, xT.t[:, kc, 0:TM], pb.t[:, 0:TM], (pb,), (xT,))

    def ln_stats(st, v_ap_fn, vbuf, mv, stt, rstd):
        K.op("dve", lambda e: e.bn_stats(out=stt.t[:, 0:6], in_=v_ap_fn(0)), (vbuf,), (stt,))
        K.op("dve", lambda e: e.bn_stats(out=stt.t[:, 6:12], in_=v_ap_fn(1)), (vbuf,), (stt,))
        K.op("dve", lambda e: e.bn_aggr(out=mv.t[:, 0:2], in_=stt.t[:, 0:12].rearrange("p (c s) -> p c s", s=6)), (stt,), (mv,))
        K.op("act", lambda e: e.activation(out=rstd.t[:, 0:1], in_=mv.t[:, 1:2], func=AF.Sqrt, bias=EPS), (mv,), (rstd,))
        K.op("dve", lambda e: e.reciprocal(out=rstd.t[:, 0:1], in_=rstd.t[:, 0:1]), (rstd,), (rstd,))

    with ExitStack() as st:
        zt = K.sb(st, "zt", [128, 4096], BF16)
        K.op("pool", lambda e: e.memset(zt.t[:], 0.0), (), (zt,))
        for g in range(3):
            n = NTP[g]
            for h in range(4):
                for c0 in range(0, n, 4096):
                    w = min(4096, n - c0)
                    K.store("sp", zt, KT[g][h, :, c0:c0 + w], zt.t[:, 0:w])
            vv = VV[g].rearrange("(a p) f -> p a f", p=128)
            na = n // 128
            for a0 in range(0, na, 8):
                w = min(8, na - a0)
                K.store("sp", zt, vv[:, a0:a0 + w, :], zt.t[:, 0:w * 512].rearrange("p (a f) -> p a f", f=512))
        K.store("sp", zt, X1[NT:NT + 128, :], zt.t[:].bitcast(F32)[:, 0:1024])
        K.barrier()

    for l in range(NL):
        XA = x_in if l == 0 else XS
        XO = y_out if l == NL - 1 else XS
        if phases is None or "qkv" in phases:
          with ExitStack() as st:
            wg = [K.sb(st, f"wg{i}", [128, 8, 1536], BF16) for i in range(2)]
            bcol = K.sb(st, "bcol", [128, 68], F32)
            brow = K.sb(st, "brow", [1, DIN], BF16)
            xts = [K.sb(st, f"xt{i}", [128, 4, D], F32) for i in range(2)]
            xTs = [K.sb(st, f"xT{i}", [128, 8, 512], BF16) for i in range(2)]
            qks = [K.sb(st, f"qk{i}", [128, 8, 512], BF16) for i in range(2)]
            vss = [K.sb(st, f"vs{i}", [128, 4, 512], BF16) for i in range(2)]
            K.load("sp", bcol, bcol.t[:], bin_col[l])
            K.load("pool", brow, brow.t[:], bin_row[l])
            it = 0
            for g, (win, d) in enumerate(PATTERNS):
                L = S // d
                LP = L + 128
                TM = min(512, L)
                nj = TM // 128
                c0 = 2048 + g * 1536
                W = wg[g % 2]
                K.load("pool", W, W.t[:], w_in[l][:, c0:c0 + 1536].rearrange("(kc p) c -> p kc c", p=128))
                XAv = XA.rearrange("(s i dd) c -> s dd i c", s=NSEQ, dd=d)
                tiles = [(s, r, m) for s in range(NSEQ) for r in range(d) for m in range(L // TM)]

                def src(tl):
                    s, r, m = tl
                    return XAv[s, r, m * TM:(m + 1) * TM].rearrange("(j a) c -> a j c", a=128)
                K.load("sp", xts[it % 2], xts[it % 2].t[:, 0:nj, :], src(tiles[0]))
                for ti, tl in enumerate(tiles):
                    s, r, m = tl
                    xt, xT, qk, vs = xts[it % 2], xTs[it % 2], qks[it % 2], vss[it % 2]
                    if ti + 1 < len(tiles):
                        nx = xts[(it + 1) % 2]
                        K.load("sp", nx, nx.t[:, 0:nj, :], src(tiles[ti + 1]))
                    transposes_x(xt, xT, TM)
                    col0 = (s * d + r) * LP + 64 + m * TM
                    for hs in range(8):
                        pb = K.nextps(PSB)
                        K.mm(pb.t[:, 0:TM], [(W.t[:, kc, hs * 128:(hs + 1) * 128], xT.t[:, kc, 0:TM]) for kc in range(8)],
                             reads=(W, xT), writes=(pb,))
                        bc = (c0 // 128) + hs
                        K.op("act", lambda e, pb=pb, hs=hs, bc=bc: e.activation(out=qk.t[:, hs, 0:TM], in_=pb.t[:, 0:TM], func=AF.Identity,
                                                                              bias=bcol.t[:, bc:bc + 1]),
                             (pb, bcol), (qk,))
                    K.store("sp", qk, QT[g][:, :, col0:col0 + TM].rearrange("h p c -> p h c"), qk.t[:, 0:4, 0:TM])
                    K.store("sp", qk, KT[g][:, :, col0:col0 + TM].rearrange("h p c -> p h c"), qk.t[:, 4:8, 0:TM])
                    for j in range(nj):
                        pb = K.nextps(PSB)
                        pairs = [(xT.t[:, kc, j * 128:(j + 1) * 128], W.t[:, kc, 1024:1536]) for kc in range(8)]
                        pairs.append((ones_b.t[0:1, :], brow.t[0:1, c0 + 1024:c0 + 1536]))
                        K.mm(pb.t[:, :], pairs, reads=(W, xT, ones_b, brow), writes=(pb,))
                        evac(j, vs.t[:, j, :], pb.t[:, :], (pb,), (vs,))
                    K.store("sp", vs, VV[g][col0:col0 + TM, :].rearrange("(j p) f -> p j f", p=128), vs.t[:, 0:nj, :])
                    it += 1
            K.barrier()

        if phases is None or "att" in phases:
          with ExitStack() as st:
            bts = [K.sb(st, f"bt{i}", [128, 12, 256], F32) for i in range(3)]
            qcs = [K.sb(st, f"qc{i}", [128, 4, 1024], BF16) for i in range(2)]
            kcs = [K.sb(st, f"kc{i}", [128, 4, 1152], BF16) for i in range(2)]
            vcs = [K.sb(st, f"vc{i}", [128, 9, 512], BF16) for i in range(2)]
            NB_ = 3
            ogs = [K.sb(st, f"og{i}", [128, 516], F32) for i in range(NB_)]
            nmx = [K.sb(st, f"nmx{i}", [128, 4], F32) for i in range(NB_)]
            den = [K.sb(st, f"den{i}", [128, 4], F32) for i in range(NB_)]
            rdn = [K.sb(st, f"rdn{i}", [128, 4], F32) for i in range(NB_)]
            lnd = [K.sb(st, f"lnd{i}", [128, 4], F32) for i in range(NB_)]
            NU_ = 4
            sbs = [K.sb(st, f"sS{i}", [128, 256], F32) for i in range(NU_)]
            pbs = [K.sb(st, f"sP{i}", [128, 256], BF16) for i in range(NU_)]
            pts = [K.sb(st, f"sPT{i}", [128, 256], BF16) for i in range(NU_)]
            for g in range(3):
                K.load("sp", bts[g], bts[g].t[:], cin["att_bias"][g])
            chunks = []
            for g, (win, d) in enumerate(PATTERNS):
                L = S // d
                CH = min(1024, L)
                for s_ in range(NSEQ):
                    for r in range(d):
                        for ch in range(L // CH):
                            chunks.append((g, d, L, CH, s_, r, ch))

            def ldchunk(ci):
                g, d, L, CH, s_, r, ch = chunks[ci]
                LP = L + 128
                nb = CH // 128
                col0 = (s_ * d + r) * LP + 64 + ch * CH
                qc, kc_, vc = qcs[ci % 2], kcs[ci % 2], vcs[ci % 2]
                K.load("sp", qc, qc.t[:, :, 0:CH], QT[g][:, :, col0:col0 + CH].rearrange("h p c -> p h c"))
                K.load("sp", kc_, kc_.t[:, :, 0:CH + 128], KT[g][:, :, col0 - 64:col0 + CH + 64].rearrange("h p c -> p h c"))
                K.load("sp", vc, vc.t[:, 0:nb + 1, :], VV[g][col0 - 64:col0 + CH + 64, :].rearrange("(c p) f -> p c f", p=128))

            units = []
            bidx = 0
            for ci, (g, d, L, CH, s_, r, ch) in enumerate(chunks):
                nb = CH // 128
                nblk = L // 128
                for i in range(nb):
                    blk = ch * nb + i
                    var = 1 if blk == 0 else (2 if blk == nblk - 1 else 0)
                    for h in range(4):
                        units.append(dict(ci=ci, g=g, d=d, s=s_, r=r, i=i, h=h, var=var, b=bidx, p0=ch * CH + i * 128,
                                          lastc=(i == nb - 1 and h == 3)))
                    bidx += 1
            NU = len(units)
            state = {}

            def stA(u):
                U = units[u]
                qc, kc_ = qcs[U["ci"] % 2], kcs[U["ci"] % 2]
                bt = bts[U["g"]]
                i, h, var = U["i"], U["h"], U["var"]
                sS, sP = sbs[u % NU_], pbs[u % NU_]
                nm_, dn_ = nmx[U["b"] % NB_], den[U["b"] % NB_]
                pS = K.nextps(PSB)
                K.mm(pS.t[:, 0:256], [(qc.t[:, h, i * 128:(i + 1) * 128], kc_.t[:, h, i * 128:i * 128 + 256])], reads=(qc, kc_), writes=(pS,))
                K.op("dve", lambda e: e.scalar_tensor_tensor(out=sS.t[:], in0=pS.t[:, 0:256], scalar=SCALE, in1=bt.t[:, h * 3 + var, :], op0=ALU.mult, op1=ALU.add),
                     (pS, bt), (sS,))
                K.op("dve", lambda e: e.tensor_reduce(out=nm_.t[:, h:h + 1], in_=sS.t[:], axis=AX.X, op=ALU.max, negate=True), (sS,), (nm_,))
                K.op("act", lambda e: e.activation(out=sP.t[:], in_=sS.t[:], func=AF.Exp, bias=nm_.t[:, h:h + 1], accum_out=dn_.t[:, h:h + 1]),
                     (sS, nm_), (sP, dn_))

            def stB(u):
                sP, sPT = pbs[u % NU_], pts[u % NU_]
                pT = K.nextps(PSB)
                pTb = pT.t[:].bitcast(BF16)
                K.pe([lambda t, c=c: t.transpose(out=pTb[:, c * 128:(c + 1) * 128], in_=sP.t[:, c * 128:(c + 1) * 128], identity=ident_b.t[:])
                      for c in range(2)], reads=(sP, ident_b), writes=(pT,))
                K.op("act", lambda e: e.activation(out=sPT.t[:], in_=pTb[:, 0:256], func=AF.Copy), (pT,), (sPT,))

            def stC(u):
                U = units[u]
                vc = vcs[U["ci"] % 2]
                i, h = U["i"], U["h"]
                sPT = pts[u % NU_]
                q_ = U["b"] % NB_
                og, nm_, dn_, rd_, ld_ = ogs[q_], nmx[q_], den[q_], rdn[q_], lnd[q_]
                pO = K.nextps(PSB)
                K.mm(pO.t[:, 0:128], [(sPT.t[:, c * 128:(c + 1) * 128], vc.t[:, i + c, h * 128:(h + 1) * 128]) for c in range(2)], reads=(sPT, vc), writes=(pO,))
                K.op("dve", lambda e: e.reciprocal(out=rd_.t[:, h:h + 1], in_=dn_.t[:, h:h + 1]), (dn_,), (rd_,))
                K.op("act", lambda e: e.activation(out=og.t[:, h * 128:(h + 1) * 128], in_=pO.t[:, 0:128], func=AF.Identity, scale=rd_.t[:, h:h + 1]),
                     (pO, rd_), (og,))
                if h == 3:
                    K.op("act", lambda e: e.activation(out=ld_.t[:], in_=dn_.t[:], func=AF.Ln), (dn_,), (ld_,))
                    K.op("dve", lambda e: e.tensor_tensor(out=og.t[:, 512:516], in0=ld_.t[:], in1=nm_.t[:], op=ALU.subtract), (ld_, nm_), (og,))
                    XOv = OG[U["g"]].rearrange("(s i dd) c -> s dd i c", s=NSEQ, dd=U["d"])
                    K.store("sp", og, XOv[U["s"], U["r"], U["p0"]:U["p0"] + 128, :], og.t[:])
                if U["lastc"] and U["ci"] + 2 < len(chunks):
                    ldchunk(U["ci"] + 2)

            ldchunk(0)
            if len(chunks) > 1:
                ldchunk(1)
            for idx in range(NU + 2):
                if idx < NU:
                    stA(idx)
                if 1 <= idx <= NU:
                    stB(idx - 1)
                if idx >= 2:
                    stC(idx - 2)
            K.barrier()

        if phases is None or "pc1" in phases:
          with ExitStack() as st:
            Wu = K.sb(st, "Wu", [128, 8, 1024], BF16)
            Wv = K.sb(st, "Wv", [128, 8, 1024], BF16)
            Wga = K.sb(st, "Wga", [128, 8, 1024], BF16)
            Wpa = K.sb(st, "Wpa", [128, 8, 1024], BF16)
            Wst = K.sb(st, "Wst", [128, 8, 128], BF16)
            bcol = K.sb(st, "bcol", [128, 68], F32)
            brow = K.sb(st, "brow", [1, 1024], BF16)
            lnv = K.sb(st, "lnv", [128, 16], F32)
            bsb = K.sb(st, "bsb", [128, 8, 128], F32)
            Cgt = K.sb(st, "Cgt", [128, 8, 128], F32)
            xts = [K.sb(st, f"xt{i}", [128, 4, D], F32) for i in range(2)]
            xT = K.sb(st, "xT", [128, 8, 512], BF16)
            uT = K.sb(st, "uT", [128, 8, 512], BF16)
            gaT = K.sb(st, "gaT", [128, 8, 512], BF16)
            aoT = K.sb(st, "aoT", [128, 8, 512], BF16)
            maTs = [K.sb(st, f"maT{i}", [128, 8, 512], BF16) for i in range(2)]
            vsb = [K.sb(st, f"v{i}", [128, 1024], F32) for i in range(2)]
            nsb = [K.sb(st, f"n{i}", [128, 1024], BF16) for i in range(2)]
            tmpa = [K.sb(st, f"tmpa{i}", [128, 8, 128], F32) for i in range(2)]
            stt = [K.sb(st, f"stt{i}", [128, 12], F32) for i in range(2)]
            mv = [K.sb(st, f"mv{i}", [128, 2], F32) for i in range(2)]
            rstd = [K.sb(st, f"rstd{i}", [128, 1], F32) for i in range(2)]
            wsrc = w_in[l]
            K.load("pool", Wu, Wu.t[:], wsrc[:, 0:1024].rearrange("(kc p) c -> p kc c", p=128))
            K.load("pool", Wv, Wv.t[:], wsrc[:, 1024:2048].rearrange("(kc p) c -> p kc c", p=128))
            K.load("pool", Wga, Wga.t[:], wsrc[:, 6656:7680].rearrange("(kc p) c -> p kc c", p=128))
            K.load("pool", Wpa, Wpa.t[:], w_pa[l].rearrange("(kc p) c -> p kc c", p=128))
            K.load("pool", Wst, Wst.t[:], w_st[l])
            K.load("pool", brow, brow.t[:], bin_row[l][:, 1024:2048])
            K.load("sp", bcol, bcol.t[:], bin_col[l])
            K.load("sp", lnv, lnv.t[:], lnv_col[l])
            K.load("sp", bsb, bsb.t[:], bs_row[l].rearrange("o (g t) -> o g t", g=8).to_broadcast([128, 8, 128]))
            for half in range(2):
                pb = K.nextps(PSB)
                K.mm(pb.t[:, :], [(ones_b.t[:, :], Wst.t[:, half * 4:(half + 1) * 4, :])], reads=(ones_b, Wst), writes=(pb,))
                for gg in range(4):
                    g_ = half * 4 + gg
                    K.op("dve", lambda e, pb=pb, gg=gg, g_=g_: e.scalar_tensor_tensor(
                        out=Cgt.t[:, g_, :], in0=pb.t[:, gg * 128:(gg + 1) * 128], scalar=lnv.t[:, 8 + g_:9 + g_], in1=bsb.t[:, g_, :],
                        op0=ALU.mult, op1=ALU.add), (pb, lnv, bsb), (Cgt,))
            XAt = XA.rearrange("(m j a) c -> m a j c", j=4, a=128)
            NM = NT // 512
            K.load("sp", xts[0], xts[0].t[:], XAt[0])
            sj = 0
            for m in range(NM):
                xt = xts[m % 2]
                maT = maTs[m % 2]
                if m + 1 < NM:
                    K.load("sp", xts[(m + 1) % 2], xts[(m + 1) % 2].t[:], XAt[m + 1])
                transposes_x(xt, xT, 512)
                for c8 in range(8):
                    pb = K.nextps(PSB)
                    K.mm(pb.t[:, :], [(Wu.t[:, kc, c8 * 128:(c8 + 1) * 128], xT.t[:, kc, :]) for kc in range(8)], reads=(Wu, xT), writes=(pb,))
                    K.op("act", lambda e, pb=pb, c8=c8: e.activation(out=uT.t[:, c8, :], in_=pb.t[:, :], func=AF.Gelu, bias=bcol.t[:, c8:c8 + 1]),
                         (pb, bcol), (uT,))
                for j in range(4):
                    v_, n_, ta, st_, mv_, rs_ = vsb[sj % 2], nsb[sj % 2], tmpa[sj % 2], stt[sj % 2], mv[sj % 2], rstd[sj % 2]
                    for nh in range(2):
                        pb = K.nextps(PSB)
                        pairs = [(xT.t[:, kc, j * 128:(j + 1) * 128], Wv.t[:, kc, nh * 512:(nh + 1) * 512]) for kc in range(8)]
                        pairs.append((ones_b.t[0:1, :], brow.t[0:1, nh * 512:(nh + 1) * 512]))
                        K.mm(pb.t[:, :], pairs, reads=(xT, Wv, ones_b, brow), writes=(pb,))
                        K.op("act", lambda e, pb=pb, nh=nh, v_=v_: e.activation(out=v_.t[:, nh * 512:(nh + 1) * 512], in_=pb.t[:, :], func=AF.Gelu),
                             (pb,), (v_,))
                    ln_stats(st, lambda c, v_=v_: v_.t[:, c * 512:(c + 1) * 512], v_, mv_, st_, rs_)
                    K.op("dve", lambda e, v_=v_, n_=n_, mv_=mv_, rs_=rs_: e.tensor_scalar(
                        out=n_.t[:], in0=v_.t[:], scalar1=mv_.t[:, 0:1], scalar2=rs_.t[:, 0:1], op0=ALU.subtract, op1=ALU.mult),
                        (v_, mv_, rs_), (n_,))
                    for half in range(2):
                        pb = K.nextps(PSB)
                        fns = [lambda t, gg=gg, half=half, pb=pb, n_=n_: t.matmul(
                            pb.t[:, gg * 128:(gg + 1) * 128], lhsT=n_.t[:, (half * 4 + gg) * 128:(half * 4 + gg + 1) * 128],
                            rhs=Wst.t[:, half * 4 + gg, :], start=True, stop=True) for gg in range(4)]
                        K.pe(fns, reads=(n_, Wst), writes=(pb,))
                        for gg in range(4):
                            g_ = half * 4 + gg
                            K.op("dve", lambda e, pb=pb, gg=gg, g_=g_, ta=ta: e.scalar_tensor_tensor(
                                out=ta.t[:, g_, :], in0=pb.t[:, gg * 128:(gg + 1) * 128], scalar=lnv.t[:, g_:g_ + 1], in1=Cgt.t[:, g_, :],
                                op0=ALU.mult, op1=ALU.add), (pb, lnv, Cgt), (ta,))
                    K.op("pool", lambda e, ta=ta, j=j: e.tensor_tensor(out=aoT.t[:, :, j * 128:(j + 1) * 128], in0=ta.t[:], in1=uT.t[:, :, j * 128:(j + 1) * 128], op=ALU.mult),
                         (ta, uT), (aoT,))
                    sj += 1
                for c8 in range(8):
                    pb = K.nextps(PSB)
                    K.mm(pb.t[:, :], [(Wga.t[:, kc, c8 * 128:(c8 + 1) * 128], xT.t[:, kc, :]) for kc in range(8)], reads=(Wga, xT), writes=(pb,))
                    K.op("act", lambda e, pb=pb, c8=c8: e.activation(out=gaT.t[:, c8, :], in_=pb.t[:, :], func=AF.Sigmoid, bias=bcol.t[:, 52 + c8:53 + c8]),
                         (pb, bcol), (gaT,))
                for c8 in range(8):
                    pb = K.nextps(PSB)
                    K.mm(pb.t[:, :], [(Wpa.t[:, kc, c8 * 128:(c8 + 1) * 128], aoT.t[:, kc, :]) for kc in range(8)], reads=(Wpa, aoT), writes=(pb,))
                    K.op("dve", lambda e, pb=pb, c8=c8, maT=maT: e.tensor_tensor(out=maT.t[:, c8, :], in0=pb.t[:, :], in1=gaT.t[:, c8, :], op=ALU.mult),
                         (pb, gaT), (maT,))
                K.store("sp", maT, MA[:, :, m * 512:(m + 1) * 512].rearrange("n p t -> p n t"), maT.t[:])
            K.barrier()

        if phases is None or "pc2" in phases:
          with ExitStack() as st:
            Wgb = K.sb(st, "Wgb", [128, 8, 1024], BF16)
            Wpb = K.sb(st, "Wpb", [128, 4, 1024], BF16)
            Wo = K.sb(st, "Wo", [128, 8, 1024], BF16)
            Wr = K.sb(st, "Wr", [128, 8, 32], F32)
            bcol = K.sb(st, "bcol", [128, 68], F32)
            g1b = K.sb(st, "g1b", [128, 1024], F32)
            b1b = K.sb(st, "b1b", [128, 1024], F32)
            brb = K.sb(st, "brb", [128, 32], F32)
            rin = K.sb(st, "rin", [128, 128, 4], I32)
            xts = [K.sb(st, f"xt{i}", [128, 4, D], F32) for i in range(2)]
            xT = K.sb(st, "xT", [128, 8, 512], BF16)
            gbT = K.sb(st, "gbT", [128, 8, 512], BF16)
            maTs = [K.sb(st, f"maT{i}", [128, 8, 512], BF16) for i in range(2)]
            boT = K.sb(st, "boT", [128, 4, 512], BF16)
            mgT = K.sb(st, "mgT", [128, 8, 512], BF16)
            tmpm = [K.sb(st, f"tmpm{i}", [128, 512], F32) for i in range(2)]
            bof = [K.sb(st, f"bof{i}", [128, 512], F32) for i in range(2)]
            sm = [K.sb(st, f"sm{i}", [128, 64], F32) for i in range(2)]
            x1p = [K.sb(st, f"x1p{i}", [128, 1024], F32) for i in range(2)]
            x1 = [K.sb(st, f"x1{i}", [128, 1024], F32) for i in range(2)]
            x1T = [K.sb(st, f"x1T{i}", [128, 8, 128], F32) for i in range(2)]
            stt = [K.sb(st, f"stt{i}", [128, 12], F32) for i in range(2)]
            mv = [K.sb(st, f"mv{i}", [128, 2], F32) for i in range(2)]
            rstd = [K.sb(st, f"rstd{i}", [128, 1], F32) for i in range(2)]
            lg = [K.sb(st, f"lg{i}", [128, 32], F32) for i in range(2)]
            t8 = [K.sb(st, f"t8{i}", [128, 8], F32) for i in range(2)]
            rs = [K.sb(st, f"rs{i}", [128, 16], F32) for i in range(2)]
            mk = [K.sb(st, f"mk{i}", [128, 32], BF16) for i in range(2)]
            posg = [K.sb(st, f"posg{i}", [128, 32], F32) for i in range(2)]
            slotv = [K.sb(st, f"slotv{i}", [128, 32], F32) for i in range(2)]
            oh = [K.sb(st, f"oh{i}", [128, 32], F32) for i in range(2)]
            ohx = [K.sb(st, f"ohx{i}", [128, 32], F32) for i in range(2)]
            slf = [K.sb(st, f"slf{i}", [128, 4], F32) for i in range(2)]
            sli = [K.sb(st, f"sli{i}", [128, 4], I32) for i in range(2)]
            wsrc = w_in[l]
            K.load("pool", Wgb, Wgb.t[:], wsrc[:, 7680:8704].rearrange("(kc p) c -> p kc c", p=128))
            K.load("pool", Wpb, Wpb.t[:], w_pb[l].rearrange("(kc p) c -> p kc c", p=128))
            K.load("pool", Wo, Wo.t[:], w_o[l].rearrange("(kc p) c -> p kc c", p=128))
            K.load("sp", Wr, Wr.t[:], w_r[l].rearrange("(kc p) c -> p kc c", p=128))
            K.load("sp", bcol, bcol.t[:], bin_col[l])
            K.load("sp", g1b, g1b.t[:], ln1_row[l][0:1, :].to_broadcast([128, 1024]))
            K.load("sp", b1b, b1b.t[:], ln1_row[l][1:2, :].to_broadcast([128, 1024]))
            K.load("sp", brb, brb.t[:], br_row[l].to_broadcast([128, 32]))
            K.load("sp", rin, rin.t[:], cin["rinit"])
            for c0 in range(0, NE * CAP, 16384):
                K.store("sp", rin, ROUTE[c0:c0 + 16384, :].rearrange("(p a) f -> p a f", p=128), rin.t[:])
            K.store("sp", rin, ROUTE[NE * CAP:NE * CAP + 128, :], rin.t[:, 0, :])
            K.op("dve", lambda e: e.memset(RUN.t[:], 0.0), (), (RUN,))
            K.barrier()
            XAt = XA.rearrange("(m j a) c -> m a j c", j=4, a=128)
            NM = NT // 512
            K.load("sp", xts[0], xts[0].t[:], XAt[0])
            K.load("sp", maTs[0], maTs[0].t[:], MA[:, :, 0:512].rearrange("n p t -> p n t"))
            og3 = [K.sb(st, f"og3x{i}", [128, 3, 516], F32) for i in range(4)]
            bo = [K.sb(st, f"box{i}", [128, 512], BF16) for i in range(4)]
            nmr = [K.sb(st, f"nmr{i}", [128, 1], F32) for i in range(2)]
            tmpq = [K.sb(st, f"tmpq{i}", [128, 512], F32) for i in range(2)]
            oh4 = [K.sb(st, f"oh4{i}", [128, 4, 32], F32) for i in range(2)]
            ox4 = [K.sb(st, f"ox4{i}", [128, 4, 32], F32) for i in range(2)]

            def M1(m, j):
                tile = m * 4 + j
                o3, sm_, bo_, bof_ = og3[j], sm[j % 2], bo[j], bof[j % 2]
                K.load("sp", o3, o3.t[:], OGall[:, tile * 128:(tile + 1) * 128, :].rearrange("g p c -> p g c"))
                lse3 = o3.t[:, :, 512:516]
                K.op("dve", lambda e: e.tensor_reduce(out=sm_.t[:, 0:4], in_=lse3.rearrange("p g h -> p h g"), axis=AX.X, op=ALU.max), (o3,), (sm_,))
                K.op("dve", lambda e: e.tensor_tensor(out=sm_.t[:, 4:16].rearrange("p (g h) -> p g h", g=3), in0=lse3,
                                                      in1=sm_.t[:, 0:4].unsqueeze(1).to_broadcast([128, 3, 4]), op=ALU.subtract), (o3, sm_), (sm_,))
                K.op("act", lambda e: e.activation(out=sm_.t[:, 4:16], in_=sm_.t[:, 4:16], func=AF.Exp), (sm_,), (sm_,))
                K.op("dve", lambda e: e.tensor_reduce(out=sm_.t[:, 16:20], in_=sm_.t[:, 4:16].rearrange("p (g h) -> p h g", g=3), axis=AX.X, op=ALU.add), (sm_,), (sm_,))
                K.op("dve", lambda e: e.reciprocal(out=sm_.t[:, 20:24], in_=sm_.t[:, 16:20]), (sm_,), (sm_,))
                K.op("dve", lambda e: e.tensor_tensor(out=sm_.t[:, 24:36].rearrange("p (g h) -> p g h", g=3), in0=sm_.t[:, 4:16].rearrange("p (g h) -> p g h", g=3),
                                                      in1=sm_.t[:, 20:24].unsqueeze(1).to_broadcast([128, 3, 4]), op=ALU.mult), (sm_,), (sm_,))
                tmq = tmpq[j % 2]

                def wnb(g):
                    return sm_.t[:, 24 + 4 * g:28 + 4 * g].unsqueeze(2).to_broadcast([128, 4, 128])

                def o3v(g):
                    return o3.t[:, g, 0:512].rearrange("p (h c) -> p h c", h=4)
                K.op("dve", lambda e: e.tensor_tensor(out=bof_.t[:, :].rearrange("p (h c) -> p h c", h=4), in0=o3v(0), in1=wnb(0), op=ALU.mult), (o3, sm_), (bof_,))
                K.op("pool", lambda e: e.tensor_tensor(out=tmq.t[:, :].rearrange("p (h c) -> p h c", h=4), in0=o3v(1), in1=wnb(1), op=ALU.mult), (o3, sm_), (tmq,))
                K.op("dve", lambda e: e.tensor_tensor(out=bof_.t[:, :], in0=bof_.t[:, :], in1=tmq.t[:, :], op=ALU.add), (bof_, tmq), (bof_,))
                K.op("pool", lambda e: e.tensor_tensor(out=tmq.t[:, :].rearrange("p (h c) -> p h c", h=4), in0=o3v(2), in1=wnb(2), op=ALU.mult), (o3, sm_), (tmq,))
                K.op("dve", lambda e: e.tensor_tensor(out=bo_.t[:, :], in0=bof_.t[:, :], in1=tmq.t[:, :], op=ALU.add), (bof_, tmq), (bo_,))

            def M2(m, j):
                bo_ = bo[j]
                pT = K.nextps(PSB)
                pTb = pT.t[:].bitcast(BF16)
                K.pe([lambda t, c=c: t.transpose(out=pTb[:, c * 128:(c + 1) * 128], in_=bo_.t[:, c * 128:(c + 1) * 128], identity=ident_b.t[:])
                      for c in range(4)], reads=(bo_, ident_b), writes=(pT,))
                K.op("act", lambda e: e.activation(out=boT.t[:, :, j * 128:(j + 1) * 128], in_=pTb[:, 0:512].rearrange("p (c t) -> p c t", c=4), func=AF.Copy), (pT,), (boT,))

            def Pst(m, j, xt):
                tile = m * 4 + j
                q = tile % 2
                xp, x1_, st_, mv_, rs_, nm_ = x1p[q], x1[q], stt[q], mv[q], rstd[q], nmr[q]
                for nh in range(2):
                    pb = K.nextps(PSB)
                    K.mm(pb.t[:, :], [(mgT.t[:, kc, j * 128:(j + 1) * 128], Wo.t[:, kc, nh * 512:(nh + 1) * 512]) for kc in range(8)], reads=(mgT, Wo), writes=(pb,))
                    K.op("dve", lambda e, pb=pb, nh=nh: e.scalar_tensor_tensor(
                        out=xp.t[:, nh * 512:(nh + 1) * 512], in0=xt.t[:, j, nh * 512:(nh + 1) * 512], scalar=ALPHA, in1=pb.t[:, :], op0=ALU.mult, op1=ALU.add),
                        (pb, xt), (xp,))
                ln_stats(st, lambda c: xp.t[:, c * 512:(c + 1) * 512], xp, mv_, st_, rs_)
                K.op("dve", lambda e: e.scalar_tensor_tensor(out=nm_.t[:], in0=mv_.t[:, 0:1], scalar=-1.0, in1=rs_.t[:, 0:1], op0=ALU.mult, op1=ALU.mult), (mv_, rs_), (nm_,))
                K.op("act", lambda e: e.activation(out=xp.t[:], in_=xp.t[:], func=AF.Identity, scale=rs_.t[:, 0:1], bias=nm_.t[:, 0:1]), (xp, rs_, nm_), (xp,))
                K.op("dve", lambda e: e.tensor_tensor(out=xp.t[:], in0=xp.t[:], in1=g1b.t[:], op=ALU.mult), (xp, g1b), (xp,))
                K.op("pool", lambda e: e.tensor_tensor(out=x1_.t[:], in0=xp.t[:], in1=b1b.t[:], op=ALU.add), (xp, b1b), (x1_,))
                K.store("sp", x1_, X1[tile * 128:(tile + 1) * 128, :], x1_.t[:])

            def Rst(m, j):
                tile = m * 4 + j
                q = tile % 2
                x1_, x1T_ = x1[q], x1T[q]
                lg_, t8_, r_, mk_, pg_, sv_, oh_, ox_, sf_, si_ = lg[q], t8[q], rs[q], mk[q], posg[q], slotv[q], oh[q], ohx[q], slf[q], sli[q]
                for half in range(2):
                    pb = K.nextps(PSB)
                    K.pe([lambda t, c=c, half=half, pb=pb: t.transpose(out=pb.t[:, c * 128:(c + 1) * 128], in_=x1_.t[:, (half * 4 + c) * 128:(half * 4 + c + 1) * 128], identity=ident_f.t[:])
                          for c in range(4)], reads=(x1_, ident_f), writes=(pb,))
                    evac(half, x1T_.t[:, half * 4:(half + 1) * 4, :], pb.t[:, :].rearrange("p (c t) -> p c t", c=4), (pb,), (x1T_,))
                pb = K.nextps(PSB)
                K.mm(pb.t[:, 0:32], [(x1T_.t[:, kc, :], Wr.t[:, kc, :]) for kc in range(8)], reads=(x1T_, Wr), writes=(pb,))
                K.op("dve", lambda e, pb=pb: e.tensor_tensor(out=lg_.t[:], in0=pb.t[:, 0:32], in1=brb.t[:], op=ALU.add), (pb, brb), (lg_,))
                K.op("dve", lambda e: e.max(out=t8_.t[:], in_=lg_.t[:]), (lg_,), (t8_,))
                K.op("dve", lambda e: e.tensor_scalar(out=r_.t[:, 0:1], in0=t8_.t[:, 0:1], scalar1=-1.0, scalar2=None, op0=ALU.mult), (t8_,), (r_,))
                K.op("act", lambda e: e.activation(out=r_.t[:, 1:5], in_=t8_.t[:, 0:4], func=AF.Exp, bias=r_.t[:, 0:1], accum_out=r_.t[:, 5:6]), (t8_, r_), (r_,))
                K.op("dve", lambda e: e.reciprocal(out=r_.t[:, 6:7], in_=r_.t[:, 5:6]), (r_,), (r_,))
                K.op("dve", lambda e: e.tensor_scalar(out=GATES.t[:, tile, :], in0=r_.t[:, 1:5], scalar1=r_.t[:, 6:7], scalar2=None, op0=ALU.mult), (r_,), (GATES,))
                K.op("dve", lambda e: e.tensor_scalar(out=mk_.t[:], in0=lg_.t[:], scalar1=t8_.t[:, 3:4], scalar2=None, op0=ALU.is_ge), (lg_, t8_), (mk_,))
                pb = K.nextps(PSB)
                K.mm(pb.t[:, 0:32], [(tri_b.t[:, :], mk_.t[:, :])], reads=(tri_b, mk_), writes=(pb,))
                K.op("dve", lambda e, pb=pb: e.tensor_tensor(out=pg_.t[:], in0=pb.t[:, 0:32], in1=RUN.t[:], op=ALU.add), (pb, RUN), (pg_,))
                pb2 = K.nextps(PSB)
                K.mm(pb2.t[:, 0:32], [(ones_b.t[:, :], mk_.t[:, :])], reads=(ones_b, mk_), writes=(pb2,))
                K.op("dve", lambda e, pb2=pb2: e.tensor_tensor(out=RUN.t[:], in0=pb2.t[:, 0:32], in1=RUN.t[:], op=ALU.add), (pb2, RUN), (RUN,))
                K.op("dve", lambda e: e.tensor_tensor(out=sv_.t[:], in0=pg_.t[:], in1=ecap.t[:], op=ALU.add), (pg_, ecap), (sv_,))
                o4, x4 = oh4[q], ox4[q]
                K.op("dve", lambda e: e.tensor_tensor(out=o4.t[:], in0=lg_.t[:, :].unsqueeze(1).to_broadcast([128, 4, 32]),
                                                      in1=t8_.t[:, 0:4].unsqueeze(2).to_broadcast([128, 4, 32]), op=ALU.is_equal), (lg_, t8_), (o4,))
                for V, dst in ((sv_, lambda: sf_.t[:, 0:4]), (pg_, lambda: POSK.t[:, tile, :]), (iota32, lambda: EKF.t[:, tile, :])):
                    dbuf = sf_ if V is sv_ else (POSK if V is pg_ else EKF)
                    K.op("dve", lambda e, V=V: e.tensor_tensor(out=x4.t[:], in0=o4.t[:], in1=V.t[:, :].unsqueeze(1).to_broadcast([128, 4, 32]), op=ALU.mult), (o4, V), (x4,))
                    K.op("dve", lambda e, dst=dst: e.tensor_reduce(out=dst(), in_=x4.t[:], axis=AX.X, op=ALU.add), (x4,), (dbuf,))
                K.op("dve", lambda e: e.tensor_copy(out=si_.t[:], in_=sf_.t[:]), (sf_,), (si_,))
                for k in range(4):
                    K.dma("pool", si_, lambda e, k=k: e.indirect_dma_start(
                        out=ROUTE[:, :], out_offset=bass.IndirectOffsetOnAxis(ap=si_.t[:, k:k + 1], axis=0),
                        in_=tokid4.t[:, tile, :], in_offset=None), reads=(si_, tokid4), writes=())

            for m in range(NM):
                xt = xts[m % 2]
                maT = maTs[m % 2]
                if m + 1 < NM:
                    K.load("sp", xts[(m + 1) % 2], xts[(m + 1) % 2].t[:], XAt[m + 1])
                    K.load("sp", maTs[(m + 1) % 2], maTs[(m + 1) % 2].t[:], MA[:, :, (m + 1) * 512:(m + 2) * 512].rearrange("n p t -> p n t"))
                for j in range(4):
                    M1(m, j)
                transposes_x(xt, xT, 512, act_only=True)
                for c8 in range(8):
                    pb = K.nextps(PSB)
                    K.mm(pb.t[:, :], [(Wgb.t[:, kc, c8 * 128:(c8 + 1) * 128], xT.t[:, kc, :]) for kc in range(8)], reads=(Wgb, xT), writes=(pb,))
                    K.op("act", lambda e, pb=pb, c8=c8: e.activation(out=gbT.t[:, c8, :], in_=pb.t[:, :], func=AF.Sigmoid, bias=bcol.t[:, 60 + c8:61 + c8]),
                         (pb, bcol), (gbT,))
                for j in range(4):
                    M2(m, j)
                for c8 in range(8):
                    pb = K.nextps(PSB)
                    tm = tmpm[c8 % 2]
                    K.mm(pb.t[:, :], [(Wpb.t[:, kc, c8 * 128:(c8 + 1) * 128], boT.t[:, kc, :]) for kc in range(4)], reads=(Wpb, boT), writes=(pb,))
                    K.op("dve", lambda e, pb=pb, c8=c8, tm=tm: e.tensor_tensor(out=tm.t[:], in0=pb.t[:, :], in1=gbT.t[:, c8, :], op=ALU.mult), (pb, gbT), (tm,))
                    K.op("pool", lambda e, c8=c8, tm=tm: e.tensor_tensor(out=mgT.t[:, c8, :], in0=tm.t[:], in1=maT.t[:, c8, :], op=ALU.add), (tm, maT), (mgT,))
                Pst(m, 0, xt); Pst(m, 1, xt); Rst(m, 0); Pst(m, 2, xt); Rst(m, 1); Pst(m, 3, xt); Rst(m, 2); Rst(m, 3)
            K.barrier()

        if phases is None or "bl" in phases:
          with ExitStack() as st:
            cntc = K.sb(st, "cntc", [32, 1], F32)
            tmp32 = K.sb(st, "tmp32", [32, 32], F32)
            jthr = K.sb(st, "jthr", [32, 64], F32)
            tmpj = K.sb(st, "tmpj", [32, 64], F32)
            nbc = K.sb(st, "nbc", [32, 1], F32)
            endc = K.sb(st, "endc", [32, 1], F32)
            triu = K.sb(st, "triu", [32, 32], F32)
            iob = K.sb(st, "iob", [128, NBLK], F32)
            Gm = K.sb(st, "Gm", [32, NBLK], F32)
            nbm = K.sb(st, "nbm", [32, 128], F32)
            ebr = K.sb(st, "ebr", [128, NBLK], F32)
            eb = K.sb(st, "eb", [128, NBLK], F32)
            stb = K.sb(st, "stb", [128, NBLK], F32)
            val = K.sb(st, "val", [128, NBLK], F32)
            sbase = K.sb(st, "sbase", [128, NBLK], F32)
            need = K.sb(st, "need", [128, NBLK], F32)
            tA = K.sb(st, "tA", [128, NBLK], F32)
            tB = K.sb(st, "tB", [128, NBLK], F32)
            endr = K.sb(st, "endr", [128, 32], F32)
            nbr = K.sb(st, "nbr", [128, 32], F32)
            K.load("sp", jthr, jthr.t[:], cin["jthr"])
            K.load("sp", triu, triu.t[:], cin["triu32"])
            K.load("sp", iob, iob.t[:], cin["iota_b"])
            K.op("dve", lambda e: e.tensor_tensor(out=tmp32.t[:], in0=RUN.t[0:32, :], in1=ident_f.t[0:32, 0:32], op=ALU.mult), (RUN, ident_f), (tmp32,))
            K.op("dve", lambda e: e.tensor_reduce(out=cntc.t[:], in_=tmp32.t[:], axis=AX.X, op=ALU.add), (tmp32,), (cntc,))
            K.op("dve", lambda e: e.tensor_scalar(out=tmpj.t[:], in0=jthr.t[:], scalar1=cntc.t[:, 0:1], scalar2=None, op0=ALU.is_lt), (jthr, cntc), (tmpj,))
            K.op("dve", lambda e: e.tensor_reduce(out=nbc.t[:], in_=tmpj.t[:], axis=AX.X, op=ALU.add), (tmpj,), (nbc,))
            pb = K.nextps(PSB)
            K.mm(pb.t[0:32, 0:1], [(triu.t[:, :], nbc.t[:, 0:1])], reads=(triu, nbc), writes=(pb,))
            K.op("dve", lambda e, pb=pb: e.tensor_copy(out=endc.t[:], in_=pb.t[0:32, 0:1]), (pb,), (endc,))
            K.op("dve", lambda e: e.tensor_scalar(out=Gm.t[:], in0=iob.t[0:32, :], scalar1=endc.t[:, 0:1], scalar2=None, op0=ALU.is_ge), (iob, endc), (Gm,))
            K.op("dve", lambda e: e.tensor_scalar(out=nbm.t[:], in0=ones_f.t[0:32, :], scalar1=nbc.t[:, 0:1], scalar2=None, op0=ALU.mult), (ones_f, nbc), (nbm,))
            pb = K.nextps(PSB)
            K.mm(pb.t[:, 0:NBLK], [(ones_f.t[0:32, :], Gm.t[:, :])], reads=(ones_f, Gm), writes=(pb,))
            K.op("dve", lambda e, pb=pb: e.tensor_copy(out=ebr.t[:], in_=pb.t[:, 0:NBLK]), (pb,), (ebr,))
            pb = K.nextps(PSB)
            K.mm(pb.t[:, 0:NBLK], [(nbm.t[:, :], Gm.t[:, :])], reads=(nbm, Gm), writes=(pb,))
            K.op("dve", lambda e, pb=pb: e.tensor_copy(out=stb.t[:], in_=pb.t[:, 0:NBLK]), (pb,), (stb,))
            pb = K.nextps(PSB)
            K.mm(pb.t[:, 0:32], [(nbm.t[:, :], triu.t[:, :])], reads=(nbm, triu), writes=(pb,))
            K.op("dve", lambda e, pb=pb: e.tensor_copy(out=endr.t[:], in_=pb.t[:, 0:32]), (pb,), (endr,))
            pb = K.nextps(PSB)
            K.mm(pb.t[:, 0:32], [(nbm.t[:, :], ident_f.t[0:32, 0:32])], reads=(nbm, ident_f), writes=(pb,))
            K.op("dve", lambda e, pb=pb: e.tensor_copy(out=nbr.t[:], in_=pb.t[:, 0:32]), (pb,), (nbr,))
            K.op("dve", lambda e: e.tensor_tensor(out=BS128.t[:], in0=endr.t[:], in1=nbr.t[:], op=ALU.subtract), (endr, nbr), (BS128,))
            K.op("dve", lambda e: e.tensor_scalar(out=BS128.t[:], in0=BS128.t[:], scalar1=128.0, scalar2=None, op0=ALU.mult), (BS128,), (BS128,))
            K.op("dve", lambda e: e.tensor_scalar(out=val.t[:], in0=ebr.t[:], scalar1=31.5, scalar2=None, op0=ALU.is_lt), (ebr,), (val,))
            K.op("dve", lambda e: e.tensor_scalar(out=eb.t[:], in0=ebr.t[:], scalar1=31.0, scalar2=None, op0=ALU.min), (ebr,), (eb,))
            K.op("dve", lambda e: e.tensor_tensor(out=tA.t[:], in0=iob.t[:], in1=stb.t[:], op=ALU.subtract), (iob, stb), (tA,))
            K.op("dve", lambda e: e.tensor_scalar(out=tA.t[:], in0=tA.t[:], scalar1=128.0, scalar2=float(-NE * CAP), op0=ALU.mult, op1=ALU.add), (tA,), (tA,))
            K.op("dve", lambda e: e.scalar_tensor_tensor(out=tA.t[:], in0=eb.t[:], scalar=float(CAP), in1=tA.t[:], op0=ALU.mult, op1=ALU.add), (eb, tA), (tA,))
            K.op("dve", lambda e: e.tensor_tensor(out=tA.t[:], in0=tA.t[:], in1=val.t[:], op=ALU.mult), (tA, val), (tA,))
            K.op("dve", lambda e: e.tensor_scalar(out=sbase.t[:], in0=tA.t[:], scalar1=pcol.t[:, 0:1], scalar2=float(NE * CAP), op0=ALU.add, op1=ALU.add), (tA, pcol), (sbase,))
            K.op("dve", lambda e: e.tensor_copy(out=RIDX.t[:], in_=sbase.t[:]), (sbase,), (RIDX,))
            K.op("dve", lambda e: e.memset(need.t[:], 1.0), (), (need,))
            K.op("dve", lambda e: e.tensor_tensor(out=need.t[:, 1:NBLK], in0=eb.t[:, 1:NBLK], in1=eb.t[:, 0:NBLK - 1], op=ALU.not_equal), (eb,), (need,))
            K.op("dve", lambda e: e.memset(need.t[:, NBLK // 2:NBLK // 2 + 1], 1.0), (), (need,))
            nn = K.sb(st, "nn", [128, NBLK], F32)
            K.op("dve", lambda e: e.tensor_scalar(out=nn.t[:], in0=need.t[:], scalar1=float(-BIG), scalar2=float(BIG), op0=ALU.mult, op1=ALU.add), (need,), (nn,))
            K.op("dve", lambda e: e.tensor_scalar(out=tB.t[:], in0=eb.t[:], scalar1=128.0, scalar2=pcol.t[:, 0:1], op0=ALU.mult, op1=ALU.add), (eb, pcol), (tB,))
            K.op("dve", lambda e: e.tensor_scalar(out=tB.t[:], in0=tB.t[:], scalar1=float(l * NE * 128), scalar2=None, op0=ALU.add), (tB,), (tB,))
            K.op("dve", lambda e: e.tensor_tensor(out=tB.t[:], in0=tB.t[:], in1=need.t[:], op=ALU.mult), (tB, need), (tB,))
            K.op("dve", lambda e: e.tensor_tensor(out=tB.t[:], in0=tB.t[:], in1=nn.t[:], op=ALU.add), (tB, nn), (tB,))
            K.op("dve", lambda e: e.tensor_copy(out=WIDX.t[:], in_=tB.t[:]), (tB,), (WIDX,))
            K.op("dve", lambda e: e.scalar_tensor_tensor(out=tB.t[:], in0=eb.t[:], scalar=float(l * NE), in1=need.t[:], op0=ALU.add, op1=ALU.mult), (eb, need), (tB,))
            K.op("dve", lambda e: e.tensor_tensor(out=tB.t[:], in0=tB.t[:], in1=nn.t[:], op=ALU.add), (tB, nn), (tB,))
            K.op("dve", lambda e: e.tensor_copy(out=BIDX.t[:], in_=tB.t[:]), (tB,), (BIDX,))
            K.barrier()

        if phases is None or "moe" in phases:
          with ExitStack() as st:
            Wg = [K.sb(st, f"Wg{i}", [128, 8 * 2048], BF16) for i in range(2)]
            Wd = [K.sb(st, f"Wd{i}", [128, 8 * 1024], BF16) for i in range(2)]
            Bg = [K.sb(st, f"Bg{i}", [128, 2048], BF16) for i in range(2)]
            Bd = [K.sb(st, f"Bd{i}", [128, 1024], BF16) for i in range(2)]
            rt = [K.sb(st, f"rt{i}", [128, 4], I32) for i in range(4)]
            xg = [K.sb(st, f"xg{i}", [128, 1024], BF16) for i in range(3)]
            xgT = [K.sb(st, f"xgT{i}", [128, 8, 128], BF16) for i in range(2)]
            gs = [K.sb(st, f"gs{i}", [128, 1024], F32) for i in range(2)]
            sg = [K.sb(st, f"sg{i}", [128, 1024], F32) for i in range(2)]
            u1 = [K.sb(st, f"u1{i}", [128, 1024], F32) for i in range(2)]
            hb = [K.sb(st, f"hb{i}", [128, 1024], BF16) for i in range(2)]
            hT = [K.sb(st, f"hT{i}", [128, 8, 128], BF16) for i in range(2)]
            ysb = [K.sb(st, f"ysb{i}", [128, 1024], F32) for i in range(2)]
            if "bcW" not in K.__dict__:
                rW = nc.gpsimd.alloc_register("bcW")
                nc.gpsimd.reg_mov(rW, NL * NE * 128 - 1)
                K.bcW = nc.gpsimd.snap(rW, donate=True)
                rB = nc.gpsimd.alloc_register("bcB")
                nc.gpsimd.reg_mov(rB, NL * NE - 1)
                K.bcB = nc.gpsimd.snap(rB, donate=True)
            wguv = w_gu.rearrange("l e (p k) f -> (l e p) (k f)", k=8)
            wdnv = w_dn.rearrange("l e (p k) f -> (l e p) (k f)", k=8)
            bguv = b_gu.rearrange("l e f -> (l e) f")
            bdnv = b_dn.rearrange("l e f -> (l e) f")
            HB = NBLK // 2

            def blk(s_):
                return s_ // 2 if s_ % 2 == 0 else HB + s_ // 2

            def wloadG(s_):
                if s_ >= NBLK:
                    return
                par, b = s_ % 2, blk(s_)
                K.dma("pool", Wg[par], lambda e: e.indirect_dma_start(out=Wg[par].t[:, :], out_offset=None, in_=wguv,
                      in_offset=bass.IndirectOffsetOnAxis(ap=WIDX.t[:, b:b + 1], axis=0), bounds_check=K.bcW, oob_is_err=False),
                      reads=(WIDX,), writes=(Wg[par],))
                K.dma("pool", Bg[par], lambda e: e.indirect_dma_start(out=Bg[par].t[:, :], out_offset=None, in_=bguv,
                      in_offset=bass.IndirectOffsetOnAxis(ap=BIDX.t[:, b:b + 1], axis=0), bounds_check=K.bcB, oob_is_err=False),
                      reads=(BIDX,), writes=(Bg[par],))

            def wloadD(s_):
                if s_ >= NBLK:
                    return
                par, b = s_ % 2, blk(s_)
                K.dma("pool", Wd[par], lambda e: e.indirect_dma_start(out=Wd[par].t[:, :], out_offset=None, in_=wdnv,
                      in_offset=bass.IndirectOffsetOnAxis(ap=WIDX.t[:, b:b + 1], axis=0), bounds_check=K.bcW, oob_is_err=False),
                      reads=(WIDX,), writes=(Wd[par],))
                K.dma("pool", Bd[par], lambda e: e.indirect_dma_start(out=Bd[par].t[:, :], out_offset=None, in_=bdnv,
                      in_offset=bass.IndirectOffsetOnAxis(ap=BIDX.t[:, b:b + 1], axis=0), bounds_check=K.bcB, oob_is_err=False),
                      reads=(BIDX,), writes=(Bd[par],))

            def rgath(s_):
                if s_ >= NBLK:
                    return
                r_, b = rt[s_ % 4], blk(s_)
                K.dma("pool", r_, lambda e: e.indirect_dma_start(out=r_.t[:, :], out_offset=None, in_=ROUTE[:, :],
                      in_offset=bass.IndirectOffsetOnAxis(ap=RIDX.t[:, b:b + 1], axis=0)), reads=(RIDX,), writes=(r_,))

            def xgath(s_):
                if s_ >= NBLK:
                    return
                r_, x_ = rt[s_ % 4], xg[s_ % 3]
                K.dma("pool", x_, lambda e: e.indirect_dma_start(out=x_.t[:, :], out_offset=None, in_=X1[:, :],
                      in_offset=bass.IndirectOffsetOnAxis(ap=r_.t[:, 0:1], axis=0)), reads=(r_,), writes=(x_,))

            pT_, pG, pH, pY = PSB[0], PSB[1:5], PSB[5], PSB[6:8]

            def S1a(s_):
                x_, xT_ = xg[s_ % 3], xgT[s_ % 2]
                xv = x_.t[:, :].rearrange("t (p k) -> t k p", k=8)
                pTb = pT_.t[:].bitcast(BF16)
                K.pe([lambda t, c=c: t.transpose(out=pTb[:, c * 128:(c + 1) * 128], in_=xv[:, c, :], identity=ident_b.t[:])
                      for c in range(8)], reads=(x_, ident_b), writes=(pT_,))
                K.op("act", lambda e: e.activation(out=xT_.t[:], in_=pTb[:, 0:1024].rearrange("p (c t) -> p c t", c=8), func=AF.Copy), (pT_,), (xT_,))

            def S1b(s_):
                par = s_ % 2
                xT_ = xgT[par]
                for n in range(4):
                    pb = pG[n]
                    pairs = [(xT_.t[:, kc, :], Wg[par].t[:, kc * 2048 + n * 512:kc * 2048 + (n + 1) * 512]) for kc in range(8)]
                    pairs.append((ones_b.t[0:1, :], Bg[par].t[0:1, n * 512:(n + 1) * 512]))
                    K.mm(pb.t[:, :], pairs, reads=(xT_, Wg[par], ones_b, Bg[par]), writes=(pb,))

            def S2a(s_):
                par = s_ % 2
                g_, u_ = gs[par], u1[par]
                for n in range(2):
                    K.op("dve", lambda e, n=n: e.tensor_scalar(out=g_.t[:, n * 512:(n + 1) * 512], in0=pG[n].t[:, :], scalar1=LIMIT, scalar2=None, op0=ALU.min), (pG[n],), (g_,))
                    K.op("dve", lambda e, n=n: e.tensor_scalar(out=u_.t[:, n * 512:(n + 1) * 512], in0=pG[2 + n].t[:, :], scalar1=LIMIT, scalar2=-LIMIT, op0=ALU.min, op1=ALU.max), (pG[2 + n],), (u_,))

            def S2b(s_):
                par, b = s_ % 2, blk(s_)
                g_, s2_, u_, h_, hT_, y_ = gs[par], sg[par], u1[par], hb[par], hT[par], ysb[par]
                K.op("act", lambda e: e.activation(out=s2_.t[:], in_=g_.t[:], func=AF.Sigmoid, scale=SW_ALPHA), (g_,), (s2_,))
                K.op("dve", lambda e: e.scalar_tensor_tensor(out=u_.t[:], in0=u_.t[:], scalar=1.0, in1=g_.t[:], op0=ALU.add, op1=ALU.mult), (u_, g_), (u_,))
                K.op("dve", lambda e: e.tensor_tensor(out=h_.t[:], in0=u_.t[:], in1=s2_.t[:], op=ALU.mult), (u_, s2_), (h_,))
                hv = h_.t[:, :].rearrange("t (p k) -> t k p", k=8)
                pTb = pH.t[:].bitcast(BF16)
                K.pe([lambda t, c=c: t.transpose(out=pTb[:, c * 128:(c + 1) * 128], in_=hv[:, c, :], identity=ident_b.t[:])
                      for c in range(8)], reads=(h_, ident_b), writes=(pH,))
                K.op("act", lambda e: e.activation(out=hT_.t[:], in_=pTb[:, 0:1024].rearrange("p (c t) -> p c t", c=8), func=AF.Copy), (pH,), (hT_,))
                for n in range(2):
                    pb = pY[n]
                    pairs = [(hT_.t[:, kc, :], Wd[par].t[:, kc * 1024 + n * 512:kc * 1024 + (n + 1) * 512]) for kc in range(8)]
                    pairs.append((ones_b.t[0:1, :], Bd[par].t[0:1, n * 512:(n + 1) * 512]))
                    K.mm(pb.t[:, :], pairs, reads=(hT_, Wd[par], ones_b, Bd[par]), writes=(pb,))
                    evac(n, y_.t[:, n * 512:(n + 1) * 512], pb.t[:, :], (pb,), (y_,))
                K.store("sp", y_, YB[b * 128:(b + 1) * 128, :], y_.t[:])

            wloadG(0); wloadG(1); wloadD(0); wloadD(1)
            rgath(0); rgath(1); rgath(2)
            xgath(0); xgath(1)
            S1a(0); S1b(0)
            wloadG(2)
            for s_ in range(NBLK):
                rgath(s_ + 3)
                xgath(s_ + 2)
                if s_ + 1 < NBLK:
                    S1a(s_ + 1)
                S2a(s_)
                if s_ + 1 < NBLK:
                    S1b(s_ + 1)
                    wloadG(s_ + 3)
                S2b(s_)
                wloadD(s_ + 2)
            K.barrier()

        if phases is None or "comb" in phases:
          with ExitStack() as st:
            g2b = K.sb(st, "g2b", [128, 1024], F32)
            b2b = K.sb(st, "b2b", [128, 1024], F32)
            yk = [[K.sb(st, f"yk{i}_{k}", [128, 1024], F32) for k in range(4)] for i in range(2)]
            x1t = [K.sb(st, f"x1t{i}", [128, 1024], F32) for i in range(2)]
            acc = [K.sb(st, f"acc{i}", [128, 1024], F32) for i in range(2)]
            xo = [K.sb(st, f"xo{i}", [128, 1024], F32) for i in range(2)]
            oh = [K.sb(st, f"oh{i}", [128, 32], F32) for i in range(2)]
            yrf = [K.sb(st, f"yrf{i}", [128, 4], F32) for i in range(2)]
            yri = [K.sb(st, f"yri{i}", [128, 4], I32) for i in range(2)]
            nmr = [K.sb(st, f"nmrc{i}", [128, 1], F32) for i in range(2)]
            stt = [K.sb(st, f"stt{i}", [128, 12], F32) for i in range(2)]
            mv = [K.sb(st, f"mv{i}", [128, 2], F32) for i in range(2)]
            rstd = [K.sb(st, f"rstd{i}", [128, 1], F32) for i in range(2)]
            K.load("sp", g2b, g2b.t[:], ln2_row[l][0:1, :].to_broadcast([128, 1024]))
            K.load("sp", b2b, b2b.t[:], ln2_row[l][1:2, :].to_broadcast([128, 1024]))

            def cload(tile):
                q = tile % 2
                for k in range(4):
                    K.op("dve", lambda e, k=k, q=q, tile=tile: e.tensor_scalar(out=oh[q].t[:], in0=iota32.t[:], scalar1=EKF.t[:, tile, k:k + 1], scalar2=None, op0=ALU.is_equal), (iota32, EKF), (oh[q],))
                    K.op("dve", lambda e, q=q: e.tensor_tensor(out=oh[q].t[:], in0=oh[q].t[:], in1=BS128.t[:], op=ALU.mult), (oh[q], BS128), (oh[q],))
                    K.op("dve", lambda e, k=k, q=q: e.tensor_reduce(out=yrf[q].t[:, k:k + 1], in_=oh[q].t[:], axis=AX.X, op=ALU.add), (oh[q],), (yrf[q],))
                K.op("dve", lambda e, q=q, tile=tile: e.tensor_tensor(out=yrf[q].t[:], in0=yrf[q].t[:], in1=POSK.t[:, tile, :], op=ALU.add), (yrf[q], POSK), (yrf[q],))
                K.op("dve", lambda e, q=q: e.tensor_copy(out=yri[q].t[:], in_=yrf[q].t[:]), (yrf[q],), (yri[q],))
                for k in range(4):
                    K.dma("pool", yk[q][k], lambda e, k=k, q=q: e.indirect_dma_start(out=yk[q][k].t[:, :], out_offset=None, in_=YB[:, :],
                          in_offset=bass.IndirectOffsetOnAxis(ap=yri[q].t[:, k:k + 1], axis=0)), reads=(yri[q],), writes=(yk[q][k],))
                K.load("sp", x1t[q], x1t[q].t[:], X1[tile * 128:(tile + 1) * 128, :])
            cload(0)
            for tile in range(NTILE):
                q = tile % 2
                if tile + 1 < NTILE:
                    cload(tile + 1)
                a_, o_ = acc[q], xo[q]
                K.op("dve", lambda e: e.tensor_scalar(out=a_.t[:], in0=yk[q][0].t[:], scalar1=GATES.t[:, tile, 0:1], scalar2=None, op0=ALU.mult), (yk[q][0], GATES), (a_,))
                for k in range(1, 4):
                    K.op("dve", lambda e, k=k: e.scalar_tensor_tensor(out=a_.t[:], in0=yk[q][k].t[:], scalar=GATES.t[:, tile, k:k + 1], in1=a_.t[:], op0=ALU.mult, op1=ALU.add), (yk[q][k], GATES, a_), (a_,))
                K.op("dve", lambda e: e.scalar_tensor_tensor(out=a_.t[:], in0=x1t[q].t[:], scalar=ALPHA, in1=a_.t[:], op0=ALU.mult, op1=ALU.add), (x1t[q], a_), (a_,))
                ln_stats(st, lambda c: a_.t[:, c * 512:(c + 1) * 512], a_, mv[q], stt[q], rstd[q])
                K.op("dve", lambda e: e.scalar_tensor_tensor(out=nmr[q].t[:], in0=mv[q].t[:, 0:1], scalar=-1.0, in1=rstd[q].t[:, 0:1], op0=ALU.mult, op1=ALU.mult), (mv[q], rstd[q]), (nmr[q],))
                K.op("act", lambda e: e.activation(out=a_.t[:], in_=a_.t[:], func=AF.Identity, scale=rstd[q].t[:, 0:1], bias=nmr[q].t[:, 0:1]), (a_, rstd[q], nmr[q]), (a_,))
                K.op("dve", lambda e: e.tensor_tensor(out=a_.t[:], in0=a_.t[:], in1=g2b.t[:], op=ALU.mult), (a_, g2b), (a_,))
                K.op("pool", lambda e: e.tensor_tensor(out=o_.t[:], in0=a_.t[:], in1=b2b.t[:], op=ALU.add), (a_, b2b), (o_,))
                K.store("sp", o_, XO[tile * 128:(tile + 1) * 128, :], o_.t[:])
            K.barrier()

    K.barrier()
    top.close()
    return nc, dict(NT=NT, NBLK=NBLK, CAP=CAP, NL=NL, NSEQ=NSEQ)


def host_inputs(params, NL, NT, NBLK, CAP):
    f = lambda a: np.ascontiguousarray(np.asarray(a, dtype=np.float32))
    d = {}
    d["w_in"] = f(params["w_in"][:NL])
    d["w_st"] = f(np.transpose(np.asarray(params["w_s"][:NL]), (0, 3, 1, 2)))
    d["w_pa"] = f(params["w_pa"][:NL])
    d["w_pb"] = f(params["w_pb"][:NL])
    d["w_o"] = f(params["w_o"][:NL])
    d["w_r"] = f(params["w_r"][:NL])
    d["w_gu"] = f(params["w_gu"][:NL])
    d["w_down"] = f(params["w_down"][:NL])
    d["b_gu"] = f(params["b_gu"][:NL])
    d["b_down"] = f(params["b_down"][:NL])
    b_in = np.asarray(params["b_in"][:NL], dtype=np.float32)
    d["bin_col"] = f(np.transpose(b_in.reshape(NL, 68, 128), (0, 2, 1)))
    d["bin_row"] = f(b_in.reshape(NL, 1, DIN))
    lg = np.asarray(params["ln_v_g"][:NL], dtype=np.float32).reshape(NL, 8, 128)
    lb = np.asarray(params["ln_v_b"][:NL], dtype=np.float32).reshape(NL, 8, 128)
    d["lnv_col"] = f(np.concatenate([np.transpose(lg, (0, 2, 1)), np.transpose(lb, (0, 2, 1))], axis=2))
    d["bs_row"] = f(np.asarray(params["b_s"][:NL]).reshape(NL, 1, 1024))
    d["ln1_row"] = f(np.stack([np.asarray(params["ln1_g"][:NL]), np.asarray(params["ln1_b"][:NL])], axis=1))
    d["ln2_row"] = f(np.stack([np.asarray(params["ln2_g"][:NL]), np.asarray(params["ln2_b"][:NL])], axis=1))
    d["br_row"] = f(np.asarray(params["b_r"][:NL]).reshape(NL, 1, NE))
    for k, v in _consts(NT, NBLK, CAP).items():
        d["c_" + k] = v
    return d


def kernel(x_prompt, x_sample, w_in, b_in, ln_v_g, ln_v_b, w_s, b_s, w_pa, w_pb, w_o, ln1_g, ln1_b,
           w_r, b_r, w_gu, b_gu, w_down, b_down, ln2_g, ln2_b):
    params = dict(w_in=w_in, b_in=b_in, ln_v_g=ln_v_g, ln_v_b=ln_v_b, w_s=w_s, b_s=b_s, w_pa=w_pa, w_pb=w_pb, w_o=w_o,
                  ln1_g=ln1_g, ln1_b=ln1_b, w_r=w_r, b_r=b_r, w_gu=w_gu, b_gu=b_gu, w_down=w_down, b_down=b_down,
                  ln2_g=ln2_g, ln2_b=ln2_b)
    NL, NSEQ = 4, 2
    nc, meta = build_program(NL=NL, NSEQ=NSEQ)
    shared = host_inputs(params, NL, meta["NT"], meta["NBLK"], meta["CAP"])
    xp = np.asarray(x_prompt, dtype=np.float32)
    xs = np.asarray(x_sample, dtype=np.float32)
    seqs = []
    for c in range(8):
        if c < 4:
            seqs.append(np.concatenate([xp[2 * c], xp[2 * c + 1]], axis=0))
        else:
            seqs.append(np.concatenate([xs[c - 4], xs[c - 4]], axis=0))
    in_maps = []
    for c in range(8):
        m = dict(shared)
        m["x"] = np.ascontiguousarray(seqs[c])
        in_maps.append(m)
    res = run_bass_kernel_spmd(nc, in_maps, core_ids=list(range(8)))
    yp = np.zeros_like(xp)
    ys = np.zeros_like(xs)
    for c in range(8):
        y = np.asarray(res.results[c]["y"], dtype=np.float32).reshape(2, S, D)
        if c < 4:
            yp[2 * c] = y[0]
            yp[2 * c + 1] = y[1]
        else:
            ys[c - 4] = y[0]
    return (yp, ys)
```

```python
import numpy as np
import ml_dtypes
from contextlib import ExitStack
import concourse.bass as bass
import concourse.mybir as mybir
from concourse.bass_utils import run_bass_kernel_spmd

F32 = mybir.dt.float32
BF16 = mybir.dt.bfloat16
I32 = mybir.dt.int32
AF = mybir.ActivationFunctionType
ALU = mybir.AluOpType
AX = mybir.AxisListType

S = 4096
D = 1024
DIN = 8704
NE = 32
TOPK = 4
PATTERNS = ((128, 1), (512, 4), (2048, 16))
NHEAD = 12
SLOPES = np.array([2.0 ** (-8.0 * (h + 1) / NHEAD) for h in range(NHEAD)], dtype=np.float32).reshape(3, 4)
DEPTH_FULL = 4
ALPHA = (2.0 * DEPTH_FULL) ** 0.25
EPS = 1e-5
NEG = -1e30
LIMIT = 7.0
SW_ALPHA = 1.702
SCALE = 128.0 ** -0.5
BIG = 1 << 28


class Buf:
    def __init__(self, t, name):
        self.t = t
        self.name = name
        self.lw = None
        self.rd = {}
        self.ds = None

    def __getitem__(self, idx):
        return self.t[idx]


class DSem:
    def __init__(self, h, key):
        self.h = h
        self.key = key
        self.cnt = 0


class KB:
    SAME_ENG = True

    def __init__(self, nc):
        self.nc = nc
        self.E = {"pe": nc.tensor, "act": nc.scalar, "dve": nc.vector, "pool": nc.gpsimd, "sp": nc.sync}
        self.sem = {k: nc.alloc_semaphore("es_" + k) for k in self.E}
        self.cnt = {k: 0 for k in self.E}
        self.seen = {k: {} for k in self.E}
        self.semobj = dict(self.sem)
        self.dsems = {}
        self.dfree = []
        self.uid = 0
        self.psrr = 0

    def sb(self, stack, name, shape, dt):
        self.uid += 1
        t = stack.enter_context(self.nc.sbuf_tensor(f"{name}_{self.uid}", list(shape), dt))
        b = Buf(t, name)
        stack.callback(self._release, b)
        return b

    def ps(self, stack, name, shape, dt):
        self.uid += 1
        t = stack.enter_context(self.nc.psum_tensor(f"{name}_{self.uid}", list(shape), dt))
        return Buf(t, name)

    def _release(self, b):
        if b.ds is not None:
            self.dfree.append(b.ds)
            b.ds = None

    def _getds(self, b):
        if b.ds is None:
            if self.dfree:
                b.ds = self.dfree.pop()
            else:
                key = f"d{len(self.dsems)}"
                d = DSem(self.nc.alloc_semaphore("ds_" + key), key)
                self.dsems[key] = d
                self.semobj[key] = d.h
                b.ds = d
        return b.ds

    def _deps(self, reads, writes):
        d = []
        for b in reads:
            if b.lw is not None:
                d.append(b.lw)
        for b in writes:
            if b.lw is not None:
                d.append(b.lw)
            d.extend(b.rd.items())
        return d

    def _wait(self, eng, deps):
        need = {}
        for k, v in deps:
            if k == eng and (eng == "pe" or not self.SAME_ENG):
                continue
            if need.get(k, 0) < v:
                need[k] = v
        for k, v in need.items():
            if self.seen[eng].get(k, 0) < v:
                self.E[eng].wait_ge(self.semobj[k], v)
                self.seen[eng][k] = v

    def _mark(self, tok, reads, writes):
        k, v = tok
        for b in writes:
            b.lw = tok
            b.rd = {}
        for b in reads:
            if b in writes:
                continue
            if b.rd.get(k, 0) < v:
                b.rd[k] = v

    def op(self, eng, fn, reads=(), writes=()):
        self._wait(eng, self._deps(reads, writes))
        ins = fn(self.E[eng])
        self.cnt[eng] += 1
        ins.then_inc(self.sem[eng], 1)
        tok = (eng, self.cnt[eng])
        self._mark(tok, reads, writes)
        return tok

    def pe(self, fns, reads=(), writes=()):
        self._wait("pe", self._deps(reads, writes))
        ins = None
        for f in fns:
            ins = f(self.nc.tensor)
        self.cnt["pe"] += 1
        ins.then_inc(self.sem["pe"], 1)
        tok = ("pe", self.cnt["pe"])
        self._mark(tok, reads, writes)
        return tok

    def mm(self, out, pairs, reads=(), writes=()):
        n = len(pairs)
        fns = []
        for i, (l, r) in enumerate(pairs):
            fns.append(lambda t, l=l, r=r, i=i: t.matmul(out, lhsT=l, rhs=r, start=(i == 0), stop=(i == n - 1)))
        return self.pe(fns, reads, writes)

    def dma(self, q, sbuf, fn, reads=(), writes=()):
        self._wait(q, self._deps(reads, writes))
        d = self._getds(sbuf)
        ins = fn(self.E[q])
        d.cnt += 16
        ins.then_inc(d.h, 16)
        tok = (d.key, d.cnt)
        self._mark(tok, reads, writes)
        return tok

    def load(self, q, dst, dst_ap, src_ap, **kw):
        return self.dma(q, dst, lambda e: e.dma_start(out=dst_ap, in_=src_ap, **kw), reads=(), writes=(dst,))

    def store(self, q, src, dst_ap, src_ap, **kw):
        return self.dma(q, src, lambda e: e.dma_start(out=dst_ap, in_=src_ap, **kw), reads=(src,), writes=())

    def barrier(self):
        tot = dict(self.cnt)
        for k, d in self.dsems.items():
            tot[k] = d.cnt
        for e in self.E:
            for k, v in tot.items():
                if v > 0 and self.seen[e].get(k, 0) < v:
                    self.E[e].wait_ge(self.semobj[k], v)
                    self.seen[e][k] = v

    def nextps(self, banks):
        b = banks[self.psrr % len(banks)]
        self.psrr += 1
        return b


def _consts(NT, NBLK, CAP):
    c = {}
    c["ident_f"] = np.eye(128, dtype=np.float32)
    c["ident_b"] = np.eye(128, dtype=np.float32).astype(ml_dtypes.bfloat16)
    tri = (np.arange(128)[:, None] < np.arange(128)[None, :]).astype(np.float32)
    c["tri_b"] = tri.astype(ml_dtypes.bfloat16)
    c["ones_b"] = np.ones((128, 128), dtype=ml_dtypes.bfloat16)
    c["ones_f"] = np.ones((128, 128), dtype=np.float32)
    tab = np.zeros((3, 128, 12, 256), dtype=np.float32)
    a = np.arange(128)[:, None]
    cc = np.arange(256)[None, :]
    rel = cc - 64 - a
    for g, (win, d) in enumerate(PATTERNS):
        for h in range(4):
            base = np.where(np.abs(rel) <= 64, -SLOPES[g, h] * d * np.abs(rel), NEG).astype(np.float32)
            for var in range(3):
                t = base.copy()
                if var == 1:
                    t[:, :64] = NEG
                if var == 2:
                    t[:, 192:] = NEG
                tab[g, :, h * 3 + var, :] = t
    c["att_bias"] = tab
    c["iota32"] = np.tile(np.arange(32, dtype=np.float32)[None, :], (128, 1))
    c["ecap"] = np.tile((np.arange(32, dtype=np.float32) * CAP)[None, :], (128, 1))
    c["pcol"] = np.arange(128, dtype=np.float32)[:, None].copy()
    c["jthr"] = np.tile((np.arange(64, dtype=np.float32) * 128.0)[None, :], (32, 1))
    c["iota_b"] = np.tile(np.arange(NBLK, dtype=np.float32)[None, :], (128, 1))
    c["triu32"] = (np.arange(32)[:, None] <= np.arange(32)[None, :]).astype(np.float32)
    tok = np.zeros((128, NT // 128, 4), dtype=np.int32)
    tok[:, :, 0] = np.arange(NT // 128)[None, :] * 128 + np.arange(128)[:, None]
    c["tokid4"] = tok
    ri = np.zeros((128, 128, 4), dtype=np.int32)
    ri[:, :, 0] = NT
    c["rinit"] = ri
    return c


CONST_DT = {"ident_f": F32, "ident_b": BF16, "tri_b": BF16, "ones_b": BF16, "ones_f": F32, "att_bias": F32,
            "iota32": F32, "ecap": F32, "pcol": F32, "jthr": F32, "iota_b": F32, "triu32": F32,
            "tokid4": I32, "rinit": I32}


def build_program(NL=4, NSEQ=2, debug=False, phases=None):
    NT = NSEQ * S
    NTILE = NT // 128
    CAP = NT
    NBLK = NT * TOPK // 128 + NE
    RROWS = NE * CAP + 128
    nc = bass.Bass("TRN2", target_bir_lowering=False)
    K = KB(nc)
    sk = "ExternalOutput" if debug else "Internal"

    def din(name, shape, dt=F32):
        return nc.dram_tensor(name, list(shape), dt, kind="ExternalInput").ap()

    def dscr(name, shape, dt=F32):
        return nc.dram_tensor(name, list(shape), dt, kind=sk).ap()

    x_in = din("x", [NT, D])
    w_in = din("w_in", [NL, D, DIN])
    w_st = din("w_st", [NL, 128, 8, 128])
    w_pa = din("w_pa", [NL, D, D])
    w_pb = din("w_pb", [NL, 512, D])
    w_o = din("w_o", [NL, D, D])
    w_r = din("w_r", [NL, D, NE])
    w_gu = din("w_gu", [NL, NE, D, 2 * D])
    w_dn = din("w_down", [NL, NE, D, D])
    b_gu = din("b_gu", [NL, NE, 2 * D])
    b_dn = din("b_down", [NL, NE, D])
    bin_col = din("bin_col", [NL, 128, 68])
    bin_row = din("bin_row", [NL, 1, DIN])
    lnv_col = din("lnv_col", [NL, 128, 16])
    bs_row = din("bs_row", [NL, 1, D])
    ln1_row = din("ln1_row", [NL, 2, D])
    ln2_row = din("ln2_row", [NL, 2, D])
    br_row = din("br_row", [NL, 1, NE])
    cshape = {k: v.shape for k, v in _consts(128 * 2, 8, 1).items()}
    cshape["iota_b"] = (128, NBLK)
    cshape["tokid4"] = (128, NTILE, 4)
    cin = {k: din("c_" + k, cshape[k], CONST_DT[k]) for k in cshape}
    y_out = nc.dram_tensor("y", [NT, D], F32, kind="ExternalOutput").ap()

    XS = dscr("XS", [NT, D])
    X1 = dscr("X1", [NT + 128, D])
    X1P = dscr("X1P", [NT + 128, D], BF16)
    NTP = [NSEQ * d * (S // d + 128) for (_, d) in PATTERNS]
    QT = [dscr(f"QT{g}", [4, 128, NTP[g]], BF16) for g in range(3)]
    KT = [dscr(f"KT{g}", [4, 128, NTP[g]], BF16) for g in range(3)]
    VV = [dscr(f"VV{g}", [NTP[g], 512], BF16) for g in range(3)]
    OGall = dscr("OG", [3, NT, 516])
    OG = [OGall[g] for g in range(3)]
    MA = dscr("MA", [8, 128, NT], BF16)
    ROUTE = dscr("ROUTE", [RROWS, 4], I32)
    YB = dscr("YB", [NBLK * 128, D])

    top = ExitStack()
    ident_f = K.sb(top, "ident_f", [128, 128], F32)
    ident_b = K.sb(top, "ident_b", [128, 128], BF16)
    ones_b = K.sb(top, "ones_b", [128, 128], BF16)
    ones_f = K.sb(top, "ones_f", [128, 128], F32)
    tri_b = K.sb(top, "tri_b", [128, 128], BF16)
    iota32 = K.sb(top, "iota32", [128, 32], F32)
    ecap = K.sb(top, "ecap", [128, 32], F32)
    pcol = K.sb(top, "pcol", [128, 1], F32)
    tokid4 = K.sb(top, "tokid4", [128, NTILE, 4], I32)
    GATES = K.sb(top, "GATES", [128, NTILE, 4], F32)
    EKF = K.sb(top, "EKF", [128, NTILE, 4], F32)
    POSK = K.sb(top, "POSK", [128, NTILE, 4], F32)
    RUN = K.sb(top, "RUN", [128, 32], F32)
    BS128 = K.sb(top, "BS128", [128, 32], F32)
    RIDX = K.sb(top, "RIDX", [128, NBLK], I32)
    WIDX = K.sb(top, "WIDX", [128, NBLK], I32)
    BIDX = K.sb(top, "BIDX", [128, NBLK], I32)
    PSB = [K.ps(top, f"psb{i}", [128, 512], F32) for i in range(8)]

    for nm, b in (("ident_f", ident_f), ("ident_b", ident_b), ("ones_b", ones_b), ("ones_f", ones_f), ("tri_b", tri_b),
                  ("iota32", iota32), ("ecap", ecap), ("pcol", pcol), ("tokid4", tokid4)):
        K.load("sp", b, b.t[:], cin[nm])

    def evac(i, out_ap, in_ap, reads, writes):
        if i % 2 == 0:
            return K.op("act", lambda e: e.activation(out=out_ap, in_=in_ap, func=AF.Copy), reads, writes)
        return K.op("dve", lambda e: e.tensor_copy(out=out_ap, in_=in_ap), reads, writes)

    def load_xT(st, xt, xT, src_rows_ap, TM):
        nj = TM // 128
        K.load("sp", xt, xt.t[:, 0:nj, :], src_rows_ap)

    def transposes_x(xt, xT, TM, evi=0, act_only=False):
        nj = TM // 128
        for kc in range(8):
            pb = K.nextps(PSB)
            fns = [lambda t, j=j, kc=kc, pb=pb: t.transpose(out=pb.t[:, j * 128:(j + 1) * 128], in_=xt.t[:, j, kc * 128:(kc + 1) * 128], identity=ident_f.t[:])
                   for j in range(nj)]
            K.pe(fns, reads=(xt, ident_f), writes=(pb,))
            evac(0 if act_only else kc + evi, xT.t[:, kc, 0:TM], pb.t[:, 0:TM], (pb,), (xT,))

    def ln_stats(st, v_ap_fn, vbuf, mv, stt, rstd):
        K.op("dve", lambda e: e.bn_stats(out=stt.t[:, 0:6], in_=v_ap_fn(0)), (vbuf,), (stt,))
        K.op("dve", lambda e: e.bn_stats(out=stt.t[:, 6:12], in_=v_ap_fn(1)), (vbuf,), (stt,))
        K.op("dve", lambda e: e.bn_aggr(out=mv.t[:, 0:2], in_=stt.t[:, 0:12].rearrange("p (c s) -> p c s", s=6)), (stt,), (mv,))
        K.op("act", lambda e: e.activation(out=rstd.t[:, 0:1], in_=mv.t[:, 1:2], func=AF.Sqrt, bias=EPS), (mv,), (rstd,))
        K.op("dve", lambda e: e.reciprocal(out=rstd.t[:, 0:1], in_=rstd.t[:, 0:1]), (rstd,), (rstd,))

    with ExitStack() as st:
        zt = K.sb(st, "zt", [128, 4096], BF16)
        K.op("pool", lambda e: e.memset(zt.t[:], 0.0), (), (zt,))
        for g in range(3):
            n = NTP[g]
            for h in range(4):
                for c0 in range(0, n, 4096):
                    w = min(4096, n - c0)
                    K.store("sp", zt, KT[g][h, :, c0:c0 + w], zt.t[:, 0:w])
            vv = VV[g].rearrange("(a p) f -> p a f", p=128)
            na = n // 128
            for a0 in range(0, na, 8):
                w = min(8, na - a0)
                K.store("sp", zt, vv[:, a0:a0 + w, :], zt.t[:, 0:w * 512].rearrange("p (a f) -> p a f", f=512))
        K.store("sp", zt, X1[NT:NT + 128, :], zt.t[:].bitcast(F32)[:, 0:1024])
        K.store("sp", zt, X1P[NT:NT + 128, :], zt.t[:, 0:1024])
        K.barrier()

    for l in range(NL):
        XA = x_in if l == 0 else XS
        XO = y_out if l == NL - 1 else XS
        if phases is None or "qkv" in phases:
          with ExitStack() as st:
            wg = [K.sb(st, f"wg{i}", [128, 8, 1536], BF16) for i in range(2)]
            bcol = K.sb(st, "bcol", [128, 68], F32)
            brow = K.sb(st, "brow", [1, DIN], BF16)
            xts = [K.sb(st, f"xt{i}", [128, 4, D], F32) for i in range(2)]
            xTs = [K.sb(st, f"xT{i}", [128, 8, 512], BF16) for i in range(2)]
            qks = [K.sb(st, f"qk{i}", [128, 8, 512], BF16) for i in range(2)]
            vss = [K.sb(st, f"vs{i}", [128, 4, 512], BF16) for i in range(2)]
            K.load("sp", bcol, bcol.t[:], bin_col[l])
            K.load("pool", brow, brow.t[:], bin_row[l])
            it = 0
            for g, (win, d) in enumerate(PATTERNS):
                L = S // d
                LP = L + 128
                TM = min(512, L)
                nj = TM // 128
                c0 = 2048 + g * 1536
                W = wg[g % 2]
                K.load("pool", W, W.t[:], w_in[l][:, c0:c0 + 1536].rearrange("(kc p) c -> p kc c", p=128))
                XAv = XA.rearrange("(s i dd) c -> s dd i c", s=NSEQ, dd=d)
                tiles = [(s, r, m) for s in range(NSEQ) for r in range(d) for m in range(L // TM)]

                def src(tl):
                    s, r, m = tl
                    return XAv[s, r, m * TM:(m + 1) * TM].rearrange("(j a) c -> a j c", a=128)
                K.load("sp", xts[it % 2], xts[it % 2].t[:, 0:nj, :], src(tiles[0]))
                for ti, tl in enumerate(tiles):
                    s, r, m = tl
                    xt, xT, qk, vs = xts[it % 2], xTs[it % 2], qks[it % 2], vss[it % 2]
                    if ti + 1 < len(tiles):
                        nx = xts[(it + 1) % 2]
                        K.load("sp", nx, nx.t[:, 0:nj, :], src(tiles[ti + 1]))
                    transposes_x(xt, xT, TM)
                    col0 = (s * d + r) * LP + 64 + m * TM
                    for hs in range(8):
                        pb = K.nextps(PSB)
                        K.mm(pb.t[:, 0:TM], [(W.t[:, kc, hs * 128:(hs + 1) * 128], xT.t[:, kc, 0:TM]) for kc in range(8)],
                             reads=(W, xT), writes=(pb,))
                        bc = (c0 // 128) + hs
                        K.op("act", lambda e, pb=pb, hs=hs, bc=bc: e.activation(out=qk.t[:, hs, 0:TM], in_=pb.t[:, 0:TM], func=AF.Identity,
                                                                              bias=bcol.t[:, bc:bc + 1]),
                             (pb, bcol), (qk,))
                    K.store("sp", qk, QT[g][:, :, col0:col0 + TM].rearrange("h p c -> p h c"), qk.t[:, 0:4, 0:TM])
                    K.store("sp", qk, KT[g][:, :, col0:col0 + TM].rearrange("h p c -> p h c"), qk.t[:, 4:8, 0:TM])
                    for j in range(nj):
                        pb = K.nextps(PSB)
                        pairs = [(xT.t[:, kc, j * 128:(j + 1) * 128], W.t[:, kc, 1024:1536]) for kc in range(8)]
                        pairs.append((ones_b.t[0:1, :], brow.t[0:1, c0 + 1024:c0 + 1536]))
                        K.mm(pb.t[:, :], pairs, reads=(W, xT, ones_b, brow), writes=(pb,))
                        evac(j, vs.t[:, j, :], pb.t[:, :], (pb,), (vs,))
                    K.store("sp", vs, VV[g][col0:col0 + TM, :].rearrange("(j p) f -> p j f", p=128), vs.t[:, 0:nj, :])
                    it += 1
            K.barrier()

        if phases is None or "att" in phases:
          with ExitStack() as st:
            bts = [K.sb(st, f"bt{i}", [128, 12, 256], F32) for i in range(3)]
            qcs = [K.sb(st, f"qc{i}", [128, 4, 1024], BF16) for i in range(2)]
            kcs = [K.sb(st, f"kc{i}", [128, 4, 1152], BF16) for i in range(2)]
            vcs = [K.sb(st, f"vc{i}", [128, 9, 512], BF16) for i in range(2)]
            NB_ = 3
            ogs = [K.sb(st, f"og{i}", [128, 516], F32) for i in range(NB_)]
            nmx = [K.sb(st, f"nmx{i}", [128, 4], F32) for i in range(NB_)]
            den = [K.sb(st, f"den{i}", [128, 4], F32) for i in range(NB_)]
            rdn = [K.sb(st, f"rdn{i}", [128, 4], F32) for i in range(NB_)]
            lnd = [K.sb(st, f"lnd{i}", [128, 4], F32) for i in range(NB_)]
            NU_ = 4
            sbs = [K.sb(st, f"sS{i}", [128, 256], F32) for i in range(NU_)]
            pbs = [K.sb(st, f"sP{i}", [128, 256], BF16) for i in range(NU_)]
            pts = [K.sb(st, f"sPT{i}", [128, 256], BF16) for i in range(NU_)]
            for g in range(3):
                K.load("sp", bts[g], bts[g].t[:], cin["att_bias"][g])
            chunks = []
            for g, (win, d) in enumerate(PATTERNS):
                L = S // d
                CH = min(1024, L)
                for s_ in range(NSEQ):
                    for r in range(d):
                        for ch in range(L // CH):
                            chunks.append((g, d, L, CH, s_, r, ch))

            def ldchunk(ci):
                g, d, L, CH, s_, r, ch = chunks[ci]
                LP = L + 128
                nb = CH // 128
                col0 = (s_ * d + r) * LP + 64 + ch * CH
                qc, kc_, vc = qcs[ci % 2], kcs[ci % 2], vcs[ci % 2]
                K.load("sp", qc, qc.t[:, :, 0:CH], QT[g][:, :, col0:col0 + CH].rearrange("h p c -> p h c"))
                K.load("sp", kc_, kc_.t[:, :, 0:CH + 128], KT[g][:, :, col0 - 64:col0 + CH + 64].rearrange("h p c -> p h c"))
                K.load("sp", vc, vc.t[:, 0:nb + 1, :], VV[g][col0 - 64:col0 + CH + 64, :].rearrange("(c p) f -> p c f", p=128))

            units = []
            bidx = 0
            for ci, (g, d, L, CH, s_, r, ch) in enumerate(chunks):
                nb = CH // 128
                nblk = L // 128
                for i in range(nb):
                    blk = ch * nb + i
                    var = 1 if blk == 0 else (2 if blk == nblk - 1 else 0)
                    for h in range(4):
                        units.append(dict(ci=ci, g=g, d=d, s=s_, r=r, i=i, h=h, var=var, b=bidx, p0=ch * CH + i * 128,
                                          lastc=(i == nb - 1 and h == 3)))
                    bidx += 1
            NU = len(units)
            state = {}

            def stA(u):
                U = units[u]
                qc, kc_ = qcs[U["ci"] % 2], kcs[U["ci"] % 2]
                bt = bts[U["g"]]
                i, h, var = U["i"], U["h"], U["var"]
                sS, sP = sbs[u % NU_], pbs[u % NU_]
                nm_, dn_ = nmx[U["b"] % NB_], den[U["b"] % NB_]
                pS = K.nextps(PSB)
                K.mm(pS.t[:, 0:256], [(qc.t[:, h, i * 128:(i + 1) * 128], kc_.t[:, h, i * 128:i * 128 + 256])], reads=(qc, kc_), writes=(pS,))
                K.op("dve", lambda e: e.scalar_tensor_tensor(out=sS.t[:], in0=pS.t[:, 0:256], scalar=SCALE, in1=bt.t[:, h * 3 + var, :], op0=ALU.mult, op1=ALU.add),
                     (pS, bt), (sS,))
                K.op("dve", lambda e: e.tensor_reduce(out=nm_.t[:, h:h + 1], in_=sS.t[:], axis=AX.X, op=ALU.max, negate=True), (sS,), (nm_,))
                K.op("act", lambda e: e.activation(out=sP.t[:], in_=sS.t[:], func=AF.Exp, bias=nm_.t[:, h:h + 1], accum_out=dn_.t[:, h:h + 1]),
                     (sS, nm_), (sP, dn_))

            def stB(u):
                sP, sPT = pbs[u % NU_], pts[u % NU_]
                pT = K.nextps(PSB)
                pTb = pT.t[:].bitcast(BF16)
                K.pe([lambda t, c=c: t.transpose(out=pTb[:, c * 128:(c + 1) * 128], in_=sP.t[:, c * 128:(c + 1) * 128], identity=ident_b.t[:])
                      for c in range(2)], reads=(sP, ident_b), writes=(pT,))
                K.op("act", lambda e: e.activation(out=sPT.t[:], in_=pTb[:, 0:256], func=AF.Copy), (pT,), (sPT,))

            def stC(u):
                U = units[u]
                vc = vcs[U["ci"] % 2]
                i, h = U["i"], U["h"]
                sPT = pts[u % NU_]
                q_ = U["b"] % NB_
                og, nm_, dn_, rd_, ld_ = ogs[q_], nmx[q_], den[q_], rdn[q_], lnd[q_]
                pO = K.nextps(PSB)
                K.mm(pO.t[:, 0:128], [(sPT.t[:, c * 128:(c + 1) * 128], vc.t[:, i + c, h * 128:(h + 1) * 128]) for c in range(2)], reads=(sPT, vc), writes=(pO,))
                K.op("dve", lambda e: e.reciprocal(out=rd_.t[:, h:h + 1], in_=dn_.t[:, h:h + 1]), (dn_,), (rd_,))
                K.op("act", lambda e: e.activation(out=og.t[:, h * 128:(h + 1) * 128], in_=pO.t[:, 0:128], func=AF.Identity, scale=rd_.t[:, h:h + 1]),
                     (pO, rd_), (og,))
                if h == 3:
                    K.op("act", lambda e: e.activation(out=ld_.t[:], in_=dn_.t[:], func=AF.Ln), (dn_,), (ld_,))
                    K.op("dve", lambda e: e.tensor_tensor(out=og.t[:, 512:516], in0=ld_.t[:], in1=nm_.t[:], op=ALU.subtract), (ld_, nm_), (og,))
                    XOv = OG[U["g"]].rearrange("(s i dd) c -> s dd i c", s=NSEQ, dd=U["d"])
                    K.store("sp", og, XOv[U["s"], U["r"], U["p0"]:U["p0"] + 128, :], og.t[:])
                if U["lastc"] and U["ci"] + 2 < len(chunks):
                    ldchunk(U["ci"] + 2)

            ldchunk(0)
            if len(chunks) > 1:
                ldchunk(1)
            for idx in range(NU + 2):
                if idx < NU:
                    stA(idx)
                if 1 <= idx <= NU:
                    stB(idx - 1)
                if idx >= 2:
                    stC(idx - 2)
            K.barrier()

        if phases is None or "pc1" in phases:
          with ExitStack() as st:
            Wu = K.sb(st, "Wu", [128, 8, 1024], BF16)
            Wv = K.sb(st, "Wv", [128, 8, 1024], BF16)
            Wga = K.sb(st, "Wga", [128, 8, 1024], BF16)
            Wpa = K.sb(st, "Wpa", [128, 8, 1024], BF16)
            Wst = K.sb(st, "Wst", [128, 8, 128], BF16)
            bcol = K.sb(st, "bcol", [128, 68], F32)
            brow = K.sb(st, "brow", [1, 1024], BF16)
            lnv = K.sb(st, "lnv", [128, 16], F32)
            bsb = K.sb(st, "bsb", [128, 8, 128], F32)
            Cgt = K.sb(st, "Cgt", [128, 8, 128], F32)
            xts = [K.sb(st, f"xt{i}", [128, 4, D], F32) for i in range(2)]
            xT = K.sb(st, "xT", [128, 8, 512], BF16)
            uT = K.sb(st, "uT", [128, 8, 512], BF16)
            gaT = K.sb(st, "gaT", [128, 8, 512], BF16)
            aoT = K.sb(st, "aoT", [128, 8, 512], BF16)
            maTs = [K.sb(st, f"maT{i}", [128, 8, 512], BF16) for i in range(2)]
            vsb = [K.sb(st, f"v{i}", [128, 1024], F32) for i in range(2)]
            nsb = [K.sb(st, f"n{i}", [128, 1024], BF16) for i in range(2)]
            tmpa = [K.sb(st, f"tmpa{i}", [128, 8, 128], F32) for i in range(2)]
            stt = [K.sb(st, f"stt{i}", [128, 12], F32) for i in range(2)]
            mv = [K.sb(st, f"mv{i}", [128, 2], F32) for i in range(2)]
            rstd = [K.sb(st, f"rstd{i}", [128, 1], F32) for i in range(2)]
            wsrc = w_in[l]
            K.load("pool", Wu, Wu.t[:], wsrc[:, 0:1024].rearrange("(kc p) c -> p kc c", p=128))
            K.load("pool", Wv, Wv.t[:], wsrc[:, 1024:2048].rearrange("(kc p) c -> p kc c", p=128))
            K.load("pool", Wga, Wga.t[:], wsrc[:, 6656:7680].rearrange("(kc p) c -> p kc c", p=128))
            K.load("pool", Wpa, Wpa.t[:], w_pa[l].rearrange("(kc p) c -> p kc c", p=128))
            K.load("pool", Wst, Wst.t[:], w_st[l])
            K.load("pool", brow, brow.t[:], bin_row[l][:, 1024:2048])
            K.load("sp", bcol, bcol.t[:], bin_col[l])
            K.load("sp", lnv, lnv.t[:], lnv_col[l])
            K.load("sp", bsb, bsb.t[:], bs_row[l].rearrange("o (g t) -> o g t", g=8).to_broadcast([128, 8, 128]))
            for half in range(2):
                pb = K.nextps(PSB)
                K.mm(pb.t[:, :], [(ones_b.t[:, :], Wst.t[:, half * 4:(half + 1) * 4, :])], reads=(ones_b, Wst), writes=(pb,))
                for gg in range(4):
                    g_ = half * 4 + gg
                    K.op("dve", lambda e, pb=pb, gg=gg, g_=g_: e.scalar_tensor_tensor(
                        out=Cgt.t[:, g_, :], in0=pb.t[:, gg * 128:(gg + 1) * 128], scalar=lnv.t[:, 8 + g_:9 + g_], in1=bsb.t[:, g_, :],
                        op0=ALU.mult, op1=ALU.add), (pb, lnv, bsb), (Cgt,))
            XAt = XA.rearrange("(m j a) c -> m a j c", j=4, a=128)
            NM = NT // 512
            K.load("sp", xts[0], xts[0].t[:], XAt[0])
            sj = 0
            for m in range(NM):
                xt = xts[m % 2]
                maT = maTs[m % 2]
                if m + 1 < NM:
                    K.load("sp", xts[(m + 1) % 2], xts[(m + 1) % 2].t[:], XAt[m + 1])
                transposes_x(xt, xT, 512)
                for c8 in range(8):
                    pb = K.nextps(PSB)
                    K.mm(pb.t[:, :], [(Wu.t[:, kc, c8 * 128:(c8 + 1) * 128], xT.t[:, kc, :]) for kc in range(8)], reads=(Wu, xT), writes=(pb,))
                    K.op("act", lambda e, pb=pb, c8=c8: e.activation(out=uT.t[:, c8, :], in_=pb.t[:, :], func=AF.Gelu, bias=bcol.t[:, c8:c8 + 1]),
                         (pb, bcol), (uT,))
                for j in range(4):
                    v_, n_, ta, st_, mv_, rs_ = vsb[sj % 2], nsb[sj % 2], tmpa[sj % 2], stt[sj % 2], mv[sj % 2], rstd[sj % 2]
                    for nh in range(2):
                        pb = K.nextps(PSB)
                        pairs = [(xT.t[:, kc, j * 128:(j + 1) * 128], Wv.t[:, kc, nh * 512:(nh + 1) * 512]) for kc in range(8)]
                        pairs.append((ones_b.t[0:1, :], brow.t[0:1, nh * 512:(nh + 1) * 512]))
                        K.mm(pb.t[:, :], pairs, reads=(xT, Wv, ones_b, brow), writes=(pb,))
                        K.op("act", lambda e, pb=pb, nh=nh, v_=v_: e.activation(out=v_.t[:, nh * 512:(nh + 1) * 512], in_=pb.t[:, :], func=AF.Gelu),
                             (pb,), (v_,))
                    ln_stats(st, lambda c, v_=v_: v_.t[:, c * 512:(c + 1) * 512], v_, mv_, st_, rs_)
                    K.op("dve", lambda e, v_=v_, n_=n_, mv_=mv_, rs_=rs_: e.tensor_scalar(
                        out=n_.t[:], in0=v_.t[:], scalar1=mv_.t[:, 0:1], scalar2=rs_.t[:, 0:1], op0=ALU.subtract, op1=ALU.mult),
                        (v_, mv_, rs_), (n_,))
                    for half in range(2):
                        pb = K.nextps(PSB)
                        fns = [lambda t, gg=gg, half=half, pb=pb, n_=n_: t.matmul(
                            pb.t[:, gg * 128:(gg + 1) * 128], lhsT=n_.t[:, (half * 4 + gg) * 128:(half * 4 + gg + 1) * 128],
                            rhs=Wst.t[:, half * 4 + gg, :], start=True, stop=True) for gg in range(4)]
                        K.pe(fns, reads=(n_, Wst), writes=(pb,))
                        for gg in range(4):
                            g_ = half * 4 + gg
                            K.op("dve", lambda e, pb=pb, gg=gg, g_=g_, ta=ta: e.scalar_tensor_tensor(
                                out=ta.t[:, g_, :], in0=pb.t[:, gg * 128:(gg + 1) * 128], scalar=lnv.t[:, g_:g_ + 1], in1=Cgt.t[:, g_, :],
                                op0=ALU.mult, op1=ALU.add), (pb, lnv, Cgt), (ta,))
                    K.op("pool", lambda e, ta=ta, j=j: e.tensor_tensor(out=aoT.t[:, :, j * 128:(j + 1) * 128], in0=ta.t[:], in1=uT.t[:, :, j * 128:(j + 1) * 128], op=ALU.mult),
                         (ta, uT), (aoT,))
                    sj += 1
                for c8 in range(8):
                    pb = K.nextps(PSB)
                    K.mm(pb.t[:, :], [(Wga.t[:, kc, c8 * 128:(c8 + 1) * 128], xT.t[:, kc, :]) for kc in range(8)], reads=(Wga, xT), writes=(pb,))
                    K.op("act", lambda e, pb=pb, c8=c8: e.activation(out=gaT.t[:, c8, :], in_=pb.t[:, :], func=AF.Sigmoid, bias=bcol.t[:, 52 + c8:53 + c8]),
                         (pb, bcol), (gaT,))
                for c8 in range(8):
                    pb = K.nextps(PSB)
                    K.mm(pb.t[:, :], [(Wpa.t[:, kc, c8 * 128:(c8 + 1) * 128], aoT.t[:, kc, :]) for kc in range(8)], reads=(Wpa, aoT), writes=(pb,))
                    K.op("dve", lambda e, pb=pb, c8=c8, maT=maT: e.tensor_tensor(out=maT.t[:, c8, :], in0=pb.t[:, :], in1=gaT.t[:, c8, :], op=ALU.mult),
                         (pb, gaT), (maT,))
                K.store("sp", maT, MA[:, :, m * 512:(m + 1) * 512].rearrange("n p t -> p n t"), maT.t[:])
            K.barrier()

        if phases is None or "pc2" in phases:
          with ExitStack() as st:
            Wgb = K.sb(st, "Wgb", [128, 8, 1024], BF16)
            Wpb = K.sb(st, "Wpb", [128, 4, 1024], BF16)
            Wo = K.sb(st, "Wo", [128, 8, 1024], BF16)
            Wr = K.sb(st, "Wr", [128, 8, 32], F32)
            bcol = K.sb(st, "bcol", [128, 68], F32)
            g1b = K.sb(st, "g1b", [128, 1024], F32)
            b1b = K.sb(st, "b1b", [128, 1024], F32)
            brb = K.sb(st, "brb", [128, 32], F32)
            rin = K.sb(st, "rin", [128, 128, 4], I32)
            xts = [K.sb(st, f"xt{i}", [128, 4, D], F32) for i in range(2)]
            xT = K.sb(st, "xT", [128, 8, 512], BF16)
            gbT = K.sb(st, "gbT", [128, 8, 512], BF16)
            maTs = [K.sb(st, f"maT{i}", [128, 8, 512], BF16) for i in range(2)]
            boT = K.sb(st, "boT", [128, 4, 512], BF16)
            mgT = K.sb(st, "mgT", [128, 8, 512], BF16)
            tmpm = [K.sb(st, f"tmpm{i}", [128, 512], F32) for i in range(2)]
            bof = [K.sb(st, f"bof{i}", [128, 512], F32) for i in range(2)]
            sm = [K.sb(st, f"sm{i}", [128, 64], F32) for i in range(2)]
            x1p = [K.sb(st, f"x1p{i}", [128, 1024], F32) for i in range(2)]
            x1 = [K.sb(st, f"x1{i}", [128, 1024], F32) for i in range(2)]
            x1T = [K.sb(st, f"x1T{i}", [128, 8, 128], F32) for i in range(1)] * 2
            stt = [K.sb(st, f"stt{i}", [128, 12], F32) for i in range(2)]
            mv = [K.sb(st, f"mv{i}", [128, 2], F32) for i in range(2)]
            rstd = [K.sb(st, f"rstd{i}", [128, 1], F32) for i in range(2)]
            lg = [K.sb(st, f"lg{i}", [128, 32], F32) for i in range(2)]
            t8 = [K.sb(st, f"t8{i}", [128, 8], F32) for i in range(2)]
            rs = [K.sb(st, f"rs{i}", [128, 16], F32) for i in range(2)]
            mk = [K.sb(st, f"mk{i}", [128, 32], BF16) for i in range(2)]
            posg = [K.sb(st, f"posg{i}", [128, 32], F32) for i in range(2)]
            slotv = [K.sb(st, f"slotv{i}", [128, 32], F32) for i in range(2)]
            oh = [K.sb(st, f"oh{i}", [128, 32], F32) for i in range(2)]
            ohx = [K.sb(st, f"ohx{i}", [128, 32], F32) for i in range(2)]
            slf = [K.sb(st, f"slf{i}", [128, 4], F32) for i in range(2)]
            sli = [K.sb(st, f"sli{i}", [128, 4], I32) for i in range(2)]
            wsrc = w_in[l]
            K.load("pool", Wgb, Wgb.t[:], wsrc[:, 7680:8704].rearrange("(kc p) c -> p kc c", p=128))
            K.load("pool", Wpb, Wpb.t[:], w_pb[l].rearrange("(kc p) c -> p kc c", p=128))
            K.load("pool", Wo, Wo.t[:], w_o[l].rearrange("(kc p) c -> p kc c", p=128))
            K.load("sp", Wr, Wr.t[:], w_r[l].rearrange("(kc p) c -> p kc c", p=128))
            K.load("sp", bcol, bcol.t[:], bin_col[l])
            K.load("sp", g1b, g1b.t[:], ln1_row[l][0:1, :].to_broadcast([128, 1024]))
            K.load("sp", b1b, b1b.t[:], ln1_row[l][1:2, :].to_broadcast([128, 1024]))
            K.load("sp", brb, brb.t[:], br_row[l].to_broadcast([128, 32]))
            K.load("sp", rin, rin.t[:], cin["rinit"])
            for c0 in range(0, NE * CAP, 16384):
                K.store("sp", rin, ROUTE[c0:c0 + 16384, :].rearrange("(p a) f -> p a f", p=128), rin.t[:])
            K.store("sp", rin, ROUTE[NE * CAP:NE * CAP + 128, :], rin.t[:, 0, :])
            K.op("dve", lambda e: e.memset(RUN.t[:], 0.0), (), (RUN,))
            K.barrier()
            XAt = XA.rearrange("(m j a) c -> m a j c", j=4, a=128)
            NM = NT // 512
            K.load("sp", xts[0], xts[0].t[:], XAt[0])
            K.load("sp", maTs[0], maTs[0].t[:], MA[:, :, 0:512].rearrange("n p t -> p n t"))
            og3 = [K.sb(st, f"og3x{i}", [128, 3, 516], F32) for i in range(4)]
            bo = [K.sb(st, f"box{i}", [128, 512], BF16) for i in range(4)]
            nmr = [K.sb(st, f"nmr{i}", [128, 1], F32) for i in range(2)]
            tmpq = [K.sb(st, f"tmpq{i}", [128, 512], F32) for i in range(2)]
            x1pb = [K.sb(st, f"x1pb{i}", [128, 1024], BF16) for i in range(1)] * 2
            oh4 = [K.sb(st, f"oh4{i}", [128, 4, 32], F32) for i in range(2)]
            ox4 = [K.sb(st, f"ox4{i}", [128, 4, 32], F32) for i in range(2)]

            def M1(m, j):
                tile = m * 4 + j
                o3, sm_, bo_, bof_ = og3[j], sm[j % 2], bo[j], bof[j % 2]
                K.load("sp", o3, o3.t[:], OGall[:, tile * 128:(tile + 1) * 128, :].rearrange("g p c -> p g c"))
                lse3 = o3.t[:, :, 512:516]
                K.op("dve", lambda e: e.tensor_reduce(out=sm_.t[:, 0:4], in_=lse3.rearrange("p g h -> p h g"), axis=AX.X, op=ALU.max), (o3,), (sm_,))
                K.op("dve", lambda e: e.tensor_tensor(out=sm_.t[:, 4:16].rearrange("p (g h) -> p g h", g=3), in0=lse3,
                                                      in1=sm_.t[:, 0:4].unsqueeze(1).to_broadcast([128, 3, 4]), op=ALU.subtract), (o3, sm_), (sm_,))
                K.op("act", lambda e: e.activation(out=sm_.t[:, 4:16], in_=sm_.t[:, 4:16], func=AF.Exp), (sm_,), (sm_,))
                K.op("dve", lambda e: e.tensor_reduce(out=sm_.t[:, 16:20], in_=sm_.t[:, 4:16].rearrange("p (g h) -> p h g", g=3), axis=AX.X, op=ALU.add), (sm_,), (sm_,))
                K.op("dve", lambda e: e.reciprocal(out=sm_.t[:, 20:24], in_=sm_.t[:, 16:20]), (sm_,), (sm_,))
                K.op("dve", lambda e: e.tensor_tensor(out=sm_.t[:, 24:36].rearrange("p (g h) -> p g h", g=3), in0=sm_.t[:, 4:16].rearrange("p (g h) -> p g h", g=3),
                                                      in1=sm_.t[:, 20:24].unsqueeze(1).to_broadcast([128, 3, 4]), op=ALU.mult), (sm_,), (sm_,))
                tmq = tmpq[j % 2]

                def wnb(g):
                    return sm_.t[:, 24 + 4 * g:28 + 4 * g].unsqueeze(2).to_broadcast([128, 4, 128])

                def o3v(g):
                    return o3.t[:, g, 0:512].rearrange("p (h c) -> p h c", h=4)
                K.op("dve", lambda e: e.tensor_tensor(out=bof_.t[:, :].rearrange("p (h c) -> p h c", h=4), in0=o3v(0), in1=wnb(0), op=ALU.mult), (o3, sm_), (bof_,))
                K.op("pool", lambda e: e.tensor_tensor(out=tmq.t[:, :].rearrange("p (h c) -> p h c", h=4), in0=o3v(1), in1=wnb(1), op=ALU.mult), (o3, sm_), (tmq,))
                K.op("dve", lambda e: e.tensor_tensor(out=bof_.t[:, :], in0=bof_.t[:, :], in1=tmq.t[:, :], op=ALU.add), (bof_, tmq), (bof_,))
                K.op("pool", lambda e: e.tensor_tensor(out=tmq.t[:, :].rearrange("p (h c) -> p h c", h=4), in0=o3v(2), in1=wnb(2), op=ALU.mult), (o3, sm_), (tmq,))
                K.op("dve", lambda e: e.tensor_tensor(out=bo_.t[:, :], in0=bof_.t[:, :], in1=tmq.t[:, :], op=ALU.add), (bof_, tmq), (bo_,))

            def M2(m, j):
                bo_ = bo[j]
                pT = K.nextps(PSB)
                pTb = pT.t[:].bitcast(BF16)
                K.pe([lambda t, c=c: t.transpose(out=pTb[:, c * 128:(c + 1) * 128], in_=bo_.t[:, c * 128:(c + 1) * 128], identity=ident_b.t[:])
                      for c in range(4)], reads=(bo_, ident_b), writes=(pT,))
                K.op("act", lambda e: e.activation(out=boT.t[:, :, j * 128:(j + 1) * 128], in_=pTb[:, 0:512].rearrange("p (c t) -> p c t", c=4), func=AF.Copy), (pT,), (boT,))

            def Pst(m, j, xt):
                tile = m * 4 + j
                q = tile % 2
                xp, x1_, st_, mv_, rs_, nm_ = x1p[q], x1[q], stt[q], mv[q], rstd[q], nmr[q]
                for nh in range(2):
                    pb = K.nextps(PSB)
                    K.mm(pb.t[:, :], [(mgT.t[:, kc, j * 128:(j + 1) * 128], Wo.t[:, kc, nh * 512:(nh + 1) * 512]) for kc in range(8)], reads=(mgT, Wo), writes=(pb,))
                    K.op("dve", lambda e, pb=pb, nh=nh: e.scalar_tensor_tensor(
                        out=xp.t[:, nh * 512:(nh + 1) * 512], in0=xt.t[:, j, nh * 512:(nh + 1) * 512], scalar=ALPHA, in1=pb.t[:, :], op0=ALU.mult, op1=ALU.add),
                        (pb, xt), (xp,))
                ln_stats(st, lambda c: xp.t[:, c * 512:(c + 1) * 512], xp, mv_, st_, rs_)
                K.op("dve", lambda e: e.scalar_tensor_tensor(out=nm_.t[:], in0=mv_.t[:, 0:1], scalar=-1.0, in1=rs_.t[:, 0:1], op0=ALU.mult, op1=ALU.mult), (mv_, rs_), (nm_,))
                K.op("act", lambda e: e.activation(out=xp.t[:], in_=xp.t[:], func=AF.Identity, scale=rs_.t[:, 0:1], bias=nm_.t[:, 0:1]), (xp, rs_, nm_), (xp,))
                K.op("dve", lambda e: e.tensor_tensor(out=xp.t[:], in0=xp.t[:], in1=g1b.t[:], op=ALU.mult), (xp, g1b), (xp,))
                K.op("pool", lambda e: e.tensor_tensor(out=x1_.t[:], in0=xp.t[:], in1=b1b.t[:], op=ALU.add), (xp, b1b), (x1_,))
                K.store("sp", x1_, X1[tile * 128:(tile + 1) * 128, :], x1_.t[:])
                xb_ = x1pb[q]
                K.op("act", lambda e: e.activation(out=xb_.t[:, :].rearrange("t (k p) -> t k p", k=8), in_=x1_.t[:, :].rearrange("t (p k) -> t k p", k=8), func=AF.Copy), (x1_,), (xb_,))
                K.store("sp", xb_, X1P[tile * 128:(tile + 1) * 128, :], xb_.t[:])

            def Rst(m, j):
                tile = m * 4 + j
                q = tile % 2
                x1_, x1T_ = x1[q], x1T[q]
                lg_, t8_, r_, mk_, pg_, sv_, oh_, ox_, sf_, si_ = lg[q], t8[q], rs[q], mk[q], posg[q], slotv[q], oh[q], ohx[q], slf[q], sli[q]
                for half in range(2):
                    pb = K.nextps(PSB)
                    K.pe([lambda t, c=c, half=half, pb=pb: t.transpose(out=pb.t[:, c * 128:(c + 1) * 128], in_=x1_.t[:, (half * 4 + c) * 128:(half * 4 + c + 1) * 128], identity=ident_f.t[:])
                          for c in range(4)], reads=(x1_, ident_f), writes=(pb,))
                    evac(half, x1T_.t[:, half * 4:(half + 1) * 4, :], pb.t[:, :].rearrange("p (c t) -> p c t", c=4), (pb,), (x1T_,))
                pb = K.nextps(PSB)
                K.mm(pb.t[:, 0:32], [(x1T_.t[:, kc, :], Wr.t[:, kc, :]) for kc in range(8)], reads=(x1T_, Wr), writes=(pb,))
                K.op("dve", lambda e, pb=pb: e.tensor_tensor(out=lg_.t[:], in0=pb.t[:, 0:32], in1=brb.t[:], op=ALU.add), (pb, brb), (lg_,))
                K.op("dve", lambda e: e.max(out=t8_.t[:], in_=lg_.t[:]), (lg_,), (t8_,))
                K.op("dve", lambda e: e.tensor_scalar(out=r_.t[:, 0:1], in0=t8_.t[:, 0:1], scalar1=-1.0, scalar2=None, op0=ALU.mult), (t8_,), (r_,))
                K.op("act", lambda e: e.activation(out=r_.t[:, 1:5], in_=t8_.t[:, 0:4], func=AF.Exp, bias=r_.t[:, 0:1], accum_out=r_.t[:, 5:6]), (t8_, r_), (r_,))
                K.op("dve", lambda e: e.reciprocal(out=r_.t[:, 6:7], in_=r_.t[:, 5:6]), (r_,), (r_,))
                K.op("dve", lambda e: e.tensor_scalar(out=GATES.t[:, tile, :], in0=r_.t[:, 1:5], scalar1=r_.t[:, 6:7], scalar2=None, op0=ALU.mult), (r_,), (GATES,))
                K.op("dve", lambda e: e.tensor_scalar(out=mk_.t[:], in0=lg_.t[:], scalar1=t8_.t[:, 3:4], scalar2=None, op0=ALU.is_ge), (lg_, t8_), (mk_,))
                pb = K.nextps(PSB)
                K.mm(pb.t[:, 0:32], [(tri_b.t[:, :], mk_.t[:, :])], reads=(tri_b, mk_), writes=(pb,))
                K.op("dve", lambda e, pb=pb: e.tensor_tensor(out=pg_.t[:], in0=pb.t[:, 0:32], in1=RUN.t[:], op=ALU.add), (pb, RUN), (pg_,))
                pb2 = K.nextps(PSB)
                K.mm(pb2.t[:, 0:32], [(ones_b.t[:, :], mk_.t[:, :])], reads=(ones_b, mk_), writes=(pb2,))
                K.op("dve", lambda e, pb2=pb2: e.tensor_tensor(out=RUN.t[:], in0=pb2.t[:, 0:32], in1=RUN.t[:], op=ALU.add), (pb2, RUN), (RUN,))
                K.op("dve", lambda e: e.tensor_tensor(out=sv_.t[:], in0=pg_.t[:], in1=ecap.t[:], op=ALU.add), (pg_, ecap), (sv_,))
                o4, x4 = oh4[q], ox4[q]
                K.op("dve", lambda e: e.tensor_tensor(out=o4.t[:], in0=lg_.t[:, :].unsqueeze(1).to_broadcast([128, 4, 32]),
                                                      in1=t8_.t[:, 0:4].unsqueeze(2).to_broadcast([128, 4, 32]), op=ALU.is_equal), (lg_, t8_), (o4,))
                for V, dst in ((sv_, lambda: sf_.t[:, 0:4]), (pg_, lambda: POSK.t[:, tile, :]), (iota32, lambda: EKF.t[:, tile, :])):
                    dbuf = sf_ if V is sv_ else (POSK if V is pg_ else EKF)
                    K.op("dve", lambda e, V=V: e.tensor_tensor(out=x4.t[:], in0=o4.t[:], in1=V.t[:, :].unsqueeze(1).to_broadcast([128, 4, 32]), op=ALU.mult), (o4, V), (x4,))
                    K.op("dve", lambda e, dst=dst: e.tensor_reduce(out=dst(), in_=x4.t[:], axis=AX.X, op=ALU.add), (x4,), (dbuf,))
                K.op("dve", lambda e: e.tensor_copy(out=si_.t[:], in_=sf_.t[:]), (sf_,), (si_,))
                for k in range(4):
                    K.dma("pool", si_, lambda e, k=k: e.indirect_dma_start(
                        out=ROUTE[:, :], out_offset=bass.IndirectOffsetOnAxis(ap=si_.t[:, k:k + 1], axis=0),
                        in_=tokid4.t[:, tile, :], in_offset=None), reads=(si_, tokid4), writes=())

            for m in range(NM):
                xt = xts[m % 2]
                maT = maTs[m % 2]
                if m + 1 < NM:
                    K.load("sp", xts[(m + 1) % 2], xts[(m + 1) % 2].t[:], XAt[m + 1])
                    K.load("sp", maTs[(m + 1) % 2], maTs[(m + 1) % 2].t[:], MA[:, :, (m + 1) * 512:(m + 2) * 512].rearrange("n p t -> p n t"))
                for j in range(4):
                    M1(m, j)
                transposes_x(xt, xT, 512, act_only=True)
                for c8 in range(8):
                    pb = K.nextps(PSB)
                    K.mm(pb.t[:, :], [(Wgb.t[:, kc, c8 * 128:(c8 + 1) * 128], xT.t[:, kc, :]) for kc in range(8)], reads=(Wgb, xT), writes=(pb,))
                    K.op("act", lambda e, pb=pb, c8=c8: e.activation(out=gbT.t[:, c8, :], in_=pb.t[:, :], func=AF.Sigmoid, bias=bcol.t[:, 60 + c8:61 + c8]),
                         (pb, bcol), (gbT,))
                for j in range(4):
                    M2(m, j)
                for c8 in range(8):
                    pb = K.nextps(PSB)
                    tm = tmpm[c8 % 2]
                    K.mm(pb.t[:, :], [(Wpb.t[:, kc, c8 * 128:(c8 + 1) * 128], boT.t[:, kc, :]) for kc in range(4)], reads=(Wpb, boT), writes=(pb,))
                    K.op("dve", lambda e, pb=pb, c8=c8, tm=tm: e.tensor_tensor(out=tm.t[:], in0=pb.t[:, :], in1=gbT.t[:, c8, :], op=ALU.mult), (pb, gbT), (tm,))
                    K.op("pool", lambda e, c8=c8, tm=tm: e.tensor_tensor(out=mgT.t[:, c8, :], in0=tm.t[:], in1=maT.t[:, c8, :], op=ALU.add), (tm, maT), (mgT,))
                Pst(m, 0, xt); Pst(m, 1, xt); Rst(m, 0); Pst(m, 2, xt); Rst(m, 1); Pst(m, 3, xt); Rst(m, 2); Rst(m, 3)
            K.barrier()

        if phases is None or "bl" in phases:
          with ExitStack() as st:
            cntc = K.sb(st, "cntc", [32, 1], F32)
            tmp32 = K.sb(st, "tmp32", [32, 32], F32)
            jthr = K.sb(st, "jthr", [32, 64], F32)
            tmpj = K.sb(st, "tmpj", [32, 64], F32)
            nbc = K.sb(st, "nbc", [32, 1], F32)
            endc = K.sb(st, "endc", [32, 1], F32)
            triu = K.sb(st, "triu", [32, 32], F32)
            iob = K.sb(st, "iob", [128, NBLK], F32)
            Gm = K.sb(st, "Gm", [32, NBLK], F32)
            nbm = K.sb(st, "nbm", [32, 128], F32)
            ebr = K.sb(st, "ebr", [128, NBLK], F32)
            eb = K.sb(st, "eb", [128, NBLK], F32)
            stb = K.sb(st, "stb", [128, NBLK], F32)
            val = K.sb(st, "val", [128, NBLK], F32)
            sbase = K.sb(st, "sbase", [128, NBLK], F32)
            need = K.sb(st, "need", [128, NBLK], F32)
            tA = K.sb(st, "tA", [128, NBLK], F32)
            tB = K.sb(st, "tB", [128, NBLK], F32)
            endr = K.sb(st, "endr", [128, 32], F32)
            nbr = K.sb(st, "nbr", [128, 32], F32)
            K.load("sp", jthr, jthr.t[:], cin["jthr"])
            K.load("sp", triu, triu.t[:], cin["triu32"])
            K.load("sp", iob, iob.t[:], cin["iota_b"])
            K.op("dve", lambda e: e.tensor_tensor(out=tmp32.t[:], in0=RUN.t[0:32, :], in1=ident_f.t[0:32, 0:32], op=ALU.mult), (RUN, ident_f), (tmp32,))
            K.op("dve", lambda e: e.tensor_reduce(out=cntc.t[:], in_=tmp32.t[:], axis=AX.X, op=ALU.add), (tmp32,), (cntc,))
            K.op("dve", lambda e: e.tensor_scalar(out=tmpj.t[:], in0=jthr.t[:], scalar1=cntc.t[:, 0:1], scalar2=None, op0=ALU.is_lt), (jthr, cntc), (tmpj,))
            K.op("dve", lambda e: e.tensor_reduce(out=nbc.t[:], in_=tmpj.t[:], axis=AX.X, op=ALU.add), (tmpj,), (nbc,))
            pb = K.nextps(PSB)
            K.mm(pb.t[0:32, 0:1], [(triu.t[:, :], nbc.t[:, 0:1])], reads=(triu, nbc), writes=(pb,))
            K.op("dve", lambda e, pb=pb: e.tensor_copy(out=endc.t[:], in_=pb.t[0:32, 0:1]), (pb,), (endc,))
            K.op("dve", lambda e: e.tensor_scalar(out=Gm.t[:], in0=iob.t[0:32, :], scalar1=endc.t[:, 0:1], scalar2=None, op0=ALU.is_ge), (iob, endc), (Gm,))
            K.op("dve", lambda e: e.tensor_scalar(out=nbm.t[:], in0=ones_f.t[0:32, :], scalar1=nbc.t[:, 0:1], scalar2=None, op0=ALU.mult), (ones_f, nbc), (nbm,))
            pb = K.nextps(PSB)
            K.mm(pb.t[:, 0:NBLK], [(ones_f.t[0:32, :], Gm.t[:, :])], reads=(ones_f, Gm), writes=(pb,))
            K.op("dve", lambda e, pb=pb: e.tensor_copy(out=ebr.t[:], in_=pb.t[:, 0:NBLK]), (pb,), (ebr,))
            pb = K.nextps(PSB)
            K.mm(pb.t[:, 0:NBLK], [(nbm.t[:, :], Gm.t[:, :])], reads=(nbm, Gm), writes=(pb,))
            K.op("dve", lambda e, pb=pb: e.tensor_copy(out=stb.t[:], in_=pb.t[:, 0:NBLK]), (pb,), (stb,))
            pb = K.nextps(PSB)
            K.mm(pb.t[:, 0:32], [(nbm.t[:, :], triu.t[:, :])], reads=(nbm, triu), writes=(pb,))
            K.op("dve", lambda e, pb=pb: e.tensor_copy(out=endr.t[:], in_=pb.t[:, 0:32]), (pb,), (endr,))
            pb = K.nextps(PSB)
            K.mm(pb.t[:, 0:32], [(nbm.t[:, :], ident_f.t[0:32, 0:32])], reads=(nbm, ident_f), writes=(pb,))
            K.op("dve", lambda e, pb=pb: e.tensor_copy(out=nbr.t[:], in_=pb.t[:, 0:32]), (pb,), (nbr,))
            K.op("dve", lambda e: e.tensor_tensor(out=BS128.t[:], in0=endr.t[:], in1=nbr.t[:], op=ALU.subtract), (endr, nbr), (BS128,))
            K.op("dve", lambda e: e.tensor_scalar(out=BS128.t[:], in0=BS128.t[:], scalar1=128.0, scalar2=None, op0=ALU.mult), (BS128,), (BS128,))
            K.op("dve", lambda e: e.tensor_scalar(out=val.t[:], in0=ebr.t[:], scalar1=31.5, scalar2=None, op0=ALU.is_lt), (ebr,), (val,))
            K.op("dve", lambda e: e.tensor_scalar(out=eb.t[:], in0=ebr.t[:], scalar1=31.0, scalar2=None, op0=ALU.min), (ebr,), (eb,))
            K.op("dve", lambda e: e.tensor_tensor(out=tA.t[:], in0=iob.t[:], in1=stb.t[:], op=ALU.subtract), (iob, stb), (tA,))
            K.op("dve", lambda e: e.tensor_scalar(out=tA.t[:], in0=tA.t[:], scalar1=128.0, scalar2=float(-NE * CAP), op0=ALU.mult, op1=ALU.add), (tA,), (tA,))
            K.op("dve", lambda e: e.scalar_tensor_tensor(out=tA.t[:], in0=eb.t[:], scalar=float(CAP), in1=tA.t[:], op0=ALU.mult, op1=ALU.add), (eb, tA), (tA,))
            K.op("dve", lambda e: e.tensor_tensor(out=tA.t[:], in0=tA.t[:], in1=val.t[:], op=ALU.mult), (tA, val), (tA,))
            K.op("dve", lambda e: e.tensor_scalar(out=sbase.t[:], in0=tA.t[:], scalar1=pcol.t[:, 0:1], scalar2=float(NE * CAP), op0=ALU.add, op1=ALU.add), (tA, pcol), (sbase,))
            K.op("dve", lambda e: e.tensor_copy(out=RIDX.t[:], in_=sbase.t[:]), (sbase,), (RIDX,))
            K.op("dve", lambda e: e.memset(need.t[:], 1.0), (), (need,))
            K.op("dve", lambda e: e.tensor_tensor(out=need.t[:, 1:NBLK], in0=eb.t[:, 1:NBLK], in1=eb.t[:, 0:NBLK - 1], op=ALU.not_equal), (eb,), (need,))
            K.op("dve", lambda e: e.memset(need.t[:, NBLK // 2:NBLK // 2 + 1], 1.0), (), (need,))
            nn = K.sb(st, "nn", [128, NBLK], F32)
            K.op("dve", lambda e: e.tensor_scalar(out=nn.t[:], in0=need.t[:], scalar1=float(-BIG), scalar2=float(BIG), op0=ALU.mult, op1=ALU.add), (need,), (nn,))
            K.op("dve", lambda e: e.tensor_scalar(out=tB.t[:], in0=eb.t[:], scalar1=128.0, scalar2=pcol.t[:, 0:1], op0=ALU.mult, op1=ALU.add), (eb, pcol), (tB,))
            K.op("dve", lambda e: e.tensor_scalar(out=tB.t[:], in0=tB.t[:], scalar1=float(l * NE * 128), scalar2=None, op0=ALU.add), (tB,), (tB,))
            K.op("dve", lambda e: e.tensor_tensor(out=tB.t[:], in0=tB.t[:], in1=need.t[:], op=ALU.mult), (tB, need), (tB,))
            K.op("dve", lambda e: e.tensor_tensor(out=tB.t[:], in0=tB.t[:], in1=nn.t[:], op=ALU.add), (tB, nn), (tB,))
            K.op("dve", lambda e: e.tensor_copy(out=WIDX.t[:], in_=tB.t[:]), (tB,), (WIDX,))
            K.op("dve", lambda e: e.scalar_tensor_tensor(out=tB.t[:], in0=eb.t[:], scalar=float(l * NE), in1=need.t[:], op0=ALU.add, op1=ALU.mult), (eb, need), (tB,))
            K.op("dve", lambda e: e.tensor_tensor(out=tB.t[:], in0=tB.t[:], in1=nn.t[:], op=ALU.add), (tB, nn), (tB,))
            K.op("dve", lambda e: e.tensor_copy(out=BIDX.t[:], in_=tB.t[:]), (tB,), (BIDX,))
            K.barrier()

        if phases is None or "moe" in phases:
          with ExitStack() as st:
            Wg = [K.sb(st, f"Wg{i}", [128, 8 * 2048], BF16) for i in range(2)]
            Wd = [K.sb(st, f"Wd{i}", [128, 8 * 1024], BF16) for i in range(2)]
            Bg = [K.sb(st, f"Bg{i}", [128, 2048], BF16) for i in range(2)]
            Bd = [K.sb(st, f"Bd{i}", [128, 1024], BF16) for i in range(2)]
            rt = [K.sb(st, f"rt{i}", [128, 4], I32) for i in range(4)]
            xg = [K.sb(st, f"xg{i}", [128, 1024], BF16) for i in range(3)]
            xgT = [K.sb(st, f"xgT{i}", [128, 8, 128], BF16) for i in range(2)]
            gs = [K.sb(st, f"gs{i}", [128, 1024], F32) for i in range(2)]
            sg = [K.sb(st, f"sg{i}", [128, 1024], F32) for i in range(2)]
            u1 = [K.sb(st, f"u1{i}", [128, 1024], F32) for i in range(2)]
            hb = [K.sb(st, f"hb{i}", [128, 1024], BF16) for i in range(2)]
            hT = [K.sb(st, f"hT{i}", [128, 8, 128], BF16) for i in range(2)]
            ysb = [K.sb(st, f"ysb{i}", [128, 1024], F32) for i in range(2)]
            if "bcW" not in K.__dict__:
                rW = nc.gpsimd.alloc_register("bcW")
                nc.gpsimd.reg_mov(rW, NL * NE * 128 - 1)
                K.bcW = nc.gpsimd.snap(rW, donate=True)
                rB = nc.gpsimd.alloc_register("bcB")
                nc.gpsimd.reg_mov(rB, NL * NE - 1)
                K.bcB = nc.gpsimd.snap(rB, donate=True)
            wguv = w_gu.rearrange("l e (p k) f -> (l e p) (k f)", k=8)
            wdnv = w_dn.rearrange("l e (p k) f -> (l e p) (k f)", k=8)
            bguv = b_gu.rearrange("l e f -> (l e) f")
            bdnv = b_dn.rearrange("l e f -> (l e) f")
            HB = NBLK // 2

            def blk(s_):
                return s_ // 2 if s_ % 2 == 0 else HB + s_ // 2

            def wloadG(s_):
                if s_ >= NBLK:
                    return
                par, b = s_ % 2, blk(s_)
                K.dma("pool", Wg[par], lambda e: e.indirect_dma_start(out=Wg[par].t[:, :], out_offset=None, in_=wguv,
                      in_offset=bass.IndirectOffsetOnAxis(ap=WIDX.t[:, b:b + 1], axis=0), bounds_check=K.bcW, oob_is_err=False),
                      reads=(WIDX,), writes=(Wg[par],))
                K.dma("pool", Bg[par], lambda e: e.indirect_dma_start(out=Bg[par].t[:, :], out_offset=None, in_=bguv,
                      in_offset=bass.IndirectOffsetOnAxis(ap=BIDX.t[:, b:b + 1], axis=0), bounds_check=K.bcB, oob_is_err=False),
                      reads=(BIDX,), writes=(Bg[par],))

            def wloadD(s_):
                if s_ >= NBLK:
                    return
                par, b = s_ % 2, blk(s_)
                K.dma("pool", Wd[par], lambda e: e.indirect_dma_start(out=Wd[par].t[:, :], out_offset=None, in_=wdnv,
                      in_offset=bass.IndirectOffsetOnAxis(ap=WIDX.t[:, b:b + 1], axis=0), bounds_check=K.bcW, oob_is_err=False),
                      reads=(WIDX,), writes=(Wd[par],))
                K.dma("pool", Bd[par], lambda e: e.indirect_dma_start(out=Bd[par].t[:, :], out_offset=None, in_=bdnv,
                      in_offset=bass.IndirectOffsetOnAxis(ap=BIDX.t[:, b:b + 1], axis=0), bounds_check=K.bcB, oob_is_err=False),
                      reads=(BIDX,), writes=(Bd[par],))

            def rgath(s_):
                if s_ >= NBLK:
                    return
                r_, b = rt[s_ % 4], blk(s_)
                K.dma("pool", r_, lambda e: e.indirect_dma_start(out=r_.t[:, :], out_offset=None, in_=ROUTE[:, :],
                      in_offset=bass.IndirectOffsetOnAxis(ap=RIDX.t[:, b:b + 1], axis=0)), reads=(RIDX,), writes=(r_,))

            def xgath(s_):
                if s_ >= NBLK:
                    return
                r_, x_ = rt[s_ % 4], xg[s_ % 3]
                K.dma("pool", x_, lambda e: e.indirect_dma_start(out=x_.t[:, :], out_offset=None, in_=X1P[:, :],
                      in_offset=bass.IndirectOffsetOnAxis(ap=r_.t[:, 0:1], axis=0)), reads=(r_,), writes=(x_,))

            pT_, pG, pH, pY = PSB[0], PSB[1:5], PSB[5], PSB[6:8]

            def S1a(s_):
                x_, xT_ = xg[s_ % 3], xgT[s_ % 2]
                xv = x_.t[:, :].rearrange("t (k p) -> t k p", k=8)
                pTb = pT_.t[:].bitcast(BF16)
                K.pe([lambda t, c=c: t.transpose(out=pTb[:, c * 128:(c + 1) * 128], in_=xv[:, c, :], identity=ident_b.t[:])
                      for c in range(8)], reads=(x_, ident_b), writes=(pT_,))
                K.op("act", lambda e: e.activation(out=xT_.t[:], in_=pTb[:, 0:1024].rearrange("p (c t) -> p c t", c=8), func=AF.Copy), (pT_,), (xT_,))

            def S1b(s_):
                par = s_ % 2
                xT_ = xgT[par]
                for n in range(4):
                    pb = pG[n]
                    pairs = [(xT_.t[:, kc, :], Wg[par].t[:, kc * 2048 + n * 512:kc * 2048 + (n + 1) * 512]) for kc in range(8)]
                    pairs.append((ones_b.t[0:1, :], Bg[par].t[0:1, n * 512:(n + 1) * 512]))
                    K.mm(pb.t[:, :], pairs, reads=(xT_, Wg[par], ones_b, Bg[par]), writes=(pb,))

            def S2a(s_):
                par = s_ % 2
                g_, u_ = gs[par], u1[par]
                for n in range(2):
                    K.op("dve", lambda e, n=n: e.tensor_scalar(out=g_.t[:, n * 512:(n + 1) * 512], in0=pG[n].t[:, :], scalar1=LIMIT, scalar2=None, op0=ALU.min), (pG[n],), (g_,))
                    K.op("dve", lambda e, n=n: e.tensor_scalar(out=u_.t[:, n * 512:(n + 1) * 512], in0=pG[2 + n].t[:, :], scalar1=LIMIT, scalar2=-LIMIT, op0=ALU.min, op1=ALU.max), (pG[2 + n],), (u_,))

            def S2b(s_):
                par, b = s_ % 2, blk(s_)
                g_, s2_, u_, h_, hT_, y_ = gs[par], sg[par], u1[par], hb[par], hT[par], ysb[par]
                K.op("act", lambda e: e.activation(out=s2_.t[:], in_=g_.t[:], func=AF.Sigmoid, scale=SW_ALPHA), (g_,), (s2_,))
                K.op("dve", lambda e: e.scalar_tensor_tensor(out=u_.t[:], in0=u_.t[:], scalar=1.0, in1=g_.t[:], op0=ALU.add, op1=ALU.mult), (u_, g_), (u_,))
                K.op("dve", lambda e: e.tensor_tensor(out=h_.t[:, :].rearrange("t (k p) -> t k p", k=8), in0=u_.t[:, :].rearrange("t (p k) -> t k p", k=8),
                                                      in1=s2_.t[:, :].rearrange("t (p k) -> t k p", k=8), op=ALU.mult), (u_, s2_), (h_,))
                hv = h_.t[:, :].rearrange("t (k p) -> t k p", k=8)
                pTb = pH.t[:].bitcast(BF16)
                K.pe([lambda t, c=c: t.transpose(out=pTb[:, c * 128:(c + 1) * 128], in_=hv[:, c, :], identity=ident_b.t[:])
                      for c in range(8)], reads=(h_, ident_b), writes=(pH,))
                K.op("act", lambda e: e.activation(out=hT_.t[:], in_=pTb[:, 0:1024].rearrange("p (c t) -> p c t", c=8), func=AF.Copy), (pH,), (hT_,))
                for n in range(2):
                    pb = pY[n]
                    pairs = [(hT_.t[:, kc, :], Wd[par].t[:, kc * 1024 + n * 512:kc * 1024 + (n + 1) * 512]) for kc in range(8)]
                    pairs.append((ones_b.t[0:1, :], Bd[par].t[0:1, n * 512:(n + 1) * 512]))
                    K.mm(pb.t[:, :], pairs, reads=(hT_, Wd[par], ones_b, Bd[par]), writes=(pb,))
                    evac(n, y_.t[:, n * 512:(n + 1) * 512], pb.t[:, :], (pb,), (y_,))
                K.store("sp", y_, YB[b * 128:(b + 1) * 128, :], y_.t[:])

            wloadG(0); wloadG(1); wloadD(0); wloadD(1)
            rgath(0); rgath(1); rgath(2)
            xgath(0); xgath(1)
            S1a(0); S1b(0)
            wloadG(2)
            for s_ in range(NBLK):
                rgath(s_ + 3)
                xgath(s_ + 2)
                if s_ + 1 < NBLK:
                    S1a(s_ + 1)
                S2a(s_)
                if s_ + 1 < NBLK:
                    S1b(s_ + 1)
                    wloadG(s_ + 3)
                S2b(s_)
                wloadD(s_ + 2)
            K.barrier()

        if phases is None or "comb" in phases:
          with ExitStack() as st:
            g2b = K.sb(st, "g2b", [128, 1024], F32)
            b2b = K.sb(st, "b2b", [128, 1024], F32)
            yk = [[K.sb(st, f"yk{i}_{k}", [128, 1024], F32) for k in range(4)] for i in range(2)]
            x1t = [K.sb(st, f"x1t{i}", [128, 1024], F32) for i in range(2)]
            acc = [K.sb(st, f"acc{i}", [128, 1024], F32) for i in range(2)]
            xo = [K.sb(st, f"xo{i}", [128, 1024], F32) for i in range(2)]
            oh = [K.sb(st, f"oh{i}", [128, 32], F32) for i in range(2)]
            yrf = [K.sb(st, f"yrf{i}", [128, 4], F32) for i in range(2)]
            yri = [K.sb(st, f"yri{i}", [128, 4], I32) for i in range(2)]
            nmr = [K.sb(st, f"nmrc{i}", [128, 1], F32) for i in range(2)]
            stt = [K.sb(st, f"stt{i}", [128, 12], F32) for i in range(2)]
            mv = [K.sb(st, f"mv{i}", [128, 2], F32) for i in range(2)]
            rstd = [K.sb(st, f"rstd{i}", [128, 1], F32) for i in range(2)]
            K.load("sp", g2b, g2b.t[:], ln2_row[l][0:1, :].to_broadcast([128, 1024]))
            K.load("sp", b2b, b2b.t[:], ln2_row[l][1:2, :].to_broadcast([128, 1024]))

            def cload(tile):
                q = tile % 2
                for k in range(4):
                    K.op("dve", lambda e, k=k, q=q, tile=tile: e.tensor_scalar(out=oh[q].t[:], in0=iota32.t[:], scalar1=EKF.t[:, tile, k:k + 1], scalar2=None, op0=ALU.is_equal), (iota32, EKF), (oh[q],))
                    K.op("dve", lambda e, q=q: e.tensor_tensor(out=oh[q].t[:], in0=oh[q].t[:], in1=BS128.t[:], op=ALU.mult), (oh[q], BS128), (oh[q],))
                    K.op("dve", lambda e, k=k, q=q: e.tensor_reduce(out=yrf[q].t[:, k:k + 1], in_=oh[q].t[:], axis=AX.X, op=ALU.add), (oh[q],), (yrf[q],))
                K.op("dve", lambda e, q=q, tile=tile: e.tensor_tensor(out=yrf[q].t[:], in0=yrf[q].t[:], in1=POSK.t[:, tile, :], op=ALU.add), (yrf[q], POSK), (yrf[q],))
                K.op("dve", lambda e, q=q: e.tensor_copy(out=yri[q].t[:], in_=yrf[q].t[:]), (yrf[q],), (yri[q],))
                for k in range(4):
                    K.dma("pool", yk[q][k], lambda e, k=k, q=q: e.indirect_dma_start(out=yk[q][k].t[:, :], out_offset=None, in_=YB[:, :],
                          in_offset=bass.IndirectOffsetOnAxis(ap=yri[q].t[:, k:k + 1], axis=0)), reads=(yri[q],), writes=(yk[q][k],))
                K.load("sp", x1t[q], x1t[q].t[:], X1[tile * 128:(tile + 1) * 128, :])
            cload(0)
            for tile in range(NTILE):
                q = tile % 2
                if tile + 1 < NTILE:
                    cload(tile + 1)
                a_, o_ = acc[q], xo[q]
                K.op("dve", lambda e: e.tensor_scalar(out=a_.t[:], in0=yk[q][0].t[:], scalar1=GATES.t[:, tile, 0:1], scalar2=None, op0=ALU.mult), (yk[q][0], GATES), (a_,))
                for k in range(1, 4):
                    K.op("dve", lambda e, k=k: e.scalar_tensor_tensor(out=a_.t[:], in0=yk[q][k].t[:], scalar=GATES.t[:, tile, k:k + 1], in1=a_.t[:], op0=ALU.mult, op1=ALU.add), (yk[q][k], GATES, a_), (a_,))
                K.op("dve", lambda e: e.scalar_tensor_tensor(out=a_.t[:], in0=x1t[q].t[:], scalar=ALPHA, in1=a_.t[:], op0=ALU.mult, op1=ALU.add), (x1t[q], a_), (a_,))
                ln_stats(st, lambda c: a_.t[:, c * 512:(c + 1) * 512], a_, mv[q], stt[q], rstd[q])
                K.op("dve", lambda e: e.scalar_tensor_tensor(out=nmr[q].t[:], in0=mv[q].t[:, 0:1], scalar=-1.0, in1=rstd[q].t[:, 0:1], op0=ALU.mult, op1=ALU.mult), (mv[q], rstd[q]), (nmr[q],))
                K.op("act", lambda e: e.activation(out=a_.t[:], in_=a_.t[:], func=AF.Identity, scale=rstd[q].t[:, 0:1], bias=nmr[q].t[:, 0:1]), (a_, rstd[q], nmr[q]), (a_,))
                K.op("dve", lambda e: e.tensor_tensor(out=a_.t[:], in0=a_.t[:], in1=g2b.t[:], op=ALU.mult), (a_, g2b), (a_,))
                K.op("pool", lambda e: e.tensor_tensor(out=o_.t[:], in0=a_.t[:], in1=b2b.t[:], op=ALU.add), (a_, b2b), (o_,))
                K.store("sp", o_, XO[tile * 128:(tile + 1) * 128, :], o_.t[:])
            K.barrier()

    K.barrier()
    top.close()
    return nc, dict(NT=NT, NBLK=NBLK, CAP=CAP, NL=NL, NSEQ=NSEQ)


def host_inputs(params, NL, NT, NBLK, CAP):
    f = lambda a: np.ascontiguousarray(np.asarray(a, dtype=np.float32))
    d = {}
    d["w_in"] = f(params["w_in"][:NL])
    d["w_st"] = f(np.transpose(np.asarray(params["w_s"][:NL]), (0, 3, 1, 2)))
    d["w_pa"] = f(params["w_pa"][:NL])
    d["w_pb"] = f(params["w_pb"][:NL])
    d["w_o"] = f(params["w_o"][:NL])
    d["w_r"] = f(params["w_r"][:NL])
    d["w_gu"] = f(params["w_gu"][:NL])
    d["w_down"] = f(params["w_down"][:NL])
    d["b_gu"] = f(params["b_gu"][:NL])
    d["b_down"] = f(params["b_down"][:NL])
    b_in = np.asarray(params["b_in"][:NL], dtype=np.float32)
    d["bin_col"] = f(np.transpose(b_in.reshape(NL, 68, 128), (0, 2, 1)))
    d["bin_row"] = f(b_in.reshape(NL, 1, DIN))
    lg = np.asarray(params["ln_v_g"][:NL], dtype=np.float32).reshape(NL, 8, 128)
    lb = np.asarray(params["ln_v_b"][:NL], dtype=np.float32).reshape(NL, 8, 128)
    d["lnv_col"] = f(np.concatenate([np.transpose(lg, (0, 2, 1)), np.transpose(lb, (0, 2, 1))], axis=2))
    d["bs_row"] = f(np.asarray(params["b_s"][:NL]).reshape(NL, 1, 1024))
    d["ln1_row"] = f(np.stack([np.asarray(params["ln1_g"][:NL]), np.asarray(params["ln1_b"][:NL])], axis=1))
    d["ln2_row"] = f(np.stack([np.asarray(params["ln2_g"][:NL]), np.asarray(params["ln2_b"][:NL])], axis=1))
    d["br_row"] = f(np.asarray(params["b_r"][:NL]).reshape(NL, 1, NE))
    for k, v in _consts(NT, NBLK, CAP).items():
        d["c_" + k] = v
    return d


def kernel(x_prompt, x_sample, w_in, b_in, ln_v_g, ln_v_b, w_s, b_s, w_pa, w_pb, w_o, ln1_g, ln1_b,
           w_r, b_r, w_gu, b_gu, w_down, b_down, ln2_g, ln2_b):
    params = dict(w_in=w_in, b_in=b_in, ln_v_g=ln_v_g, ln_v_b=ln_v_b, w_s=w_s, b_s=b_s, w_pa=w_pa, w_pb=w_pb, w_o=w_o,
                  ln1_g=ln1_g, ln1_b=ln1_b, w_r=w_r, b_r=b_r, w_gu=w_gu, b_gu=b_gu, w_down=w_down, b_down=b_down,
                  ln2_g=ln2_g, ln2_b=ln2_b)
    NL, NSEQ = 4, 2
    nc, meta = build_program(NL=NL, NSEQ=NSEQ)
    shared = host_inputs(params, NL, meta["NT"], meta["NBLK"], meta["CAP"])
    xp = np.asarray(x_prompt, dtype=np.float32)
    xs = np.asarray(x_sample, dtype=np.float32)
    seqs = []
    for c in range(8):
        if c < 4:
            seqs.append(np.concatenate([xp[2 * c], xp[2 * c + 1]], axis=0))
        else:
            seqs.append(np.concatenate([xs[c - 4], xs[c - 4]], axis=0))
    in_maps = []
    for c in range(8):
        m = dict(shared)
        m["x"] = np.ascontiguousarray(seqs[c])
        in_maps.append(m)
    res = run_bass_kernel_spmd(nc, in_maps, core_ids=list(range(8)))
    yp = np.zeros_like(xp)
    ys = np.zeros_like(xs)
    for c in range(8):
        y = np.asarray(res.results[c]["y"], dtype=np.float32).reshape(2, S, D)
        if c < 4:
            yp[2 * c] = y[0]
            yp[2 * c + 1] = y[1]
        else:
            ys[c - 4] = y[0]
    return (yp, ys)
```

```python
import numpy as np
import ml_dtypes
from contextlib import ExitStack
import concourse.bass as bass
import concourse.mybir as mybir
from concourse.bass_utils import run_bass_kernel_spmd

F32 = mybir.dt.float32
BF16 = mybir.dt.bfloat16
I32 = mybir.dt.int32
AF = mybir.ActivationFunctionType
ALU = mybir.AluOpType
AX = mybir.AxisListType

S = 4096
D = 1024
DIN = 8704
NE = 32
TOPK = 4
PATTERNS = ((128, 1), (512, 4), (2048, 16))
NHEAD = 12
SLOPES = np.array([2.0 ** (-8.0 * (h + 1) / NHEAD) for h in range(NHEAD)], dtype=np.float32).reshape(3, 4)
DEPTH_FULL = 4
ALPHA = (2.0 * DEPTH_FULL) ** 0.25
EPS = 1e-5
NEG = -1e30
LIMIT = 7.0
SW_ALPHA = 1.702
SCALE = 128.0 ** -0.5
BIG = 1 << 28


class Buf:
    def __init__(self, t, name):
        self.t = t
        self.name = name
        self.lw = None
        self.rd = {}
        self.ds = None

    def __getitem__(self, idx):
        return self.t[idx]


class DSem:
    def __init__(self, h, key):
        self.h = h
        self.key = key
        self.cnt = 0


class KB:
    SAME_ENG = True

    def __init__(self, nc):
        self.nc = nc
        self.E = {"pe": nc.tensor, "act": nc.scalar, "dve": nc.vector, "pool": nc.gpsimd, "sp": nc.sync}
        self.sem = {k: nc.alloc_semaphore("es_" + k) for k in self.E}
        self.cnt = {k: 0 for k in self.E}
        self.seen = {k: {} for k in self.E}
        self.semobj = dict(self.sem)
        self.dsems = {}
        self.dfree = []
        self.uid = 0
        self.psrr = 0

    def sb(self, stack, name, shape, dt):
        self.uid += 1
        t = stack.enter_context(self.nc.sbuf_tensor(f"{name}_{self.uid}", list(shape), dt))
        b = Buf(t, name)
        stack.callback(self._release, b)
        return b

    def ps(self, stack, name, shape, dt):
        self.uid += 1
        t = stack.enter_context(self.nc.psum_tensor(f"{name}_{self.uid}", list(shape), dt))
        return Buf(t, name)

    def _release(self, b):
        if b.ds is not None:
            self.dfree.append(b.ds)
            b.ds = None

    def _getds(self, b):
        if b.ds is None:
            if self.dfree:
                b.ds = self.dfree.pop()
            else:
                key = f"d{len(self.dsems)}"
                d = DSem(self.nc.alloc_semaphore("ds_" + key), key)
                self.dsems[key] = d
                self.semobj[key] = d.h
                b.ds = d
        return b.ds

    def _deps(self, reads, writes):
        d = []
        for b in reads:
            if b.lw is not None:
                d.append(b.lw)
        for b in writes:
            if b.lw is not None:
                d.append(b.lw)
            d.extend(b.rd.items())
        return d

    def _wait(self, eng, deps):
        need = {}
        for k, v in deps:
            if k == eng and (eng == "pe" or not self.SAME_ENG):
                continue
            if need.get(k, 0) < v:
                need[k] = v
        for k, v in need.items():
            if self.seen[eng].get(k, 0) < v:
                self.E[eng].wait_ge(self.semobj[k], v)
                self.seen[eng][k] = v

    def _mark(self, tok, reads, writes):
        k, v = tok
        for b in writes:
            b.lw = tok
            b.rd = {}
        for b in reads:
            if b in writes:
                continue
            if b.rd.get(k, 0) < v:
                b.rd[k] = v

    def op(self, eng, fn, reads=(), writes=()):
        self._wait(eng, self._deps(reads, writes))
        ins = fn(self.E[eng])
        self.cnt[eng] += 1
        ins.then_inc(self.sem[eng], 1)
        tok = (eng, self.cnt[eng])
        self._mark(tok, reads, writes)
        return tok

    def pe(self, fns, reads=(), writes=()):
        self._wait("pe", self._deps(reads, writes))
        ins = None
        for f in fns:
            ins = f(self.nc.tensor)
        self.cnt["pe"] += 1
        ins.then_inc(self.sem["pe"], 1)
        tok = ("pe", self.cnt["pe"])
        self._mark(tok, reads, writes)
        return tok

    def mm(self, out, pairs, reads=(), writes=()):
        n = len(pairs)
        fns = []
        for i, (l, r) in enumerate(pairs):
            fns.append(lambda t, l=l, r=r, i=i: t.matmul(out, lhsT=l, rhs=r, start=(i == 0), stop=(i == n - 1)))
        return self.pe(fns, reads, writes)

    def dma(self, q, sbuf, fn, reads=(), writes=()):
        self._wait(q, self._deps(reads, writes))
        d = self._getds(sbuf)
        ins = fn(self.E[q])
        d.cnt += 16
        ins.then_inc(d.h, 16)
        tok = (d.key, d.cnt)
        self._mark(tok, reads, writes)
        return tok

    def load(self, q, dst, dst_ap, src_ap, **kw):
        return self.dma(q, dst, lambda e: e.dma_start(out=dst_ap, in_=src_ap, **kw), reads=(), writes=(dst,))

    def store(self, q, src, dst_ap, src_ap, **kw):
        return self.dma(q, src, lambda e: e.dma_start(out=dst_ap, in_=src_ap, **kw), reads=(src,), writes=())

    def barrier(self):
        tot = dict(self.cnt)
        for k, d in self.dsems.items():
            tot[k] = d.cnt
        for e in self.E:
            for k, v in tot.items():
                if v > 0 and self.seen[e].get(k, 0) < v:
                    self.E[e].wait_ge(self.semobj[k], v)
                    self.seen[e][k] = v

    def nextps(self, banks):
        b = banks[self.psrr % len(banks)]
        self.psrr += 1
        return b


def _consts(NT, NBLK, CAP):
    c = {}
    c["ident_f"] = np.eye(128, dtype=np.float32)
    c["ident_b"] = np.eye(128, dtype=np.float32).astype(ml_dtypes.bfloat16)
    tri = (np.arange(128)[:, None] < np.arange(128)[None, :]).astype(np.float32)
    c["tri_b"] = tri.astype(ml_dtypes.bfloat16)
    c["ones_b"] = np.ones((128, 128), dtype=ml_dtypes.bfloat16)
    c["ones_f"] = np.ones((128, 128), dtype=np.float32)
    tab = np.zeros((3, 128, 12, 256), dtype=np.float32)
    a = np.arange(128)[:, None]
    cc = np.arange(256)[None, :]
    rel = cc - 64 - a
    for g, (win, d) in enumerate(PATTERNS):
        for h in range(4):
            base = np.where(np.abs(rel) <= 64, -SLOPES[g, h] * d * np.abs(rel), NEG).astype(np.float32)
            for var in range(3):
                t = base.copy()
                if var == 1:
                    t[:, :64] = NEG
                if var == 2:
                    t[:, 192:] = NEG
                tab[g, :, h * 3 + var, :] = t
    c["att_bias"] = tab
    c["iota32"] = np.tile(np.arange(32, dtype=np.float32)[None, :], (128, 1))
    c["ecap"] = np.tile((np.arange(32, dtype=np.float32) * CAP)[None, :], (128, 1))
    c["pcol"] = np.arange(128, dtype=np.float32)[:, None].copy()
    c["jthr"] = np.tile((np.arange(64, dtype=np.float32) * 128.0)[None, :], (32, 1))
    c["iota_b"] = np.tile(np.arange(NBLK, dtype=np.float32)[None, :], (128, 1))
    c["triu32"] = (np.arange(32)[:, None] <= np.arange(32)[None, :]).astype(np.float32)
    tok = np.zeros((128, NT // 128, 4), dtype=np.int32)
    tok[:, :, 0] = np.arange(NT // 128)[None, :] * 128 + np.arange(128)[:, None]
    c["tokid4"] = tok
    ri = np.zeros((128, 128, 4), dtype=np.int32)
    ri[:, :, 0] = NT
    c["rinit"] = ri
    return c


CONST_DT = {"ident_f": F32, "ident_b": BF16, "tri_b": BF16, "ones_b": BF16, "ones_f": F32, "att_bias": F32,
            "iota32": F32, "ecap": F32, "pcol": F32, "jthr": F32, "iota_b": F32, "triu32": F32,
            "tokid4": I32, "rinit": I32}


def build_program(NL=4, NSEQ=2, debug=False, phases=None):
    NT = NSEQ * S
    NTILE = NT // 128
    CAP = NT
    NBLK = NT * TOPK // 128 + NE
    RROWS = NE * CAP + 128
    nc = bass.Bass("TRN2", target_bir_lowering=False)
    K = KB(nc)
    sk = "ExternalOutput" if debug else "Internal"

    def din(name, shape, dt=F32):
        return nc.dram_tensor(name, list(shape), dt, kind="ExternalInput").ap()

    def dscr(name, shape, dt=F32):
        return nc.dram_tensor(name, list(shape), dt, kind=sk).ap()

    x_in = din("x", [NT, D])
    w_in = din("w_in", [NL, D, DIN])
    w_st = din("w_st", [NL, 128, 8, 128])
    w_pa = din("w_pa", [NL, D, D])
    w_pb = din("w_pb", [NL, 512, D])
    w_o = din("w_o", [NL, D, D])
    w_r = din("w_r", [NL, D, NE])
    w_gu = din("w_gu", [NL, NE, D, 2 * D])
    w_dn = din("w_down", [NL, NE, D, D])
    b_gu = din("b_gu", [NL, NE, 2 * D])
    b_dn = din("b_down", [NL, NE, D])
    bin_col = din("bin_col", [NL, 128, 68])
    bin_row = din("bin_row", [NL, 1, DIN])
    lnv_col = din("lnv_col", [NL, 128, 16])
    bs_row = din("bs_row", [NL, 1, D])
    ln1_row = din("ln1_row", [NL, 2, D])
    ln2_row = din("ln2_row", [NL, 2, D])
    br_row = din("br_row", [NL, 1, NE])
    cshape = {k: v.shape for k, v in _consts(128 * 2, 8, 1).items()}
    cshape["iota_b"] = (128, NBLK)
    cshape["tokid4"] = (128, NTILE, 4)
    cin = {k: din("c_" + k, cshape[k], CONST_DT[k]) for k in cshape}
    y_out = nc.dram_tensor("y", [NT, D], F32, kind="ExternalOutput").ap()

    XS = dscr("XS", [NT, D])
    X1 = dscr("X1", [NT + 128, D])
    NTP = [NSEQ * d * (S // d + 128) for (_, d) in PATTERNS]
    QT = [dscr(f"QT{g}", [4, 128, NTP[g]], BF16) for g in range(3)]
    KT = [dscr(f"KT{g}", [4, 128, NTP[g]], BF16) for g in range(3)]
    VV = [dscr(f"VV{g}", [NTP[g], 512], BF16) for g in range(3)]
    OGall = dscr("OG", [3, NT, 516])
    OG = [OGall[g] for g in range(3)]
    MA = dscr("MA", [8, 128, NT], BF16)
    ROUTE = dscr("ROUTE", [RROWS, 4], I32)
    YB = dscr("YB", [NBLK * 128, D])

    top = ExitStack()
    ident_f = K.sb(top, "ident_f", [128, 128], F32)
    ident_b = K.sb(top, "ident_b", [128, 128], BF16)
    ones_b = K.sb(top, "ones_b", [128, 128], BF16)
    ones_f = K.sb(top, "ones_f", [128, 128], F32)
    tri_b = K.sb(top, "tri_b", [128, 128], BF16)
    iota32 = K.sb(top, "iota32", [128, 32], F32)
    ecap = K.sb(top, "ecap", [128, 32], F32)
    pcol = K.sb(top, "pcol", [128, 1], F32)
    tokid4 = K.sb(top, "tokid4", [128, NTILE, 4], I32)
    GATES = K.sb(top, "GATES", [128, NTILE, 4], F32)
    EKF = K.sb(top, "EKF", [128, NTILE, 4], F32)
    POSK = K.sb(top, "POSK", [128, NTILE, 4], F32)
    RUN = K.sb(top, "RUN", [128, 32], F32)
    BS128 = K.sb(top, "BS128", [128, 32], F32)
    RIDX = K.sb(top, "RIDX", [128, NBLK], I32)
    WIDX = K.sb(top, "WIDX", [128, NBLK], I32)
    BIDX = K.sb(top, "BIDX", [128, NBLK], I32)
    PSB = [K.ps(top, f"psb{i}", [128, 512], F32) for i in range(8)]

    for nm, b in (("ident_f", ident_f), ("ident_b", ident_b), ("ones_b", ones_b), ("ones_f", ones_f), ("tri_b", tri_b),
                  ("iota32", iota32), ("ecap", ecap), ("pcol", pcol), ("tokid4", tokid4)):
        K.load("sp", b, b.t[:], cin[nm])

    def evac(i, out_ap, in_ap, reads, writes):
        if i % 2 == 0:
            return K.op("act", lambda e: e.activation(out=out_ap, in_=in_ap, func=AF.Copy), reads, writes)
        return K.op("dve", lambda e: e.tensor_copy(out=out_ap, in_=in_ap), reads, writes)

    def load_xT(st, xt, xT, src_rows_ap, TM):
        nj = TM // 128
        K.load("sp", xt, xt.t[:, 0:nj, :], src_rows_ap)

    def transposes_x(xt, xT, TM, evi=0, act_only=False):
        nj = TM // 128
        for kc in range(8):
            pb = K.nextps(PSB)
            fns = [lambda t, j=j, kc=kc, pb=pb: t.transpose(out=pb.t[:, j * 128:(j + 1) * 128], in_=xt.t[:, j, kc * 128:(kc + 1) * 128], identity=ident_f.t[:])
                   for j in range(nj)]
            K.pe(fns, reads=(xt, ident_f), writes=(pb,))
            evac(0 if act_only else kc + evi, xT.t[:, kc, 0:TM], pb.t[:, 0:TM], (pb,), (xT,))

    def transposes_xb(xtb, xT, TM):
        nj = TM // 128
        for k2 in range(4):
            pb = K.nextps(PSB)
            pbb = pb.t[:].bitcast(BF16)
            fns = [lambda t, j=j, kk=kk, pbb=pbb: t.transpose(out=pbb[:, kk * 512 + j * 128:kk * 512 + (j + 1) * 128],
                                                             in_=xtb.t[:, j, (k2 * 2 + kk) * 128:(k2 * 2 + kk + 1) * 128], identity=ident_b.t[:])
                   for kk in range(2) for j in range(nj)]
            K.pe(fns, reads=(xtb, ident_b), writes=(pb,))
            src = pbb[:, 0:1024].rearrange("p (k t) -> p k t", k=2)[:, :, 0:TM]
            evac(k2, xT.t[:, k2 * 2:k2 * 2 + 2, 0:TM], src, (pb,), (xT,))

    def ln_stats(st, v_ap_fn, vbuf, mv, stt, rstd):
        K.op("dve", lambda e: e.bn_stats(out=stt.t[:, 0:6], in_=v_ap_fn(0)), (vbuf,), (stt,))
        K.op("dve", lambda e: e.bn_stats(out=stt.t[:, 6:12], in_=v_ap_fn(1)), (vbuf,), (stt,))
        K.op("dve", lambda e: e.bn_aggr(out=mv.t[:, 0:2], in_=stt.t[:, 0:12].rearrange("p (c s) -> p c s", s=6)), (stt,), (mv,))
        K.op("act", lambda e: e.activation(out=rstd.t[:, 0:1], in_=mv.t[:, 1:2], func=AF.Sqrt, bias=EPS), (mv,), (rstd,))
        K.op("dve", lambda e: e.reciprocal(out=rstd.t[:, 0:1], in_=rstd.t[:, 0:1]), (rstd,), (rstd,))

    with ExitStack() as st:
        zt = K.sb(st, "zt", [128, 4096], BF16)
        K.op("pool", lambda e: e.memset(zt.t[:], 0.0), (), (zt,))
        for g in range(3):
            n = NTP[g]
            for h in range(4):
                for c0 in range(0, n, 4096):
                    w = min(4096, n - c0)
                    K.store("sp", zt, KT[g][h, :, c0:c0 + w], zt.t[:, 0:w])
            vv = VV[g].rearrange("(a p) f -> p a f", p=128)
            na = n // 128
            for a0 in range(0, na, 8):
                w = min(8, na - a0)
                K.store("sp", zt, vv[:, a0:a0 + w, :], zt.t[:, 0:w * 512].rearrange("p (a f) -> p a f", f=512))
        K.store("sp", zt, X1[NT:NT + 128, :], zt.t[:].bitcast(F32)[:, 0:1024])
        K.barrier()

    for l in range(NL):
        XA = x_in if l == 0 else XS
        XO = y_out if l == NL - 1 else XS
        if phases is None or "qkv" in phases:
          with ExitStack() as st:
            wg = [K.sb(st, f"wg{i}", [128, 8, 1536], BF16) for i in range(2)]
            bcol = K.sb(st, "bcol", [128, 68], F32)
            brow = K.sb(st, "brow", [1, DIN], BF16)
            xts = [K.sb(st, f"xt{i}", [128, 4, D], BF16) for i in range(2)]
            xTs = [K.sb(st, f"xT{i}", [128, 8, 512], BF16) for i in range(2)]
            qks = [K.sb(st, f"qk{i}", [128, 8, 512], BF16) for i in range(2)]
            vss = [K.sb(st, f"vs{i}", [128, 4, 512], BF16) for i in range(2)]
            K.load("sp", bcol, bcol.t[:], bin_col[l])
            K.load("pool", brow, brow.t[:], bin_row[l])
            it = 0
            for g, (win, d) in enumerate(PATTERNS):
                L = S // d
                LP = L + 128
                TM = min(512, L)
                nj = TM // 128
                c0 = 2048 + g * 1536
                W = wg[g % 2]
                K.load("pool", W, W.t[:], w_in[l][:, c0:c0 + 1536].rearrange("(kc p) c -> p kc c", p=128))
                XAv = XA.rearrange("(s i dd) c -> s dd i c", s=NSEQ, dd=d)
                tiles = [(s, r, m) for s in range(NSEQ) for r in range(d) for m in range(L // TM)]

                def src(tl):
                    s, r, m = tl
                    return XAv[s, r, m * TM:(m + 1) * TM].rearrange("(j a) c -> a j c", a=128)
                K.load("pool", xts[it % 2], xts[it % 2].t[:, 0:nj, :], src(tiles[0]))
                for ti, tl in enumerate(tiles):
                    s, r, m = tl
                    xt, xT, qk, vs = xts[it % 2], xTs[it % 2], qks[it % 2], vss[it % 2]
                    if ti + 1 < len(tiles):
                        nx = xts[(it + 1) % 2]
                        K.load("pool", nx, nx.t[:, 0:nj, :], src(tiles[ti + 1]))
                    transposes_xb(xt, xT, TM)
                    col0 = (s * d + r) * LP + 64 + m * TM
                    for hs in range(8):
                        pb = K.nextps(PSB)
                        K.mm(pb.t[:, 0:TM], [(W.t[:, kc, hs * 128:(hs + 1) * 128], xT.t[:, kc, 0:TM]) for kc in range(8)],
                             reads=(W, xT), writes=(pb,))
                        bc = (c0 // 128) + hs
                        K.op("act", lambda e, pb=pb, hs=hs, bc=bc: e.activation(out=qk.t[:, hs, 0:TM], in_=pb.t[:, 0:TM], func=AF.Identity,
                                                                              bias=bcol.t[:, bc:bc + 1]),
                             (pb, bcol), (qk,))
                    K.store("sp", qk, QT[g][:, :, col0:col0 + TM].rearrange("h p c -> p h c"), qk.t[:, 0:4, 0:TM])
                    K.store("sp", qk, KT[g][:, :, col0:col0 + TM].rearrange("h p c -> p h c"), qk.t[:, 4:8, 0:TM])
                    for j in range(nj):
                        pb = K.nextps(PSB)
                        pairs = [(xT.t[:, kc, j * 128:(j + 1) * 128], W.t[:, kc, 1024:1536]) for kc in range(8)]
                        pairs.append((ones_b.t[0:1, :], brow.t[0:1, c0 + 1024:c0 + 1536]))
                        K.mm(pb.t[:, :], pairs, reads=(W, xT, ones_b, brow), writes=(pb,))
                        evac(j, vs.t[:, j, :], pb.t[:, :], (pb,), (vs,))
                    K.store("sp", vs, VV[g][col0:col0 + TM, :].rearrange("(j p) f -> p j f", p=128), vs.t[:, 0:nj, :])
                    it += 1
            K.barrier()

        if phases is None or "att" in phases:
          with ExitStack() as st:
            bts = [K.sb(st, f"bt{i}", [128, 12, 256], F32) for i in range(3)]
            qcs = [K.sb(st, f"qc{i}", [128, 4, 1024], BF16) for i in range(2)]
            kcs = [K.sb(st, f"kc{i}", [128, 4, 1152], BF16) for i in range(2)]
            vcs = [K.sb(st, f"vc{i}", [128, 9, 512], BF16) for i in range(2)]
            NB_ = 3
            ogs = [K.sb(st, f"og{i}", [128, 516], F32) for i in range(NB_)]
            nmx = [K.sb(st, f"nmx{i}", [128, 4], F32) for i in range(NB_)]
            den = [K.sb(st, f"den{i}", [128, 4], F32) for i in range(NB_)]
            rdn = [K.sb(st, f"rdn{i}", [128, 4], F32) for i in range(NB_)]
            lnd = [K.sb(st, f"lnd{i}", [128, 4], F32) for i in range(NB_)]
            NU_ = 4
            sbs = [K.sb(st, f"sS{i}", [128, 256], F32) for i in range(NU_)]
            pbs = [K.sb(st, f"sP{i}", [128, 256], BF16) for i in range(NU_)]
            pts = [K.sb(st, f"sPT{i}", [128, 256], BF16) for i in range(NU_)]
            for g in range(3):
                K.load("sp", bts[g], bts[g].t[:], cin["att_bias"][g])
            chunks = []
            for g, (win, d) in enumerate(PATTERNS):
                L = S // d
                CH = min(1024, L)
                for s_ in range(NSEQ):
                    for r in range(d):
                        for ch in range(L // CH):
                            chunks.append((g, d, L, CH, s_, r, ch))

            def ldchunk(ci):
                g, d, L, CH, s_, r, ch = chunks[ci]
                LP = L + 128
                nb = CH // 128
                col0 = (s_ * d + r) * LP + 64 + ch * CH
                qc, kc_, vc = qcs[ci % 2], kcs[ci % 2], vcs[ci % 2]
                K.load("sp", qc, qc.t[:, :, 0:CH], QT[g][:, :, col0:col0 + CH].rearrange("h p c -> p h c"))
                K.load("sp", kc_, kc_.t[:, :, 0:CH + 128], KT[g][:, :, col0 - 64:col0 + CH + 64].rearrange("h p c -> p h c"))
                K.load("sp", vc, vc.t[:, 0:nb + 1, :], VV[g][col0 - 64:col0 + CH + 64, :].rearrange("(c p) f -> p c f", p=128))

            units = []
            bidx = 0
            for ci, (g, d, L, CH, s_, r, ch) in enumerate(chunks):
                nb = CH // 128
                nblk = L // 128
                for i in range(nb):
                    blk = ch * nb + i
                    var = 1 if blk == 0 else (2 if blk == nblk - 1 else 0)
                    for h in range(4):
                        units.append(dict(ci=ci, g=g, d=d, s=s_, r=r, i=i, h=h, var=var, b=bidx, p0=ch * CH + i * 128,
                                          lastc=(i == nb - 1 and h == 3)))
                    bidx += 1
            NU = len(units)
            state = {}

            def stA(u):
                U = units[u]
                qc, kc_ = qcs[U["ci"] % 2], kcs[U["ci"] % 2]
                bt = bts[U["g"]]
                i, h, var = U["i"], U["h"], U["var"]
                sS, sP = sbs[u % NU_], pbs[u % NU_]
                nm_, dn_ = nmx[U["b"] % NB_], den[U["b"] % NB_]
                pS = K.nextps(PSB)
                K.mm(pS.t[:, 0:256], [(qc.t[:, h, i * 128:(i + 1) * 128], kc_.t[:, h, i * 128:i * 128 + 256])], reads=(qc, kc_), writes=(pS,))
                K.op("dve", lambda e: e.scalar_tensor_tensor(out=sS.t[:], in0=pS.t[:, 0:256], scalar=SCALE, in1=bt.t[:, h * 3 + var, :], op0=ALU.mult, op1=ALU.add),
                     (pS, bt), (sS,))
                K.op("dve", lambda e: e.tensor_reduce(out=nm_.t[:, h:h + 1], in_=sS.t[:], axis=AX.X, op=ALU.max, negate=True), (sS,), (nm_,))
                K.op("act", lambda e: e.activation(out=sP.t[:], in_=sS.t[:], func=AF.Exp, bias=nm_.t[:, h:h + 1], accum_out=dn_.t[:, h:h + 1]),
                     (sS, nm_), (sP, dn_))

            def stB(u):
                sP, sPT = pbs[u % NU_], pts[u % NU_]
                pT = K.nextps(PSB)
                pTb = pT.t[:].bitcast(BF16)
                K.pe([lambda t, c=c: t.transpose(out=pTb[:, c * 128:(c + 1) * 128], in_=sP.t[:, c * 128:(c + 1) * 128], identity=ident_b.t[:])
                      for c in range(2)], reads=(sP, ident_b), writes=(pT,))
                K.op("act", lambda e: e.activation(out=sPT.t[:], in_=pTb[:, 0:256], func=AF.Copy), (pT,), (sPT,))

            def stC(u):
                U = units[u]
                vc = vcs[U["ci"] % 2]
                i, h = U["i"], U["h"]
                sPT = pts[u % NU_]
                q_ = U["b"] % NB_
                og, nm_, dn_, rd_, ld_ = ogs[q_], nmx[q_], den[q_], rdn[q_], lnd[q_]
                pO = K.nextps(PSB)
                K.mm(pO.t[:, 0:128], [(sPT.t[:, c * 128:(c + 1) * 128], vc.t[:, i + c, h * 128:(h + 1) * 128]) for c in range(2)], reads=(sPT, vc), writes=(pO,))
                K.op("dve", lambda e: e.reciprocal(out=rd_.t[:, h:h + 1], in_=dn_.t[:, h:h + 1]), (dn_,), (rd_,))
                K.op("act", lambda e: e.activation(out=og.t[:, h * 128:(h + 1) * 128], in_=pO.t[:, 0:128], func=AF.Identity, scale=rd_.t[:, h:h + 1]),
                     (pO, rd_), (og,))
                if h == 3:
                    K.op("act", lambda e: e.activation(out=ld_.t[:], in_=dn_.t[:], func=AF.Ln), (dn_,), (ld_,))
                    K.op("dve", lambda e: e.tensor_tensor(out=og.t[:, 512:516], in0=ld_.t[:], in1=nm_.t[:], op=ALU.subtract), (ld_, nm_), (og,))
                    XOv = OG[U["g"]].rearrange("(s i dd) c -> s dd i c", s=NSEQ, dd=U["d"])
                    K.store("sp", og, XOv[U["s"], U["r"], U["p0"]:U["p0"] + 128, :], og.t[:])
                if U["lastc"] and U["ci"] + 2 < len(chunks):
                    ldchunk(U["ci"] + 2)

            ldchunk(0)
            if len(chunks) > 1:
                ldchunk(1)
            for idx in range(NU + 2):
                if idx < NU:
                    stA(idx)
                if 1 <= idx <= NU:
                    stB(idx - 1)
                if idx >= 2:
                    stC(idx - 2)
            K.barrier()

        if phases is None or "pc1" in phases:
          with ExitStack() as st:
            Wu = K.sb(st, "Wu", [128, 8, 1024], BF16)
            Wv = K.sb(st, "Wv", [128, 8, 1024], BF16)
            Wga = K.sb(st, "Wga", [128, 8, 1024], BF16)
            Wpa = K.sb(st, "Wpa", [128, 8, 1024], BF16)
            Wst = K.sb(st, "Wst", [128, 8, 128], BF16)
            bcol = K.sb(st, "bcol", [128, 68], F32)
            brow = K.sb(st, "brow", [1, 1024], BF16)
            lnv = K.sb(st, "lnv", [128, 16], F32)
            bsb = K.sb(st, "bsb", [128, 8, 128], F32)
            Cgt = K.sb(st, "Cgt", [128, 8, 128], F32)
            xts = [K.sb(st, f"xt{i}", [128, 4, D], BF16) for i in range(2)]
            xT = K.sb(st, "xT", [128, 8, 512], BF16)
            uT = K.sb(st, "uT", [128, 8, 512], BF16)
            gaT = K.sb(st, "gaT", [128, 8, 512], BF16)
            aoT = K.sb(st, "aoT", [128, 8, 512], BF16)
            maTs = [K.sb(st, f"maT{i}", [128, 8, 512], BF16) for i in range(2)]
            vsb = [K.sb(st, f"v{i}", [128, 1024], F32) for i in range(2)]
            nsb = [K.sb(st, f"n{i}", [128, 1024], BF16) for i in range(2)]
            tmpa = [K.sb(st, f"tmpa{i}", [128, 8, 128], F32) for i in range(2)]
            stt = [K.sb(st, f"stt{i}", [128, 12], F32) for i in range(2)]
            mv = [K.sb(st, f"mv{i}", [128, 2], F32) for i in range(2)]
            rstd = [K.sb(st, f"rstd{i}", [128, 1], F32) for i in range(2)]
            wsrc = w_in[l]
            K.load("pool", Wu, Wu.t[:], wsrc[:, 0:1024].rearrange("(kc p) c -> p kc c", p=128))
            K.load("pool", Wv, Wv.t[:], wsrc[:, 1024:2048].rearrange("(kc p) c -> p kc c", p=128))
            K.load("pool", Wga, Wga.t[:], wsrc[:, 6656:7680].rearrange("(kc p) c -> p kc c", p=128))
            K.load("pool", Wpa, Wpa.t[:], w_pa[l].rearrange("(kc p) c -> p kc c", p=128))
            K.load("pool", Wst, Wst.t[:], w_st[l])
            K.load("pool", brow, brow.t[:], bin_row[l][:, 1024:2048])
            K.load("sp", bcol, bcol.t[:], bin_col[l])
            K.load("sp", lnv, lnv.t[:], lnv_col[l])
            K.load("sp", bsb, bsb.t[:], bs_row[l].rearrange("o (g t) -> o g t", g=8).to_broadcast([128, 8, 128]))
            for half in range(2):
                pb = K.nextps(PSB)
                K.mm(pb.t[:, :], [(ones_b.t[:, :], Wst.t[:, half * 4:(half + 1) * 4, :])], reads=(ones_b, Wst), writes=(pb,))
                for gg in range(4):
                    g_ = half * 4 + gg
                    K.op("dve", lambda e, pb=pb, gg=gg, g_=g_: e.scalar_tensor_tensor(
                        out=Cgt.t[:, g_, :], in0=pb.t[:, gg * 128:(gg + 1) * 128], scalar=lnv.t[:, 8 + g_:9 + g_], in1=bsb.t[:, g_, :],
                        op0=ALU.mult, op1=ALU.add), (pb, lnv, bsb), (Cgt,))
            XAt = XA.rearrange("(m j a) c -> m a j c", j=4, a=128)
            NM = NT // 512
            K.load("pool", xts[0], xts[0].t[:], XAt[0])
            sj = 0
            for m in range(NM):
                xt = xts[m % 2]
                maT = maTs[m % 2]
                if m + 1 < NM:
                    K.load("pool", xts[(m + 1) % 2], xts[(m + 1) % 2].t[:], XAt[m + 1])
                transposes_xb(xt, xT, 512)
                for c8 in range(8):
                    pb = K.nextps(PSB)
                    K.mm(pb.t[:, :], [(Wu.t[:, kc, c8 * 128:(c8 + 1) * 128], xT.t[:, kc, :]) for kc in range(8)], reads=(Wu, xT), writes=(pb,))
                    K.op("act", lambda e, pb=pb, c8=c8: e.activation(out=uT.t[:, c8, :], in_=pb.t[:, :], func=AF.Gelu, bias=bcol.t[:, c8:c8 + 1]),
                         (pb, bcol), (uT,))
                for j in range(4):
                    v_, n_, ta, st_, mv_, rs_ = vsb[sj % 2], nsb[sj % 2], tmpa[sj % 2], stt[sj % 2], mv[sj % 2], rstd[sj % 2]
                    for nh in range(2):
                        pb = K.nextps(PSB)
                        pairs = [(xT.t[:, kc, j * 128:(j + 1) * 128], Wv.t[:, kc, nh * 512:(nh + 1) * 512]) for kc in range(8)]
                        pairs.append((ones_b.t[0:1, :], brow.t[0:1, nh * 512:(nh + 1) * 512]))
                        K.mm(pb.t[:, :], pairs, reads=(xT, Wv, ones_b, brow), writes=(pb,))
                        K.op("act", lambda e, pb=pb, nh=nh, v_=v_: e.activation(out=v_.t[:, nh * 512:(nh + 1) * 512], in_=pb.t[:, :], func=AF.Gelu),
                             (pb,), (v_,))
                    ln_stats(st, lambda c, v_=v_: v_.t[:, c * 512:(c + 1) * 512], v_, mv_, st_, rs_)
                    K.op("dve", lambda e, v_=v_, n_=n_, mv_=mv_, rs_=rs_: e.tensor_scalar(
                        out=n_.t[:], in0=v_.t[:], scalar1=mv_.t[:, 0:1], scalar2=rs_.t[:, 0:1], op0=ALU.subtract, op1=ALU.mult),
                        (v_, mv_, rs_), (n_,))
                    for half in range(2):
                        pb = K.nextps(PSB)
                        fns = [lambda t, gg=gg, half=half, pb=pb, n_=n_: t.matmul(
                            pb.t[:, gg * 128:(gg + 1) * 128], lhsT=n_.t[:, (half * 4 + gg) * 128:(half * 4 + gg + 1) * 128],
                            rhs=Wst.t[:, half * 4 + gg, :], start=True, stop=True) for gg in range(4)]
                        K.pe(fns, reads=(n_, Wst), writes=(pb,))
                        for gg in range(4):
                            g_ = half * 4 + gg
                            K.op("dve", lambda e, pb=pb, gg=gg, g_=g_, ta=ta: e.scalar_tensor_tensor(
                                out=ta.t[:, g_, :], in0=pb.t[:, gg * 128:(gg + 1) * 128], scalar=lnv.t[:, g_:g_ + 1], in1=Cgt.t[:, g_, :],
                                op0=ALU.mult, op1=ALU.add), (pb, lnv, Cgt), (ta,))
                    K.op("pool", lambda e, ta=ta, j=j: e.tensor_tensor(out=aoT.t[:, :, j * 128:(j + 1) * 128], in0=ta.t[:], in1=uT.t[:, :, j * 128:(j + 1) * 128], op=ALU.mult),
                         (ta, uT), (aoT,))
                    sj += 1
                for c8 in range(8):
                    pb = K.nextps(PSB)
                    K.mm(pb.t[:, :], [(Wga.t[:, kc, c8 * 128:(c8 + 1) * 128], xT.t[:, kc, :]) for kc in range(8)], reads=(Wga, xT), writes=(pb,))
                    K.op("act", lambda e, pb=pb, c8=c8: e.activation(out=gaT.t[:, c8, :], in_=pb.t[:, :], func=AF.Sigmoid, bias=bcol.t[:, 52 + c8:53 + c8]),
                         (pb, bcol), (gaT,))
                for c8 in range(8):
                    pb = K.nextps(PSB)
                    K.mm(pb.t[:, :], [(Wpa.t[:, kc, c8 * 128:(c8 + 1) * 128], aoT.t[:, kc, :]) for kc in range(8)], reads=(Wpa, aoT), writes=(pb,))
                    K.op("dve", lambda e, pb=pb, c8=c8, maT=maT: e.tensor_tensor(out=maT.t[:, c8, :], in0=pb.t[:, :], in1=gaT.t[:, c8, :], op=ALU.mult),
                         (pb, gaT), (maT,))
                K.store("sp", maT, MA[:, :, m * 512:(m + 1) * 512].rearrange("n p t -> p n t"), maT.t[:])
            K.barrier()

        if phases is None or "pc2" in phases:
          with ExitStack() as st:
            Wgb = K.sb(st, "Wgb", [128, 8, 1024], BF16)
            Wpb = K.sb(st, "Wpb", [128, 4, 1024], BF16)
            Wo = K.sb(st, "Wo", [128, 8, 1024], BF16)
            Wr = K.sb(st, "Wr", [128, 8, 32], F32)
            bcol = K.sb(st, "bcol", [128, 68], F32)
            g1b = K.sb(st, "g1b", [128, 1024], F32)
            b1b = K.sb(st, "b1b", [128, 1024], F32)
            brb = K.sb(st, "brb", [128, 32], F32)
            rin = K.sb(st, "rin", [128, 128, 4], I32)
            xts = [K.sb(st, f"xt{i}", [128, 4, D], F32) for i in range(2)]
            xT = K.sb(st, "xT", [128, 8, 512], BF16)
            gbT = K.sb(st, "gbT", [128, 8, 512], BF16)
            maTs = [K.sb(st, f"maT{i}", [128, 8, 512], BF16) for i in range(2)]
            boT = K.sb(st, "boT", [128, 4, 512], BF16)
            mgT = K.sb(st, "mgT", [128, 8, 512], BF16)
            tmpm = [K.sb(st, f"tmpm{i}", [128, 512], F32) for i in range(2)]
            bof = [K.sb(st, f"bof{i}", [128, 512], F32) for i in range(2)]
            sm = [K.sb(st, f"sm{i}", [128, 64], F32) for i in range(2)]
            x1p = [K.sb(st, f"x1p{i}", [128, 1024], F32) for i in range(2)]
            x1 = [K.sb(st, f"x1{i}", [128, 1024], F32) for i in range(2)]
            x1T = [K.sb(st, f"x1T{i}", [128, 8, 128], F32) for i in range(2)]
            stt = [K.sb(st, f"stt{i}", [128, 12], F32) for i in range(2)]
            mv = [K.sb(st, f"mv{i}", [128, 2], F32) for i in range(2)]
            rstd = [K.sb(st, f"rstd{i}", [128, 1], F32) for i in range(2)]
            lg = [K.sb(st, f"lg{i}", [128, 32], F32) for i in range(2)]
            t8 = [K.sb(st, f"t8{i}", [128, 8], F32) for i in range(2)]
            rs = [K.sb(st, f"rs{i}", [128, 16], F32) for i in range(2)]
            mk = [K.sb(st, f"mk{i}", [128, 32], BF16) for i in range(2)]
            posg = [K.sb(st, f"posg{i}", [128, 32], F32) for i in range(2)]
            slotv = [K.sb(st, f"slotv{i}", [128, 32], F32) for i in range(2)]
            oh = [K.sb(st, f"oh{i}", [128, 32], F32) for i in range(2)]
            ohx = [K.sb(st, f"ohx{i}", [128, 32], F32) for i in range(2)]
            slf = [K.sb(st, f"slf{i}", [128, 4], F32) for i in range(2)]
            sli = [K.sb(st, f"sli{i}", [128, 4], I32) for i in range(2)]
            wsrc = w_in[l]
            K.load("pool", Wgb, Wgb.t[:], wsrc[:, 7680:8704].rearrange("(kc p) c -> p kc c", p=128))
            K.load("pool", Wpb, Wpb.t[:], w_pb[l].rearrange("(kc p) c -> p kc c", p=128))
            K.load("pool", Wo, Wo.t[:], w_o[l].rearrange("(kc p) c -> p kc c", p=128))
            K.load("sp", Wr, Wr.t[:], w_r[l].rearrange("(kc p) c -> p kc c", p=128))
            K.load("sp", bcol, bcol.t[:], bin_col[l])
            K.load("sp", g1b, g1b.t[:], ln1_row[l][0:1, :].to_broadcast([128, 1024]))
            K.load("sp", b1b, b1b.t[:], ln1_row[l][1:2, :].to_broadcast([128, 1024]))
            K.load("sp", brb, brb.t[:], br_row[l].to_broadcast([128, 32]))
            K.load("sp", rin, rin.t[:], cin["rinit"])
            for c0 in range(0, NE * CAP, 16384):
                K.store("sp", rin, ROUTE[c0:c0 + 16384, :].rearrange("(p a) f -> p a f", p=128), rin.t[:])
            K.store("sp", rin, ROUTE[NE * CAP:NE * CAP + 128, :], rin.t[:, 0, :])
            K.op("dve", lambda e: e.memset(RUN.t[:], 0.0), (), (RUN,))
            K.barrier()
            XAt = XA.rearrange("(m j a) c -> m a j c", j=4, a=128)
            NM = NT // 512
            K.load("sp", xts[0], xts[0].t[:], XAt[0])
            K.load("sp", maTs[0], maTs[0].t[:], MA[:, :, 0:512].rearrange("n p t -> p n t"))
            og3 = [K.sb(st, f"og3x{i}", [128, 3, 516], F32) for i in range(4)]
            bo = [K.sb(st, f"box{i}", [128, 512], BF16) for i in range(4)]
            nmr = [K.sb(st, f"nmr{i}", [128, 1], F32) for i in range(2)]
            tmpq = [K.sb(st, f"tmpq{i}", [128, 512], F32) for i in range(2)]
            oh4 = [K.sb(st, f"oh4{i}", [128, 4, 32], F32) for i in range(2)]
            ox4 = [K.sb(st, f"ox4{i}", [128, 4, 32], F32) for i in range(2)]

            def M1(m, j):
                tile = m * 4 + j
                o3, sm_, bo_, bof_ = og3[j], sm[j % 2], bo[j], bof[j % 2]
                K.load("sp", o3, o3.t[:], OGall[:, tile * 128:(tile + 1) * 128, :].rearrange("g p c -> p g c"))
                lse3 = o3.t[:, :, 512:516]
                K.op("dve", lambda e: e.tensor_reduce(out=sm_.t[:, 0:4], in_=lse3.rearrange("p g h -> p h g"), axis=AX.X, op=ALU.max), (o3,), (sm_,))
                K.op("dve", lambda e: e.tensor_tensor(out=sm_.t[:, 4:16].rearrange("p (g h) -> p g h", g=3), in0=lse3,
                                                      in1=sm_.t[:, 0:4].unsqueeze(1).to_broadcast([128, 3, 4]), op=ALU.subtract), (o3, sm_), (sm_,))
                K.op("act", lambda e: e.activation(out=sm_.t[:, 4:16], in_=sm_.t[:, 4:16], func=AF.Exp), (sm_,), (sm_,))
                K.op("dve", lambda e: e.tensor_reduce(out=sm_.t[:, 16:20], in_=sm_.t[:, 4:16].rearrange("p (g h) -> p h g", g=3), axis=AX.X, op=ALU.add), (sm_,), (sm_,))
                K.op("dve", lambda e: e.reciprocal(out=sm_.t[:, 20:24], in_=sm_.t[:, 16:20]), (sm_,), (sm_,))
                K.op("dve", lambda e: e.tensor_tensor(out=sm_.t[:, 24:36].rearrange("p (g h) -> p g h", g=3), in0=sm_.t[:, 4:16].rearrange("p (g h) -> p g h", g=3),
                                                      in1=sm_.t[:, 20:24].unsqueeze(1).to_broadcast([128, 3, 4]), op=ALU.mult), (sm_,), (sm_,))
                tmq = tmpq[j % 2]

                def wnb(g):
                    return sm_.t[:, 24 + 4 * g:28 + 4 * g].unsqueeze(2).to_broadcast([128, 4, 128])

                def o3v(g):
                    return o3.t[:, g, 0:512].rearrange("p (h c) -> p h c", h=4)
                K.op("dve", lambda e: e.tensor_tensor(out=bof_.t[:, :].rearrange("p (h c) -> p h c", h=4), in0=o3v(0), in1=wnb(0), op=ALU.mult), (o3, sm_), (bof_,))
                K.op("pool", lambda e: e.tensor_tensor(out=tmq.t[:, :].rearrange("p (h c) -> p h c", h=4), in0=o3v(1), in1=wnb(1), op=ALU.mult), (o3, sm_), (tmq,))
                K.op("dve", lambda e: e.tensor_tensor(out=bof_.t[:, :], in0=bof_.t[:, :], in1=tmq.t[:, :], op=ALU.add), (bof_, tmq), (bof_,))
                K.op("pool", lambda e: e.tensor_tensor(out=tmq.t[:, :].rearrange("p (h c) -> p h c", h=4), in0=o3v(2), in1=wnb(2), op=ALU.mult), (o3, sm_), (tmq,))
                K.op("dve", lambda e: e.tensor_tensor(out=bo_.t[:, :], in0=bof_.t[:, :], in1=tmq.t[:, :], op=ALU.add), (bof_, tmq), (bo_,))

            def M2(m, j):
                bo_ = bo[j]
                pT = K.nextps(PSB)
                pTb = pT.t[:].bitcast(BF16)
                K.pe([lambda t, c=c: t.transpose(out=pTb[:, c * 128:(c + 1) * 128], in_=bo_.t[:, c * 128:(c + 1) * 128], identity=ident_b.t[:])
                      for c in range(4)], reads=(bo_, ident_b), writes=(pT,))
                K.op("act", lambda e: e.activation(out=boT.t[:, :, j * 128:(j + 1) * 128], in_=pTb[:, 0:512].rearrange("p (c t) -> p c t", c=4), func=AF.Copy), (pT,), (boT,))

            def Pst(m, j, xt):
                tile = m * 4 + j
                q = tile % 2
                xp, x1_, st_, mv_, rs_, nm_ = x1p[q], x1[q], stt[q], mv[q], rstd[q], nmr[q]
                for nh in range(2):
                    pb = K.nextps(PSB)
                    K.mm(pb.t[:, :], [(mgT.t[:, kc, j * 128:(j + 1) * 128], Wo.t[:, kc, nh * 512:(nh + 1) * 512]) for kc in range(8)], reads=(mgT, Wo), writes=(pb,))
                    K.op("dve", lambda e, pb=pb, nh=nh: e.scalar_tensor_tensor(
                        out=xp.t[:, nh * 512:(nh + 1) * 512], in0=xt.t[:, j, nh * 512:(nh + 1) * 512], scalar=ALPHA, in1=pb.t[:, :], op0=ALU.mult, op1=ALU.add),
                        (pb, xt), (xp,))
                ln_stats(st, lambda c: xp.t[:, c * 512:(c + 1) * 512], xp, mv_, st_, rs_)
                K.op("dve", lambda e: e.scalar_tensor_tensor(out=nm_.t[:], in0=mv_.t[:, 0:1], scalar=-1.0, in1=rs_.t[:, 0:1], op0=ALU.mult, op1=ALU.mult), (mv_, rs_), (nm_,))
                K.op("act", lambda e: e.activation(out=xp.t[:], in_=xp.t[:], func=AF.Identity, scale=rs_.t[:, 0:1], bias=nm_.t[:, 0:1]), (xp, rs_, nm_), (xp,))
                K.op("dve", lambda e: e.tensor_tensor(out=xp.t[:], in0=xp.t[:], in1=g1b.t[:], op=ALU.mult), (xp, g1b), (xp,))
                K.op("pool", lambda e: e.tensor_tensor(out=x1_.t[:], in0=xp.t[:], in1=b1b.t[:], op=ALU.add), (xp, b1b), (x1_,))
                K.store("sp", x1_, X1[tile * 128:(tile + 1) * 128, :], x1_.t[:])

            def Rst(m, j):
                tile = m * 4 + j
                q = tile % 2
                x1_, x1T_ = x1[q], x1T[q]
                lg_, t8_, r_, mk_, pg_, sv_, oh_, ox_, sf_, si_ = lg[q], t8[q], rs[q], mk[q], posg[q], slotv[q], oh[q], ohx[q], slf[q], sli[q]
                for half in range(2):
                    pb = K.nextps(PSB)
                    K.pe([lambda t, c=c, half=half, pb=pb: t.transpose(out=pb.t[:, c * 128:(c + 1) * 128], in_=x1_.t[:, (half * 4 + c) * 128:(half * 4 + c + 1) * 128], identity=ident_f.t[:])
                          for c in range(4)], reads=(x1_, ident_f), writes=(pb,))
                    evac(half, x1T_.t[:, half * 4:(half + 1) * 4, :], pb.t[:, :].rearrange("p (c t) -> p c t", c=4), (pb,), (x1T_,))
                pb = K.nextps(PSB)
                K.mm(pb.t[:, 0:32], [(x1T_.t[:, kc, :], Wr.t[:, kc, :]) for kc in range(8)], reads=(x1T_, Wr), writes=(pb,))
                K.op("dve", lambda e, pb=pb: e.tensor_tensor(out=lg_.t[:], in0=pb.t[:, 0:32], in1=brb.t[:], op=ALU.add), (pb, brb), (lg_,))
                K.op("dve", lambda e: e.max(out=t8_.t[:], in_=lg_.t[:]), (lg_,), (t8_,))
                K.op("dve", lambda e: e.tensor_scalar(out=r_.t[:, 0:1], in0=t8_.t[:, 0:1], scalar1=-1.0, scalar2=None, op0=ALU.mult), (t8_,), (r_,))
                K.op("act", lambda e: e.activation(out=r_.t[:, 1:5], in_=t8_.t[:, 0:4], func=AF.Exp, bias=r_.t[:, 0:1], accum_out=r_.t[:, 5:6]), (t8_, r_), (r_,))
                K.op("dve", lambda e: e.reciprocal(out=r_.t[:, 6:7], in_=r_.t[:, 5:6]), (r_,), (r_,))
                K.op("dve", lambda e: e.tensor_scalar(out=GATES.t[:, tile, :], in0=r_.t[:, 1:5], scalar1=r_.t[:, 6:7], scalar2=None, op0=ALU.mult), (r_,), (GATES,))
                K.op("dve", lambda e: e.tensor_scalar(out=mk_.t[:], in0=lg_.t[:], scalar1=t8_.t[:, 3:4], scalar2=None, op0=ALU.is_ge), (lg_, t8_), (mk_,))
                pb = K.nextps(PSB)
                K.mm(pb.t[:, 0:32], [(tri_b.t[:, :], mk_.t[:, :])], reads=(tri_b, mk_), writes=(pb,))
                K.op("dve", lambda e, pb=pb: e.tensor_tensor(out=pg_.t[:], in0=pb.t[:, 0:32], in1=RUN.t[:], op=ALU.add), (pb, RUN), (pg_,))
                pb2 = K.nextps(PSB)
                K.mm(pb2.t[:, 0:32], [(ones_b.t[:, :], mk_.t[:, :])], reads=(ones_b, mk_), writes=(pb2,))
                K.op("dve", lambda e, pb2=pb2: e.tensor_tensor(out=RUN.t[:], in0=pb2.t[:, 0:32], in1=RUN.t[:], op=ALU.add), (pb2, RUN), (RUN,))
                K.op("dve", lambda e: e.tensor_tensor(out=sv_.t[:], in0=pg_.t[:], in1=ecap.t[:], op=ALU.add), (pg_, ecap), (sv_,))
                o4, x4 = oh4[q], ox4[q]
                K.op("dve", lambda e: e.tensor_tensor(out=o4.t[:], in0=lg_.t[:, :].unsqueeze(1).to_broadcast([128, 4, 32]),
                                                      in1=t8_.t[:, 0:4].unsqueeze(2).to_broadcast([128, 4, 32]), op=ALU.is_equal), (lg_, t8_), (o4,))
                for V, dst in ((sv_, lambda: sf_.t[:, 0:4]), (pg_, lambda: POSK.t[:, tile, :]), (iota32, lambda: EKF.t[:, tile, :])):
                    dbuf = sf_ if V is sv_ else (POSK if V is pg_ else EKF)
                    K.op("dve", lambda e, V=V: e.tensor_tensor(out=x4.t[:], in0=o4.t[:], in1=V.t[:, :].unsqueeze(1).to_broadcast([128, 4, 32]), op=ALU.mult), (o4, V), (x4,))
                    K.op("dve", lambda e, dst=dst: e.tensor_reduce(out=dst(), in_=x4.t[:], axis=AX.X, op=ALU.add), (x4,), (dbuf,))
                K.op("dve", lambda e: e.tensor_copy(out=si_.t[:], in_=sf_.t[:]), (sf_,), (si_,))
                for k in range(4):
                    K.dma("pool", si_, lambda e, k=k: e.indirect_dma_start(
                        out=ROUTE[:, :], out_offset=bass.IndirectOffsetOnAxis(ap=si_.t[:, k:k + 1], axis=0),
                        in_=tokid4.t[:, tile, :], in_offset=None), reads=(si_, tokid4), writes=())

            for m in range(NM):
                xt = xts[m % 2]
                maT = maTs[m % 2]
                if m + 1 < NM:
                    K.load("sp", xts[(m + 1) % 2], xts[(m + 1) % 2].t[:], XAt[m + 1])
                    K.load("sp", maTs[(m + 1) % 2], maTs[(m + 1) % 2].t[:], MA[:, :, (m + 1) * 512:(m + 2) * 512].rearrange("n p t -> p n t"))
                for j in range(4):
                    M1(m, j)
                transposes_x(xt, xT, 512, act_only=True)
                for c8 in range(8):
                    pb = K.nextps(PSB)
                    K.mm(pb.t[:, :], [(Wgb.t[:, kc, c8 * 128:(c8 + 1) * 128], xT.t[:, kc, :]) for kc in range(8)], reads=(Wgb, xT), writes=(pb,))
                    K.op("act", lambda e, pb=pb, c8=c8: e.activation(out=gbT.t[:, c8, :], in_=pb.t[:, :], func=AF.Sigmoid, bias=bcol.t[:, 60 + c8:61 + c8]),
                         (pb, bcol), (gbT,))
                for j in range(4):
                    M2(m, j)
                for c8 in range(8):
                    pb = K.nextps(PSB)
                    tm = tmpm[c8 % 2]
                    K.mm(pb.t[:, :], [(Wpb.t[:, kc, c8 * 128:(c8 + 1) * 128], boT.t[:, kc, :]) for kc in range(4)], reads=(Wpb, boT), writes=(pb,))
                    K.op("dve", lambda e, pb=pb, c8=c8, tm=tm: e.tensor_tensor(out=tm.t[:], in0=pb.t[:, :], in1=gbT.t[:, c8, :], op=ALU.mult), (pb, gbT), (tm,))
                    K.op("pool", lambda e, c8=c8, tm=tm: e.tensor_tensor(out=mgT.t[:, c8, :], in0=tm.t[:], in1=maT.t[:, c8, :], op=ALU.add), (tm, maT), (mgT,))
                Pst(m, 0, xt); Pst(m, 1, xt); Rst(m, 0); Pst(m, 2, xt); Rst(m, 1); Pst(m, 3, xt); Rst(m, 2); Rst(m, 3)
            K.barrier()

        if phases is None or "bl" in phases:
          with ExitStack() as st:
            cntc = K.sb(st, "cntc", [32, 1], F32)
            tmp32 = K.sb(st, "tmp32", [32, 32], F32)
            jthr = K.sb(st, "jthr", [32, 64], F32)
            tmpj = K.sb(st, "tmpj", [32, 64], F32)
            nbc = K.sb(st, "nbc", [32, 1], F32)
            endc = K.sb(st, "endc", [32, 1], F32)
            triu = K.sb(st, "triu", [32, 32], F32)
            iob = K.sb(st, "iob", [128, NBLK], F32)
            Gm = K.sb(st, "Gm", [32, NBLK], F32)
            nbm = K.sb(st, "nbm", [32, 128], F32)
            ebr = K.sb(st, "ebr", [128, NBLK], F32)
            eb = K.sb(st, "eb", [128, NBLK], F32)
            stb = K.sb(st, "stb", [128, NBLK], F32)
            val = K.sb(st, "val", [128, NBLK], F32)
            sbase = K.sb(st, "sbase", [128, NBLK], F32)
            need = K.sb(st, "need", [128, NBLK], F32)
            tA = K.sb(st, "tA", [128, NBLK], F32)
            tB = K.sb(st, "tB", [128, NBLK], F32)
            endr = K.sb(st, "endr", [128, 32], F32)
            nbr = K.sb(st, "nbr", [128, 32], F32)
            K.load("sp", jthr, jthr.t[:], cin["jthr"])
            K.load("sp", triu, triu.t[:], cin["triu32"])
            K.load("sp", iob, iob.t[:], cin["iota_b"])
            K.op("dve", lambda e: e.tensor_tensor(out=tmp32.t[:], in0=RUN.t[0:32, :], in1=ident_f.t[0:32, 0:32], op=ALU.mult), (RUN, ident_f), (tmp32,))
            K.op("dve", lambda e: e.tensor_reduce(out=cntc.t[:], in_=tmp32.t[:], axis=AX.X, op=ALU.add), (tmp32,), (cntc,))
            K.op("dve", lambda e: e.tensor_scalar(out=tmpj.t[:], in0=jthr.t[:], scalar1=cntc.t[:, 0:1], scalar2=None, op0=ALU.is_lt), (jthr, cntc), (tmpj,))
            K.op("dve", lambda e: e.tensor_reduce(out=nbc.t[:], in_=tmpj.t[:], axis=AX.X, op=ALU.add), (tmpj,), (nbc,))
            pb = K.nextps(PSB)
            K.mm(pb.t[0:32, 0:1], [(triu.t[:, :], nbc.t[:, 0:1])], reads=(triu, nbc), writes=(pb,))
            K.op("dve", lambda e, pb=pb: e.tensor_copy(out=endc.t[:], in_=pb.t[0:32, 0:1]), (pb,), (endc,))
            K.op("dve", lambda e: e.tensor_scalar(out=Gm.t[:], in0=iob.t[0:32, :], scalar1=endc.t[:, 0:1], scalar2=None, op0=ALU.is_ge), (iob, endc), (Gm,))
            K.op("dve", lambda e: e.tensor_scalar(out=nbm.t[:], in0=ones_f.t[0:32, :], scalar1=nbc.t[:, 0:1], scalar2=None, op0=ALU.mult), (ones_f, nbc), (nbm,))
            pb = K.nextps(PSB)
            K.mm(pb.t[:, 0:NBLK], [(ones_f.t[0:32, :], Gm.t[:, :])], reads=(ones_f, Gm), writes=(pb,))
            K.op("dve", lambda e, pb=pb: e.tensor_copy(out=ebr.t[:], in_=pb.t[:, 0:NBLK]), (pb,), (ebr,))
            pb = K.nextps(PSB)
            K.mm(pb.t[:, 0:NBLK], [(nbm.t[:, :], Gm.t[:, :])], reads=(nbm, Gm), writes=(pb,))
            K.op("dve", lambda e, pb=pb: e.tensor_copy(out=stb.t[:], in_=pb.t[:, 0:NBLK]), (pb,), (stb,))
            pb = K.nextps(PSB)
            K.mm(pb.t[:, 0:32], [(nbm.t[:, :], triu.t[:, :])], reads=(nbm, triu), writes=(pb,))
            K.op("dve", lambda e, pb=pb: e.tensor_copy(out=endr.t[:], in_=pb.t[:, 0:32]), (pb,), (endr,))
            pb = K.nextps(PSB)
            K.mm(pb.t[:, 0:32], [(nbm.t[:, :], ident_f.t[0:32, 0:32])], reads=(nbm, ident_f), writes=(pb,))
            K.op("dve", lambda e, pb=pb: e.tensor_copy(out=nbr.t[:], in_=pb.t[:, 0:32]), (pb,), (nbr,))
            K.op("dve", lambda e: e.tensor_tensor(out=BS128.t[:], in0=endr.t[:], in1=nbr.t[:], op=ALU.subtract), (endr, nbr), (BS128,))
            K.op("dve", lambda e: e.tensor_scalar(out=BS128.t[:], in0=BS128.t[:], scalar1=128.0, scalar2=None, op0=ALU.mult), (BS128,), (BS128,))
            K.op("dve", lambda e: e.tensor_scalar(out=val.t[:], in0=ebr.t[:], scalar1=31.5, scalar2=None, op0=ALU.is_lt), (ebr,), (val,))
            K.op("dve", lambda e: e.tensor_scalar(out=eb.t[:], in0=ebr.t[:], scalar1=31.0, scalar2=None, op0=ALU.min), (ebr,), (eb,))
            K.op("dve", lambda e: e.tensor_tensor(out=tA.t[:], in0=iob.t[:], in1=stb.t[:], op=ALU.subtract), (iob, stb), (tA,))
            K.op("dve", lambda e: e.tensor_scalar(out=tA.t[:], in0=tA.t[:], scalar1=128.0, scalar2=float(-NE * CAP), op0=ALU.mult, op1=ALU.add), (tA,), (tA,))
            K.op("dve", lambda e: e.scalar_tensor_tensor(out=tA.t[:], in0=eb.t[:], scalar=float(CAP), in1=tA.t[:], op0=ALU.mult, op1=ALU.add), (eb, tA), (tA,))
            K.op("dve", lambda e: e.tensor_tensor(out=tA.t[:], in0=tA.t[:], in1=val.t[:], op=ALU.mult), (tA, val), (tA,))
            K.op("dve", lambda e: e.tensor_scalar(out=sbase.t[:], in0=tA.t[:], scalar1=pcol.t[:, 0:1], scalar2=float(NE * CAP), op0=ALU.add, op1=ALU.add), (tA, pcol), (sbase,))
            K.op("dve", lambda e: e.tensor_copy(out=RIDX.t[:], in_=sbase.t[:]), (sbase,), (RIDX,))
            K.op("dve", lambda e: e.memset(need.t[:], 1.0), (), (need,))
            K.op("dve", lambda e: e.tensor_tensor(out=need.t[:, 1:NBLK], in0=eb.t[:, 1:NBLK], in1=eb.t[:, 0:NBLK - 1], op=ALU.not_equal), (eb,), (need,))
            K.op("dve", lambda e: e.memset(need.t[:, NBLK // 2:NBLK // 2 + 1], 1.0), (), (need,))
            nn = K.sb(st, "nn", [128, NBLK], F32)
            K.op("dve", lambda e: e.tensor_scalar(out=nn.t[:], in0=need.t[:], scalar1=float(-BIG), scalar2=float(BIG), op0=ALU.mult, op1=ALU.add), (need,), (nn,))
            K.op("dve", lambda e: e.tensor_scalar(out=tB.t[:], in0=eb.t[:], scalar1=128.0, scalar2=pcol.t[:, 0:1], op0=ALU.mult, op1=ALU.add), (eb, pcol), (tB,))
            K.op("dve", lambda e: e.tensor_scalar(out=tB.t[:], in0=tB.t[:], scalar1=float(l * NE * 128), scalar2=None, op0=ALU.add), (tB,), (tB,))
            K.op("dve", lambda e: e.tensor_tensor(out=tB.t[:], in0=tB.t[:], in1=need.t[:], op=ALU.mult), (tB, need), (tB,))
            K.op("dve", lambda e: e.tensor_tensor(out=tB.t[:], in0=tB.t[:], in1=nn.t[:], op=ALU.add), (tB, nn), (tB,))
            K.op("dve", lambda e: e.tensor_copy(out=WIDX.t[:], in_=tB.t[:]), (tB,), (WIDX,))
            K.op("dve", lambda e: e.scalar_tensor_tensor(out=tB.t[:], in0=eb.t[:], scalar=float(l * NE), in1=need.t[:], op0=ALU.add, op1=ALU.mult), (eb, need), (tB,))
            K.op("dve", lambda e: e.tensor_tensor(out=tB.t[:], in0=tB.t[:], in1=nn.t[:], op=ALU.add), (tB, nn), (tB,))
            K.op("dve", lambda e: e.tensor_copy(out=BIDX.t[:], in_=tB.t[:]), (tB,), (BIDX,))
            K.barrier()

        if phases is None or "moe" in phases:
          with ExitStack() as st:
            Wg = [K.sb(st, f"Wg{i}", [128, 8 * 2048], BF16) for i in range(2)]
            Wd = [K.sb(st, f"Wd{i}", [128, 8 * 1024], BF16) for i in range(2)]
            Bg = [K.sb(st, f"Bg{i}", [128, 2048], BF16) for i in range(2)]
            Bd = [K.sb(st, f"Bd{i}", [128, 1024], BF16) for i in range(2)]
            rt = [K.sb(st, f"rt{i}", [128, 4], I32) for i in range(4)]
            xg = [K.sb(st, f"xg{i}", [128, 1024], BF16) for i in range(3)]
            xgT = [K.sb(st, f"xgT{i}", [128, 8, 128], BF16) for i in range(2)]
            gs = [K.sb(st, f"gs{i}", [128, 1024], F32) for i in range(2)]
            sg = [K.sb(st, f"sg{i}", [128, 1024], F32) for i in range(2)]
            u1 = [K.sb(st, f"u1{i}", [128, 1024], F32) for i in range(2)]
            hb = [K.sb(st, f"hb{i}", [128, 1024], BF16) for i in range(2)]
            hT = [K.sb(st, f"hT{i}", [128, 8, 128], BF16) for i in range(2)]
            ysb = [K.sb(st, f"ysb{i}", [128, 1024], F32) for i in range(2)]
            if "bcW" not in K.__dict__:
                rW = nc.gpsimd.alloc_register("bcW")
                nc.gpsimd.reg_mov(rW, NL * NE * 128 - 1)
                K.bcW = nc.gpsimd.snap(rW, donate=True)
                rB = nc.gpsimd.alloc_register("bcB")
                nc.gpsimd.reg_mov(rB, NL * NE - 1)
                K.bcB = nc.gpsimd.snap(rB, donate=True)
            wguv = w_gu.rearrange("l e (p k) f -> (l e p) (k f)", k=8)
            wdnv = w_dn.rearrange("l e (p k) f -> (l e p) (k f)", k=8)
            bguv = b_gu.rearrange("l e f -> (l e) f")
            bdnv = b_dn.rearrange("l e f -> (l e) f")
            HB = NBLK // 2

            def blk(s_):
                return s_ // 2 if s_ % 2 == 0 else HB + s_ // 2

            def wloadG(s_):
                if s_ >= NBLK:
                    return
                par, b = s_ % 2, blk(s_)
                K.dma("pool", Wg[par], lambda e: e.indirect_dma_start(out=Wg[par].t[:, :], out_offset=None, in_=wguv,
                      in_offset=bass.IndirectOffsetOnAxis(ap=WIDX.t[:, b:b + 1], axis=0), bounds_check=K.bcW, oob_is_err=False),
                      reads=(WIDX,), writes=(Wg[par],))
                K.dma("pool", Bg[par], lambda e: e.indirect_dma_start(out=Bg[par].t[:, :], out_offset=None, in_=bguv,
                      in_offset=bass.IndirectOffsetOnAxis(ap=BIDX.t[:, b:b + 1], axis=0), bounds_check=K.bcB, oob_is_err=False),
                      reads=(BIDX,), writes=(Bg[par],))

            def wloadD(s_):
                if s_ >= NBLK:
                    return
                par, b = s_ % 2, blk(s_)
                K.dma("pool", Wd[par], lambda e: e.indirect_dma_start(out=Wd[par].t[:, :], out_offset=None, in_=wdnv,
                      in_offset=bass.IndirectOffsetOnAxis(ap=WIDX.t[:, b:b + 1], axis=0), bounds_check=K.bcW, oob_is_err=False),
                      reads=(WIDX,), writes=(Wd[par],))
                K.dma("pool", Bd[par], lambda e: e.indirect_dma_start(out=Bd[par].t[:, :], out_offset=None, in_=bdnv,
                      in_offset=bass.IndirectOffsetOnAxis(ap=BIDX.t[:, b:b + 1], axis=0), bounds_check=K.bcB, oob_is_err=False),
                      reads=(BIDX,), writes=(Bd[par],))

            def rgath(s_):
                if s_ >= NBLK:
                    return
                r_, b = rt[s_ % 4], blk(s_)
                K.dma("pool", r_, lambda e: e.indirect_dma_start(out=r_.t[:, :], out_offset=None, in_=ROUTE[:, :],
                      in_offset=bass.IndirectOffsetOnAxis(ap=RIDX.t[:, b:b + 1], axis=0)), reads=(RIDX,), writes=(r_,))

            def xgath(s_):
                if s_ >= NBLK:
                    return
                r_, x_ = rt[s_ % 4], xg[s_ % 3]
                K.dma("pool", x_, lambda e: e.indirect_dma_start(out=x_.t[:, :], out_offset=None, in_=X1[:, :],
                      in_offset=bass.IndirectOffsetOnAxis(ap=r_.t[:, 0:1], axis=0)), reads=(r_,), writes=(x_,))

            pT_, pG, pH, pY = PSB[0], PSB[1:5], PSB[5], PSB[6:8]

            def S1a(s_):
                x_, xT_ = xg[s_ % 3], xgT[s_ % 2]
                xv = x_.t[:, :].rearrange("t (p k) -> t k p", k=8)
                pTb = pT_.t[:].bitcast(BF16)
                K.pe([lambda t, c=c: t.transpose(out=pTb[:, c * 128:(c + 1) * 128], in_=xv[:, c, :], identity=ident_b.t[:])
                      for c in range(8)], reads=(x_, ident_b), writes=(pT_,))
                K.op("act", lambda e: e.activation(out=xT_.t[:], in_=pTb[:, 0:1024].rearrange("p (c t) -> p c t", c=8), func=AF.Copy), (pT_,), (xT_,))

            def S1b(s_):
                par = s_ % 2
                xT_ = xgT[par]
                for n in range(4):
                    pb = pG[n]
                    pairs = [(xT_.t[:, kc, :], Wg[par].t[:, kc * 2048 + n * 512:kc * 2048 + (n + 1) * 512]) for kc in range(8)]
                    pairs.append((ones_b.t[0:1, :], Bg[par].t[0:1, n * 512:(n + 1) * 512]))
                    K.mm(pb.t[:, :], pairs, reads=(xT_, Wg[par], ones_b, Bg[par]), writes=(pb,))

            def S2a(s_):
                par = s_ % 2
                g_, u_ = gs[par], u1[par]
                for n in range(2):
                    K.op("dve", lambda e, n=n: e.tensor_scalar(out=g_.t[:, n * 512:(n + 1) * 512], in0=pG[n].t[:, :], scalar1=LIMIT, scalar2=None, op0=ALU.min), (pG[n],), (g_,))
                    K.op("dve", lambda e, n=n: e.tensor_scalar(out=u_.t[:, n * 512:(n + 1) * 512], in0=pG[2 + n].t[:, :], scalar1=LIMIT, scalar2=-LIMIT, op0=ALU.min, op1=ALU.max), (pG[2 + n],), (u_,))

            def S2b(s_):
                par, b = s_ % 2, blk(s_)
                g_, s2_, u_, h_, hT_, y_ = gs[par], sg[par], u1[par], hb[par], hT[par], ysb[par]
                K.op("act", lambda e: e.activation(out=s2_.t[:], in_=g_.t[:], func=AF.Sigmoid, scale=SW_ALPHA), (g_,), (s2_,))
                K.op("dve", lambda e: e.scalar_tensor_tensor(out=u_.t[:], in0=u_.t[:], scalar=1.0, in1=g_.t[:], op0=ALU.add, op1=ALU.mult), (u_, g_), (u_,))
                K.op("dve", lambda e: e.tensor_tensor(out=h_.t[:], in0=u_.t[:], in1=s2_.t[:], op=ALU.mult), (u_, s2_), (h_,))
                hv = h_.t[:, :].rearrange("t (p k) -> t k p", k=8)
                pTb = pH.t[:].bitcast(BF16)
                K.pe([lambda t, c=c: t.transpose(out=pTb[:, c * 128:(c + 1) * 128], in_=hv[:, c, :], identity=ident_b.t[:])
                      for c in range(8)], reads=(h_, ident_b), writes=(pH,))
                K.op("act", lambda e: e.activation(out=hT_.t[:], in_=pTb[:, 0:1024].rearrange("p (c t) -> p c t", c=8), func=AF.Copy), (pH,), (hT_,))
                for n in range(2):
                    pb = pY[n]
                    pairs = [(hT_.t[:, kc, :], Wd[par].t[:, kc * 1024 + n * 512:kc * 1024 + (n + 1) * 512]) for kc in range(8)]
                    pairs.append((ones_b.t[0:1, :], Bd[par].t[0:1, n * 512:(n + 1) * 512]))
                    K.mm(pb.t[:, :], pairs, reads=(hT_, Wd[par], ones_b, Bd[par]), writes=(pb,))
                    evac(n, y_.t[:, n * 512:(n + 1) * 512], pb.t[:, :], (pb,), (y_,))
                K.store("sp", y_, YB[b * 128:(b + 1) * 128, :], y_.t[:])

            wloadG(0); wloadG(1); wloadD(0); wloadD(1)
            rgath(0); rgath(1); rgath(2)
            xgath(0); xgath(1)
            S1a(0); S1b(0)
            wloadG(2)
            for s_ in range(NBLK):
                rgath(s_ + 3)
                xgath(s_ + 2)
                if s_ + 1 < NBLK:
                    S1a(s_ + 1)
                S2a(s_)
                if s_ + 1 < NBLK:
                    S1b(s_ + 1)
                    wloadG(s_ + 3)
                S2b(s_)
                wloadD(s_ + 2)
            K.barrier()

        if phases is None or "comb" in phases:
          with ExitStack() as st:
            g2b = K.sb(st, "g2b", [128, 1024], F32)
            b2b = K.sb(st, "b2b", [128, 1024], F32)
            yk = [[K.sb(st, f"yk{i}_{k}", [128, 1024], F32) for k in range(4)] for i in range(2)]
            x1t = [K.sb(st, f"x1t{i}", [128, 1024], F32) for i in range(2)]
            acc = [K.sb(st, f"acc{i}", [128, 1024], F32) for i in range(2)]
            xo = [K.sb(st, f"xo{i}", [128, 1024], F32) for i in range(2)]
            oh = [K.sb(st, f"oh{i}", [128, 32], F32) for i in range(2)]
            yrf = [K.sb(st, f"yrf{i}", [128, 4], F32) for i in range(2)]
            yri = [K.sb(st, f"yri{i}", [128, 4], I32) for i in range(2)]
            nmr = [K.sb(st, f"nmrc{i}", [128, 1], F32) for i in range(2)]
            stt = [K.sb(st, f"stt{i}", [128, 12], F32) for i in range(2)]
            mv = [K.sb(st, f"mv{i}", [128, 2], F32) for i in range(2)]
            rstd = [K.sb(st, f"rstd{i}", [128, 1], F32) for i in range(2)]
            K.load("sp", g2b, g2b.t[:], ln2_row[l][0:1, :].to_broadcast([128, 1024]))
            K.load("sp", b2b, b2b.t[:], ln2_row[l][1:2, :].to_broadcast([128, 1024]))

            def cload(tile):
                q = tile % 2
                for k in range(4):
                    K.op("dve", lambda e, k=k, q=q, tile=tile: e.tensor_scalar(out=oh[q].t[:], in0=iota32.t[:], scalar1=EKF.t[:, tile, k:k + 1], scalar2=None, op0=ALU.is_equal), (iota32, EKF), (oh[q],))
                    K.op("dve", lambda e, q=q: e.tensor_tensor(out=oh[q].t[:], in0=oh[q].t[:], in1=BS128.t[:], op=ALU.mult), (oh[q], BS128), (oh[q],))
                    K.op("dve", lambda e, k=k, q=q: e.tensor_reduce(out=yrf[q].t[:, k:k + 1], in_=oh[q].t[:], axis=AX.X, op=ALU.add), (oh[q],), (yrf[q],))
                K.op("dve", lambda e, q=q, tile=tile: e.tensor_tensor(out=yrf[q].t[:], in0=yrf[q].t[:], in1=POSK.t[:, tile, :], op=ALU.add), (yrf[q], POSK), (yrf[q],))
                K.op("dve", lambda e, q=q: e.tensor_copy(out=yri[q].t[:], in_=yrf[q].t[:]), (yrf[q],), (yri[q],))
                for k in range(4):
                    K.dma("pool", yk[q][k], lambda e, k=k, q=q: e.indirect_dma_start(out=yk[q][k].t[:, :], out_offset=None, in_=YB[:, :],
                          in_offset=bass.IndirectOffsetOnAxis(ap=yri[q].t[:, k:k + 1], axis=0)), reads=(yri[q],), writes=(yk[q][k],))
                K.load("sp", x1t[q], x1t[q].t[:], X1[tile * 128:(tile + 1) * 128, :])
            cload(0)
            for tile in range(NTILE):
                q = tile % 2
                if tile + 1 < NTILE:
                    cload(tile + 1)
                a_, o_ = acc[q], xo[q]
                K.op("dve", lambda e: e.tensor_scalar(out=a_.t[:], in0=yk[q][0].t[:], scalar1=GATES.t[:, tile, 0:1], scalar2=None, op0=ALU.mult), (yk[q][0], GATES), (a_,))
                for k in range(1, 4):
                    K.op("dve", lambda e, k=k: e.scalar_tensor_tensor(out=a_.t[:], in0=yk[q][k].t[:], scalar=GATES.t[:, tile, k:k + 1], in1=a_.t[:], op0=ALU.mult, op1=ALU.add), (yk[q][k], GATES, a_), (a_,))
                K.op("dve", lambda e: e.scalar_tensor_tensor(out=a_.t[:], in0=x1t[q].t[:], scalar=ALPHA, in1=a_.t[:], op0=ALU.mult, op1=ALU.add), (x1t[q], a_), (a_,))
                ln_stats(st, lambda c: a_.t[:, c * 512:(c + 1) * 512], a_, mv[q], stt[q], rstd[q])
                K.op("dve", lambda e: e.scalar_tensor_tensor(out=nmr[q].t[:], in0=mv[q].t[:, 0:1], scalar=-1.0, in1=rstd[q].t[:, 0:1], op0=ALU.mult, op1=ALU.mult), (mv[q], rstd[q]), (nmr[q],))
                K.op("act", lambda e: e.activation(out=a_.t[:], in_=a_.t[:], func=AF.Identity, scale=rstd[q].t[:, 0:1], bias=nmr[q].t[:, 0:1]), (a_, rstd[q], nmr[q]), (a_,))
                K.op("dve", lambda e: e.tensor_tensor(out=a_.t[:], in0=a_.t[:], in1=g2b.t[:], op=ALU.mult), (a_, g2b), (a_,))
                K.op("pool", lambda e: e.tensor_tensor(out=o_.t[:], in0=a_.t[:], in1=b2b.t[:], op=ALU.add), (a_, b2b), (o_,))
                K.store("sp", o_, XO[tile * 128:(tile + 1) * 128, :], o_.t[:])
            K.barrier()

    K.barrier()
    top.close()
    return nc, dict(NT=NT, NBLK=NBLK, CAP=CAP, NL=NL, NSEQ=NSEQ)


def host_inputs(params, NL, NT, NBLK, CAP):
    f = lambda a: np.ascontiguousarray(np.asarray(a, dtype=np.float32))
    d = {}
    d["w_in"] = f(params["w_in"][:NL])
    d["w_st"] = f(np.transpose(np.asarray(params["w_s"][:NL]), (0, 3, 1, 2)))
    d["w_pa"] = f(params["w_pa"][:NL])
    d["w_pb"] = f(params["w_pb"][:NL])
    d["w_o"] = f(params["w_o"][:NL])
    d["w_r"] = f(params["w_r"][:NL])
    d["w_gu"] = f(params["w_gu"][:NL])
    d["w_down"] = f(params["w_down"][:NL])
    d["b_gu"] = f(params["b_gu"][:NL])
    d["b_down"] = f(params["b_down"][:NL])
    b_in = np.asarray(params["b_in"][:NL], dtype=np.float32)
    d["bin_col"] = f(np.transpose(b_in.reshape(NL, 68, 128), (0, 2, 1)))
    d["bin_row"] = f(b_in.reshape(NL, 1, DIN))
    lg = np.asarray(params["ln_v_g"][:NL], dtype=np.float32).reshape(NL, 8, 128)
    lb = np.asarray(params["ln_v_b"][:NL], dtype=np.float32).reshape(NL, 8, 128)
    d["lnv_col"] = f(np.concatenate([np.transpose(lg, (0, 2, 1)), np.transpose(lb, (0, 2, 1))], axis=2))
    d["bs_row"] = f(np.asarray(params["b_s"][:NL]).reshape(NL, 1, 1024))
    d["ln1_row"] = f(np.stack([np.asarray(params["ln1_g"][:NL]), np.asarray(params["ln1_b"][:NL])], axis=1))
    d["ln2_row"] = f(np.stack([np.asarray(params["ln2_g"][:NL]), np.asarray(params["ln2_b"][:NL])], axis=1))
    d["br_row"] = f(np.asarray(params["b_r"][:NL]).reshape(NL, 1, NE))
    for k, v in _consts(NT, NBLK, CAP).items():
        d["c_" + k] = v
    return d


def kernel(x_prompt, x_sample, w_in, b_in, ln_v_g, ln_v_b, w_s, b_s, w_pa, w_pb, w_o, ln1_g, ln1_b,
           w_r, b_r, w_gu, b_gu, w_down, b_down, ln2_g, ln2_b):
    params = dict(w_in=w_in, b_in=b_in, ln_v_g=ln_v_g, ln_v_b=ln_v_b, w_s=w_s, b_s=b_s, w_pa=w_pa, w_pb=w_pb, w_o=w_o,
                  ln1_g=ln1_g, ln1_b=ln1_b, w_r=w_r, b_r=b_r, w_gu=w_gu, b_gu=b_gu, w_down=w_down, b_down=b_down,
                  ln2_g=ln2_g, ln2_b=ln2_b)
    NL, NSEQ = 4, 2
    nc, meta = build_program(NL=NL, NSEQ=NSEQ)
    shared = host_inputs(params, NL, meta["NT"], meta["NBLK"], meta["CAP"])
    xp = np.asarray(x_prompt, dtype=np.float32)
    xs = np.asarray(x_sample, dtype=np.float32)
    seqs = []
    for c in range(8):
        if c < 4:
            seqs.append(np.concatenate([xp[2 * c], xp[2 * c + 1]], axis=0))
        else:
            seqs.append(np.concatenate([xs[c - 4], xs[c - 4]], axis=0))
    in_maps = []
    for c in range(8):
        m = dict(shared)
        m["x"] = np.ascontiguousarray(seqs[c])
        in_maps.append(m)
    res = run_bass_kernel_spmd(nc, in_maps, core_ids=list(range(8)))
    yp = np.zeros_like(xp)
    ys = np.zeros_like(xs)
    for c in range(8):
        y = np.asarray(res.results[c]["y"], dtype=np.float32).reshape(2, S, D)
        if c < 4:
            yp[2 * c] = y[0]
            yp[2 * c + 1] = y[1]
        else:
            ys[c - 4] = y[0]
    return (yp, ys)
```

```python
import numpy as np
import ml_dtypes
from contextlib import ExitStack
import concourse.bass as bass
import concourse.mybir as mybir
from concourse.bass_utils import run_bass_kernel_spmd

F32 = mybir.dt.float32
BF16 = mybir.dt.bfloat16
I32 = mybir.dt.int32
AF = mybir.ActivationFunctionType
ALU = mybir.AluOpType
AX = mybir.AxisListType

S = 4096
D = 1024
DIN = 8704
NE = 32
TOPK = 4
PATTERNS = ((128, 1), (512, 4), (2048, 16))
NHEAD = 12
SLOPES = np.array([2.0 ** (-8.0 * (h + 1) / NHEAD) for h in range(NHEAD)], dtype=np.float32).reshape(3, 4)
DEPTH_FULL = 4
ALPHA = (2.0 * DEPTH_FULL) ** 0.25
EPS = 1e-5
NEG = -1e30
LIMIT = 7.0
SW_ALPHA = 1.702
SCALE = 128.0 ** -0.5
BIG = 1 << 28


class Buf:
    def __init__(self, t, name):
        self.t = t
        self.name = name
        self.lw = None
        self.rd = {}
        self.ds = None

    def __getitem__(self, idx):
        return self.t[idx]


class DSem:
    def __init__(self, h, key):
        self.h = h
        self.key = key
        self.cnt = 0


class KB:
    SAME_ENG = True

    def __init__(self, nc):
        self.nc = nc
        self.E = {"pe": nc.tensor, "act": nc.scalar, "dve": nc.vector, "pool": nc.gpsimd, "sp": nc.sync}
        self.sem = {k: nc.alloc_semaphore("es_" + k) for k in self.E}
        self.cnt = {k: 0 for k in self.E}
        self.seen = {k: {} for k in self.E}
        self.semobj = dict(self.sem)
        self.dsems = {}
        self.dfree = []
        self.uid = 0
        self.psrr = 0

    def sb(self, stack, name, shape, dt):
        self.uid += 1
        t = stack.enter_context(self.nc.sbuf_tensor(f"{name}_{self.uid}", list(shape), dt))
        b = Buf(t, name)
        stack.callback(self._release, b)
        return b

    def ps(self, stack, name, shape, dt):
        self.uid += 1
        t = stack.enter_context(self.nc.psum_tensor(f"{name}_{self.uid}", list(shape), dt))
        return Buf(t, name)

    def _release(self, b):
        if b.ds is not None:
            self.dfree.append(b.ds)
            b.ds = None

    def _getds(self, b):
        if b.ds is None:
            if self.dfree:
                b.ds = self.dfree.pop()
            else:
                key = f"d{len(self.dsems)}"
                d = DSem(self.nc.alloc_semaphore("ds_" + key), key)
                self.dsems[key] = d
                self.semobj[key] = d.h
                b.ds = d
        return b.ds

    def _deps(self, reads, writes):
        d = []
        for b in reads:
            if b.lw is not None:
                d.append(b.lw)
        for b in writes:
            if b.lw is not None:
                d.append(b.lw)
            d.extend(b.rd.items())
        return d

    def _wait(self, eng, deps):
        need = {}
        for k, v in deps:
            if k == eng and (eng == "pe" or not self.SAME_ENG):
                continue
            if need.get(k, 0) < v:
                need[k] = v
        for k, v in need.items():
            if self.seen[eng].get(k, 0) < v:
                self.E[eng].wait_ge(self.semobj[k], v)
                self.seen[eng][k] = v

    def _mark(self, tok, reads, writes):
        k, v = tok
        for b in writes:
            b.lw = tok
            b.rd = {}
        for b in reads:
            if b in writes:
                continue
            if b.rd.get(k, 0) < v:
                b.rd[k] = v

    def op(self, eng, fn, reads=(), writes=()):
        self._wait(eng, self._deps(reads, writes))
        ins = fn(self.E[eng])
        self.cnt[eng] += 1
        ins.then_inc(self.sem[eng], 1)
        tok = (eng, self.cnt[eng])
        self._mark(tok, reads, writes)
        return tok

    def pe(self, fns, reads=(), writes=()):
        self._wait("pe", self._deps(reads, writes))
        ins = None
        for f in fns:
            ins = f(self.nc.tensor)
        self.cnt["pe"] += 1
        ins.then_inc(self.sem["pe"], 1)
        tok = ("pe", self.cnt["pe"])
        self._mark(tok, reads, writes)
        return tok

    def mm(self, out, pairs, reads=(), writes=()):
        n = len(pairs)
        fns = []
        for i, (l, r) in enumerate(pairs):
            fns.append(lambda t, l=l, r=r, i=i: t.matmul(out, lhsT=l, rhs=r, start=(i == 0), stop=(i == n - 1)))
        return self.pe(fns, reads, writes)

    def dma(self, q, sbuf, fn, reads=(), writes=()):
        self._wait(q, self._deps(reads, writes))
        d = self._getds(sbuf)
        ins = fn(self.E[q])
        d.cnt += 16
        ins.then_inc(d.h, 16)
        tok = (d.key, d.cnt)
        self._mark(tok, reads, writes)
        return tok

    def load(self, q, dst, dst_ap, src_ap, **kw):
        return self.dma(q, dst, lambda e: e.dma_start(out=dst_ap, in_=src_ap, **kw), reads=(), writes=(dst,))

    def store(self, q, src, dst_ap, src_ap, **kw):
        return self.dma(q, src, lambda e: e.dma_start(out=dst_ap, in_=src_ap, **kw), reads=(src,), writes=())

    def barrier(self):
        tot = dict(self.cnt)
        for k, d in self.dsems.items():
            tot[k] = d.cnt
        for e in self.E:
            for k, v in tot.items():
                if v > 0 and self.seen[e].get(k, 0) < v:
                    self.E[e].wait_ge(self.semobj[k], v)
                    self.seen[e][k] = v

    def nextps(self, banks):
        b = banks[self.psrr % len(banks)]
        self.psrr += 1
        return b


def _consts(NT, NBLK, CAP):
    c = {}
    c["ident_f"] = np.eye(128, dtype=np.float32)
    c["ident_b"] = np.eye(128, dtype=np.float32).astype(ml_dtypes.bfloat16)
    tri = (np.arange(128)[:, None] < np.arange(128)[None, :]).astype(np.float32)
    c["tri_b"] = tri.astype(ml_dtypes.bfloat16)
    c["ones_b"] = np.ones((128, 128), dtype=ml_dtypes.bfloat16)
    c["ones_f"] = np.ones((128, 128), dtype=np.float32)
    tab = np.zeros((3, 128, 12, 256), dtype=np.float32)
    a = np.arange(128)[:, None]
    cc = np.arange(256)[None, :]
    rel = cc - 64 - a
    for g, (win, d) in enumerate(PATTERNS):
        for h in range(4):
            base = np.where(np.abs(rel) <= 64, -SLOPES[g, h] * d * np.abs(rel), NEG).astype(np.float32)
            for var in range(3):
                t = base.copy()
                if var == 1:
                    t[:, :64] = NEG
                if var == 2:
                    t[:, 192:] = NEG
                tab[g, :, h * 3 + var, :] = t
    c["att_bias"] = tab
    c["iota32"] = np.tile(np.arange(32, dtype=np.float32)[None, :], (128, 1))
    c["ecap"] = np.tile((np.arange(32, dtype=np.float32) * CAP)[None, :], (128, 1))
    c["pcol"] = np.arange(128, dtype=np.float32)[:, None].copy()
    c["jthr"] = np.tile((np.arange(64, dtype=np.float32) * 128.0)[None, :], (32, 1))
    c["iota_b"] = np.tile(np.arange(NBLK, dtype=np.float32)[None, :], (128, 1))
    c["triu32"] = (np.arange(32)[:, None] <= np.arange(32)[None, :]).astype(np.float32)
    tok = np.zeros((128, NT // 128, 4), dtype=np.int32)
    tok[:, :, 0] = np.arange(NT // 128)[None, :] * 128 + np.arange(128)[:, None]
    c["tokid4"] = tok
    ri = np.zeros((128, 128, 4), dtype=np.int32)
    ri[:, :, 0] = NT
    c["rinit"] = ri
    return c


CONST_DT = {"ident_f": F32, "ident_b": BF16, "tri_b": BF16, "ones_b": BF16, "ones_f": F32, "att_bias": F32,
            "iota32": F32, "ecap": F32, "pcol": F32, "jthr": F32, "iota_b": F32, "triu32": F32,
            "tokid4": I32, "rinit": I32}


def build_program(NL=4, NSEQ=2, debug=False, phases=None):
    NT = NSEQ * S
    NTILE = NT // 128
    CAP = NT
    NBLK = NT * TOPK // 128 + NE
    RROWS = NE * CAP + 128
    nc = bass.Bass("TRN2", target_bir_lowering=False)
    K = KB(nc)
    sk = "ExternalOutput" if debug else "Internal"

    def din(name, shape, dt=F32):
        return nc.dram_tensor(name, list(shape), dt, kind="ExternalInput").ap()

    def dscr(name, shape, dt=F32):
        return nc.dram_tensor(name, list(shape), dt, kind=sk).ap()

    x_in = din("x", [NT, D])
    w_in = din("w_in", [NL, D, DIN])
    w_st = din("w_st", [NL, 128, 8, 128])
    w_pa = din("w_pa", [NL, D, D])
    w_pb = din("w_pb", [NL, 512, D])
    w_o = din("w_o", [NL, D, D])
    w_r = din("w_r", [NL, D, NE])
    w_gu = din("w_gu", [NL, NE, D, 2 * D])
    w_dn = din("w_down", [NL, NE, D, D])
    b_gu = din("b_gu", [NL, NE, 2 * D])
    b_dn = din("b_down", [NL, NE, D])
    bin_col = din("bin_col", [NL, 128, 68])
    bin_row = din("bin_row", [NL, 1, DIN])
    lnv_col = din("lnv_col", [NL, 128, 16])
    bs_row = din("bs_row", [NL, 1, D])
    ln1_row = din("ln1_row", [NL, 2, D])
    ln2_row = din("ln2_row", [NL, 2, D])
    br_row = din("br_row", [NL, 1, NE])
    cshape = {k: v.shape for k, v in _consts(128 * 2, 8, 1).items()}
    cshape["iota_b"] = (128, NBLK)
    cshape["tokid4"] = (128, NTILE, 4)
    cin = {k: din("c_" + k, cshape[k], CONST_DT[k]) for k in cshape}
    y_out = nc.dram_tensor("y", [NT, D], F32, kind="ExternalOutput").ap()

    XS = dscr("XS", [NT, D])
    X1 = dscr("X1", [NT + 128, D])
    NTP = [NSEQ * d * (S // d + 128) for (_, d) in PATTERNS]
    QT = [dscr(f"QT{g}", [4, 128, NTP[g]], BF16) for g in range(3)]
    KT = [dscr(f"KT{g}", [4, 128, NTP[g]], BF16) for g in range(3)]
    VV = [dscr(f"VV{g}", [NTP[g], 512], BF16) for g in range(3)]
    OGall = dscr("OG", [3, NT, 516])
    OG = [OGall[g] for g in range(3)]
    MA = dscr("MA", [8, 128, NT], BF16)
    ROUTE = dscr("ROUTE", [RROWS, 4], I32)
    YB = dscr("YB", [NBLK * 128, D])

    top = ExitStack()
    ident_f = K.sb(top, "ident_f", [128, 128], F32)
    ident_b = K.sb(top, "ident_b", [128, 128], BF16)
    ones_b = K.sb(top, "ones_b", [128, 128], BF16)
    ones_f = K.sb(top, "ones_f", [128, 128], F32)
    tri_b = K.sb(top, "tri_b", [128, 128], BF16)
    iota32 = K.sb(top, "iota32", [128, 32], F32)
    ecap = K.sb(top, "ecap", [128, 32], F32)
    pcol = K.sb(top, "pcol", [128, 1], F32)
    tokid4 = K.sb(top, "tokid4", [128, NTILE, 4], I32)
    GATES = K.sb(top, "GATES", [128, NTILE, 4], F32)
    EKF = K.sb(top, "EKF", [128, NTILE, 4], F32)
    POSK = K.sb(top, "POSK", [128, NTILE, 4], F32)
    RUN = K.sb(top, "RUN", [128, 32], F32)
    BS128 = K.sb(top, "BS128", [128, 32], F32)
    RIDX = K.sb(top, "RIDX", [128, NBLK], I32)
    WIDX = K.sb(top, "WIDX", [128, NBLK], I32)
    BIDX = K.sb(top, "BIDX", [128, NBLK], I32)
    PSB = [K.ps(top, f"psb{i}", [128, 512], F32) for i in range(8)]

    for nm, b in (("ident_f", ident_f), ("ident_b", ident_b), ("ones_b", ones_b), ("ones_f", ones_f), ("tri_b", tri_b),
                  ("iota32", iota32), ("ecap", ecap), ("pcol", pcol), ("tokid4", tokid4)):
        K.load("sp", b, b.t[:], cin[nm])

    def evac(i, out_ap, in_ap, reads, writes):
        if i % 2 == 0:
            return K.op("act", lambda e: e.activation(out=out_ap, in_=in_ap, func=AF.Copy), reads, writes)
        return K.op("dve", lambda e: e.tensor_copy(out=out_ap, in_=in_ap), reads, writes)

    def load_xT(st, xt, xT, src_rows_ap, TM):
        nj = TM // 128
        K.load("sp", xt, xt.t[:, 0:nj, :], src_rows_ap)

    def transposes_x(xt, xT, TM, evi=0, act_only=False):
        nj = TM // 128
        for kc in range(8):
            pb = K.nextps(PSB)
            fns = [lambda t, j=j, kc=kc, pb=pb: t.transpose(out=pb.t[:, j * 128:(j + 1) * 128], in_=xt.t[:, j, kc * 128:(kc + 1) * 128], identity=ident_f.t[:])
                   for j in range(nj)]
            K.pe(fns, reads=(xt, ident_f), writes=(pb,))
            evac(0 if act_only else kc + evi, xT.t[:, kc, 0:TM], pb.t[:, 0:TM], (pb,), (xT,))

    def transposes_xb(xtb, xT, TM):
        nj = TM // 128
        for k2 in range(4):
            pb = K.nextps(PSB)
            pbb = pb.t[:].bitcast(BF16)
            fns = [lambda t, j=j, kk=kk, pbb=pbb: t.transpose(out=pbb[:, kk * 512 + j * 128:kk * 512 + (j + 1) * 128],
                                                             in_=xtb.t[:, j, (k2 * 2 + kk) * 128:(k2 * 2 + kk + 1) * 128], identity=ident_b.t[:])
                   for kk in range(2) for j in range(nj)]
            K.pe(fns, reads=(xtb, ident_b), writes=(pb,))
            src = pbb[:, 0:1024].rearrange("p (k t) -> p k t", k=2)[:, :, 0:TM]
            evac(k2, xT.t[:, k2 * 2:k2 * 2 + 2, 0:TM], src, (pb,), (xT,))

    def ln_stats(st, v_ap_fn, vbuf, mv, stt, rstd):
        K.op("dve", lambda e: e.bn_stats(out=stt.t[:, 0:6], in_=v_ap_fn(0)), (vbuf,), (stt,))
        K.op("dve", lambda e: e.bn_stats(out=stt.t[:, 6:12], in_=v_ap_fn(1)), (vbuf,), (stt,))
        K.op("dve", lambda e: e.bn_aggr(out=mv.t[:, 0:2], in_=stt.t[:, 0:12].rearrange("p (c s) -> p c s", s=6)), (stt,), (mv,))
        K.op("act", lambda e: e.activation(out=rstd.t[:, 0:1], in_=mv.t[:, 1:2], func=AF.Sqrt, bias=EPS), (mv,), (rstd,))
        K.op("dve", lambda e: e.reciprocal(out=rstd.t[:, 0:1], in_=rstd.t[:, 0:1]), (rstd,), (rstd,))

    with ExitStack() as st:
        zt = K.sb(st, "zt", [128, 4096], BF16)
        K.op("pool", lambda e: e.memset(zt.t[:], 0.0), (), (zt,))
        for g in range(3):
            n = NTP[g]
            for h in range(4):
                for c0 in range(0, n, 4096):
                    w = min(4096, n - c0)
                    K.store("sp", zt, KT[g][h, :, c0:c0 + w], zt.t[:, 0:w])
            vv = VV[g].rearrange("(a p) f -> p a f", p=128)
            na = n // 128
            for a0 in range(0, na, 8):
                w = min(8, na - a0)
                K.store("sp", zt, vv[:, a0:a0 + w, :], zt.t[:, 0:w * 512].rearrange("p (a f) -> p a f", f=512))
        K.store("sp", zt, X1[NT:NT + 128, :], zt.t[:].bitcast(F32)[:, 0:1024])
        K.barrier()

    for l in range(NL):
        XA = x_in if l == 0 else XS
        XO = y_out if l == NL - 1 else XS
        if phases is None or "qkv" in phases:
          with ExitStack() as st:
            wg = [K.sb(st, f"wg{i}", [128, 8, 1536], BF16) for i in range(2)]
            bcol = K.sb(st, "bcol", [128, 68], F32)
            brow = K.sb(st, "brow", [1, DIN], BF16)
            xts = [K.sb(st, f"xt{i}", [128, 4, D], BF16) for i in range(2)]
            xTs = [K.sb(st, f"xT{i}", [128, 8, 512], BF16) for i in range(2)]
            qks = [K.sb(st, f"qk{i}", [128, 8, 512], BF16) for i in range(2)]
            vss = [K.sb(st, f"vs{i}", [128, 4, 512], BF16) for i in range(2)]
            K.load("sp", bcol, bcol.t[:], bin_col[l])
            K.load("pool", brow, brow.t[:], bin_row[l])
            it = 0
            for g, (win, d) in enumerate(PATTERNS):
                L = S // d
                LP = L + 128
                TM = min(512, L)
                nj = TM // 128
                c0 = 2048 + g * 1536
                W = wg[g % 2]
                K.load("pool", W, W.t[:], w_in[l][:, c0:c0 + 1536].rearrange("(kc p) c -> p kc c", p=128))
                XAv = XA.rearrange("(s i dd) c -> s dd i c", s=NSEQ, dd=d)
                tiles = [(s, r, m) for s in range(NSEQ) for r in range(d) for m in range(L // TM)]

                def src(tl):
                    s, r, m = tl
                    return XAv[s, r, m * TM:(m + 1) * TM].rearrange("(j a) c -> a j c", a=128)
                K.load("pool", xts[it % 2], xts[it % 2].t[:, 0:nj, :], src(tiles[0]))
                for ti, tl in enumerate(tiles):
                    s, r, m = tl
                    xt, xT, qk, vs = xts[it % 2], xTs[it % 2], qks[it % 2], vss[it % 2]
                    if ti + 1 < len(tiles):
                        nx = xts[(it + 1) % 2]
                        K.load("pool", nx, nx.t[:, 0:nj, :], src(tiles[ti + 1]))
                    transposes_xb(xt, xT, TM)
                    col0 = (s * d + r) * LP + 64 + m * TM
                    for hs in range(8):
                        pb = K.nextps(PSB)
                        K.mm(pb.t[:, 0:TM], [(W.t[:, kc, hs * 128:(hs + 1) * 128], xT.t[:, kc, 0:TM]) for kc in range(8)],
                             reads=(W, xT), writes=(pb,))
                        bc = (c0 // 128) + hs
                        K.op("act", lambda e, pb=pb, hs=hs, bc=bc: e.activation(out=qk.t[:, hs, 0:TM], in_=pb.t[:, 0:TM], func=AF.Identity,
                                                                              bias=bcol.t[:, bc:bc + 1]),
                             (pb, bcol), (qk,))
                    K.store("sp", qk, QT[g][:, :, col0:col0 + TM].rearrange("h p c -> p h c"), qk.t[:, 0:4, 0:TM])
                    K.store("sp", qk, KT[g][:, :, col0:col0 + TM].rearrange("h p c -> p h c"), qk.t[:, 4:8, 0:TM])
                    for j in range(nj):
                        pb = K.nextps(PSB)
                        pairs = [(xT.t[:, kc, j * 128:(j + 1) * 128], W.t[:, kc, 1024:1536]) for kc in range(8)]
                        pairs.append((ones_b.t[0:1, :], brow.t[0:1, c0 + 1024:c0 + 1536]))
                        K.mm(pb.t[:, :], pairs, reads=(W, xT, ones_b, brow), writes=(pb,))
                        evac(j, vs.t[:, j, :], pb.t[:, :], (pb,), (vs,))
                    K.store("sp", vs, VV[g][col0:col0 + TM, :].rearrange("(j p) f -> p j f", p=128), vs.t[:, 0:nj, :])
                    it += 1
            K.barrier()

        if phases is None or "att" in phases:
          with ExitStack() as st:
            bts = [K.sb(st, f"bt{i}", [128, 12, 256], F32) for i in range(3)]
            qcs = [K.sb(st, f"qc{i}", [128, 4, 1024], BF16) for i in range(2)]
            kcs = [K.sb(st, f"kc{i}", [128, 4, 1152], BF16) for i in range(2)]
            vcs = [K.sb(st, f"vc{i}", [128, 9, 512], BF16) for i in range(2)]
            NB_ = 3
            ogs = [K.sb(st, f"og{i}", [128, 516], F32) for i in range(NB_)]
            nmx = [K.sb(st, f"nmx{i}", [128, 4], F32) for i in range(NB_)]
            den = [K.sb(st, f"den{i}", [128, 4], F32) for i in range(NB_)]
            rdn = [K.sb(st, f"rdn{i}", [128, 4], F32) for i in range(NB_)]
            lnd = [K.sb(st, f"lnd{i}", [128, 4], F32) for i in range(NB_)]
            NU_ = 4
            sbs = [K.sb(st, f"sS{i}", [128, 256], F32) for i in range(NU_)]
            pbs = [K.sb(st, f"sP{i}", [128, 256], BF16) for i in range(NU_)]
            pts = [K.sb(st, f"sPT{i}", [128, 256], BF16) for i in range(NU_)]
            for g in range(3):
                K.load("sp", bts[g], bts[g].t[:], cin["att_bias"][g])
            chunks = []
            for g, (win, d) in enumerate(PATTERNS):
                L = S // d
                CH = min(1024, L)
                for s_ in range(NSEQ):
                    for r in range(d):
                        for ch in range(L // CH):
                            chunks.append((g, d, L, CH, s_, r, ch))

            def ldchunk(ci):
                g, d, L, CH, s_, r, ch = chunks[ci]
                LP = L + 128
                nb = CH // 128
                col0 = (s_ * d + r) * LP + 64 + ch * CH
                qc, kc_, vc = qcs[ci % 2], kcs[ci % 2], vcs[ci % 2]
                K.load("sp", qc, qc.t[:, :, 0:CH], QT[g][:, :, col0:col0 + CH].rearrange("h p c -> p h c"))
                K.load("sp", kc_, kc_.t[:, :, 0:CH + 128], KT[g][:, :, col0 - 64:col0 + CH + 64].rearrange("h p c -> p h c"))
                K.load("sp", vc, vc.t[:, 0:nb + 1, :], VV[g][col0 - 64:col0 + CH + 64, :].rearrange("(c p) f -> p c f", p=128))

            units = []
            bidx = 0
            for ci, (g, d, L, CH, s_, r, ch) in enumerate(chunks):
                nb = CH // 128
                nblk = L // 128
                for i in range(nb):
                    blk = ch * nb + i
                    var = 1 if blk == 0 else (2 if blk == nblk - 1 else 0)
                    for h in range(4):
                        units.append(dict(ci=ci, g=g, d=d, s=s_, r=r, i=i, h=h, var=var, b=bidx, p0=ch * CH + i * 128,
                                          lastc=(i == nb - 1 and h == 3)))
                    bidx += 1
            NU = len(units)
            state = {}

            def stA(u):
                U = units[u]
                qc, kc_ = qcs[U["ci"] % 2], kcs[U["ci"] % 2]
                bt = bts[U["g"]]
                i, h, var = U["i"], U["h"], U["var"]
                sS, sP = sbs[u % NU_], pbs[u % NU_]
                nm_, dn_ = nmx[U["b"] % NB_], den[U["b"] % NB_]
                pS = K.nextps(PSB)
                K.mm(pS.t[:, 0:256], [(qc.t[:, h, i * 128:(i + 1) * 128], kc_.t[:, h, i * 128:i * 128 + 256])], reads=(qc, kc_), writes=(pS,))
                K.op("dve", lambda e: e.scalar_tensor_tensor(out=sS.t[:], in0=pS.t[:, 0:256], scalar=SCALE, in1=bt.t[:, h * 3 + var, :], op0=ALU.mult, op1=ALU.add),
                     (pS, bt), (sS,))
                K.op("dve", lambda e: e.tensor_reduce(out=nm_.t[:, h:h + 1], in_=sS.t[:], axis=AX.X, op=ALU.max, negate=True), (sS,), (nm_,))
                K.op("act", lambda e: e.activation(out=sP.t[:], in_=sS.t[:], func=AF.Exp, bias=nm_.t[:, h:h + 1], accum_out=dn_.t[:, h:h + 1]),
                     (sS, nm_), (sP, dn_))

            def stB(u):
                sP, sPT = pbs[u % NU_], pts[u % NU_]
                pT = K.nextps(PSB)
                pTb = pT.t[:].bitcast(BF16)
                K.pe([lambda t, c=c: t.transpose(out=pTb[:, c * 128:(c + 1) * 128], in_=sP.t[:, c * 128:(c + 1) * 128], identity=ident_b.t[:])
                      for c in range(2)], reads=(sP, ident_b), writes=(pT,))
                K.op("act", lambda e: e.activation(out=sPT.t[:], in_=pTb[:, 0:256], func=AF.Copy), (pT,), (sPT,))

            def stC(u):
                U = units[u]
                vc = vcs[U["ci"] % 2]
                i, h = U["i"], U["h"]
                sPT = pts[u % NU_]
                q_ = U["b"] % NB_
                og, nm_, dn_, rd_, ld_ = ogs[q_], nmx[q_], den[q_], rdn[q_], lnd[q_]
                pO = K.nextps(PSB)
                K.mm(pO.t[:, 0:128], [(sPT.t[:, c * 128:(c + 1) * 128], vc.t[:, i + c, h * 128:(h + 1) * 128]) for c in range(2)], reads=(sPT, vc), writes=(pO,))
                K.op("dve", lambda e: e.reciprocal(out=rd_.t[:, h:h + 1], in_=dn_.t[:, h:h + 1]), (dn_,), (rd_,))
                K.op("act", lambda e: e.activation(out=og.t[:, h * 128:(h + 1) * 128], in_=pO.t[:, 0:128], func=AF.Identity, scale=rd_.t[:, h:h + 1]),
                     (pO, rd_), (og,))
                if h == 3:
                    K.op("act", lambda e: e.activation(out=ld_.t[:], in_=dn_.t[:], func=AF.Ln), (dn_,), (ld_,))
                    K.op("dve", lambda e: e.tensor_tensor(out=og.t[:, 512:516], in0=ld_.t[:], in1=nm_.t[:], op=ALU.subtract), (ld_, nm_), (og,))
                    XOv = OG[U["g"]].rearrange("(s i dd) c -> s dd i c", s=NSEQ, dd=U["d"])
                    K.store("sp", og, XOv[U["s"], U["r"], U["p0"]:U["p0"] + 128, :], og.t[:])
                if U["lastc"] and U["ci"] + 2 < len(chunks):
                    ldchunk(U["ci"] + 2)

            ldchunk(0)
            if len(chunks) > 1:
                ldchunk(1)
            for idx in range(NU + 2):
                if idx < NU:
                    stA(idx)
                if 1 <= idx <= NU:
                    stB(idx - 1)
                if idx >= 2:
                    stC(idx - 2)
            K.barrier()

        if phases is None or "pc1" in phases:
          with ExitStack() as st:
            Wu = K.sb(st, "Wu", [128, 8, 1024], BF16)
            Wv = K.sb(st, "Wv", [128, 8, 1024], BF16)
            Wga = K.sb(st, "Wga", [128, 8, 1024], BF16)
            Wpa = K.sb(st, "Wpa", [128, 8, 1024], BF16)
            Wst = K.sb(st, "Wst", [128, 8, 128], BF16)
            bcol = K.sb(st, "bcol", [128, 68], F32)
            brow = K.sb(st, "brow", [1, 1024], BF16)
            lnv = K.sb(st, "lnv", [128, 16], F32)
            bsb = K.sb(st, "bsb", [128, 8, 128], F32)
            Cgt = K.sb(st, "Cgt", [128, 8, 128], F32)
            xts = [K.sb(st, f"xt{i}", [128, 4, D], BF16) for i in range(2)]
            xT = K.sb(st, "xT", [128, 8, 512], BF16)
            uT = K.sb(st, "uT", [128, 8, 512], BF16)
            gaT = K.sb(st, "gaT", [128, 8, 512], BF16)
            aoT = K.sb(st, "aoT", [128, 8, 512], BF16)
            maTs = [K.sb(st, f"maT{i}", [128, 8, 512], BF16) for i in range(2)]
            vsb = [K.sb(st, f"v{i}", [128, 1024], F32) for i in range(2)]
            nsb = [K.sb(st, f"n{i}", [128, 1024], BF16) for i in range(2)]
            tmpa = [K.sb(st, f"tmpa{i}", [128, 8, 128], F32) for i in range(2)]
            stt = [K.sb(st, f"stt{i}", [128, 12], F32) for i in range(2)]
            mv = [K.sb(st, f"mv{i}", [128, 2], F32) for i in range(2)]
            rstd = [K.sb(st, f"rstd{i}", [128, 1], F32) for i in range(2)]
            wsrc = w_in[l]
            K.load("pool", Wu, Wu.t[:], wsrc[:, 0:1024].rearrange("(kc p) c -> p kc c", p=128))
            K.load("pool", Wv, Wv.t[:], wsrc[:, 1024:2048].rearrange("(kc p) c -> p kc c", p=128))
            K.load("pool", Wga, Wga.t[:], wsrc[:, 6656:7680].rearrange("(kc p) c -> p kc c", p=128))
            K.load("pool", Wpa, Wpa.t[:], w_pa[l].rearrange("(kc p) c -> p kc c", p=128))
            K.load("pool", Wst, Wst.t[:], w_st[l])
            K.load("pool", brow, brow.t[:], bin_row[l][:, 1024:2048])
            K.load("sp", bcol, bcol.t[:], bin_col[l])
            K.load("sp", lnv, lnv.t[:], lnv_col[l])
            K.load("sp", bsb, bsb.t[:], bs_row[l].rearrange("o (g t) -> o g t", g=8).to_broadcast([128, 8, 128]))
            for half in range(2):
                pb = K.nextps(PSB)
                K.mm(pb.t[:, :], [(ones_b.t[:, :], Wst.t[:, half * 4:(half + 1) * 4, :])], reads=(ones_b, Wst), writes=(pb,))
                for gg in range(4):
                    g_ = half * 4 + gg
                    K.op("dve", lambda e, pb=pb, gg=gg, g_=g_: e.scalar_tensor_tensor(
                        out=Cgt.t[:, g_, :], in0=pb.t[:, gg * 128:(gg + 1) * 128], scalar=lnv.t[:, 8 + g_:9 + g_], in1=bsb.t[:, g_, :],
                        op0=ALU.mult, op1=ALU.add), (pb, lnv, bsb), (Cgt,))
            XAt = XA.rearrange("(m j a) c -> m a j c", j=4, a=128)
            NM = NT // 512
            K.load("pool", xts[0], xts[0].t[:], XAt[0])
            sj = 0
            for m in range(NM):
                xt = xts[m % 2]
                maT = maTs[m % 2]
                if m + 1 < NM:
                    K.load("pool", xts[(m + 1) % 2], xts[(m + 1) % 2].t[:], XAt[m + 1])
                transposes_xb(xt, xT, 512)
                for c8 in range(8):
                    pb = K.nextps(PSB)
                    K.mm(pb.t[:, :], [(Wu.t[:, kc, c8 * 128:(c8 + 1) * 128], xT.t[:, kc, :]) for kc in range(8)], reads=(Wu, xT), writes=(pb,))
                    K.op("act", lambda e, pb=pb, c8=c8: e.activation(out=uT.t[:, c8, :], in_=pb.t[:, :], func=AF.Gelu, bias=bcol.t[:, c8:c8 + 1]),
                         (pb, bcol), (uT,))
                for j in range(4):
                    v_, n_, ta, st_, mv_, rs_ = vsb[sj % 2], nsb[sj % 2], tmpa[sj % 2], stt[sj % 2], mv[sj % 2], rstd[sj % 2]
                    for nh in range(2):
                        pb = K.nextps(PSB)
                        pairs = [(xT.t[:, kc, j * 128:(j + 1) * 128], Wv.t[:, kc, nh * 512:(nh + 1) * 512]) for kc in range(8)]
                        pairs.append((ones_b.t[0:1, :], brow.t[0:1, nh * 512:(nh + 1) * 512]))
                        K.mm(pb.t[:, :], pairs, reads=(xT, Wv, ones_b, brow), writes=(pb,))
                        K.op("act", lambda e, pb=pb, nh=nh, v_=v_: e.activation(out=v_.t[:, nh * 512:(nh + 1) * 512], in_=pb.t[:, :], func=AF.Gelu),
                             (pb,), (v_,))
                    ln_stats(st, lambda c, v_=v_: v_.t[:, c * 512:(c + 1) * 512], v_, mv_, st_, rs_)
                    K.op("dve", lambda e, v_=v_, n_=n_, mv_=mv_, rs_=rs_: e.tensor_scalar(
                        out=n_.t[:], in0=v_.t[:], scalar1=mv_.t[:, 0:1], scalar2=rs_.t[:, 0:1], op0=ALU.subtract, op1=ALU.mult),
                        (v_, mv_, rs_), (n_,))
                    for half in range(2):
                        pb = K.nextps(PSB)
                        fns = [lambda t, gg=gg, half=half, pb=pb, n_=n_: t.matmul(
                            pb.t[:, gg * 128:(gg + 1) * 128], lhsT=n_.t[:, (half * 4 + gg) * 128:(half * 4 + gg + 1) * 128],
                            rhs=Wst.t[:, half * 4 + gg, :], start=True, stop=True) for gg in range(4)]
                        K.pe(fns, reads=(n_, Wst), writes=(pb,))
                        for gg in range(4):
                            g_ = half * 4 + gg
                            K.op("dve", lambda e, pb=pb, gg=gg, g_=g_, ta=ta: e.scalar_tensor_tensor(
                                out=ta.t[:, g_, :], in0=pb.t[:, gg * 128:(gg + 1) * 128], scalar=lnv.t[:, g_:g_ + 1], in1=Cgt.t[:, g_, :],
                                op0=ALU.mult, op1=ALU.add), (pb, lnv, Cgt), (ta,))
                    K.op("pool", lambda e, ta=ta, j=j: e.tensor_tensor(out=aoT.t[:, :, j * 128:(j + 1) * 128], in0=ta.t[:], in1=uT.t[:, :, j * 128:(j + 1) * 128], op=ALU.mult),
                         (ta, uT), (aoT,))
                    sj += 1
                for c8 in range(8):
                    pb = K.nextps(PSB)
                    K.mm(pb.t[:, :], [(Wga.t[:, kc, c8 * 128:(c8 + 1) * 128], xT.t[:, kc, :]) for kc in range(8)], reads=(Wga, xT), writes=(pb,))
                    K.op("act", lambda e, pb=pb, c8=c8: e.activation(out=gaT.t[:, c8, :], in_=pb.t[:, :], func=AF.Sigmoid, bias=bcol.t[:, 52 + c8:53 + c8]),
                         (pb, bcol), (gaT,))
                for c8 in range(8):
                    pb = K.nextps(PSB)
                    K.mm(pb.t[:, :], [(Wpa.t[:, kc, c8 * 128:(c8 + 1) * 128], aoT.t[:, kc, :]) for kc in range(8)], reads=(Wpa, aoT), writes=(pb,))
                    K.op("dve", lambda e, pb=pb, c8=c8, maT=maT: e.tensor_tensor(out=maT.t[:, c8, :], in0=pb.t[:, :], in1=gaT.t[:, c8, :], op=ALU.mult),
                         (pb, gaT), (maT,))
                K.store("sp", maT, MA[:, :, m * 512:(m + 1) * 512].rearrange("n p t -> p n t"), maT.t[:])
            K.barrier()

        if phases is None or "pc2" in phases:
          with ExitStack() as st:
            Wgb = K.sb(st, "Wgb", [128, 8, 1024], BF16)
            Wpb = K.sb(st, "Wpb", [128, 4, 1024], BF16)
            Wo = K.sb(st, "Wo", [128, 8, 1024], BF16)
            Wr = K.sb(st, "Wr", [128, 8, 32], F32)
            bcol = K.sb(st, "bcol", [128, 68], F32)
            g1b = K.sb(st, "g1b", [128, 1024], F32)
            b1b = K.sb(st, "b1b", [128, 1024], F32)
            brb = K.sb(st, "brb", [128, 32], F32)
            rin = K.sb(st, "rin", [128, 128, 4], I32)
            xts = [K.sb(st, f"xt{i}", [128, 4, D], F32) for i in range(2)]
            xT = K.sb(st, "xT", [128, 8, 512], BF16)
            gbT = K.sb(st, "gbT", [128, 8, 512], BF16)
            maTs = [K.sb(st, f"maT{i}", [128, 8, 512], BF16) for i in range(2)]
            boT = K.sb(st, "boT", [128, 4, 512], BF16)
            mgT = K.sb(st, "mgT", [128, 8, 512], BF16)
            tmpm = [K.sb(st, f"tmpm{i}", [128, 512], F32) for i in range(2)]
            bof = [K.sb(st, f"bof{i}", [128, 512], F32) for i in range(2)]
            sm = [K.sb(st, f"sm{i}", [128, 64], F32) for i in range(2)]
            x1p = [K.sb(st, f"x1p{i}", [128, 1024], F32) for i in range(2)]
            x1 = [K.sb(st, f"x1{i}", [128, 1024], F32) for i in range(2)]
            x1T = [K.sb(st, f"x1T{i}", [128, 8, 128], F32) for i in range(2)]
            stt = [K.sb(st, f"stt{i}", [128, 12], F32) for i in range(2)]
            mv = [K.sb(st, f"mv{i}", [128, 2], F32) for i in range(2)]
            rstd = [K.sb(st, f"rstd{i}", [128, 1], F32) for i in range(2)]
            lg = [K.sb(st, f"lg{i}", [128, 32], F32) for i in range(2)]
            t8 = [K.sb(st, f"t8{i}", [128, 8], F32) for i in range(2)]
            rs = [K.sb(st, f"rs{i}", [128, 16], F32) for i in range(2)]
            mk = [K.sb(st, f"mk{i}", [128, 32], BF16) for i in range(2)]
            posg = [K.sb(st, f"posg{i}", [128, 32], F32) for i in range(2)]
            slotv = [K.sb(st, f"slotv{i}", [128, 32], F32) for i in range(2)]
            oh = [K.sb(st, f"oh{i}", [128, 32], F32) for i in range(2)]
            ohx = [K.sb(st, f"ohx{i}", [128, 32], F32) for i in range(2)]
            slf = [K.sb(st, f"slf{i}", [128, 4], F32) for i in range(2)]
            sli = [K.sb(st, f"sli{i}", [128, 4], I32) for i in range(2)]
            wsrc = w_in[l]
            K.load("pool", Wgb, Wgb.t[:], wsrc[:, 7680:8704].rearrange("(kc p) c -> p kc c", p=128))
            K.load("pool", Wpb, Wpb.t[:], w_pb[l].rearrange("(kc p) c -> p kc c", p=128))
            K.load("pool", Wo, Wo.t[:], w_o[l].rearrange("(kc p) c -> p kc c", p=128))
            K.load("sp", Wr, Wr.t[:], w_r[l].rearrange("(kc p) c -> p kc c", p=128))
            K.load("sp", bcol, bcol.t[:], bin_col[l])
            K.load("sp", g1b, g1b.t[:], ln1_row[l][0:1, :].to_broadcast([128, 1024]))
            K.load("sp", b1b, b1b.t[:], ln1_row[l][1:2, :].to_broadcast([128, 1024]))
            K.load("sp", brb, brb.t[:], br_row[l].to_broadcast([128, 32]))
            K.load("sp", rin, rin.t[:], cin["rinit"])
            for c0 in range(0, NE * CAP, 16384):
                K.store("sp", rin, ROUTE[c0:c0 + 16384, :].rearrange("(p a) f -> p a f", p=128), rin.t[:])
            K.store("sp", rin, ROUTE[NE * CAP:NE * CAP + 128, :], rin.t[:, 0, :])
            K.op("dve", lambda e: e.memset(RUN.t[:], 0.0), (), (RUN,))
            K.barrier()
            XAt = XA.rearrange("(m j a) c -> m a j c", j=4, a=128)
            NM = NT // 512
            K.load("sp", xts[0], xts[0].t[:], XAt[0])
            K.load("sp", maTs[0], maTs[0].t[:], MA[:, :, 0:512].rearrange("n p t -> p n t"))
            og3 = [K.sb(st, f"og3x{i}", [128, 3, 516], F32) for i in range(4)]
            bo = [K.sb(st, f"box{i}", [128, 512], BF16) for i in range(4)]
            nmr = [K.sb(st, f"nmr{i}", [128, 1], F32) for i in range(2)]
            tmpq = [K.sb(st, f"tmpq{i}", [128, 512], F32) for i in range(2)]
            oh4 = [K.sb(st, f"oh4{i}", [128, 4, 32], F32) for i in range(2)]
            ox4 = [K.sb(st, f"ox4{i}", [128, 4, 32], F32) for i in range(2)]

            def M1(m, j):
                tile = m * 4 + j
                o3, sm_, bo_, bof_ = og3[j], sm[j % 2], bo[j], bof[j % 2]
                K.load("sp", o3, o3.t[:], OGall[:, tile * 128:(tile + 1) * 128, :].rearrange("g p c -> p g c"))
                lse3 = o3.t[:, :, 512:516]
                K.op("dve", lambda e: e.tensor_reduce(out=sm_.t[:, 0:4], in_=lse3.rearrange("p g h -> p h g"), axis=AX.X, op=ALU.max), (o3,), (sm_,))
                K.op("dve", lambda e: e.tensor_tensor(out=sm_.t[:, 4:16].rearrange("p (g h) -> p g h", g=3), in0=lse3,
                                                      in1=sm_.t[:, 0:4].unsqueeze(1).to_broadcast([128, 3, 4]), op=ALU.subtract), (o3, sm_), (sm_,))
                K.op("act", lambda e: e.activation(out=sm_.t[:, 4:16], in_=sm_.t[:, 4:16], func=AF.Exp), (sm_,), (sm_,))
                K.op("dve", lambda e: e.tensor_reduce(out=sm_.t[:, 16:20], in_=sm_.t[:, 4:16].rearrange("p (g h) -> p h g", g=3), axis=AX.X, op=ALU.add), (sm_,), (sm_,))
                K.op("dve", lambda e: e.reciprocal(out=sm_.t[:, 20:24], in_=sm_.t[:, 16:20]), (sm_,), (sm_,))
                K.op("dve", lambda e: e.tensor_tensor(out=sm_.t[:, 24:36].rearrange("p (g h) -> p g h", g=3), in0=sm_.t[:, 4:16].rearrange("p (g h) -> p g h", g=3),
                                                      in1=sm_.t[:, 20:24].unsqueeze(1).to_broadcast([128, 3, 4]), op=ALU.mult), (sm_,), (sm_,))
                tmq = tmpq[j % 2]

                def wnb(g):
                    return sm_.t[:, 24 + 4 * g:28 + 4 * g].unsqueeze(2).to_broadcast([128, 4, 128])

                def o3v(g):
                    return o3.t[:, g, 0:512].rearrange("p (h c) -> p h c", h=4)
                K.op("dve", lambda e: e.tensor_tensor(out=bof_.t[:, :].rearrange("p (h c) -> p h c", h=4), in0=o3v(0), in1=wnb(0), op=ALU.mult), (o3, sm_), (bof_,))
                K.op("pool", lambda e: e.tensor_tensor(out=tmq.t[:, :].rearrange("p (h c) -> p h c", h=4), in0=o3v(1), in1=wnb(1), op=ALU.mult), (o3, sm_), (tmq,))
                K.op("dve", lambda e: e.tensor_tensor(out=bof_.t[:, :], in0=bof_.t[:, :], in1=tmq.t[:, :], op=ALU.add), (bof_, tmq), (bof_,))
                K.op("pool", lambda e: e.tensor_tensor(out=tmq.t[:, :].rearrange("p (h c) -> p h c", h=4), in0=o3v(2), in1=wnb(2), op=ALU.mult), (o3, sm_), (tmq,))
                K.op("dve", lambda e: e.tensor_tensor(out=bo_.t[:, :], in0=bof_.t[:, :], in1=tmq.t[:, :], op=ALU.add), (bof_, tmq), (bo_,))

            def M2(m, j):
                bo_ = bo[j]
                pT = K.nextps(PSB)
                pTb = pT.t[:].bitcast(BF16)
                K.pe([lambda t, c=c: t.transpose(out=pTb[:, c * 128:(c + 1) * 128], in_=bo_.t[:, c * 128:(c + 1) * 128], identity=ident_b.t[:])
                      for c in range(4)], reads=(bo_, ident_b), writes=(pT,))
                K.op("act", lambda e: e.activation(out=boT.t[:, :, j * 128:(j + 1) * 128], in_=pTb[:, 0:512].rearrange("p (c t) -> p c t", c=4), func=AF.Copy), (pT,), (boT,))

            def Pst(m, j, xt):
                tile = m * 4 + j
                q = tile % 2
                xp, x1_, st_, mv_, rs_, nm_ = x1p[q], x1[q], stt[q], mv[q], rstd[q], nmr[q]
                for nh in range(2):
                    pb = K.nextps(PSB)
                    K.mm(pb.t[:, :], [(mgT.t[:, kc, j * 128:(j + 1) * 128], Wo.t[:, kc, nh * 512:(nh + 1) * 512]) for kc in range(8)], reads=(mgT, Wo), writes=(pb,))
                    K.op("dve", lambda e, pb=pb, nh=nh: e.scalar_tensor_tensor(
                        out=xp.t[:, nh * 512:(nh + 1) * 512], in0=xt.t[:, j, nh * 512:(nh + 1) * 512], scalar=ALPHA, in1=pb.t[:, :], op0=ALU.mult, op1=ALU.add),
                        (pb, xt), (xp,))
                ln_stats(st, lambda c: xp.t[:, c * 512:(c + 1) * 512], xp, mv_, st_, rs_)
                K.op("dve", lambda e: e.scalar_tensor_tensor(out=nm_.t[:], in0=mv_.t[:, 0:1], scalar=-1.0, in1=rs_.t[:, 0:1], op0=ALU.mult, op1=ALU.mult), (mv_, rs_), (nm_,))
                K.op("act", lambda e: e.activation(out=xp.t[:], in_=xp.t[:], func=AF.Identity, scale=rs_.t[:, 0:1], bias=nm_.t[:, 0:1]), (xp, rs_, nm_), (xp,))
                K.op("dve", lambda e: e.tensor_tensor(out=xp.t[:], in0=xp.t[:], in1=g1b.t[:], op=ALU.mult), (xp, g1b), (xp,))
                K.op("pool", lambda e: e.tensor_tensor(out=x1_.t[:], in0=xp.t[:], in1=b1b.t[:], op=ALU.add), (xp, b1b), (x1_,))
                K.store("sp", x1_, X1[tile * 128:(tile + 1) * 128, :], x1_.t[:])

            def Rst(m, j):
                tile = m * 4 + j
                q = tile % 2
                x1_, x1T_ = x1[q], x1T[q]
                lg_, t8_, r_, mk_, pg_, sv_, oh_, ox_, sf_, si_ = lg[q], t8[q], rs[q], mk[q], posg[q], slotv[q], oh[q], ohx[q], slf[q], sli[q]
                for half in range(2):
                    pb = K.nextps(PSB)
                    K.pe([lambda t, c=c, half=half, pb=pb: t.transpose(out=pb.t[:, c * 128:(c + 1) * 128], in_=x1_.t[:, (half * 4 + c) * 128:(half * 4 + c + 1) * 128], identity=ident_f.t[:])
                          for c in range(4)], reads=(x1_, ident_f), writes=(pb,))
                    evac(half, x1T_.t[:, half * 4:(half + 1) * 4, :], pb.t[:, :].rearrange("p (c t) -> p c t", c=4), (pb,), (x1T_,))
                pb = K.nextps(PSB)
                K.mm(pb.t[:, 0:32], [(x1T_.t[:, kc, :], Wr.t[:, kc, :]) for kc in range(8)], reads=(x1T_, Wr), writes=(pb,))
                K.op("dve", lambda e, pb=pb: e.tensor_tensor(out=lg_.t[:], in0=pb.t[:, 0:32], in1=brb.t[:], op=ALU.add), (pb, brb), (lg_,))
                K.op("dve", lambda e: e.max(out=t8_.t[:], in_=lg_.t[:]), (lg_,), (t8_,))
                K.op("dve", lambda e: e.tensor_scalar(out=r_.t[:, 0:1], in0=t8_.t[:, 0:1], scalar1=-1.0, scalar2=None, op0=ALU.mult), (t8_,), (r_,))
                K.op("act", lambda e: e.activation(out=r_.t[:, 1:5], in_=t8_.t[:, 0:4], func=AF.Exp, bias=r_.t[:, 0:1], accum_out=r_.t[:, 5:6]), (t8_, r_), (r_,))
                K.op("dve", lambda e: e.reciprocal(out=r_.t[:, 6:7], in_=r_.t[:, 5:6]), (r_,), (r_,))
                K.op("dve", lambda e: e.tensor_scalar(out=GATES.t[:, tile, :], in0=r_.t[:, 1:5], scalar1=r_.t[:, 6:7], scalar2=None, op0=ALU.mult), (r_,), (GATES,))
                K.op("dve", lambda e: e.tensor_scalar(out=mk_.t[:], in0=lg_.t[:], scalar1=t8_.t[:, 3:4], scalar2=None, op0=ALU.is_ge), (lg_, t8_), (mk_,))
                pb = K.nextps(PSB)
                K.mm(pb.t[:, 0:32], [(tri_b.t[:, :], mk_.t[:, :])], reads=(tri_b, mk_), writes=(pb,))
                K.op("dve", lambda e, pb=pb: e.tensor_tensor(out=pg_.t[:], in0=pb.t[:, 0:32], in1=RUN.t[:], op=ALU.add), (pb, RUN), (pg_,))
                pb2 = K.nextps(PSB)
                K.mm(pb2.t[:, 0:32], [(ones_b.t[:, :], mk_.t[:, :])], reads=(ones_b, mk_), writes=(pb2,))
                K.op("dve", lambda e, pb2=pb2: e.tensor_tensor(out=RUN.t[:], in0=pb2.t[:, 0:32], in1=RUN.t[:], op=ALU.add), (pb2, RUN), (RUN,))
                K.op("dve", lambda e: e.tensor_tensor(out=sv_.t[:], in0=pg_.t[:], in1=ecap.t[:], op=ALU.add), (pg_, ecap), (sv_,))
                o4, x4 = oh4[q], ox4[q]
                K.op("dve", lambda e: e.tensor_tensor(out=o4.t[:], in0=lg_.t[:, :].unsqueeze(1).to_broadcast([128, 4, 32]),
                                                      in1=t8_.t[:, 0:4].unsqueeze(2).to_broadcast([128, 4, 32]), op=ALU.is_equal), (lg_, t8_), (o4,))
                for V, dst in ((sv_, lambda: sf_.t[:, 0:4]), (pg_, lambda: POSK.t[:, tile, :]), (iota32, lambda: EKF.t[:, tile, :])):
                    dbuf = sf_ if V is sv_ else (POSK if V is pg_ else EKF)
                    K.op("dve", lambda e, V=V: e.tensor_tensor(out=x4.t[:], in0=o4.t[:], in1=V.t[:, :].unsqueeze(1).to_broadcast([128, 4, 32]), op=ALU.mult), (o4, V), (x4,))
                    K.op("dve", lambda e, dst=dst: e.tensor_reduce(out=dst(), in_=x4.t[:], axis=AX.X, op=ALU.add), (x4,), (dbuf,))
                K.op("dve", lambda e: e.tensor_copy(out=si_.t[:], in_=sf_.t[:]), (sf_,), (si_,))
                for k in range(4):
                    K.dma("pool", si_, lambda e, k=k: e.indirect_dma_start(
                        out=ROUTE[:, :], out_offset=bass.IndirectOffsetOnAxis(ap=si_.t[:, k:k + 1], axis=0),
                        in_=tokid4.t[:, tile, :], in_offset=None), reads=(si_, tokid4), writes=())

            for m in range(NM):
                xt = xts[m % 2]
                maT = maTs[m % 2]
                if m + 1 < NM:
                    K.load("sp", xts[(m + 1) % 2], xts[(m + 1) % 2].t[:], XAt[m + 1])
                    K.load("sp", maTs[(m + 1) % 2], maTs[(m + 1) % 2].t[:], MA[:, :, (m + 1) * 512:(m + 2) * 512].rearrange("n p t -> p n t"))
                for j in range(4):
                    M1(m, j)
                transposes_x(xt, xT, 512, act_only=True)
                for c8 in range(8):
                    pb = K.nextps(PSB)
                    K.mm(pb.t[:, :], [(Wgb.t[:, kc, c8 * 128:(c8 + 1) * 128], xT.t[:, kc, :]) for kc in range(8)], reads=(Wgb, xT), writes=(pb,))
                    K.op("act", lambda e, pb=pb, c8=c8: e.activation(out=gbT.t[:, c8, :], in_=pb.t[:, :], func=AF.Sigmoid, bias=bcol.t[:, 60 + c8:61 + c8]),
                         (pb, bcol), (gbT,))
                for j in range(4):
                    M2(m, j)
                for c8 in range(8):
                    pb = K.nextps(PSB)
                    tm = tmpm[c8 % 2]
                    K.mm(pb.t[:, :], [(Wpb.t[:, kc, c8 * 128:(c8 + 1) * 128], boT.t[:, kc, :]) for kc in range(4)], reads=(Wpb, boT), writes=(pb,))
                    K.op("dve", lambda e, pb=pb, c8=c8, tm=tm: e.tensor_tensor(out=tm.t[:], in0=pb.t[:, :], in1=gbT.t[:, c8, :], op=ALU.mult), (pb, gbT), (tm,))
                    K.op("pool", lambda e, c8=c8, tm=tm: e.tensor_tensor(out=mgT.t[:, c8, :], in0=tm.t[:], in1=maT.t[:, c8, :], op=ALU.add), (tm, maT), (mgT,))
                Pst(m, 0, xt); Pst(m, 1, xt); Rst(m, 0); Pst(m, 2, xt); Rst(m, 1); Pst(m, 3, xt); Rst(m, 2); Rst(m, 3)
            K.barrier()

        if phases is None or "bl" in phases:
          with ExitStack() as st:
            cntc = K.sb(st, "cntc", [32, 1], F32)
            tmp32 = K.sb(st, "tmp32", [32, 32], F32)
            jthr = K.sb(st, "jthr", [32, 64], F32)
            tmpj = K.sb(st, "tmpj", [32, 64], F32)
            nbc = K.sb(st, "nbc", [32, 1], F32)
            endc = K.sb(st, "endc", [32, 1], F32)
            triu = K.sb(st, "triu", [32, 32], F32)
            iob = K.sb(st, "iob", [128, NBLK], F32)
            Gm = K.sb(st, "Gm", [32, NBLK], F32)
            nbm = K.sb(st, "nbm", [32, 128], F32)
            ebr = K.sb(st, "ebr", [128, NBLK], F32)
            eb = K.sb(st, "eb", [128, NBLK], F32)
            stb = K.sb(st, "stb", [128, NBLK], F32)
            val = K.sb(st, "val", [128, NBLK], F32)
            sbase = K.sb(st, "sbase", [128, NBLK], F32)
            need = K.sb(st, "need", [128, NBLK], F32)
            tA = K.sb(st, "tA", [128, NBLK], F32)
            tB = K.sb(st, "tB", [128, NBLK], F32)
            endr = K.sb(st, "endr", [128, 32], F32)
            nbr = K.sb(st, "nbr", [128, 32], F32)
            K.load("sp", jthr, jthr.t[:], cin["jthr"])
            K.load("sp", triu, triu.t[:], cin["triu32"])
            K.load("sp", iob, iob.t[:], cin["iota_b"])
            K.op("dve", lambda e: e.tensor_tensor(out=tmp32.t[:], in0=RUN.t[0:32, :], in1=ident_f.t[0:32, 0:32], op=ALU.mult), (RUN, ident_f), (tmp32,))
            K.op("dve", lambda e: e.tensor_reduce(out=cntc.t[:], in_=tmp32.t[:], axis=AX.X, op=ALU.add), (tmp32,), (cntc,))
            K.op("dve", lambda e: e.tensor_scalar(out=tmpj.t[:], in0=jthr.t[:], scalar1=cntc.t[:, 0:1], scalar2=None, op0=ALU.is_lt), (jthr, cntc), (tmpj,))
            K.op("dve", lambda e: e.tensor_reduce(out=nbc.t[:], in_=tmpj.t[:], axis=AX.X, op=ALU.add), (tmpj,), (nbc,))
            pb = K.nextps(PSB)
            K.mm(pb.t[0:32, 0:1], [(triu.t[:, :], nbc.t[:, 0:1])], reads=(triu, nbc), writes=(pb,))
            K.op("dve", lambda e, pb=pb: e.tensor_copy(out=endc.t[:], in_=pb.t[0:32, 0:1]), (pb,), (endc,))
            K.op("dve", lambda e: e.tensor_scalar(out=Gm.t[:], in0=iob.t[0:32, :], scalar1=endc.t[:, 0:1], scalar2=None, op0=ALU.is_ge), (iob, endc), (Gm,))
            K.op("dve", lambda e: e.tensor_scalar(out=nbm.t[:], in0=ones_f.t[0:32, :], scalar1=nbc.t[:, 0:1], scalar2=None, op0=ALU.mult), (ones_f, nbc), (nbm,))
            pb = K.nextps(PSB)
            K.mm(pb.t[:, 0:NBLK], [(ones_f.t[0:32, :], Gm.t[:, :])], reads=(ones_f, Gm), writes=(pb,))
            K.op("dve", lambda e, pb=pb: e.tensor_copy(out=ebr.t[:], in_=pb.t[:, 0:NBLK]), (pb,), (ebr,))
            pb = K.nextps(PSB)
            K.mm(pb.t[:, 0:NBLK], [(nbm.t[:, :], Gm.t[:, :])], reads=(nbm, Gm), writes=(pb,))
            K.op("dve", lambda e, pb=pb: e.tensor_copy(out=stb.t[:], in_=pb.t[:, 0:NBLK]), (pb,), (stb,))
            pb = K.nextps(PSB)
            K.mm(pb.t[:, 0:32], [(nbm.t[:, :], triu.t[:, :])], reads=(nbm, triu), writes=(pb,))
            K.op("dve", lambda e, pb=pb: e.tensor_copy(out=endr.t[:], in_=pb.t[:, 0:32]), (pb,), (endr,))
            pb = K.nextps(PSB)
            K.mm(pb.t[:, 0:32], [(nbm.t[:, :], ident_f.t[0:32, 0:32])], reads=(nbm, ident_f), writes=(pb,))
            K.op("dve", lambda e, pb=pb: e.tensor_copy(out=nbr.t[:], in_=pb.t[:, 0:32]), (pb,), (nbr,))
            K.op("dve", lambda e: e.tensor_tensor(out=BS128.t[:], in0=endr.t[:], in1=nbr.t[:], op=ALU.subtract), (endr, nbr), (BS128,))
            K.op("dve", lambda e: e.tensor_scalar(out=BS128.t[:], in0=BS128.t[:], scalar1=128.0, scalar2=None, op0=ALU.mult), (BS128,), (BS128,))
            K.op("dve", lambda e: e.tensor_scalar(out=val.t[:], in0=ebr.t[:], scalar1=31.5, scalar2=None, op0=ALU.is_lt), (ebr,), (val,))
            K.op("dve", lambda e: e.tensor_scalar(out=eb.t[:], in0=ebr.t[:], scalar1=31.0, scalar2=None, op0=ALU.min), (ebr,), (eb,))
            K.op("dve", lambda e: e.tensor_tensor(out=tA.t[:], in0=iob.t[:], in1=stb.t[:], op=ALU.subtract), (iob, stb), (tA,))
            K.op("dve", lambda e: e.tensor_scalar(out=tA.t[:], in0=tA.t[:], scalar1=128.0, scalar2=float(-NE * CAP), op0=ALU.mult, op1=ALU.add), (tA,), (tA,))
            K.op("dve", lambda e: e.scalar_tensor_tensor(out=tA.t[:], in0=eb.t[:], scalar=float(CAP), in1=tA.t[:], op0=ALU.mult, op1=ALU.add), (eb, tA), (tA,))
            K.op("dve", lambda e: e.tensor_tensor(out=tA.t[:], in0=tA.t[:], in1=val.t[:], op=ALU.mult), (tA, val), (tA,))
            K.op("dve", lambda e: e.tensor_scalar(out=sbase.t[:], in0=tA.t[:], scalar1=pcol.t[:, 0:1], scalar2=float(NE * CAP), op0=ALU.add, op1=ALU.add), (tA, pcol), (sbase,))
            K.op("dve", lambda e: e.tensor_copy(out=RIDX.t[:], in_=sbase.t[:]), (sbase,), (RIDX,))
            K.op("dve", lambda e: e.memset(need.t[:], 1.0), (), (need,))
            K.op("dve", lambda e: e.tensor_tensor(out=need.t[:, 1:NBLK], in0=eb.t[:, 1:NBLK], in1=eb.t[:, 0:NBLK - 1], op=ALU.not_equal), (eb,), (need,))
            K.op("dve", lambda e: e.memset(need.t[:, NBLK // 2:NBLK // 2 + 1], 1.0), (), (need,))
            nn = K.sb(st, "nn", [128, NBLK], F32)
            K.op("dve", lambda e: e.tensor_scalar(out=nn.t[:], in0=need.t[:], scalar1=float(-BIG), scalar2=float(BIG), op0=ALU.mult, op1=ALU.add), (need,), (nn,))
            K.op("dve", lambda e: e.tensor_scalar(out=tB.t[:], in0=eb.t[:], scalar1=128.0, scalar2=pcol.t[:, 0:1], op0=ALU.mult, op1=ALU.add), (eb, pcol), (tB,))
            K.op("dve", lambda e: e.tensor_scalar(out=tB.t[:], in0=tB.t[:], scalar1=float(l * NE * 128), scalar2=None, op0=ALU.add), (tB,), (tB,))
            K.op("dve", lambda e: e.tensor_tensor(out=tB.t[:], in0=tB.t[:], in1=need.t[:], op=ALU.mult), (tB, need), (tB,))
            K.op("dve", lambda e: e.tensor_tensor(out=tB.t[:], in0=tB.t[:], in1=nn.t[:], op=ALU.add), (tB, nn), (tB,))
            K.op("dve", lambda e: e.tensor_copy(out=WIDX.t[:], in_=tB.t[:]), (tB,), (WIDX,))
            K.op("dve", lambda e: e.scalar_tensor_tensor(out=tB.t[:], in0=eb.t[:], scalar=float(l * NE), in1=need.t[:], op0=ALU.add, op1=ALU.mult), (eb, need), (tB,))
            K.op("dve", lambda e: e.tensor_tensor(out=tB.t[:], in0=tB.t[:], in1=nn.t[:], op=ALU.add), (tB, nn), (tB,))
            K.op("dve", lambda e: e.tensor_copy(out=BIDX.t[:], in_=tB.t[:]), (tB,), (BIDX,))
            K.barrier()

        if phases is None or "moe" in phases:
          with ExitStack() as st:
            Wg = [K.sb(st, f"Wg{i}", [128, 8 * 2048], BF16) for i in range(2)]
            Wd = [K.sb(st, f"Wd{i}", [128, 8 * 1024], BF16) for i in range(2)]
            Bg = [K.sb(st, f"Bg{i}", [128, 2048], BF16) for i in range(2)]
            Bd = [K.sb(st, f"Bd{i}", [128, 1024], BF16) for i in range(2)]
            rt = [K.sb(st, f"rt{i}", [128, 4], I32) for i in range(4)]
            xg = [K.sb(st, f"xg{i}", [128, 1024], BF16) for i in range(3)]
            xgT = [K.sb(st, f"xgT{i}", [128, 8, 128], BF16) for i in range(2)]
            gs = [K.sb(st, f"gs{i}", [128, 1024], F32) for i in range(2)]
            sg = [K.sb(st, f"sg{i}", [128, 1024], F32) for i in range(2)]
            u1 = [K.sb(st, f"u1{i}", [128, 1024], F32) for i in range(2)]
            hb = [K.sb(st, f"hb{i}", [128, 1024], BF16) for i in range(2)]
            hT = [K.sb(st, f"hT{i}", [128, 8, 128], BF16) for i in range(2)]
            ysb = [K.sb(st, f"ysb{i}", [128, 1024], F32) for i in range(2)]
            if "bcW" not in K.__dict__:
                rW = nc.gpsimd.alloc_register("bcW")
                nc.gpsimd.reg_mov(rW, NL * NE * 128 - 1)
                K.bcW = nc.gpsimd.snap(rW, donate=True)
                rB = nc.gpsimd.alloc_register("bcB")
                nc.gpsimd.reg_mov(rB, NL * NE - 1)
                K.bcB = nc.gpsimd.snap(rB, donate=True)
            wguv = w_gu.rearrange("l e (p k) f -> (l e p) (k f)", k=8)
            wdnv = w_dn.rearrange("l e (p k) f -> (l e p) (k f)", k=8)
            bguv = b_gu.rearrange("l e f -> (l e) f")
            bdnv = b_dn.rearrange("l e f -> (l e) f")
            HB = NBLK // 2

            def blk(s_):
                return s_ // 2 if s_ % 2 == 0 else HB + s_ // 2

            def wloadG(s_):
                if s_ >= NBLK:
                    return
                par, b = s_ % 2, blk(s_)
                K.dma("pool", Wg[par], lambda e: e.indirect_dma_start(out=Wg[par].t[:, :], out_offset=None, in_=wguv,
                      in_offset=bass.IndirectOffsetOnAxis(ap=WIDX.t[:, b:b + 1], axis=0), bounds_check=K.bcW, oob_is_err=False),
                      reads=(WIDX,), writes=(Wg[par],))
                K.dma("pool", Bg[par], lambda e: e.indirect_dma_start(out=Bg[par].t[:, :], out_offset=None, in_=bguv,
                      in_offset=bass.IndirectOffsetOnAxis(ap=BIDX.t[:, b:b + 1], axis=0), bounds_check=K.bcB, oob_is_err=False),
                      reads=(BIDX,), writes=(Bg[par],))

            def wloadD(s_):
                if s_ >= NBLK:
                    return
                par, b = s_ % 2, blk(s_)
                K.dma("pool", Wd[par], lambda e: e.indirect_dma_start(out=Wd[par].t[:, :], out_offset=None, in_=wdnv,
                      in_offset=bass.IndirectOffsetOnAxis(ap=WIDX.t[:, b:b + 1], axis=0), bounds_check=K.bcW, oob_is_err=False),
                      reads=(WIDX,), writes=(Wd[par],))
                K.dma("pool", Bd[par], lambda e: e.indirect_dma_start(out=Bd[par].t[:, :], out_offset=None, in_=bdnv,
                      in_offset=bass.IndirectOffsetOnAxis(ap=BIDX.t[:, b:b + 1], axis=0), bounds_check=K.bcB, oob_is_err=False),
                      reads=(BIDX,), writes=(Bd[par],))

            def rgath(s_):
                if s_ >= NBLK:
                    return
                r_, b = rt[s_ % 4], blk(s_)
                K.dma("pool", r_, lambda e: e.indirect_dma_start(out=r_.t[:, :], out_offset=None, in_=ROUTE[:, :],
                      in_offset=bass.IndirectOffsetOnAxis(ap=RIDX.t[:, b:b + 1], axis=0)), reads=(RIDX,), writes=(r_,))

            def xgath(s_):
                if s_ >= NBLK:
                    return
                r_, x_ = rt[s_ % 4], xg[s_ % 3]
                K.dma("pool", x_, lambda e: e.indirect_dma_start(out=x_.t[:, :], out_offset=None, in_=X1[:, :],
                      in_offset=bass.IndirectOffsetOnAxis(ap=r_.t[:, 0:1], axis=0)), reads=(r_,), writes=(x_,))

            pT_, pG, pH, pY = PSB[0], PSB[1:5], PSB[5], PSB[6:8]

            def S1a(s_):
                x_, xT_ = xg[s_ % 3], xgT[s_ % 2]
                xv = x_.t[:, :].rearrange("t (p k) -> t k p", k=8)
                pTb = pT_.t[:].bitcast(BF16)
                K.pe([lambda t, c=c: t.transpose(out=pTb[:, c * 128:(c + 1) * 128], in_=xv[:, c, :], identity=ident_b.t[:])
                      for c in range(8)], reads=(x_, ident_b), writes=(pT_,))
                K.op("act", lambda e: e.activation(out=xT_.t[:], in_=pTb[:, 0:1024].rearrange("p (c t) -> p c t", c=8), func=AF.Copy), (pT_,), (xT_,))

            def S1b(s_):
                par = s_ % 2
                xT_ = xgT[par]
                for n in range(4):
                    pb = pG[n]
                    pairs = [(xT_.t[:, kc, :], Wg[par].t[:, kc * 2048 + n * 512:kc * 2048 + (n + 1) * 512]) for kc in range(8)]
                    pairs.append((ones_b.t[0:1, :], Bg[par].t[0:1, n * 512:(n + 1) * 512]))
                    K.mm(pb.t[:, :], pairs, reads=(xT_, Wg[par], ones_b, Bg[par]), writes=(pb,))

            def S2a(s_):
                par = s_ % 2
                g_, u_ = gs[par], u1[par]
                for n in range(2):
                    K.op("dve", lambda e, n=n: e.tensor_scalar(out=g_.t[:, n * 512:(n + 1) * 512], in0=pG[n].t[:, :], scalar1=LIMIT, scalar2=None, op0=ALU.min), (pG[n],), (g_,))
                    K.op("dve", lambda e, n=n: e.tensor_scalar(out=u_.t[:, n * 512:(n + 1) * 512], in0=pG[2 + n].t[:, :], scalar1=LIMIT, scalar2=-LIMIT, op0=ALU.min, op1=ALU.max), (pG[2 + n],), (u_,))

            def S2b(s_):
                par, b = s_ % 2, blk(s_)
                g_, s2_, u_, h_, hT_, y_ = gs[par], sg[par], u1[par], hb[par], hT[par], ysb[par]
                K.op("act", lambda e: e.activation(out=s2_.t[:], in_=g_.t[:], func=AF.Sigmoid, scale=SW_ALPHA), (g_,), (s2_,))
                K.op("dve", lambda e: e.scalar_tensor_tensor(out=u_.t[:], in0=u_.t[:], scalar=1.0, in1=g_.t[:], op0=ALU.add, op1=ALU.mult), (u_, g_), (u_,))
                K.op("dve", lambda e: e.tensor_tensor(out=h_.t[:], in0=u_.t[:], in1=s2_.t[:], op=ALU.mult), (u_, s2_), (h_,))
                hv = h_.t[:, :].rearrange("t (p k) -> t k p", k=8)
                pTb = pH.t[:].bitcast(BF16)
                K.pe([lambda t, c=c: t.transpose(out=pTb[:, c * 128:(c + 1) * 128], in_=hv[:, c, :], identity=ident_b.t[:])
                      for c in range(8)], reads=(h_, ident_b), writes=(pH,))
                K.op("act", lambda e: e.activation(out=hT_.t[:], in_=pTb[:, 0:1024].rearrange("p (c t) -> p c t", c=8), func=AF.Copy), (pH,), (hT_,))
                for n in range(2):
                    pb = pY[n]
                    pairs = [(hT_.t[:, kc, :], Wd[par].t[:, kc * 1024 + n * 512:kc * 1024 + (n + 1) * 512]) for kc in range(8)]
                    pairs.append((ones_b.t[0:1, :], Bd[par].t[0:1, n * 512:(n + 1) * 512]))
                    K.mm(pb.t[:, :], pairs, reads=(hT_, Wd[par], ones_b, Bd[par]), writes=(pb,))
                    evac(n, y_.t[:, n * 512:(n + 1) * 512], pb.t[:, :], (pb,), (y_,))
                K.store("sp", y_, YB[b * 128:(b + 1) * 128, :], y_.t[:])

            wloadG(0); wloadG(1); wloadD(0); wloadD(1)
            rgath(0); rgath(1); rgath(2)
            xgath(0); xgath(1)
            S1a(0); S1b(0)
            wloadG(2)
            for s_ in range(NBLK):
                rgath(s_ + 3)
                xgath(s_ + 2)
                if s_ + 1 < NBLK:
                    S1a(s_ + 1)
                S2a(s_)
                if s_ + 1 < NBLK:
                    S1b(s_ + 1)
                    wloadG(s_ + 3)
                S2b(s_)
                wloadD(s_ + 2)
            K.barrier()

        if phases is None or "comb" in phases:
          with ExitStack() as st:
            g2b = K.sb(st, "g2b", [128, 1024], F32)
            b2b = K.sb(st, "b2b", [128, 1024], F32)
            yk = [[K.sb(st, f"yk{i}_{k}", [128, 1024], F32) for k in range(4)] for i in range(2)]
            x1t = [K.sb(st, f"x1t{i}", [128, 1024], F32) for i in range(2)]
            acc = [K.sb(st, f"acc{i}", [128, 1024], F32) for i in range(2)]
            xo = [K.sb(st, f"xo{i}", [128, 1024], F32) for i in range(2)]
            oh = [K.sb(st, f"oh{i}", [128, 32], F32) for i in range(2)]
            yrf = [K.sb(st, f"yrf{i}", [128, 4], F32) for i in range(2)]
            yri = [K.sb(st, f"yri{i}", [128, 4], I32) for i in range(2)]
            nmr = [K.sb(st, f"nmrc{i}", [128, 1], F32) for i in range(2)]
            stt = [K.sb(st, f"stt{i}", [128, 12], F32) for i in range(2)]
            mv = [K.sb(st, f"mv{i}", [128, 2], F32) for i in range(2)]
            rstd = [K.sb(st, f"rstd{i}", [128, 1], F32) for i in range(2)]
            K.load("sp", g2b, g2b.t[:], ln2_row[l][0:1, :].to_broadcast([128, 1024]))
            K.load("sp", b2b, b2b.t[:], ln2_row[l][1:2, :].to_broadcast([128, 1024]))

            def cload(tile):
                q = tile % 2
                for k in range(4):
                    K.op("dve", lambda e, k=k, q=q, tile=tile: e.tensor_scalar(out=oh[q].t[:], in0=iota32.t[:], scalar1=EKF.t[:, tile, k:k + 1], scalar2=None, op0=ALU.is_equal), (iota32, EKF), (oh[q],))
                    K.op("dve", lambda e, q=q: e.tensor_tensor(out=oh[q].t[:], in0=oh[q].t[:], in1=BS128.t[:], op=ALU.mult), (oh[q], BS128), (oh[q],))
                    K.op("dve", lambda e, k=k, q=q: e.tensor_reduce(out=yrf[q].t[:, k:k + 1], in_=oh[q].t[:], axis=AX.X, op=ALU.add), (oh[q],), (yrf[q],))
                K.op("dve", lambda e, q=q, tile=tile: e.tensor_tensor(out=yrf[q].t[:], in0=yrf[q].t[:], in1=POSK.t[:, tile, :], op=ALU.add), (yrf[q], POSK), (yrf[q],))
                K.op("dve", lambda e, q=q: e.tensor_copy(out=yri[q].t[:], in_=yrf[q].t[:]), (yrf[q],), (yri[q],))
                for k in range(4):
                    K.dma("pool", yk[q][k], lambda e, k=k, q=q: e.indirect_dma_start(out=yk[q][k].t[:, :], out_offset=None, in_=YB[:, :],
                          in_offset=bass.IndirectOffsetOnAxis(ap=yri[q].t[:, k:k + 1], axis=0)), reads=(yri[q],), writes=(yk[q][k],))
                K.load("sp", x1t[q], x1t[q].t[:], X1[tile * 128:(tile + 1) * 128, :])
            cload(0)
            for tile in range(NTILE):
                q = tile % 2
                if tile + 1 < NTILE:
                    cload(tile + 1)
                a_, o_ = acc[q], xo[q]
                K.op("act", lambda e: e.activation(out=a_.t[:], in_=yk[q][0].t[:], func=AF.Identity, scale=GATES.t[:, tile, 0:1]), (yk[q][0], GATES), (a_,))
                for k in range(1, 4):
                    K.op("dve", lambda e, k=k: e.scalar_tensor_tensor(out=a_.t[:], in0=yk[q][k].t[:], scalar=GATES.t[:, tile, k:k + 1], in1=a_.t[:], op0=ALU.mult, op1=ALU.add), (yk[q][k], GATES, a_), (a_,))
                K.op("dve", lambda e: e.scalar_tensor_tensor(out=a_.t[:], in0=x1t[q].t[:], scalar=ALPHA, in1=a_.t[:], op0=ALU.mult, op1=ALU.add), (x1t[q], a_), (a_,))
                ln_stats(st, lambda c: a_.t[:, c * 512:(c + 1) * 512], a_, mv[q], stt[q], rstd[q])
                K.op("dve", lambda e: e.scalar_tensor_tensor(out=nmr[q].t[:], in0=mv[q].t[:, 0:1], scalar=-1.0, in1=rstd[q].t[:, 0:1], op0=ALU.mult, op1=ALU.mult), (mv[q], rstd[q]), (nmr[q],))
                K.op("act", lambda e: e.activation(out=a_.t[:], in_=a_.t[:], func=AF.Identity, scale=rstd[q].t[:, 0:1], bias=nmr[q].t[:, 0:1]), (a_, rstd[q], nmr[q]), (a_,))
                K.op("pool", lambda e: e.tensor_tensor(out=a_.t[:], in0=a_.t[:], in1=g2b.t[:], op=ALU.mult), (a_, g2b), (a_,))
                K.op("pool", lambda e: e.tensor_tensor(out=o_.t[:], in0=a_.t[:], in1=b2b.t[:], op=ALU.add), (a_, b2b), (o_,))
                K.store("sp", o_, XO[tile * 128:(tile + 1) * 128, :], o_.t[:])
            K.barrier()

    K.barrier()
    top.close()
    return nc, dict(NT=NT, NBLK=NBLK, CAP=CAP, NL=NL, NSEQ=NSEQ)


def host_inputs(params, NL, NT, NBLK, CAP):
    f = lambda a: np.ascontiguousarray(np.asarray(a, dtype=np.float32))
    d = {}
    d["w_in"] = f(params["w_in"][:NL])
    d["w_st"] = f(np.transpose(np.asarray(params["w_s"][:NL]), (0, 3, 1, 2)))
    d["w_pa"] = f(params["w_pa"][:NL])
    d["w_pb"] = f(params["w_pb"][:NL])
    d["w_o"] = f(params["w_o"][:NL])
    d["w_r"] = f(params["w_r"][:NL])
    d["w_gu"] = f(params["w_gu"][:NL])
    d["w_down"] = f(params["w_down"][:NL])
    d["b_gu"] = f(params["b_gu"][:NL])
    d["b_down"] = f(params["b_down"][:NL])
    b_in = np.asarray(params["b_in"][:NL], dtype=np.float32)
    d["bin_col"] = f(np.transpose(b_in.reshape(NL, 68, 128), (0, 2, 1)))
    d["bin_row"] = f(b_in.reshape(NL, 1, DIN))
    lg = np.asarray(params["ln_v_g"][:NL], dtype=np.float32).reshape(NL, 8, 128)
    lb = np.asarray(params["ln_v_b"][:NL], dtype=np.float32).reshape(NL, 8, 128)
    d["lnv_col"] = f(np.concatenate([np.transpose(lg, (0, 2, 1)), np.transpose(lb, (0, 2, 1))], axis=2))
    d["bs_row"] = f(np.asarray(params["b_s"][:NL]).reshape(NL, 1, 1024))
    d["ln1_row"] = f(np.stack([np.asarray(params["ln1_g"][:NL]), np.asarray(params["ln1_b"][:NL])], axis=1))
    d["ln2_row"] = f(np.stack([np.asarray(params["ln2_g"][:NL]), np.asarray(params["ln2_b"][:NL])], axis=1))
    d["br_row"] = f(np.asarray(params["b_r"][:NL]).reshape(NL, 1, NE))
    for k, v in _consts(NT, NBLK, CAP).items():
        d["c_" + k] = v
    return d


def kernel(x_prompt, x_sample, w_in, b_in, ln_v_g, ln_v_b, w_s, b_s, w_pa, w_pb, w_o, ln1_g, ln1_b,
           w_r, b_r, w_gu, b_gu, w_down, b_down, ln2_g, ln2_b):
    params = dict(w_in=w_in, b_in=b_in, ln_v_g=ln_v_g, ln_v_b=ln_v_b, w_s=w_s, b_s=b_s, w_pa=w_pa, w_pb=w_pb, w_o=w_o,
                  ln1_g=ln1_g, ln1_b=ln1_b, w_r=w_r, b_r=b_r, w_gu=w_gu, b_gu=b_gu, w_down=w_down, b_down=b_down,
                  ln2_g=ln2_g, ln2_b=ln2_b)
    NL, NSEQ = 4, 2
    nc, meta = build_program(NL=NL, NSEQ=NSEQ)
    shared = host_inputs(params, NL, meta["NT"], meta["NBLK"], meta["CAP"])
    xp = np.asarray(x_prompt, dtype=np.float32)
    xs = np.asarray(x_sample, dtype=np.float32)
    seqs = []
    for c in range(8):
        if c < 4:
            seqs.append(np.concatenate([xp[2 * c], xp[2 * c + 1]], axis=0))
        else:
            seqs.append(np.concatenate([xs[c - 4], xs[c - 4]], axis=0))
    in_maps = []
    for c in range(8):
        m = dict(shared)
        m["x"] = np.ascontiguousarray(seqs[c])
        in_maps.append(m)
    res = run_bass_kernel_spmd(nc, in_maps, core_ids=list(range(8)))
    yp = np.zeros_like(xp)
    ys = np.zeros_like(xs)
    for c in range(8):
        y = np.asarray(res.results[c]["y"], dtype=np.float32).reshape(2, S, D)
        if c < 4:
            yp[2 * c] = y[0]
            yp[2 * c + 1] = y[1]
        else:
            ys[c - 4] = y[0]
    return (yp, ys)
```
